# Optimizing a Trainium2 kernel written in Bass

```python
import math
import jax
import jax.numpy as jnp
from jax import lax
import numpy as np

D_MODEL = 1024
BATCH = 8
SEQ = 2048
DEPTH = 4

GRID_W = 64
CTX_LEN = 256
N_EVEN = (DEPTH + 1) // 2
N_ODD = DEPTH // 2
N_VRES = max(N_ODD - 1, 0)

ALPHA = (2.0 * DEPTH) ** 0.25
BETA = (8.0 * DEPTH) ** -0.25
LN_EPS = 1e-6
ROPE_BASE = 10000.0

D_FF = ((8 * D_MODEL + 3 * 256 - 1) // (3 * 256)) * 256

DA_QK = 64
DA_V = 2 * DA_QK
DA_HEADS = (D_MODEL // 2) // DA_V
DA_WIDTH = DA_HEADS * DA_V
Q_BLOCK = 128
RT_QK = 128
RT_V = 128
RT_HEADS = (D_MODEL // 2) // RT_V
RT_WIDTH = RT_HEADS * RT_V
RT_CHUNK = 128
IN_SIZES = (DA_HEADS * 2 * DA_QK, DA_HEADS * 2 * DA_QK, DA_WIDTH,
            RT_HEADS * RT_QK, RT_HEADS * RT_QK, RT_WIDTH, RT_WIDTH)
IN_SPLITS = tuple(int(s) for s in np.cumsum(IN_SIZES)[:-1])
IN_COLS = sum(IN_SIZES)

RW_HEAD = 64
RW_HEADS = D_MODEL // RW_HEAD
RW_DECAY_LORA = max(32, int(round(1.8 * D_MODEL ** 0.5 / 32)) * 32)
RW_AAA_LORA = max(32, int(round(1.8 * D_MODEL ** 0.5 / 32)) * 32)
RW_MV_LORA = max(32, int(round(1.3 * D_MODEL ** 0.5 / 32)) * 32)
RW_GATE_LORA = max(32, int(round(0.6 * D_MODEL ** 0.8 / 32)) * 32)
RW_GN_EPS = 64e-5

kernel_name = "hybrid_diffattn_retention_rwkv7_prefix_dit"


def layer_norm(x, g, b):
    xf = x.astype(jnp.float32)
    mu = jnp.mean(xf, -1, keepdims=True)
    var = jnp.mean(jnp.square(xf - mu), -1, keepdims=True)
    return ((xf - mu) * lax.rsqrt(var + LN_EPS) * g + b).astype(x.dtype)


def heads_rms(x):
    xf = x.astype(jnp.float32)
    return xf * lax.rsqrt(jnp.mean(xf * xf, -1, keepdims=True) + LN_EPS)


def modulate(x, shift, scale):
    return x * (1.0 + scale) + shift


def swiglu(h, w1, w3, w2):
    return (jax.nn.silu(h @ w1) * (h @ w3)) @ w2


def axial_rope_angles(n, dim):
    t = jnp.arange(n)
    row = (t // GRID_W).astype(jnp.float32)
    col = (t % GRID_W).astype(jnp.float32)
    n_freq = dim // 4
    inv = ROPE_BASE ** (-jnp.arange(n_freq, dtype=jnp.float32) / n_freq)
    return jnp.concatenate([row[:, None] * inv, col[:, None] * inv], -1)


def apply_rope(x, ang):
    T = x.shape[1]
    d2 = x.shape[-1] // 2
    shape = (1, T) + (1,) * (x.ndim - 3) + (d2,)
    cos = jnp.cos(ang).reshape(shape).astype(x.dtype)
    sin = jnp.sin(ang).reshape(shape).astype(x.dtype)
    x1, x2 = x[..., :d2], x[..., d2:]
    return jnp.concatenate([x1 * cos - x2 * sin, x1 * sin + x2 * cos], -1)


def diff_scores(q, k, v, lam):
    s = jnp.einsum('bqhmd,bkhmd->bhmqk', q, k).astype(jnp.float32) * (DA_QK ** -0.5)
    p = jax.nn.softmax(s, axis=-1)
    w = p[:, :, 0] - lam * p[:, :, 1]
    return jnp.einsum('bhqk,bkhv->bqhv', w.astype(v.dtype), v)


def diff_attention_blocks(q, k_all, v_all, lam):
    B_, T = q.shape[:2]
    nb = T // Q_BLOCK
    qb = jnp.moveaxis(q.reshape((B_, nb, Q_BLOCK) + q.shape[2:]), 1, 0)
    ob = lax.map(lambda blk: diff_scores(blk, k_all, v_all, lam), qb)
    return jnp.moveaxis(ob, 0, 1).reshape(B_, T, DA_HEADS, DA_V)


def retention_chunkwise(q, k, v, gamma, s0):
    B_, T, H, dk = q.shape
    dv = v.shape[-1]
    C = RT_CHUNK
    n = T // C
    qc = q.reshape(B_, n, C, H, dk)
    kc = k.reshape(B_, n, C, H, dk)
    vc = v.reshape(B_, n, C, H, dv)
    log_g = jnp.log(gamma.astype(jnp.float32))
    idx = jnp.arange(C, dtype=jnp.float32)
    diff = idx[:, None] - idx[None, :]
    decay = jnp.where(diff >= 0, jnp.exp(log_g[:, None, None] * jnp.maximum(diff, 0.0)), 0.0)
    inner = jnp.einsum('bnqhd,bnkhd->bnhqk', qc, kc) * decay
    o_inner = jnp.einsum('bnhqk,bnkhv->bnqhv', inner, vc)
    zeta = jnp.exp(log_g[:, None] * (C - 1.0 - idx))
    u = jnp.einsum('bnkhd,hk,bnkhv->bnhdv', kc, zeta, vc)
    g_chunk = jnp.exp(log_g * C)[:, None, None]

    def step(s, u_i):
        return g_chunk * s + u_i, s

    s_fin, s_prev = lax.scan(step, s0, jnp.moveaxis(u, 1, 0))
    xi = jnp.exp(log_g[:, None] * (idx + 1.0))
    o_cross = jnp.einsum('bnqhd,nbhdv,hq->bnqhv', qc, s_prev, xi)
    return (o_inner + o_cross).reshape(B_, T, H, dv), s_fin


def bi_retention(q_lat, k_lat, v_lat, q_ctx, k_ctx, v_ctx, gam):
    B_ = q_lat.shape[0]
    zero = jnp.zeros((B_, RT_HEADS, RT_QK, RT_V), jnp.float32)
    fl = lambda t: jnp.flip(t, axis=1)
    c_f, s_f = retention_chunkwise(q_ctx, k_ctx, v_ctx, gam[0], zero)
    c_b, s_b = retention_chunkwise(fl(q_ctx), fl(k_ctx), fl(v_ctx), gam[1], zero)
    l_f, _ = retention_chunkwise(q_lat, k_lat, v_lat, gam[0], s_f)
    l_b, _ = retention_chunkwise(fl(q_lat), fl(k_lat), fl(v_lat), gam[1], s_b)
    return l_f + fl(l_b), c_f + fl(c_b)


def even_mixer(h_lat, h_ctx, w_in, w_out, lq1, lk1, lq2, lk2, gn_g, dec_logit, lam_init, with_ctx):
    B_, T, _ = h_lat.shape

    def project(h):
        n = h.shape[1]
        aq, ak, av, bq, bk, bv, bg = jnp.split(h @ w_in, IN_SPLITS, axis=-1)
        return (aq.reshape(B_, n, DA_HEADS, 2, DA_QK), ak.reshape(B_, n, DA_HEADS, 2, DA_QK),
                av.reshape(B_, n, DA_HEADS, DA_V), bq.reshape(B_, n, RT_HEADS, RT_QK),
                bk.reshape(B_, n, RT_HEADS, RT_QK) * (RT_QK ** -0.5), bv.reshape(B_, n, RT_HEADS, RT_V), bg)

    aq, ak, av, bq, bk, bv, bg = project(h_lat)
    caq, cak, cav, cbq, cbk, cbv, cbg = project(h_ctx)
    ang_a = axial_rope_angles(T, DA_QK)
    ang_b = axial_rope_angles(T, RT_QK)
    aq, ak = apply_rope(aq, ang_a), apply_rope(ak, ang_a)
    bq, bk = apply_rope(bq, ang_b), apply_rope(bk, ang_b)

    lam = jnp.exp(jnp.sum(lq1 * lk1)) - jnp.exp(jnp.sum(lq2 * lk2)) + lam_init
    gn = gn_g.reshape(DA_HEADS, DA_V) * (1.0 - lam_init)
    k_all = jnp.concatenate([cak, ak], axis=1)
    v_all = jnp.concatenate([cav, av], axis=1)
    a_lat = (heads_rms(diff_attention_blocks(aq, k_all, v_all, lam)) * gn).reshape(B_, T, DA_WIDTH)

    gam = jax.nn.sigmoid(dec_logit.astype(jnp.float32))
    b_lat, b_ctx = bi_retention(bq, bk, bv, cbq, cbk, cbv, gam)
    b_lat = heads_rms(b_lat).reshape(B_, T, RT_WIDTH) * jax.nn.silu(bg)

    o_lat = (jnp.concatenate([a_lat, b_lat], -1) @ w_out).astype(h_lat.dtype)
    if not with_ctx:
        return o_lat, None
    Tc = h_ctx.shape[1]
    a_ctx = (heads_rms(diff_scores(caq, cak, cav, lam)) * gn).reshape(B_, Tc, DA_WIDTH)
    b_ctx = heads_rms(b_ctx).reshape(B_, Tc, RT_WIDTH) * jax.nn.silu(cbg)
    o_ctx = (jnp.concatenate([a_ctx, b_ctx], -1) @ w_out).astype(h_ctx.dtype)
    return o_lat, o_ctx


def q_shift(x, rows):
    B_, T, D = x.shape
    g = x.reshape(B_, rows, GRID_W, D)
    q = D // 4
    left = jnp.pad(g[:, :, :-1, :q], ((0, 0), (0, 0), (1, 0), (0, 0)))
    right = jnp.pad(g[:, :, 1:, q:2 * q], ((0, 0), (0, 0), (0, 1), (0, 0)))
    up = jnp.pad(g[:, :-1, :, 2 * q:3 * q], ((0, 0), (1, 0), (0, 0), (0, 0)))
    down = jnp.pad(g[:, 1:, :, 3 * q:], ((0, 0), (0, 1), (0, 0), (0, 0)))
    return jnp.concatenate([left, right, up, down], -1).reshape(B_, T, D)


def bi_shift(x):
    h = x.shape[-1] // 2
    prev = jnp.pad(x[:, :-1, :h], ((0, 0), (1, 0), (0, 0)))
    nxt = jnp.pad(x[:, 1:, h:], ((0, 0), (0, 1), (0, 0)))
    return jnp.concatenate([prev, nxt], -1)


def rwkv_project(x, x_shift, v_first, mu, wr, wk, wv, w0, w1, w2, a0, a1, a2, g1, g2, k_k, k_a, vres):
    B_, T, _ = x.shape
    hd = lambda t: t.astype(jnp.float32).reshape(B_, T, RW_HEADS, RW_HEAD)
    xx = x_shift - x
    xr, xw, xk, xv, xa, xg = (x + xx * mu[i] for i in range(6))
    r = xr @ wr
    k = xk @ wk
    v = xv @ wv
    if vres is None:
        v_first = v
    else:
        v0, v1, v2 = vres
        v = v + (v_first - v) * jax.nn.sigmoid(v0 + (xv @ v1) @ v2)
    g = jax.nn.sigmoid(xg @ g1) @ g2
    kk = hd(k * k_k)
    kk = kk / jnp.maximum(jnp.sqrt(jnp.sum(kk * kk, -1, keepdims=True)), 1e-12)
    k_a_h = k_a.astype(jnp.float32).reshape(RW_HEADS, RW_HEAD)
    dirs = []
    for d in range(2):
        w_pre = (w0[d] + jnp.tanh(xw @ w1[d]) @ w2[d]).astype(jnp.float32)
        w_log = -jax.nn.softplus(-w_pre) - 0.5
        decay = hd(jnp.exp(-jnp.exp(w_log)))
        a = hd(jax.nn.sigmoid(a0[d] + (xa @ a1[d]) @ a2[d]))
        kd = hd(k) * (1.0 + (a - 1.0) * k_a_h)
        dirs.append((decay, kd, a))
    return (hd(r), hd(v), g, kk, dirs, v_first)


def wkv7_scan(r, w, k, v, a, b, s0, reverse):
    def step(s, inp):
        r_t, w_t, k_t, v_t, a_t, b_t = inp
        sa = jnp.einsum('bhvk,bhk->bhv', s, a_t)
        s = s * w_t[:, :, None, :] + sa[..., None] * b_t[:, :, None, :] + v_t[..., None] * k_t[:, :, None, :]
        return s, jnp.einsum('bhvk,bhk->bhv', s, r_t)

    xs = tuple(jnp.moveaxis(t, 1, 0) for t in (r, w, k, v, a, b))
    s_fin, y = lax.scan(step, s0, xs, reverse=reverse)
    return jnp.moveaxis(y, 0, 1), s_fin


def rwkv_readout(stream, y, wo, r_k, lnx_g, lnx_b):
    r, v, g, _, dirs, _ = stream
    B_, T = y.shape[:2]
    mu = jnp.mean(y, -1, keepdims=True)
    var = jnp.mean(jnp.square(y - mu), -1, keepdims=True)
    yn = ((y - mu) * lax.rsqrt(var + RW_GN_EPS)).reshape(B_, T, D_MODEL) * lnx_g + lnx_b
    bonus = jnp.sum(r * (dirs[0][1] + dirs[1][1]) * r_k, -1, keepdims=True) * v
    return ((yn + bonus.reshape(B_, T, D_MODEL)) * g) @ wo


def rwkv_mixer(h_lat, h_ctx, vf_lat, vf_ctx, rows, mu, wr, wk, wv, wo, w0, w1, w2, a0, a1, a2,
               g1, g2, k_k, k_a, r_k, lnx_g, lnx_b, vres, with_ctx):
    proj = lambda h, hs, vf: rwkv_project(h, hs, vf, mu, wr, wk, wv, w0, w1, w2, a0, a1, a2,
                                          g1, g2, k_k, k_a, vres)
    lat = proj(h_lat, q_shift(h_lat, rows), vf_lat)
    ctx = proj(h_ctx, bi_shift(h_ctx), vf_ctx)
    zero = jnp.zeros((h_lat.shape[0], RW_HEADS, RW_HEAD, RW_HEAD), jnp.float32)

    def run(stream, d, s0):
        r, v, _, kk, dirs, _ = stream
        decay, kd, a = dirs[d]
        return wkv7_scan(r, decay, kd, v, -kk, kk * a, s0, reverse=(d == 1))

    yc_f, sc_f = run(ctx, 0, zero)
    yc_b, sc_b = run(ctx, 1, zero)
    yl_f, _ = run(lat, 0, sc_f)
    yl_b, _ = run(lat, 1, sc_b)
    o_lat = rwkv_readout(lat, yl_f + yl_b, wo, r_k, lnx_g, lnx_b).astype(h_lat.dtype)
    o_ctx = rwkv_readout(ctx, yc_f + yc_b, wo, r_k, lnx_g, lnx_b).astype(h_ctx.dtype) if with_ctx else None
    return o_lat, o_ctx, lat[5], ctx[5]


def setup_inputs(seed: int = 0) -> dict:
    key = jax.random.key(seed)
    ks = iter(jax.random.split(key, 64))
    nrm = lambda shape, s: jax.random.normal(next(ks), shape, jnp.float32) * s
    unif = lambda shape, lo, hi: jax.random.uniform(next(ks), shape, jnp.float32, lo, hi)
    D = D_MODEL
    inv = D ** -0.5
    g_ret = 1.0 - 2.0 ** (-5.0 - np.arange(RT_HEADS))
    ret_logit = jnp.asarray(np.log(g_ret / (1.0 - g_ret)), jnp.float32)
    return {
        "x": nrm((BATCH, SEQ, D), 1.0),
        "c": nrm((BATCH, D), 1.0),
        "ctx": nrm((BATCH, CTX_LEN, D), 1.0),
        "c_ctx": nrm((D,), 1.0),
        "mod_w": nrm((DEPTH, D, 6 * D), inv),
        "mod_b": nrm((DEPTH, 6 * D), 0.02),
        "ln1_g": 1.0 + nrm((DEPTH, D), 0.02),
        "ln1_b": nrm((DEPTH, D), 0.02),
        "ln2_g": 1.0 + nrm((DEPTH, D), 0.02),
        "ln2_b": nrm((DEPTH, D), 0.02),
        "ffn_w1": nrm((DEPTH, D, D_FF), inv),
        "ffn_w3": nrm((DEPTH, D, D_FF), inv),
        "ffn_w2": nrm((DEPTH, D_FF, D), BETA * D_FF ** -0.5),
        "ev_w_in": nrm((N_EVEN, D, IN_COLS), inv),
        "ev_w_out": nrm((N_EVEN, DA_WIDTH + RT_WIDTH, D), BETA * (DA_WIDTH + RT_WIDTH) ** -0.5),
        "da_lam_q1": nrm((N_EVEN, DA_QK), 0.1),
        "da_lam_k1": nrm((N_EVEN, DA_QK), 0.1),
        "da_lam_q2": nrm((N_EVEN, DA_QK), 0.1),
        "da_lam_k2": nrm((N_EVEN, DA_QK), 0.1),
        "da_gn_g": 1.0 + nrm((N_EVEN, DA_WIDTH), 0.02),
        "rt_decay_logit": ret_logit[None, None, :] + nrm((N_EVEN, 2, RT_HEADS), 0.1),
        "rw_mu": unif((N_ODD, 6, D), 0.0, 1.0),
        "rw_wr": nrm((N_ODD, D, D), inv),
        "rw_wk": nrm((N_ODD, D, D), inv),
        "rw_wv": nrm((N_ODD, D, D), inv),
        "rw_wo": nrm((N_ODD, D, D), BETA * inv),
        "rw_w0": unif((N_ODD, 2, D), -6.0, -1.0),
        "rw_w1": nrm((N_ODD, 2, D, RW_DECAY_LORA), inv),
        "rw_w2": nrm((N_ODD, 2, RW_DECAY_LORA, D), 0.1 * RW_DECAY_LORA ** -0.5),
        "rw_a0": nrm((N_ODD, 2, D), 0.1),
        "rw_a1": nrm((N_ODD, 2, D, RW_AAA_LORA), inv),
        "rw_a2": nrm((N_ODD, 2, RW_AAA_LORA, D), RW_AAA_LORA ** -0.5),
        "rw_v0": 1.0 + nrm((N_VRES, D), 0.1),
        "rw_v1": nrm((N_VRES, D, RW_MV_LORA), inv),
        "rw_v2": nrm((N_VRES, RW_MV_LORA, D), RW_MV_LORA ** -0.5),
        "rw_g1": nrm((N_ODD, D, RW_GATE_LORA), inv),
        "rw_g2": nrm((N_ODD, RW_GATE_LORA, D), RW_GATE_LORA ** -0.5),
        "rw_kk": 0.85 + nrm((N_ODD, D), 0.05),
        "rw_ka": 1.0 + nrm((N_ODD, D), 0.05),
        "rw_rk": nrm((N_ODD, RW_HEADS, RW_HEAD), 0.1),
        "rw_lnx_g": 1.0 + nrm((N_ODD, D), 0.02),
        "rw_lnx_b": nrm((N_ODD, D), 0.02),
    }


def reference(x, c, ctx, c_ctx, mod_w, mod_b, ln1_g, ln1_b, ln2_g, ln2_b, ffn_w1, ffn_w3, ffn_w2,
              ev_w_in, ev_w_out, da_lam_q1, da_lam_k1, da_lam_q2, da_lam_k2, da_gn_g, rt_decay_logit,
              rw_mu, rw_wr, rw_wk, rw_wv, rw_wo, rw_w0, rw_w1, rw_w2, rw_a0, rw_a1, rw_a2,
              rw_v0, rw_v1, rw_v2, rw_g1, rw_g2, rw_kk, rw_ka, rw_rk, rw_lnx_g, rw_lnx_b):
    rows = x.shape[1] // GRID_W
    xc = ctx
    c_act = jax.nn.silu(c)
    cc_act = jax.nn.silu(c_ctx)
    vf_lat = None
    vf_ctx = None
    for l in range(DEPTH):
        last = l == DEPTH - 1
        m = jnp.split((c_act @ mod_w[l] + mod_b[l])[:, None, :], 6, axis=-1)
        mc = jnp.split(cc_act @ mod_w[l] + mod_b[l], 6, axis=-1)
        h_lat = modulate(x, m[0], m[1])
        h_ctx = modulate(xc, mc[0], mc[1])
        if l % 2 == 0:
            e = l // 2
            lam_init = 0.8 - 0.6 * math.exp(-0.3 * l)
            o_lat, o_ctx = even_mixer(h_lat, h_ctx, ev_w_in[e], ev_w_out[e], da_lam_q1[e], da_lam_k1[e],
                                      da_lam_q2[e], da_lam_k2[e], da_gn_g[e], rt_decay_logit[e],
                                      lam_init, not last)
        else:
            j = l // 2
            vres = None if j == 0 else (rw_v0[j - 1], rw_v1[j - 1], rw_v2[j - 1])
            o_lat, o_ctx, vf_lat, vf_ctx = rwkv_mixer(
                h_lat, h_ctx, vf_lat, vf_ctx, rows, rw_mu[j], rw_wr[j], rw_wk[j], rw_wv[j], rw_wo[j],
                rw_w0[j], rw_w1[j], rw_w2[j], rw_a0[j], rw_a1[j], rw_a2[j], rw_g1[j], rw_g2[j],
                rw_kk[j], rw_ka[j], rw_rk[j], rw_lnx_g[j], rw_lnx_b[j], vres, not last)
        x = layer_norm(ALPHA * x + m[2] * o_lat, ln1_g[l], ln1_b[l])
        f_lat = swiglu(modulate(x, m[3], m[4]), ffn_w1[l], ffn_w3[l], ffn_w2[l])
        x = layer_norm(ALPHA * x + m[5] * f_lat, ln2_g[l], ln2_b[l])
        if not last:
            xc = layer_norm(ALPHA * xc + mc[2] * o_ctx, ln1_g[l], ln1_b[l])
            f_ctx = swiglu(modulate(xc, mc[3], mc[4]), ffn_w1[l], ffn_w3[l], ffn_w2[l])
            xc = layer_norm(ALPHA * xc + mc[5] * f_ctx, ln2_g[l], ln2_b[l])
    return x
```

```python
import math
import numpy as np
from contextlib import ExitStack
import concourse.bass as bass
import concourse.mybir as mybir
from concourse.bass_utils import run_bass_kernel_spmd

F32 = mybir.dt.float32
BF16 = mybir.dt.bfloat16
AF = mybir.ActivationFunctionType
ALU = mybir.AluOpType
AX = mybir.AxisListType

NDS = 8
SAME_ENGINE_SYNC = True
EMBED_WAIT = True

D = 1024
TL = 2048
TC = 256
T = TL + TC
NT = T // 128
DFF = 2816
NFC = DFF // 128
DEPTH = 4
ALPHA = (2.0 * DEPTH) ** 0.25
LN_EPS = 1e-6

WEIGHT_SPECS = [
    ("mod_w", (4, 1024, 6144)), ("mod_b", (4, 6144)), ("ln1_g", (4, 1024)), ("ln1_b", (4, 1024)),
    ("ln2_g", (4, 1024)), ("ln2_b", (4, 1024)), ("ffn_w1", (4, 1024, 2816)), ("ffn_w3", (4, 1024, 2816)),
    ("ffn_w2", (4, 2816, 1024)), ("ev_w_in", (2, 1024, 3584)), ("ev_w_out", (2, 1024, 1024)),
    ("da_lam_q1", (2, 64)), ("da_lam_k1", (2, 64)), ("da_lam_q2", (2, 64)), ("da_lam_k2", (2, 64)),
    ("da_gn_g", (2, 512)), ("rt_decay_logit", (2, 2, 4)), ("rw_mu", (2, 6, 1024)),
    ("rw_wr", (2, 1024, 1024)), ("rw_wk", (2, 1024, 1024)), ("rw_wv", (2, 1024, 1024)), ("rw_wo", (2, 1024, 1024)),
    ("rw_w0", (2, 2, 1024)), ("rw_w1", (2, 2, 1024, 64)), ("rw_w2", (2, 2, 64, 1024)),
    ("rw_a0", (2, 2, 1024)), ("rw_a1", (2, 2, 1024, 64)), ("rw_a2", (2, 2, 64, 1024)),
    ("rw_v0", (1, 1024)), ("rw_v1", (1, 1024, 32)), ("rw_v2", (1, 32, 1024)),
    ("rw_g1", (2, 1024, 160)), ("rw_g2", (2, 160, 1024)), ("rw_kk", (2, 1024)), ("rw_ka", (2, 1024)),
    ("rw_rk", (2, 16, 64)), ("rw_lnx_g", (2, 1024)), ("rw_lnx_b", (2, 1024)),
]


class Prog:
    def __init__(self, nc):
        self.nc = nc
        self.engs = {'pe': nc.tensor, 'act': nc.scalar, 'dve': nc.vector, 'pool': nc.gpsimd, 'sp': nc.sync}
        self.es = ExitStack()
        self.sem = {}
        for e in ['pe', 'act', 'dve', 'pool']:
            self.sem[('e', e)] = self.es.enter_context(nc.semaphore('s_' + e))
        for i in range(NDS):
            self.sem[('d', i)] = self.es.enter_context(nc.semaphore('d%d' % i))
        self.cnt = {k: 0 for k in self.sem}
        self.dnext = 0
        self.known = {e: {} for e in self.engs}
        self.res = {}
        self.nops = 0
        self.nwaits = 0

    def _get(self, key):
        name, sub = key if isinstance(key, tuple) else (key, None)
        d = self.res.setdefault(name, {})
        if sub not in d:
            d[sub] = [None, {}]
        return d[sub]

    def _conf(self, key):
        name, sub = key if isinstance(key, tuple) else (key, None)
        d = self.res.setdefault(name, {})
        if sub is None:
            return list(d.values())
        out = []
        if sub in d:
            out.append(d[sub])
        if None in d:
            out.append(d[None])
        return out

    def op(self, eng, fn, r=(), w=(), dma=False, noembed=False):
        deps = {}

        def need(k, v):
            if deps.get(k, 0) < v:
                deps[k] = v
        for key in r:
            for st in self._conf(key):
                if st[0] is not None:
                    need(*st[0])
        for key in w:
            for st in self._conf(key):
                if st[0] is not None:
                    need(*st[0])
                for k, v in st[1].items():
                    need(k, v)
        E = self.engs[eng]
        kn = self.known[eng]
        if dma:
            d = self.dnext
            self.dnext = (d + 1) % NDS
            sk = ('d', d)
            if self.cnt[sk]:
                need(sk, self.cnt[sk])
        else:
            sk = ('e', eng)
        wl = []
        for k, v in deps.items():
            if (not dma) and k == ('e', eng) and (eng == 'pe' or not SAME_ENGINE_SYNC):
                continue
            if kn.get(k, 0) >= v:
                continue
            kn[k] = v
            wl.append((k, v))
        emb = None
        if wl and EMBED_WAIT and not dma and not noembed:
            emb = wl.pop()
        for k, v in wl:
            E.wait_ge(self.sem[k], v)
            self.nwaits += 1
        ins = fn(E)
        if emb is not None:
            ins.wait_op(self.sem[emb[0]], emb[1], "sem-ge")
        inc = 16 if dma else 1
        self.cnt[sk] += inc
        ins.then_inc(self.sem[sk], inc)
        ev = (sk, self.cnt[sk])
        self.nops += 1
        for key in r:
            st = self._get(key)
            if st[1].get(ev[0], 0) < ev[1]:
                st[1][ev[0]] = ev[1]
        for key in w:
            name, sub = key if isinstance(key, tuple) else (key, None)
            if sub is None:
                self.res[name] = {None: [ev, {}]}
            else:
                st = self._get(key)
                st[0] = ev
                st[1] = {}
        return ev

    def dma(self, out, in_, r=(), w=(), eng='sp', **kw):
        return self.op(eng, lambda e: e.dma_start(out=out, in_=in_, **kw), r=r, w=w, dma=True)

    def barrier(self, engines=('pe', 'act', 'dve', 'pool', 'sp')):
        for eng in engines:
            E = self.engs[eng]
            kn = self.known[eng]
            for k, v in self.cnt.items():
                if v and kn.get(k, 0) < v:
                    kn[k] = v
                    E.wait_ge(self.sem[k], v)
                    self.nwaits += 1


class Ctx:
    pass


_SBN = [0]


def sb(nc, es, name, shape, dt=F32):
    _SBN[0] += 1
    return es.enter_context(nc.sbuf_tensor("%s_u%d" % (name, _SBN[0]), list(shape), dt))


_CLN = [0]


def colload(G, es, dst2d, dkey, src_flat, n):
    nc, P = G.nc, G.P
    _CLN[0] += 1
    k = "cl_stg%d" % _CLN[0]
    stg = sb(nc, es, k, [n, 128])
    P.dma(stg[:], src_flat.rearrange("(j p) -> j p", p=128), w=[k])
    P.op('pe', lambda e: e.transpose(out=G.ps[7][:, 0:n], in_=stg[:, :], identity=G.identF[0:n, 0:n]), r=[k, "ident"], w=[("ps", 7)])
    P.op('dve', lambda e: e.tensor_copy(out=dst2d, in_=G.ps[7][:, 0:n]), r=[("ps", 7)], w=[dkey, ("ps", 7)])


def modvec_setup(G, es):
    nc, P = G.nc, G.P
    craw = sb(nc, es, "mv_craw", [128, 2, 8])
    G.mv_cT = sb(nc, es, "mv_cT", [128, 8, 2])
    colload(G, es, craw[:, 0, :], "mv_craw", G.c_d[0, :], 8)
    colload(G, es, craw[:, 1, :], "mv_craw", G.cc_d[0, :], 8)
    for r in range(2):
        P.op('act', lambda e, r=r: e.activation(out=G.mv_cT[:, :, r], in_=craw[:, r, :], func=AF.Silu),
             r=["mv_craw"], w=[("mv_cT", r)])


def modvec_gen(G, l, es, banks=(6, 7), bw=256):
    nc, P = G.nc, G.P
    cT = G.mv_cT
    wt = [sb(nc, es, "mv_w%d" % i, [128, 8, bw]) for i in range(2)]
    bt = [sb(nc, es, "mv_b%d" % i, [2, bw]) for i in range(2)]
    ot = [sb(nc, es, "mv_o%d" % i, [2, bw]) for i in range(2)]
    for nb in range(6144 // bw):
        b = nb % 2
        pbk = banks[b]
        P.dma(wt[b][:], G.Wl("mod_w", l)[:, nb * bw:(nb + 1) * bw].rearrange("(c p) n -> p c n", p=128),
              w=[("mv_w", b)])
        P.dma(bt[b][:], G.Wl("mod_b", l).rearrange("(o n) -> o n", o=1)[:, nb * bw:(nb + 1) * bw].partition_broadcast(2), w=[("mv_b", b)])
        yield
        ps = G.ps[pbk]
        for c in range(8):
            P.op('pe', lambda e, c=c, b=b, ps=ps: e.matmul(ps[0:2, 0:bw], lhsT=cT[:, c, :], rhs=wt[b][:, c, :],
                                                          start=(c == 0), stop=(c == 7)),
                 r=["mv_cT", ("mv_w", b)], w=[("ps", pbk)])
        P.op('dve', lambda e, b=b, ps=ps: e.tensor_tensor(out=ot[b][:], in0=ps[0:2, 0:bw], in1=bt[b][:], op=ALU.add),
             r=[("ps", pbk), ("mv_b", b)], w=[("mv_o", b), ("ps", pbk)])
        P.dma(G.modv[l, :, nb * bw:(nb + 1) * bw], ot[b][:], r=[("mv_o", b)], w=[("modv", l)])
        yield


def stage_modvec(G, layers):
    P = G.P
    for l in layers:
        with ExitStack() as es:
            for _ in modvec_gen(G, l, es, banks=(0, 1), bw=512):
                pass
            P.barrier()


def load_modcols(G, es, l, name):
    nc, P = G.nc, G.P
    mc = sb(nc, es, name, [128, 2, 6, 8])
    for r in range(2):
        _CLN[0] += 1
        k = "cl_stg%d" % _CLN[0]
        stg = sb(nc, es, k, [48, 128])
        P.dma(stg[:], G.modv[l, r, :].rearrange("(j p) -> j p", p=128), r=[("modv", l)], w=[k])
        P.op('pe', lambda e, stg=stg: e.transpose(out=G.ps[7][:, 0:48], in_=stg[:, :], identity=G.identF[0:48, 0:48]), r=[k, "ident"], w=[("ps", 7)])
        P.op('dve', lambda e, r=r: e.tensor_copy(out=mc[:, r, :, :].rearrange("p i c -> p (i c)"), in_=G.ps[7][:, 0:48]), r=[("ps", 7)], w=[name, ("ps", 7)])
    for r in range(2):
        for i in (1, 4):
            P.op('dve', lambda e, r=r, i=i: e.tensor_scalar_add(out=mc[:, r, i, :], in0=mc[:, r, i, :], scalar1=1.0),
                 r=[name], w=[name])
    return mc


def transpose_modulate(G, xt, xkey, hT, hkey, col0, mc, r, ish, isc, pbase):
    P = G.P
    for half in range(2):
        pb = pbase + half
        ps = G.ps[pb]
        for j in range(4):
            c = half * 4 + j
            P.op('pe', lambda e, c=c, j=j, ps=ps: e.transpose(out=ps[:, j * 128:(j + 1) * 128], in_=xt[:, c * 128:(c + 1) * 128],
                                                           identity=G.identF[:]),
                 r=[xkey, "ident"], w=[("ps", pb)])
        for j in range(4):
            c = half * 4 + j
            eng = 'act' if j % 2 == 0 else 'dve'
            if eng == 'act':
                P.op('act', lambda e, c=c, j=j, ps=ps: e.activation(out=hT[:, c, col0:col0 + 128], in_=ps[:, j * 128:(j + 1) * 128],
                                                                   func=AF.Identity, scale=mc[:, r, isc, c:c + 1], bias=mc[:, r, ish, c:c + 1]),
                     r=[("ps", pb), "mc"], w=[hkey, ("ps", pb)])
            else:
                P.op('dve', lambda e, c=c, j=j, ps=ps: e.tensor_scalar(out=hT[:, c, col0:col0 + 128], in0=ps[:, j * 128:(j + 1) * 128],
                                                                      scalar1=mc[:, r, isc, c:c + 1], scalar2=mc[:, r, ish, c:c + 1],
                                                                      op0=ALU.mult, op1=ALU.add),
                     r=[("ps", pb), "mc"], w=[hkey, ("ps", pb)])


def ln_epilogue(G, o_banks, okeys, xt, xkey, Gt, LGt, LBt, bkeys, tmp, tkey, st, skey, yt, ykey):
    P = G.P
    for h in range(2):
        sl = slice(h * 512, (h + 1) * 512)
        P.op('dve', lambda e, h=h, sl=sl: e.tensor_tensor(out=tmp[:, sl], in0=o_banks[h][:, :], in1=Gt[:, sl], op=ALU.mult),
             r=[okeys[h]] + bkeys, w=[tkey, okeys[h]])
    P.op('dve', lambda e: e.scalar_tensor_tensor(out=tmp[:, :], in0=xt[:, :], scalar=ALPHA, in1=tmp[:, :], op0=ALU.mult, op1=ALU.add),
         r=[xkey, tkey], w=[tkey])
    for h in range(2):
        P.op('dve', lambda e, h=h: e.bn_stats(out=st[:, h * 6:(h + 1) * 6], in_=tmp[:, h * 512:(h + 1) * 512]), r=[tkey], w=[skey])
    P.op('dve', lambda e: e.bn_aggr(out=st[:, 12:14], in_=st[:, 0:12]), r=[skey], w=[skey])
    P.op('act', lambda e: e.activation(out=st[:, 14:15], in_=st[:, 13:14], func=AF.Sqrt, bias=G.eps_ln[:, 0:1], scale=1.0), r=[skey, "consts"], w=[skey])
    P.op('dve', lambda e: e.reciprocal(out=st[:, 15:16], in_=st[:, 14:15]), r=[skey], w=[skey])
    P.op('dve', lambda e: e.tensor_scalar(out=tmp[:, :], in0=tmp[:, :], scalar1=st[:, 12:13], scalar2=st[:, 15:16],
                                          op0=ALU.subtract, op1=ALU.mult), r=[skey, tkey], w=[tkey])
    P.op('pool', lambda e: e.tensor_tensor(out=tmp[:, :], in0=tmp[:, :], in1=LGt[:, :], op=ALU.mult), r=[tkey] + bkeys, w=[tkey])
    P.op('pool', lambda e: e.tensor_tensor(out=yt[:, :], in0=tmp[:, :], in1=LBt[:, :], op=ALU.add), r=[tkey] + bkeys, w=[ykey])


def load_bcast(G, tile, key, src_row):
    G.P.dma(tile[:], src_row.partition_broadcast(128), w=[key])


def stage_ffn(G, l, side=None):
    nc, P = G.nc, G.P
    W1, W3, W2 = G.Wl("ffn_w1", l), G.Wl("ffn_w3", l), G.Wl("ffn_w2", l)
    with ExitStack() as es:
        mc = load_modcols(G, es, l, "mc")
        gate = [sb(nc, es, "f_gate%d" % r, [128, 1024]) for r in range(2)]
        LG = sb(nc, es, "f_lg", [128, 1024])
        LB = sb(nc, es, "f_lb", [128, 1024])
        for r in range(2):
            P.dma(gate[r][:], G.modv[l, r:r + 1, 5 * 1024:6 * 1024].partition_broadcast(128), r=[("modv", l)], w=["f_bc"])
        P.dma(LG[:], G.Wl("ln2_g", l).rearrange("(o n) -> o n", o=1).partition_broadcast(128), w=["f_bc"])
        P.dma(LB[:], G.Wl("ln2_b", l).rearrange("(o n) -> o n", o=1).partition_broadcast(128), w=["f_bc"])
        w2b = sb(nc, es, "f_w2b", [128, NFC, 1024], BF16)
        w2s = [sb(nc, es, "f_w2s%d" % i, [128, 2, 1024]) for i in range(2)]
        for i in range(NFC // 2):
            b = i % 2
            P.dma(w2s[b][:], W2[i * 256:(i + 1) * 256, :].rearrange("(c p) n -> p c n", p=128), w=[("f_w2s", b)])
            P.op('pool', lambda e, i=i, b=b: e.tensor_copy(out=w2b[:, 2 * i:2 * i + 2, :], in_=w2s[b][:]),
                 r=[("f_w2s", b)], w=[("f_w2b", i)])
        NH = 3
        TPH = NT // NH
        TOKH = TPH * 128
        hT = sb(nc, es, "f_hT", [128, 8, TOKH], BF16)
        gT = sb(nc, es, "f_gT", [128, NFC, TOKH], BF16)
        xt = [sb(nc, es, "f_x%d" % i, [128, 1024]) for i in range(2)]
        w1s = [sb(nc, es, "f_w1s%d" % i, [128, 8, 128]) for i in range(2)]
        w3s = [sb(nc, es, "f_w3s%d" % i, [128, 8, 128]) for i in range(2)]
        w1b = [sb(nc, es, "f_w1b%d" % i, [128, 8, 128], BF16) for i in range(2)]
        w3b = [sb(nc, es, "f_w3b%d" % i, [128, 8, 128], BF16) for i in range(2)]
        sa = [sb(nc, es, "f_sa%d" % i, [128, 384]) for i in range(2)]
        tmp = [sb(nc, es, "f_tmp%d" % i, [128, 1024]) for i in range(2)]
        yt = [sb(nc, es, "f_y%d" % i, [128, 1024]) for i in range(2)]
        st = [sb(nc, es, "f_st%d" % i, [128, 16]) for i in range(2)]
        xi = 0
        wi = 0
        gen = side(es) if side is not None else None
        for hf in range(NH):
            for tt in range(TPH):
                t = hf * TPH + tt
                b = xi % 2
                xi += 1
                r = 0 if t < 16 else 1
                P.dma(xt[b][:], G.xres[t * 128:(t + 1) * 128, :], r=[("xres", t)], w=[("f_x", b)])
                transpose_modulate(G, xt[b], ("f_x", b), hT, ("f_hT", tt), tt * 128, mc, r, 3, 4, 0)
            for cb in range(NFC):
                if gen is not None:
                    next(gen, None)
                    next(gen, None)
                b = wi % 2
                wi += 1
                P.dma(w1s[b][:], W1[:, cb * 128:(cb + 1) * 128].rearrange("(c p) n -> p c n", p=128), w=[("f_w1s", b)])
                P.dma(w3s[b][:], W3[:, cb * 128:(cb + 1) * 128].rearrange("(c p) n -> p c n", p=128), w=[("f_w3s", b)])
                P.op('pool', lambda e, b=b: e.tensor_copy(out=w1b[b][:], in_=w1s[b][:]), r=[("f_w1s", b)], w=[("f_w1b", b)])
                P.op('pool', lambda e, b=b: e.tensor_copy(out=w3b[b][:], in_=w3s[b][:]), r=[("f_w3s", b)], w=[("f_w3b", b)])
                for sub in range(1):
                    fc = cb
                    for tb in range(2):
                        tsl = slice(tb * 384, (tb + 1) * 384)
                        k = (fc * 2 + tb) % 2
                        pa, pb_ = 2 + 2 * k, 3 + 2 * k
                        for c in range(8):
                            P.op('pe', lambda e, c=c, pa=pa, b=b, sub=sub, tsl=tsl: e.matmul(
                                G.ps[pa][:, 0:384], lhsT=w1b[b][:, c, sub * 128:(sub + 1) * 128], rhs=hT[:, c, tsl],
                                start=(c == 0), stop=(c == 7)), r=[("f_w1b", b), "f_hT"], w=[("ps", pa)])
                        for c in range(8):
                            P.op('pe', lambda e, c=c, pb_=pb_, b=b, sub=sub, tsl=tsl: e.matmul(
                                G.ps[pb_][:, 0:384], lhsT=w3b[b][:, c, sub * 128:(sub + 1) * 128], rhs=hT[:, c, tsl],
                                start=(c == 0), stop=(c == 7)), r=[("f_w3b", b), "f_hT"], w=[("ps", pb_)])
                        P.op('act', lambda e, k=k, pa=pa: e.activation(out=sa[k][:, :], in_=G.ps[pa][:, 0:384], func=AF.Silu),
                             r=[("ps", pa)], w=[("f_sa", k), ("ps", pa)])
                        P.op('dve', lambda e, k=k, pb_=pb_, fc=fc, tsl=tsl: e.tensor_tensor(
                            out=gT[:, fc, tsl], in0=G.ps[pb_][:, 0:384], in1=sa[k][:, :], op=ALU.mult),
                            r=[("ps", pb_), ("f_sa", k)], w=[("f_gT", fc), ("ps", pb_)])
            for tt in range(TPH):
                t = hf * TPH + tt
                b = xi % 2
                xi += 1
                r = 0 if t < 16 else 1
                P.dma(xt[b][:], G.xres[t * 128:(t + 1) * 128, :], r=[("xres", t)], w=[("f_x", b)])
                k = tt % 2
                banks = [G.ps[0 + 2 * k], G.ps[1 + 2 * k]]
                okeys = [("ps", 0 + 2 * k), ("ps", 1 + 2 * k)]
                for h in range(2):
                    for fc in range(NFC):
                        P.op('pe', lambda e, h=h, fc=fc, tt=tt, banks=banks: e.matmul(
                            banks[h][:, :], lhsT=gT[:, fc, tt * 128:(tt + 1) * 128], rhs=w2b[:, fc, h * 512:(h + 1) * 512],
                            start=(fc == 0), stop=(fc == NFC - 1)), r=["f_gT", "f_w2b"], w=[okeys[h]])
                ln_epilogue(G, banks, okeys, xt[b], ("f_x", b), gate[r], LG, LB, ["f_bc"], tmp[k], ("f_tmp", k),
                            st[k], ("f_st", k), yt[k], ("f_y", k))
                P.dma(G.xres[t * 128:(t + 1) * 128, :], yt[k][:], r=[("f_y", k)], w=[("xres", t)])
        if gen is not None:
            for _ in gen:
                pass
        P.barrier()


def row(ap):
    return ap.rearrange("(o n) -> o n", o=1)


def inproj_block(G, es_w, Wap, j0, ncols, hT, hkey, wtag):
    nc, P = G.nc, G.P
    ws = sb(nc, es_w, wtag + "_s", [128, 8, ncols])
    wb = sb(nc, es_w, wtag + "_b", [128, 8, ncols], BF16)
    P.dma(ws[:], Wap[:, j0:j0 + ncols].rearrange("(c p) n -> p c n", p=128), w=[wtag + "_s"])
    P.op('pool', lambda e: e.tensor_copy(out=wb[:], in_=ws[:]), r=[wtag + "_s"], w=[wtag + "_b"])
    return wb


def rope_evac(G, ps, pkey, t, qr, qkey, cosT, sinT, ckey, tmp, tkey, half, scale=None):
    P = G.P
    ng = 512 // (2 * half)
    if t >= 16:
        if scale is None:
            P.op('act', lambda e: e.copy(out=qr[:, :], in_=ps[:, :]), r=[pkey], w=[qkey, pkey])
        else:
            P.op('act', lambda e: e.mul(out=qr[:, :], in_=ps[:, :], mul=scale), r=[pkey], w=[qkey, pkey])
        return
    pv = ps[:, :].rearrange("p (g two d) -> p g two d", g=ng, two=2)
    qv = qr[:, :].rearrange("p (g two d) -> p g two d", g=ng, two=2)
    x1, x2 = pv[:, :, 0, :], pv[:, :, 1, :]
    cs = cosT[:, :].rearrange("p (g d) -> p g d", g=ng)
    sn = sinT[:, :].rearrange("p (g d) -> p g d", g=ng)
    t1 = tmp[:, 0:256].rearrange("p (g d) -> p g d", g=ng)
    t2 = tmp[:, 256:512].rearrange("p (g d) -> p g d", g=ng)
    P.op('dve', lambda e: e.tensor_tensor(out=t1, in0=x1, in1=cs, op=ALU.mult), r=[pkey, ckey], w=[tkey])
    P.op('dve', lambda e: e.tensor_tensor(out=t2, in0=x2, in1=sn, op=ALU.mult), r=[pkey, ckey], w=[tkey])
    P.op('dve', lambda e: e.tensor_tensor(out=qv[:, :, 0, :], in0=t1, in1=t2, op=ALU.subtract), r=[tkey], w=[qkey])
    P.op('dve', lambda e: e.tensor_tensor(out=t1, in0=x1, in1=sn, op=ALU.mult), r=[pkey, ckey], w=[tkey])
    P.op('dve', lambda e: e.tensor_tensor(out=t2, in0=x2, in1=cs, op=ALU.mult), r=[pkey, ckey], w=[tkey, pkey])
    P.op('dve', lambda e: e.tensor_tensor(out=qv[:, :, 1, :], in0=t1, in1=t2, op=ALU.add), r=[tkey], w=[qkey])
    if scale is not None:
        P.op('act', lambda e: e.mul(out=qr[:, :], in_=qr[:, :], mul=scale), r=[qkey], w=[qkey])


def transpose4(G, src, skey, dstT, dkey, t, pbank):
    P = G.P
    psb = G.ps[pbank][:, 0:256].bitcast(BF16)
    for g in range(4):
        P.op('pe', lambda e, g=g: e.transpose(out=psb[:, g * 128:(g + 1) * 128], in_=src[:, g * 128:(g + 1) * 128], identity=G.identB[:]),
             r=[skey, "identB"], w=[("ps", pbank)])
    P.op('act', lambda e: e.copy(out=dstT[:, :, t * 128:(t + 1) * 128], in_=psb.rearrange("p (g n) -> p g n", g=4)),
         r=[("ps", pbank)], w=[dkey, ("ps", pbank)])


def stage_even(G, l):
    nc, P = G.nc, G.P
    e = l // 2
    lam_init = 0.8 - 0.6 * math.exp(-0.3 * l)
    Win, Wout = G.Wl("ev_w_in", e), G.Wl("ev_w_out", e)
    mo_d = G.mo_d
    with ExitStack() as es:
        mc = load_modcols(G, es, l, "mc")
        hT = sb(nc, es, "e_hT", [128, 8, T], BF16)
        xt = [sb(nc, es, "e_x%d" % i, [128, 1024]) for i in range(2)]
        for t in range(NT):
            b = t % 2
            P.dma(xt[b][:], G.xres[t * 128:(t + 1) * 128, :], r=[("xres", t)], w=[("e_x", b)])
            transpose_modulate(G, xt[b], ("e_x", b), hT, ("e_hT", t), t * 128, mc, 0 if t < 16 else 1, 0, 1, 0)
        cosT = [sb(nc, es, "e_cos%d" % i, [128, 256]) for i in range(2)]
        sinT = [sb(nc, es, "e_sin%d" % i, [128, 256]) for i in range(2)]
        qr = [sb(nc, es, "e_qr%d" % i, [128, 512], BF16) for i in range(2)]
        rtmp = [sb(nc, es, "e_rtmp%d" % i, [128, 512]) for i in range(2)]

        def proj_tile(wb, wkey, t, pbank):
            ps = G.ps[pbank]
            for c in range(8):
                P.op('pe', lambda e_, c=c: e_.matmul(ps[:, :], lhsT=hT[:, c, t * 128:(t + 1) * 128], rhs=wb[:, c, :],
                                                    start=(c == 0), stop=(c == 7)), r=[("e_hT", t), wkey], w=[("ps", pbank)])
            return ps

        def load_tables(t, b, which):
            if t < 16:
                P.dma(cosT[b][:], G.KC("k_cos" + which, [TL, 256])[t * 128:(t + 1) * 128, :], w=[("e_cs", b)])
                P.dma(sinT[b][:], G.KC("k_sin" + which, [TL, 256])[t * 128:(t + 1) * 128, :], w=[("e_cs", b)])

        with ExitStack() as esA:
            aqT = sb(nc, esA, "a_qT", [128, 4, T], BF16)
            akT = sb(nc, esA, "a_kT", [128, 4, T], BF16)
            av = sb(nc, esA, "a_v", [128, NT, 512], BF16)
            for j, dst in ((0, aqT), (1, akT)):
                with ExitStack() as esw:
                    wb = inproj_block(G, esw, Win, j * 512, 512, hT, "e_hT", "a_w%d" % j)
                    for t in range(NT):
                        b = t % 2
                        load_tables(t, b, "A")
                        ps = proj_tile(wb, "a_w%d_b" % j, t, 4 + b)
                        rope_evac(G, ps, ("ps", 4 + b), t, qr[b], ("e_qr", b), cosT[b], sinT[b], ("e_cs", b), rtmp[b], ("e_rtmp", b), 32)
                        transpose4(G, qr[b], ("e_qr", b), dst, ("a_T%d" % j, t), t, 6 + b)
                    P.barrier()
            with ExitStack() as esw:
                wb = inproj_block(G, esw, Win, 2 * 512, 512, hT, "e_hT", "a_w2")
                for t in range(NT):
                    b = t % 2
                    ps = proj_tile(wb, "a_w2_b", t, 4 + b)
                    P.op('act', lambda e_, t=t, ps=ps: e_.copy(out=av[:, t, :], in_=ps[:, :]), r=[("ps", 4 + b)], w=[("a_v", t), ("ps", 4 + b)])
                P.barrier()
            lam4 = sb(nc, esA, "a_lam4", [128, 4, 64])
            lamc = sb(nc, esA, "a_lamc", [128, 8])
            for i, nm in enumerate(("da_lam_q1", "da_lam_k1", "da_lam_q2", "da_lam_k2")):
                P.dma(lam4[:, i, :], row(G.Wl(nm, e)).partition_broadcast(128), w=["a_lam4"])
            for i in range(2):
                P.op('dve', lambda e_, i=i: e_.tensor_tensor(out=lam4[:, 2 * i, :], in0=lam4[:, 2 * i, :], in1=lam4[:, 2 * i + 1, :], op=ALU.mult),
                     r=["a_lam4"], w=["a_lam4"])
                P.op('dve', lambda e_, i=i: e_.reduce_sum(out=lamc[:, i:i + 1], in_=lam4[:, 2 * i, :], axis=AX.X), r=["a_lam4"], w=["a_lamc"])
            P.op('act', lambda e_: e_.activation(out=lamc[:, 2:4], in_=lamc[:, 0:2], func=AF.Exp), r=["a_lamc"], w=["a_lamc"])
            P.op('dve', lambda e_: e_.tensor_tensor(out=lamc[:, 4:5], in0=lamc[:, 3:4], in1=lamc[:, 2:3], op=ALU.subtract), r=["a_lamc"], w=["a_lamc"])
            P.op('dve', lambda e_: e_.tensor_scalar_add(out=lamc[:, 5:6], in0=lamc[:, 4:5], scalar1=-lam_init), r=["a_lamc"], w=["a_lamc"])
            neglam = lamc[:, 5:6]
            Pm = [sb(nc, esA, "a_P%d" % i, [128, T], BF16) for i in range(2)]
            PT = [sb(nc, esA, "a_PT%d" % i, [128, NT, 128], BF16) for i in range(2)]
            sm = [sb(nc, esA, "a_sm%d" % i, [128, 16]) for i in range(2)]
            ao = sb(nc, esA, "a_ao", [128, 512])
            ao2 = sb(nc, esA, "a_ao2", [128, 512])
            aob = [sb(nc, esA, "a_aob%d" % i, [128, 512], BF16) for i in range(2)]
            rs = sb(nc, esA, "a_rs", [128, 16])
            SC = 64 ** -0.5
            def kinfo(qt):
                ktiles = list(range(NT)) if qt < 16 else [16, 17]
                k0 = ktiles[0] * 128
                nk = len(ktiles) * 128
                return ktiles, k0, nk, (nk + 511) // 512

            def stage1(ui, qt, h, m):
                ktiles, k0, nk, nbk = kinfo(qt)
                pb_ = ui % 2
                smt, Pmt = sm[pb_], Pm[pb_]
                psl = slice(m * 64, (m + 1) * 64)
                for jb in range(nbk):
                    w_ = min(512, nk - jb * 512)
                    P.op('pe', lambda e_, jb=jb, w_=w_: e_.matmul(G.ps[jb][:, 0:w_], lhsT=aqT[psl, h, qt * 128:(qt + 1) * 128],
                                                                 rhs=akT[psl, h, k0 + jb * 512:k0 + jb * 512 + w_], start=True, stop=True),
                         r=[("a_T0", qt), "a_T1"], w=[("ps", jb)])
                for jb in range(nbk):
                    w_ = min(512, nk - jb * 512)
                    P.op('dve', lambda e_, jb=jb, w_=w_: e_.reduce_max(out=smt[:, jb:jb + 1], in_=G.ps[jb][:, 0:w_], axis=AX.X),
                         r=[("ps", jb)], w=[("a_sm", pb_), ("ps", jb)])
                P.op('dve', lambda e_: e_.reduce_max(out=smt[:, 6:7], in_=smt[:, 0:nbk], axis=AX.X), r=[("a_sm", pb_)], w=[("a_sm", pb_)])
                P.op('dve', lambda e_: e_.tensor_scalar_mul(out=smt[:, 7:8], in0=smt[:, 6:7], scalar1=-SC), r=[("a_sm", pb_)], w=[("a_sm", pb_)])
                for jb in range(nbk):
                    w_ = min(512, nk - jb * 512)
                    P.op('act', lambda e_, jb=jb, w_=w_: e_.activation(out=Pmt[:, jb * 512:jb * 512 + w_], in_=G.ps[jb][:, 0:w_], func=AF.Exp,
                                                                      scale=SC, bias=smt[:, 7:8], accum_out=smt[:, 8 + jb:9 + jb]),
                         r=[("ps", jb), ("a_sm", pb_)], w=[("a_P", pb_), ("a_sm", pb_), ("ps", jb)], noembed=True)
                P.op('act', lambda e_: e_.copy(out=smt[:, 0:nbk], in_=smt[:, 8:8 + nbk]), r=[("a_sm", pb_)], w=[("a_sm", pb_)])
                P.op('dve', lambda e_: e_.reduce_sum(out=smt[:, 14:15], in_=smt[:, 0:nbk], axis=AX.X), r=[("a_sm", pb_)], w=[("a_sm", pb_)])
                P.op('dve', lambda e_: e_.reciprocal(out=smt[:, 15:16], in_=smt[:, 14:15]), r=[("a_sm", pb_)], w=[("a_sm", pb_)])

            def stage2(ui, qt, h, m):
                ktiles, k0, nk, nbk = kinfo(qt)
                pb_ = ui % 2
                smt, Pmt, PTt = sm[pb_], Pm[pb_], PT[pb_]
                nkt = len(ktiles)
                for g0 in range(0, nkt, 8):
                    gb = 5 + (g0 // 8) % 2
                    psb = G.ps[gb][:, :].bitcast(BF16)
                    n_ = min(8, nkt - g0)
                    for i in range(n_):
                        P.op('pe', lambda e_, i=i, g0=g0, psb=psb: e_.transpose(out=psb[:, i * 128:(i + 1) * 128],
                                                                             in_=Pmt[:, (g0 + i) * 128:(g0 + i + 1) * 128], identity=G.identB[:]),
                             r=[("a_P", pb_), "identB"], w=[("ps", gb)])
                    evac_eng = 'dve' if (g0 // 8) % 2 == 0 else 'act'
                    if evac_eng == 'dve':
                        P.op('dve', lambda e_, g0=g0, n_=n_, psb=psb: e_.tensor_copy(out=PTt[:, g0:g0 + n_, :],
                                                                                     in_=psb[:, 0:n_ * 128].rearrange("p (g n) -> p g n", g=n_)),
                             r=[("ps", gb)], w=[("a_PT", pb_), ("ps", gb)])
                    else:
                        P.op('act', lambda e_, g0=g0, n_=n_, psb=psb: e_.copy(out=PTt[:, g0:g0 + n_, :],
                                                                              in_=psb[:, 0:n_ * 128].rearrange("p (g n) -> p g n", g=n_)),
                             r=[("ps", gb)], w=[("a_PT", pb_), ("ps", gb)])
                osl = slice(m * 128, (m + 1) * 128)
                for i, kt in enumerate(ktiles):
                    P.op('pe', lambda e_, i=i, kt=kt: e_.matmul(G.ps[7][:, osl], lhsT=PTt[:, i, :], rhs=av[:, kt, h * 128:(h + 1) * 128],
                                                               start=(i == 0), stop=(i == nkt - 1)),
                         r=[("a_PT", pb_), "a_v"], w=[("ps", 7)])
                if m == 0:
                    P.op('dve', lambda e_: e_.tensor_scalar(out=ao2[:, 0:128], in0=G.ps[7][:, 0:128], scalar1=smt[:, 15:16], scalar2=None, op0=ALU.mult),
                         r=[("ps", 7), ("a_sm", pb_)], w=["a_ao2", ("ps", 7)])
                else:
                    P.op('dve', lambda e_: e_.tensor_tensor(out=smt[:, 13:14], in0=smt[:, 15:16], in1=neglam, op=ALU.mult),
                         r=[("a_sm", pb_), "a_lamc"], w=[("a_sm", pb_)])
                    P.op('dve', lambda e_: e_.scalar_tensor_tensor(out=ao[:, h * 128:(h + 1) * 128], in0=G.ps[7][:, 128:256], scalar=smt[:, 13:14],
                                                                  in1=ao2[:, 0:128], op0=ALU.mult, op1=ALU.add),
                         r=[("ps", 7), ("a_sm", pb_), "a_ao2"], w=["a_ao", ("ps", 7)])
                if h == 3 and m == 1:
                    P.op('pool', lambda e_: e_.tensor_tensor(out=ao2[:, :], in0=ao[:, :], in1=ao[:, :], op=ALU.mult), r=["a_ao"], w=["a_ao2"])
                    P.op('dve', lambda e_: e_.reduce_sum(out=rs[:, 0:4], in_=ao2[:, :].rearrange("p (h d) -> p h d", h=4), axis=AX.X), r=["a_ao2"], w=["a_rs"])
                    P.op('act', lambda e_: e_.activation(out=rs[:, 4:8], in_=rs[:, 0:4], func=AF.Sqrt, scale=1.0 / 128, bias=G.eps_ln[:, 0:1]), r=["a_rs", "consts"], w=["a_rs"])
                    P.op('dve', lambda e_: e_.reciprocal(out=rs[:, 8:12], in_=rs[:, 4:8]), r=["a_rs"], w=["a_rs"])
                    ab = aob[qt % 2]
                    for hh in range(4):
                        P.op('pool', lambda e_, hh=hh: e_.tensor_scalar(out=ab[:, hh * 128:(hh + 1) * 128], in0=ao[:, hh * 128:(hh + 1) * 128],
                                                                      scalar1=rs[:, 8 + hh:9 + hh], scalar2=None, op0=ALU.mult),
                             r=["a_ao", "a_rs"], w=[("a_aob", qt % 2)])
                    P.dma(mo_d[qt * 128:(qt + 1) * 128, 0:512], ab[:, :], r=[("a_aob", qt % 2)], w=[("mo", (qt, 0))])

            units = [(qt, h, m) for qt in range(NT) for h in range(4) for m in range(2)]
            for ui, u_ in enumerate(units):
                stage1(ui, *u_)
                if ui > 0:
                    stage2(ui - 1, *units[ui - 1])
            stage2(len(units) - 1, *units[-1])
            P.barrier()
        with ExitStack() as esB:
            bqT = sb(nc, esB, "b_qT", [128, 4, T], BF16)
            bkT = sb(nc, esB, "b_kT", [128, 4, T], BF16)
            bk = sb(nc, esB, "b_k", [128, NT, 512], BF16)
            bv = sb(nc, esB, "b_v", [128, NT, 512], BF16)
            for j in (3, 4):
                with ExitStack() as esw:
                    wb = inproj_block(G, esw, Win, j * 512, 512, hT, "e_hT", "b_w%d" % j)
                    for t in range(NT):
                        b = t % 2
                        load_tables(t, b, "B")
                        ps = proj_tile(wb, "b_w%d_b" % j, t, 4 + b)
                        if j == 3:
                            rope_evac(G, ps, ("ps", 4 + b), t, qr[b], ("e_qr", b), cosT[b], sinT[b], ("e_cs", b), rtmp[b], ("e_rtmp", b), 64)
                            transpose4(G, qr[b], ("e_qr", b), bqT, ("b_qT", t), t, 6 + b)
                        else:
                            rope_evac(G, ps, ("ps", 4 + b), t, bk[:, t, :], ("b_k", t), cosT[b], sinT[b], ("e_cs", b), rtmp[b], ("e_rtmp", b), 64,
                                      scale=128 ** -0.5)
                            transpose4(G, bk[:, t, :], ("b_k", t), bkT, ("b_kT", t), t, 6 + b)
                    P.barrier()
            with ExitStack() as esw:
                wb = inproj_block(G, esw, Win, 5 * 512, 512, hT, "e_hT", "b_w5")
                for t in range(NT):
                    b = t % 2
                    ps = proj_tile(wb, "b_w5_b", t, 4 + b)
                    P.op('act', lambda e_, t=t, ps=ps: e_.copy(out=bv[:, t, :], in_=ps[:, :]), r=[("ps", 4 + b)], w=[("b_v", t), ("ps", 4 + b)])
                P.barrier()
            dc = sb(nc, esB, "b_dc", [128, 64])
            kcol = sb(nc, esB, "b_kcol", [128, 4])
            kmat = sb(nc, esB, "b_kmat", [128, 4, 128])
            Mm = sb(nc, esB, "b_M", [128, 4, 128])
            Mt = sb(nc, esB, "b_Mt", [128, 128])
            P.dma(dc[:, 0:8], G.Wl("rt_decay_logit", e).rearrange("(o a) b -> o (a b)", o=1).partition_broadcast(128), w=["b_dc"])
            P.dma(kcol[:], G.KC("k_cols", [128, 4])[:, :], w=["b_kc"])
            P.dma(kmat[:], G.KC("k_mats", [128, 4, 128])[:, :, :], w=["b_kc"])
            P.op('act', lambda e_: e_.activation(out=dc[:, 8:16], in_=dc[:, 0:8], func=AF.Sigmoid), r=["b_dc"], w=["b_dc"])
            P.op('act', lambda e_: e_.activation(out=dc[:, 16:24], in_=dc[:, 8:16], func=AF.Ln), r=["b_dc"], w=["b_dc"])
            lg = lambda d, h: dc[:, 16 + d * 4 + h:17 + d * 4 + h]
            P.op('act', lambda e_: e_.activation(out=dc[:, 24:32], in_=dc[:, 16:24], func=AF.Exp, scale=128.0), r=["b_dc"], w=["b_dc"])
            for h in range(4):
                for (o_, col, d) in ((32, 0, 0), (36, 1, 1), (40, 2, 0), (44, 3, 1)):
                    P.op('act', lambda e_, o_=o_, col=col, d=d, h=h: e_.activation(out=dc[:, o_ + h:o_ + h + 1], in_=kcol[:, col:col + 1], func=AF.Exp, scale=lg(d, h)),
                         r=["b_dc", "b_kc"], w=["b_dc"])
                P.op('act', lambda e_, h=h: e_.activation(out=Mt[:, :], in_=kmat[:, 0, :], func=AF.Exp, scale=lg(0, h)), r=["b_dc", "b_kc"], w=["b_Mt"])
                P.op('dve', lambda e_, h=h: e_.tensor_tensor(out=Mm[:, h, :], in0=Mt[:, :], in1=kmat[:, 2, :], op=ALU.mult), r=["b_Mt", "b_kc"], w=["b_M"])
                P.op('act', lambda e_, h=h: e_.activation(out=Mt[:, :], in_=kmat[:, 1, :], func=AF.Exp, scale=lg(1, h)), r=["b_dc", "b_kc"], w=["b_Mt"])
                P.op('dve', lambda e_, h=h: e_.tensor_tensor(out=Mt[:, :], in0=Mt[:, :], in1=kmat[:, 3, :], op=ALU.mult), r=["b_Mt", "b_kc"], w=["b_Mt"])
                P.op('dve', lambda e_, h=h: e_.tensor_tensor(out=Mm[:, h, :], in0=Mm[:, h, :], in1=Mt[:, :], op=ALU.add), r=["b_Mt", "b_M"], w=["b_M"])
            SfA = sb(nc, esB, "b_SfA", [128, NT, 512], BF16)
            Sst = sb(nc, esB, "b_S", [128, 512])
            Sbb = sb(nc, esB, "b_Sbb", [128, 512], BF16)
            kz = [sb(nc, esB, "b_kz%d" % i, [128, 512], BF16) for i in range(2)]

            def state_update(n, d, i):
                zb = kz[i % 2]
                for h in range(4):
                    P.op('dve', lambda e_, h=h: e_.tensor_scalar(out=zb[:, h * 128:(h + 1) * 128], in0=bk[:, n, h * 128:(h + 1) * 128],
                                                                scalar1=dc[:, 32 + 4 * d + h:33 + 4 * d + h], scalar2=None, op0=ALU.mult),
                         r=[("b_k", n), "b_dc"], w=[("b_kz", i % 2)])
                for h in range(4):
                    P.op('pe', lambda e_, h=h: e_.matmul(G.ps[3][:, h * 128:(h + 1) * 128], lhsT=zb[:, h * 128:(h + 1) * 128],
                                                        rhs=bv[:, n, h * 128:(h + 1) * 128], start=True, stop=True),
                         r=[("b_kz", i % 2), ("b_v", n)], w=[("ps", 3)])
                for h in range(4):
                    P.op('dve', lambda e_, h=h: e_.scalar_tensor_tensor(out=Sst[:, h * 128:(h + 1) * 128], in0=Sst[:, h * 128:(h + 1) * 128],
                                                                       scalar=dc[:, 24 + 4 * d + h:25 + 4 * d + h], in1=G.ps[3][:, h * 128:(h + 1) * 128],
                                                                       op0=ALU.mult, op1=ALU.add),
                         r=["b_S", "b_dc", ("ps", 3)], w=["b_S", ("ps", 3)])
            P.op('dve', lambda e_: e_.memset(Sst[:, :], 0.0), w=["b_S"])
            fwd = [16, 17] + list(range(16))
            for i, n in enumerate(fwd):
                P.op('act', lambda e_, n=n: e_.copy(out=SfA[:, n, :], in_=Sst[:, :]), r=["b_S"], w=[("b_SfA", n)])
                if i < len(fwd) - 1:
                    state_update(n, 0, i)
            P.op('dve', lambda e_: e_.memset(Sst[:, :], 0.0), r=["b_SfA"], w=["b_S"])
            with ExitStack() as esw:
                wg = inproj_block(G, esw, Win, 6 * 512, 512, hT, "e_hT", "b_w6")
                Sm = [sb(nc, esw, "b_Sm%d" % i, [128, 4, 128], BF16) for i in range(2)]
                bo = [sb(nc, esw, "b_o%d" % i, [128, 512]) for i in range(2)]
                bo2 = [sb(nc, esw, "b_o2%d" % i, [128, 512]) for i in range(2)]
                gs = [sb(nc, esw, "b_gs%d" % i, [128, 512]) for i in range(2)]
                bob = [sb(nc, esw, "b_ob%d" % i, [128, 512], BF16) for i in range(2)]
                brs = [sb(nc, esw, "b_rs%d" % i, [128, 16]) for i in range(2)]
                bwd = [17, 16] + list(range(15, -1, -1))
                for i, n in enumerate(bwd):
                    b = i % 2
                    csl = slice(n * 128, (n + 1) * 128)
                    P.op('act', lambda e_: e_.copy(out=Sbb[:, :], in_=Sst[:, :]), r=["b_S"], w=["b_Sbb"])
                    for h in range(4):
                        P.op('pe', lambda e_, h=h: e_.matmul(G.ps[0][:, h * 128:(h + 1) * 128], lhsT=bkT[:, h, csl], rhs=bqT[:, h, csl], start=True, stop=True),
                             r=[("b_kT", n), ("b_qT", n)], w=[("ps", 0)])
                    P.op('dve', lambda e_, b=b: e_.tensor_tensor(out=Sm[b][:, :, :], in0=G.ps[0][:, :].rearrange("p (h n) -> p h n", h=4), in1=Mm[:, :, :], op=ALU.mult),
                         r=[("ps", 0), "b_M"], w=[("b_Sm", b), ("ps", 0)])
                    for h in range(4):
                        hs = slice(h * 128, (h + 1) * 128)
                        P.op('pe', lambda e_, h=h, hs=hs, b=b: e_.matmul(G.ps[1][:, hs], lhsT=Sm[b][:, h, :], rhs=bv[:, n, hs], start=True, stop=True),
                             r=[("b_Sm", b), ("b_v", n)], w=[("ps", 1)])
                    for h in range(4):
                        hs = slice(h * 128, (h + 1) * 128)
                        P.op('pe', lambda e_, h=h, hs=hs: e_.matmul(G.ps[2][:, hs], lhsT=bqT[:, h, csl], rhs=SfA[:, n, hs], start=True, stop=True),
                             r=[("b_qT", n), ("b_SfA", n)], w=[("ps", 2)])
                    for h in range(4):
                        hs = slice(h * 128, (h + 1) * 128)
                        P.op('pe', lambda e_, h=h, hs=hs: e_.matmul(G.ps[4][:, hs], lhsT=bqT[:, h, csl], rhs=Sbb[:, hs], start=True, stop=True),
                             r=[("b_qT", n), "b_Sbb"], w=[("ps", 4)])
                    for c in range(8):
                        P.op('pe', lambda e_, c=c: e_.matmul(G.ps[5][:, :], lhsT=hT[:, c, csl], rhs=wg[:, c, :], start=(c == 0), stop=(c == 7)),
                             r=[("e_hT", n), "b_w6_b"], w=[("ps", 5)])
                    P.op('act', lambda e_, b=b: e_.activation(out=gs[b][:, :], in_=G.ps[5][:, :], func=AF.Silu), r=[("ps", 5)], w=[("b_gs", b), ("ps", 5)])
                    for h in range(4):
                        hs = slice(h * 128, (h + 1) * 128)
                        P.op('dve', lambda e_, h=h, hs=hs, b=b: e_.tensor_scalar(out=bo2[b][:, hs], in0=G.ps[2][:, hs], scalar1=dc[:, 40 + h:41 + h], scalar2=None, op0=ALU.mult),
                             r=[("ps", 2), "b_dc"], w=[("b_o2", b), ("ps", 2)])
                        P.op('dve', lambda e_, h=h, hs=hs, b=b: e_.scalar_tensor_tensor(out=bo2[b][:, hs], in0=G.ps[4][:, hs], scalar=dc[:, 44 + h:45 + h], in1=bo2[b][:, hs],
                                                                                       op0=ALU.mult, op1=ALU.add),
                             r=[("ps", 4), "b_dc", ("b_o2", b)], w=[("b_o2", b), ("ps", 4)])
                    P.op('dve', lambda e_, b=b: e_.tensor_tensor(out=bo[b][:, :], in0=G.ps[1][:, :], in1=bo2[b][:, :], op=ALU.add),
                         r=[("ps", 1), ("b_o2", b)], w=[("b_o", b), ("ps", 1)])
                    P.op('dve', lambda e_, b=b: e_.tensor_tensor(out=bo2[b][:, :], in0=bo[b][:, :], in1=bo[b][:, :], op=ALU.mult), r=[("b_o", b)], w=[("b_o2", b)])
                    P.op('dve', lambda e_, b=b: e_.reduce_sum(out=brs[b][:, 0:4], in_=bo2[b][:, :].rearrange("p (h d) -> p h d", h=4), axis=AX.X), r=[("b_o2", b)], w=[("b_rs", b)])
                    P.op('act', lambda e_, b=b: e_.activation(out=brs[b][:, 4:8], in_=brs[b][:, 0:4], func=AF.Sqrt, scale=1.0 / 128, bias=G.eps_ln[:, 0:1]), r=[("b_rs", b), "consts"], w=[("b_rs", b)])
                    P.op('dve', lambda e_, b=b: e_.reciprocal(out=brs[b][:, 8:12], in_=brs[b][:, 4:8]), r=[("b_rs", b)], w=[("b_rs", b)])
                    for h in range(4):
                        hs = slice(h * 128, (h + 1) * 128)
                        P.op('dve', lambda e_, h=h, hs=hs, b=b: e_.scalar_tensor_tensor(out=bob[b][:, hs], in0=bo[b][:, hs], scalar=brs[b][:, 8 + h:9 + h], in1=gs[b][:, hs],
                                                                                       op0=ALU.mult, op1=ALU.mult),
                             r=[("b_o", b), ("b_rs", b), ("b_gs", b)], w=[("b_ob", b)])
                    P.dma(mo_d[n * 128:(n + 1) * 128, 512:1024], bob[b][:, :], r=[("b_ob", b)], w=[("mo", (n, 1))])
                    if i < len(bwd) - 1:
                        state_update(n, 1, i)
                P.barrier()
            P.barrier()
        with ExitStack() as esO:
            wos = sb(nc, esO, "o_ws", [128, 8, 1024])
            wob = sb(nc, esO, "o_wb", [128, 8, 1024], BF16)
            gn = sb(nc, esO, "o_gn", [128, 4])
            P.dma(wos[:], Wout[:, :].rearrange("(c p) n -> p c n", p=128), w=["o_ws"])
            colload(G, esO, gn[:, :], "o_gn", G.Wl("da_gn_g", e), 4)
            P.op('dve', lambda e_: e_.tensor_scalar_mul(out=gn[:], in0=gn[:], scalar1=1.0 - lam_init), r=["o_gn"], w=["o_gn"])
            for c in range(8):
                if c < 4:
                    P.op('dve', lambda e_, c=c: e_.tensor_scalar(out=wob[:, c, :], in0=wos[:, c, :], scalar1=gn[:, c:c + 1], scalar2=None, op0=ALU.mult),
                         r=["o_ws", "o_gn"], w=[("o_wb", c)])
                else:
                    P.op('pool', lambda e_, c=c: e_.tensor_copy(out=wob[:, c, :], in_=wos[:, c, :]), r=["o_ws"], w=[("o_wb", c)])
            gate = [sb(nc, esO, "o_gate%d" % r_, [128, 1024]) for r_ in range(2)]
            LG = sb(nc, esO, "o_lg", [128, 1024])
            LB = sb(nc, esO, "o_lb", [128, 1024])
            for r_ in range(2):
                P.dma(gate[r_][:], G.modv[l, r_:r_ + 1, 2 * 1024:3 * 1024].partition_broadcast(128), r=[("modv", l)], w=["o_bc"])
            P.dma(LG[:], row(G.Wl("ln1_g", l)).partition_broadcast(128), w=["o_bc"])
            P.dma(LB[:], row(G.Wl("ln1_b", l)).partition_broadcast(128), w=["o_bc"])
            mot = [sb(nc, esO, "o_mo%d" % i, [128, 1024], BF16) for i in range(2)]
            moT = [sb(nc, esO, "o_moT%d" % i, [128, 8, 128], BF16) for i in range(2)]
            tmp = [sb(nc, esO, "o_tmp%d" % i, [128, 1024]) for i in range(2)]
            yt = [sb(nc, esO, "o_y%d" % i, [128, 1024]) for i in range(2)]
            st = [sb(nc, esO, "o_st%d" % i, [128, 16]) for i in range(2)]
            for t in range(NT):
                b = t % 2
                r_ = 0 if t < 16 else 1
                P.dma(mot[b][:], mo_d[t * 128:(t + 1) * 128, :], r=[("mo", (t, 0)), ("mo", (t, 1))], w=[("o_mo", b)])
                P.dma(xt[b][:], G.xres[t * 128:(t + 1) * 128, :], r=[("xres", t)], w=[("e_x", b)])
                for hh in range(2):
                    pbk = 4 + hh
                    psb = G.ps[pbk][:, 0:256].bitcast(BF16)
                    for g in range(4):
                        c = hh * 4 + g
                        P.op('pe', lambda e_, g=g, c=c, psb=psb: e_.transpose(out=psb[:, g * 128:(g + 1) * 128], in_=mot[b][:, c * 128:(c + 1) * 128], identity=G.identB[:]),
                             r=[("o_mo", b), "identB"], w=[("ps", pbk)])
                    P.op('act', lambda e_, hh=hh, psb=psb: e_.copy(out=moT[b][:, hh * 4:hh * 4 + 4, :], in_=psb.rearrange("p (g n) -> p g n", g=4)),
                         r=[("ps", pbk)], w=[("o_moT", b), ("ps", pbk)])
                banks = [G.ps[0 + 2 * b], G.ps[1 + 2 * b]]
                okeys = [("ps", 0 + 2 * b), ("ps", 1 + 2 * b)]
                for hh in range(2):
                    for c in range(8):
                        P.op('pe', lambda e_, hh=hh, c=c: e_.matmul(banks[hh][:, :], lhsT=moT[b][:, c, :], rhs=wob[:, c, hh * 512:(hh + 1) * 512],
                                                                   start=(c == 0), stop=(c == 7)), r=[("o_moT", b), "o_wb"], w=[okeys[hh]])
                ln_epilogue(G, banks, okeys, xt[b], ("e_x", b), gate[r_], LG, LB, ["o_bc"], tmp[b], ("o_tmp", b), st[b], ("o_st", b), yt[b], ("o_y", b))
                P.dma(G.xres[t * 128:(t + 1) * 128, :], yt[b][:], r=[("o_y", b)], w=[("xres", t)])
            P.barrier()
        P.barrier()


TBLK = [(0, 512), (512, 512), (1024, 512), (1536, 512), (2048, 256)]
RW_GN_EPS = 64e-5


def colvec(G, es, name, ap1024, key):
    t = sb(G.nc, es, name, [128, 8])
    colload(G, es, t[:, :], key, ap1024, 8)
    return t


def load_w_bf16(G, es, name, wap, rows, cols, eng='pool'):
    nc, P = G.nc, G.P
    nck = (rows + 127) // 128
    wb = sb(nc, es, name + "_b", [128, nck, cols], BF16)
    if rows % 128 == 0 and cols > 512:
        with ExitStack() as es2:
            ws = sb(nc, es2, name + "_s", [128, nck, 512])
            for h0 in range(0, cols, 512):
                P.dma(ws[:], wap[:, h0:h0 + 512].rearrange("(c p) n -> p c n", p=128), w=[name + "_s"])
                P.op(eng, lambda e, h0=h0: e.tensor_copy(out=wb[:, :, h0:h0 + 512], in_=ws[:]), r=[name + "_s"], w=[name + "_b"])
            P.barrier()
        return wb
    ws = sb(nc, es, name + "_s", [128, nck, cols])
    if rows % 128 == 0:
        P.dma(ws[:], wap.rearrange("(c p) n -> p c n", p=128), w=[name + "_s"])
        P.op(eng, lambda e: e.tensor_copy(out=wb[:], in_=ws[:]), r=[name + "_s"], w=[name + "_b"])
    else:
        for c in range(nck):
            n_ = min(128, rows - c * 128)
            P.dma(ws[0:n_, c, :], wap[c * 128:c * 128 + n_, :], w=[name + "_s"])
        for c in range(nck):
            n_ = min(128, rows - c * 128)
            P.op(eng, lambda e, c=c, n_=n_: e.tensor_copy(out=wb[0:n_, c, :], in_=ws[0:n_, c, :]), r=[name + "_s"], w=[name + "_b"])
    return wb


def fm_linear(G, xT, xkey, wb, wkey, nout, consumer, kparts=None, pbanks=(0, 1)):
    P = G.P
    if kparts is None:
        kparts = [(c, 128) for c in range(8)]
    it = 0
    for oc in range((nout + 127) // 128):
        m_ = min(128, nout - oc * 128)
        for (t0, tn) in TBLK:
            pb = pbanks[it % len(pbanks)]
            it += 1
            for i, (c, kn) in enumerate(kparts):
                P.op('pe', lambda e, c=c, kn=kn, i=i, pb=pb: e.matmul(G.ps[pb][0:m_, 0:tn], lhsT=wb[0:kn, c, oc * 128:oc * 128 + m_], rhs=xT[0:kn, c, t0:t0 + tn],
                                                                   start=(i == 0), stop=(i == len(kparts) - 1)), r=[xkey, wkey], w=[("ps", pb)])
            consumer(oc, t0, tn, G.ps[pb], ("ps", pb), m_)


F32R = mybir.dt.float32r
NCHAIN = 4
SKIP_SCAN = False
SCAN_F32R = False
STAGGER = 6


def rr(ap):
    return ap.bitcast(F32R) if SCAN_F32R else ap


def scan_stage(G):
    nc, P = G.nc, G.P
    FM = G.fm
    C = 64
    NCH = T // C
    with ExitStack() as esS:
        msk = sb(nc, esS, "s_msk", [128, 3, 128])
        P.dma(msk[:], G.KC("k_smask", [128, 3, 128])[:, :, :], w=["s_msk"])
        ones = sb(nc, esS, "s_ones", [128, 64])
        P.op('dve', lambda e: e.memset(ones[:], 1.0), w=["s_ones"])
        identR = sb(nc, esS, "s_identR", [128, 128])
        P.op('dve', lambda e: e.tensor_copy(out=rr(identR[:]), in_=G.identF[:]), r=["ident"], w=["s_identR"])
        NAMES = ("r", "v", "kk", "lw", "kd", "b")
        bufs = []
        for ci in range(NCHAIN):
            B = Ctx()
            B.src = [sb(nc, esS, "s_src%d_%d" % (ci, i), [128, 6, 256]) for i in range(2)]
            B.bd = sb(nc, esS, "s_bd%d" % ci, [128, 5, 128])
            P.op('pool', lambda e, B=B: e.memset(B.bd[:], 0.0), w=[("s_bd", ci)])
            B.cum = sb(nc, esS, "s_cum%d" % ci, [128, 4, 64])
            B.Am = sb(nc, esS, "s_Am%d" % ci, [128, 5, 128])
            B.Ap = [sb(nc, esS, "s_Ap%d_%d" % (ci, i), [128, 2, 128]) for i in range(2)]
            B.VT = sb(nc, esS, "s_VT%d" % ci, [128, 64])
            B.Z = sb(nc, esS, "s_Z%d" % ci, [128, 2, 64])
            B.BT = sb(nc, esS, "s_BT%d" % ci, [128, 2, 128])
            B.S = sb(nc, esS, "s_S%d" % ci, [128, 64])
            B.Sr = sb(nc, esS, "s_Sr%d" % ci, [128, 64])
            B.yo = [sb(nc, esS, "s_yo%d_%d" % (ci, i), [64, 2, 64]) for i in range(2)]
            bufs.append(B)

        def chain(ci, p, d):
            B = bufs[ci]
            P0, P1 = G.ps[2 * ci], G.ps[2 * ci + 1]
            k0, k1 = ("ps", 2 * ci), ("ps", 2 * ci + 1)
            K = lambda nm, sub=None: ("s%d_%s" % (ci, nm), sub)
            fmn = {"r": "rT", "v": "vT", "kk": "kk", "lw": "lw%d" % d, "kd": "kd%d" % d, "b": "b%d" % d}
            P.op('dve', lambda e: e.memset(B.S[:], 0.0), w=[K("S")])
            P.op('act', lambda e: e.copy(out=rr(B.Sr[:, :]), in_=B.S[:, :]), r=[K("S")], w=[K("Sr")])
            R_, K_, B_, A_, V_ = (B.bd[:, i, :] for i in range(5))
            for n in range(NCH):
                if n < TC // C:
                    t0 = TL + n * C if d == 0 else T - (n + 1) * C
                else:
                    m_ = n - TC // C
                    t0 = m_ * C if d == 0 else TL - (m_ + 1) * C
                blk0 = (t0 // 256) * 256
                sbi = (n // 4) % 2
                if n % 4 == 0:
                    for i, nm in enumerate(NAMES):
                        P.dma(B.src[sbi][:, i, :], FM[fmn[nm]][p, :, blk0:blk0 + 256], r=[("fm_" + fmn[nm], p)], w=[K("src", sbi)])
                off = t0 - blk0
                sk = K("src", sbi)

                def tsl(i):
                    a_ = B.src[sbi][:, i, off:off + C]
                    return a_ if d == 0 else a_[:, ::-1]
                cm = B.cum
                ck = K("cum")
                P.op('dve', lambda e: e.tensor_tensor_scan(out=cm[:, 0, :], data0=ones[:, :], data1=tsl(3), initial=0.0, op0=ALU.mult, op1=ALU.add), r=[sk, "s_ones"], w=[ck])
                P.op('act', lambda e: e.activation(out=cm[:, 1, :], in_=cm[:, 0, :], func=AF.Exp), r=[ck], w=[ck])
                P.op('act', lambda e: e.activation(out=cm[:, 2, :], in_=cm[:, 0, :], func=AF.Exp, scale=-1.0), r=[ck], w=[ck])
                P.op('pool', lambda e: e.tensor_tensor(out=cm[:, 3, :], in0=cm[:, 0, :], in1=tsl(3), op=ALU.subtract), r=[ck, sk], w=[ck])
                P.op('act', lambda e: e.activation(out=cm[:, 3, :], in_=cm[:, 3, :], func=AF.Exp), r=[ck], w=[ck])
                yield
                bk = K("bd")
                for h in range(2):
                    hp = slice(h * 64, (h + 1) * 64)
                    P.op('dve', lambda e, hp=hp: e.tensor_tensor(out=rr(R_[hp, hp]), in0=tsl(0)[hp, :], in1=cm[hp, 1, :], op=ALU.mult), r=[sk, ck], w=[K("bd", 0)])
                    P.op('dve', lambda e, hp=hp: e.tensor_tensor(out=rr(K_[hp, hp]), in0=tsl(4)[hp, :], in1=cm[hp, 2, :], op=ALU.mult), r=[sk, ck], w=[K("bd", 1)])
                    P.op('dve', lambda e, hp=hp: e.tensor_tensor(out=rr(B_[hp, hp]), in0=tsl(5)[hp, :], in1=cm[hp, 2, :], op=ALU.mult), r=[sk, ck], w=[K("bd", 2)])
                    P.op('dve', lambda e, hp=hp: e.scalar_tensor_tensor(out=rr(A_[hp, hp]), in0=tsl(2)[hp, :], scalar=-1.0, in1=cm[hp, 3, :], op0=ALU.mult, op1=ALU.mult),
                         r=[sk, ck], w=[K("bd", 3)])
                    P.op('act', lambda e, hp=hp: e.copy(out=rr(V_[hp, hp]), in_=tsl(1)[hp, :]), r=[sk], w=[K("bd", 4)])
                yield
                A = B.Am
                specs = [(0, 2, 3, 0), (1, 3, 2, 1), (2, 1, 3, 0), (3, 2, 0, 2), (4, 1, 0, 2)]
                for (ai, li, ri, mi) in specs:
                    pt, pk = (P0, k0) if ai < 4 else (P1, k1)
                    sl = slice((ai % 4) * 128, (ai % 4 + 1) * 128)
                    P.op('pe', lambda e, li=li, ri=ri, pt=pt, sl=sl: e.matmul(pt[:, sl], lhsT=rr(B.bd[:, li, :]), rhs=rr(B.bd[:, ri, :]), start=True, stop=True),
                         r=[K("bd", li), K("bd", ri)], w=[pk])
                P.op('pe', lambda e: e.transpose(out=P1[:, 128:256], in_=V_, identity=G.identF[:]), r=[K("bd", 4), "ident"], w=[k1])
                yield
                for (ai, li, ri, mi) in specs:
                    pt, pk = (P0, k0) if ai < 4 else (P1, k1)
                    sl = slice((ai % 4) * 128, (ai % 4 + 1) * 128)
                    P.op('dve', lambda e, ai=ai, pt=pt, sl=sl, mi=mi: e.tensor_tensor(out=rr(A[:, ai, :]), in0=pt[:, sl], in1=msk[:, mi, :], op=ALU.mult),
                         r=[pk, "s_msk"], w=[K("Am", ai), pk])
                for h in range(2):
                    hp = slice(h * 64, (h + 1) * 64)
                    P.op('act', lambda e, hp=hp, h=h: e.copy(out=rr(B.VT[hp, :]), in_=P1[hp, 128 + h * 64:128 + (h + 1) * 64]), r=[k1], w=[K("VT"), k1])
                yield
                zs = slice(256, 320)
                P.op('pe', lambda e: e.matmul(P1[:, zs], lhsT=rr(A_), rhs=rr(B.Sr[:, :]), start=True, stop=False), r=[K("bd", 3), K("Sr")], w=[k1])
                P.op('pe', lambda e: e.matmul(P1[:, zs], lhsT=rr(A[:, 2, :]), rhs=rr(B.VT[:, :]), start=False, stop=True), r=[K("Am", 2), K("VT")], w=[k1])
                yield
                P.op('act', lambda e: e.copy(out=rr(B.Z[:, 0, :]), in_=P1[:, zs]), r=[k1], w=[K("Z", 0), k1])
                yield
                zc = 0
                curA, curAT = (A[:, 0, :], K("Am", 0)), (A[:, 1, :], K("Am", 1))
                for step in range(6):
                    P.op('pe', lambda e, zc=zc, curA=curA: e.matmul(P1[:, zs], lhsT=rr(curA[0]), rhs=rr(B.Z[:, zc, :]), start=True, stop=True), r=[curA[1], K("Z", zc)], w=[k1])
                    if step < 5:
                        dstt = B.Ap[step % 2]
                        dn = "Ap%d" % (step % 2)
                        P.op('pe', lambda e, curA=curA, curAT=curAT: e.matmul(P0[:, 0:128], lhsT=rr(curAT[0]), rhs=rr(curA[0]), start=True, stop=True), r=[curA[1], curAT[1]], w=[k0])
                        if step < 4:
                            P.op('pe', lambda e, curA=curA, curAT=curAT: e.matmul(P0[:, 128:256], lhsT=rr(curA[0]), rhs=rr(curAT[0]), start=True, stop=True), r=[curA[1], curAT[1]], w=[k0])
                    yield
                    P.op('dve', lambda e, zc=zc: e.tensor_tensor(out=rr(B.Z[:, 1 - zc, :]), in0=P1[:, zs], in1=B.Z[:, zc, :], op=ALU.add), r=[k1, K("Z", zc)], w=[K("Z", 1 - zc), k1])
                    zc = 1 - zc
                    if step < 5:
                        if step < 4:
                            P.op('act', lambda e, dstt=dstt: e.copy(out=rr(dstt[:, :, :]), in_=P0[:, 0:256].rearrange("p (a n) -> p a n", a=2)), r=[k0], w=[K(dn), k0])
                        else:
                            P.op('act', lambda e, dstt=dstt: e.copy(out=rr(dstt[:, 0, :]), in_=P0[:, 0:128]), r=[k0], w=[K(dn), k0])
                        curA, curAT = (dstt[:, 0, :], K(dn)), (dstt[:, 1, :], K(dn))
                    yield
                UT = B.Z[:, zc, :]
                uk = K("Z", zc)
                ys = slice(384, 512)
                P.op('pe', lambda e: e.matmul(P1[0:64, ys], lhsT=rr(B.Sr[:, :]), rhs=rr(R_), start=True, stop=False), r=[K("Sr"), K("bd", 0)], w=[k1])
                P.op('pe', lambda e: e.matmul(P1[0:64, ys], lhsT=rr(UT), rhs=rr(A[:, 3, :]), start=False, stop=False), r=[uk, K("Am", 3)], w=[k1])
                P.op('pe', lambda e: e.matmul(P1[0:64, ys], lhsT=rr(B.VT[:, :]), rhs=rr(A[:, 4, :]), start=False, stop=True), r=[K("VT"), K("Am", 4)], w=[k1])
                if n < NCH - 1:
                    P.op('pe', lambda e: e.transpose(out=P0[:, 256:384], in_=B_, identity=G.identF[:]), r=[K("bd", 2), "ident"], w=[k0])
                    P.op('pe', lambda e: e.transpose(out=P0[:, 384:512], in_=K_, identity=G.identF[:]), r=[K("bd", 1), "ident"], w=[k0])
                yield
                yq = B.yo[n % 2]
                yv = P1[0:64, ys].rearrange("p (h t) -> p h t", h=2)
                yov = yq[:, :, :] if d == 0 else yq[:, :, ::-1]
                P.op('act', lambda e: e.copy(out=yov, in_=yv), r=[k1], w=[K("yo", n % 2), k1])
                P.dma(G.y_d[d, :, 2 * p:2 * p + 2, t0:t0 + C], yq[:, :, :], r=[K("yo", n % 2)], w=[("y_d", (d, p, t0 // 128))])
                if n < NCH - 1:
                    P.op('dve', lambda e: e.tensor_copy(out=rr(B.BT[:, :, :]), in_=P0[:, 256:512].rearrange("p (a n) -> p a n", a=2)), r=[k0], w=[K("BT"), k0])
                    yield
                    P.op('pe', lambda e: e.matmul(P1[:, 0:64], lhsT=rr(B.BT[:, 0, :]), rhs=rr(UT), start=True, stop=False), r=[K("BT"), uk], w=[k1])
                    P.op('pe', lambda e: e.matmul(P1[:, 0:64], lhsT=rr(B.BT[:, 1, :]), rhs=rr(B.VT[:, :]), start=False, stop=True), r=[K("BT"), K("VT")], w=[k1])
                    yield
                    P.op('dve', lambda e: e.tensor_tensor(out=B.S[:, :], in0=B.S[:, :], in1=P1[:, 0:64], op=ALU.add), r=[K("S"), k1], w=[K("S"), k1])
                    P.op('dve', lambda e: e.tensor_scalar(out=B.S[:, :], in0=B.S[:, :], scalar1=cm[:, 1, 63:64], scalar2=None, op0=ALU.mult), r=[K("S"), ck], w=[K("S")])
                    P.op('act', lambda e: e.copy(out=rr(B.Sr[:, :]), in_=B.S[:, :]), r=[K("S")], w=[K("Sr")])
                yield

        todo = [(p, d) for p in range(8) for d in range(2)]
        active = [None] * NCHAIN
        for ci in range(NCHAIN):
            p, d = todo.pop(0)
            active[ci] = chain(ci, p, d)
            for _ in range(ci * STAGGER):
                next(active[ci])
        while todo or any(a is not None for a in active):
            for ci in range(NCHAIN):
                if active[ci] is None and todo:
                    p, d = todo.pop(0)
                    active[ci] = chain(ci, p, d)
                if active[ci] is not None:
                    try:
                        next(active[ci])
                    except StopIteration:
                        active[ci] = None
        P.barrier()


def stage_rwkv(G, l):
    nc, P = G.nc, G.P
    j = l // 2
    FM = G.fm
    with ExitStack() as es:
        mc = load_modcols(G, es, l, "mc")
        BO = sb(nc, es, "r_BO", [128, 128])
        P.dma(BO[:], G.KC("k_bo", [128, 128])[:, :], w=["r_BO"])
        with ExitStack() as esP:
            hT = sb(nc, esP, "r_hT", [128, 8, T], BF16)
            dT = sb(nc, esP, "r_dT", [128, 8, T], BF16)
            xiT = sb(nc, esP, "r_xiT", [128, 8, T], BF16)
            with ExitStack() as esx:
                xt = [sb(nc, esx, "r_x%d" % i, [128, 1024]) for i in range(2)]
                for t in range(NT):
                    b = t % 2
                    P.dma(xt[b][:], G.xres[t * 128:(t + 1) * 128, :], r=[("xres", t)], w=[("r_x", b)])
                    transpose_modulate(G, xt[b], ("r_x", b), hT, ("r_hT", t), t * 128, mc, 0 if t < 16 else 1, 0, 1, 0)
                P.barrier()
            def lat(tile_, c0, c1):
                return tile_[:, c0:c1, 0:TL].rearrange("p c (r w) -> p c r w", w=64)
            hl = lambda c0, c1: lat(hT, c0, c1)
            dl = lambda c0, c1: lat(dT, c0, c1)
            sub = ALU.subtract
            ops = [
                (dl(0, 2)[:, :, :, 1:64], hl(0, 2)[:, :, :, 0:63], hl(0, 2)[:, :, :, 1:64]),
                (dl(2, 4)[:, :, :, 0:63], hl(2, 4)[:, :, :, 1:64], hl(2, 4)[:, :, :, 0:63]),
                (dl(4, 6)[:, :, 1:32, :], hl(4, 6)[:, :, 0:31, :], hl(4, 6)[:, :, 1:32, :]),
                (dl(6, 8)[:, :, 0:31, :], hl(6, 8)[:, :, 1:32, :], hl(6, 8)[:, :, 0:31, :]),
                (dT[:, 0:4, TL + 1:T], hT[:, 0:4, TL:T - 1], hT[:, 0:4, TL + 1:T]),
                (dT[:, 4:8, TL:T - 1], hT[:, 4:8, TL + 1:T], hT[:, 4:8, TL:T - 1]),
            ]
            for (o_, a_, b_) in ops:
                for cc in range(o_.shape[1]):
                    P.op('dve', lambda e, o_=o_, a_=a_, b_=b_, cc=cc: e.tensor_tensor(out=o_[:, cc], in0=a_[:, cc], in1=b_[:, cc], op=sub), r=["r_hT"], w=["r_dT"])
            bnd = [
                (dl(0, 2)[:, :, :, 0:1], hl(0, 2)[:, :, :, 0:1]), (dl(2, 4)[:, :, :, 63:64], hl(2, 4)[:, :, :, 63:64]),
                (dl(4, 6)[:, :, 0:1, :], hl(4, 6)[:, :, 0:1, :]), (dl(6, 8)[:, :, 31:32, :], hl(6, 8)[:, :, 31:32, :]),
                (dT[:, 0:4, TL:TL + 1], hT[:, 0:4, TL:TL + 1]), (dT[:, 4:8, T - 1:T], hT[:, 4:8, T - 1:T]),
            ]
            for (o_, a_) in bnd:
                for cc in range(o_.shape[1]):
                    P.op('dve', lambda e, o_=o_, a_=a_, cc=cc: e.tensor_scalar_mul(out=o_[:, cc], in0=a_[:, cc], scalar1=-1.0), r=["r_hT"], w=["r_dT"])
            mu = sb(nc, esP, "r_mu", [128, 6, 8])
            colload(G, esP, mu[:, :, :].rearrange("p i c -> p (i c)"), "r_mu", G.Wl("rw_mu", j).rearrange("i n -> (i n)"), 48)
            kkc = colvec(G, esP, "r_kkc", G.Wl("rw_kk", j), "r_cv")
            kac = colvec(G, esP, "r_kac", G.Wl("rw_ka", j), "r_cv")
            omka = sb(nc, esP, "r_omka", [128, 8])
            P.op('dve', lambda e: e.tensor_scalar(out=omka[:], in0=kac[:], scalar1=-1.0, scalar2=1.0, op0=ALU.mult, op1=ALU.add), r=["r_cv"], w=["r_cv2"])
            stg = [sb(nc, esP, "r_stg%d" % i, [128, T]) for i in range(2)]
            stg2 = [sb(nc, esP, "r_stg2_0", [128, T])] * 2
            ld1 = [sb(nc, esP, "r_ld1_0", [128, T])] * 2
            ld2 = [sb(nc, esP, "r_ld2_0", [128, T])] * 2
            tmpa = [sb(nc, esP, "r_tmpa%d" % i, [128, 512]) for i in range(2)]
            tmpb = [sb(nc, esP, "r_tmpb%d" % i, [128, 512]) for i in range(2)]
            tcnt = [0]

            def mk_xi(i):
                for c in range(8):
                    P.op('dve', lambda e, c=c: e.scalar_tensor_tensor(out=xiT[:, c, :], in0=dT[:, c, :], scalar=mu[:, i, c:c + 1], in1=hT[:, c, :], op0=ALU.mult, op1=ALU.add),
                         r=["r_dT", "r_hT", "r_mu"], w=["r_xiT"])

            def store(oc, dst, src, skey):
                P.dma(dst[oc, :, :], src[:, :], r=[skey], w=[(dst.name if hasattr(dst, "name") else "fm", oc)])

            mk_xi(0)
            with ExitStack() as esw:
                wb = load_w_bf16(G, esw, "r_wr", G.Wl("rw_wr", j), 1024, 1024)
                def cons_r(oc, t0, tn, ps, pkey, m_):
                    s = stg[oc % 2]
                    P.op('act', lambda e: e.copy(out=s[:, t0:t0 + tn], in_=ps[:, 0:tn]), r=[pkey], w=[("r_stg", oc % 2), pkey])
                    if t0 == 2048:
                        P.dma(FM["rT"][oc, :, :], s[:, :], r=[("r_stg", oc % 2)], w=[("fm_rT", oc)])
                fm_linear(G, xiT, "r_xiT", wb, "r_wr_b", 1024, cons_r)
                P.barrier()
            mk_xi(1)
            for d in range(2):
                with ExitStack() as esw:
                    w1b = load_w_bf16(G, esw, "r_w1", G.Wl("rw_w1", j)[d], 1024, 64)
                    w2b = load_w_bf16(G, esw, "r_w2", G.Wl("rw_w2", j)[d], 64, 1024)
                    w0c = colvec(G, esw, "r_w0c", G.Wl("rw_w0", j)[d], "r_w0c")
                    P.op('dve', lambda e: e.tensor_scalar_mul(out=w0c[:], in0=w0c[:], scalar1=-1.0), r=["r_w0c"], w=["r_w0c"])
                    t1 = sb(nc, esw, "r_t1", [128, 1, T], BF16)
                    def cons_t(oc, t0, tn, ps, pkey, m_):
                        P.op('act', lambda e: e.activation(out=t1[0:m_, 0, t0:t0 + tn], in_=ps[0:m_, 0:tn], func=AF.Tanh), r=[pkey], w=["r_t1", pkey])
                    fm_linear(G, xiT, "r_xiT", w1b, "r_w1_b", 64, cons_t, pbanks=(2, 3))
                    def cons_w(oc, t0, tn, ps, pkey, m_):
                        s = stg[oc % 2]
                        k_ = tcnt[0] % 2
                        tcnt[0] += 1
                        ta, tb_ = tmpa[k_], tmpb[k_]
                        P.op('act', lambda e: e.activation(out=ta[:, 0:tn], in_=ps[:, 0:tn], func=AF.Exp, scale=-1.0, bias=w0c[:, oc:oc + 1]), r=[pkey, "r_w0c"], w=[("r_tmpa", k_), pkey])
                        P.op('act', lambda e: e.activation(out=tb_[:, 0:tn], in_=ta[:, 0:tn], func=AF.Ln, bias=G.one_c[:, 0:1], scale=1.0), r=[("r_tmpa", k_), "consts"], w=[("r_tmpb", k_)])
                        P.op('act', lambda e: e.activation(out=ta[:, 0:tn], in_=tb_[:, 0:tn], func=AF.Exp, scale=-1.0, bias=G.mhalf_c[:, 0:1]), r=[("r_tmpb", k_), "consts"], w=[("r_tmpa", k_)])
                        P.op('dve', lambda e: e.tensor_scalar_mul(out=s[:, t0:t0 + tn], in0=ta[:, 0:tn], scalar1=-1.0), r=[("r_tmpa", k_)], w=[("r_stg", oc % 2)])
                        if t0 == 2048:
                            P.dma(FM["lw%d" % d][oc, :, :], s[:, :], r=[("r_stg", oc % 2)], w=[("fm_lw%d" % d, oc)])
                    fm_linear(G, t1, "r_t1", w2b, "r_w2_b", 1024, cons_w, kparts=[(0, 64)])
                    P.barrier()
            mk_xi(2)
            with ExitStack() as esw:
                wb = load_w_bf16(G, esw, "r_wk", G.Wl("rw_wk", j), 1024, 1024)
                def cons_k(oc, t0, tn, ps, pkey, m_):
                    s, s2 = stg[oc % 2], stg2[oc % 2]
                    k_ = tcnt[0] % 2
                    tcnt[0] += 1
                    ta, tb_ = tmpa[k_], tmpb[k_]
                    P.op('act', lambda e: e.copy(out=s[:, t0:t0 + tn], in_=ps[:, 0:tn]), r=[pkey], w=[("r_stg", oc % 2), pkey])
                    P.op('dve', lambda e: e.tensor_scalar(out=ta[:, 0:tn], in0=s[:, t0:t0 + tn], scalar1=kkc[:, oc:oc + 1], scalar2=None, op0=ALU.mult), r=[("r_stg", oc % 2), "r_cv"], w=[("r_tmpa", k_)])
                    P.op('pool', lambda e: e.tensor_tensor(out=tb_[:, 0:tn], in0=ta[:, 0:tn], in1=ta[:, 0:tn], op=ALU.mult), r=[("r_tmpa", k_)], w=[("r_tmpb", k_)])
                    P.op('pe', lambda e: e.matmul(G.ps[4 + k_][:, 0:tn], lhsT=BO[:, :], rhs=tb_[:, 0:tn], start=True, stop=True), r=["r_BO", ("r_tmpb", k_)], w=[("ps", 4 + k_)])
                    P.op('act', lambda e: e.activation(out=tb_[:, 0:tn], in_=G.ps[4 + k_][:, 0:tn], func=AF.Sqrt), r=[("ps", 4 + k_)], w=[("r_tmpb", k_), ("ps", 4 + k_)])
                    P.op('dve', lambda e: e.tensor_scalar_max(out=tb_[:, 0:tn], in0=tb_[:, 0:tn], scalar1=1e-12), r=[("r_tmpb", k_)], w=[("r_tmpb", k_)])
                    P.op('dve', lambda e: e.reciprocal(out=tb_[:, 0:tn], in_=tb_[:, 0:tn]), r=[("r_tmpb", k_)], w=[("r_tmpb", k_)])
                    P.op('dve', lambda e: e.tensor_tensor(out=s2[:, t0:t0 + tn], in0=ta[:, 0:tn], in1=tb_[:, 0:tn], op=ALU.mult), r=[("r_tmpa", k_), ("r_tmpb", k_)], w=[("r_stg2", 0)])
                    if t0 == 2048:
                        P.dma(FM["kT"][oc, :, :], s[:, :], r=[("r_stg", oc % 2)], w=[("fm_kT", oc)])
                        P.dma(FM["kk"][oc, :, :], s2[:, :], r=[("r_stg2", 0)], w=[("fm_kk", oc)])
                fm_linear(G, xiT, "r_xiT", wb, "r_wk_b", 1024, cons_k)
                P.barrier()
            mk_xi(3)
            with ExitStack() as esw:
                wb = load_w_bf16(G, esw, "r_wv", G.Wl("rw_wv", j), 1024, 1024)
                if j > 0:
                    v1b = load_w_bf16(G, esw, "r_v1", G.Wl("rw_v1", j - 1), 1024, 32)
                    v2b = load_w_bf16(G, esw, "r_v2", G.Wl("rw_v2", j - 1), 32, 1024)
                    v0c = colvec(G, esw, "r_v0c", G.Wl("rw_v0", j - 1), "r_v0c")
                    t1 = sb(nc, esw, "r_t1v", [128, 1, T], BF16)
                    def cons_t(oc, t0, tn, ps, pkey, m_):
                        P.op('act', lambda e: e.copy(out=t1[0:m_, 0, t0:t0 + tn], in_=ps[0:m_, 0:tn]), r=[pkey], w=["r_t1v", pkey])
                    fm_linear(G, xiT, "r_xiT", v1b, "r_v1_b", 32, cons_t, pbanks=(2, 3))
                def cons_v(oc, t0, tn, ps, pkey, m_):
                    s = stg[oc % 2]
                    if j == 0:
                        P.op('act', lambda e: e.copy(out=s[:, t0:t0 + tn], in_=ps[:, 0:tn]), r=[pkey], w=[("r_stg", oc % 2), pkey])
                    else:
                        k_ = tcnt[0] % 2
                        tcnt[0] += 1
                        ta, tb_ = tmpa[k_], tmpb[k_]
                        if t0 == 0:
                            P.dma(ld1[oc % 2][:, :], FM["vf"][oc, :, :], r=[("fm_vf", oc)], w=[("r_ld1", 0)])
                        vf = ld1[oc % 2]
                        pb2 = 4 + k_
                        P.op('pe', lambda e: e.matmul(G.ps[pb2][:, 0:tn], lhsT=v2b[0:32, 0, oc * 128:(oc + 1) * 128], rhs=t1[0:32, 0, t0:t0 + tn], start=True, stop=True),
                             r=["r_t1v", "r_v2_b"], w=[("ps", pb2)])
                        P.op('act', lambda e: e.activation(out=ta[:, 0:tn], in_=G.ps[pb2][:, 0:tn], func=AF.Sigmoid, bias=v0c[:, oc:oc + 1], scale=1.0), r=[("ps", pb2), "r_v0c"], w=[("r_tmpa", k_), ("ps", pb2)])
                        P.op('dve', lambda e: e.tensor_tensor(out=tb_[:, 0:tn], in0=vf[:, t0:t0 + tn], in1=ps[:, 0:tn], op=ALU.subtract), r=[("r_ld1", 0), pkey], w=[("r_tmpb", k_)])
                        P.op('dve', lambda e: e.tensor_tensor(out=tb_[:, 0:tn], in0=tb_[:, 0:tn], in1=ta[:, 0:tn], op=ALU.mult), r=[("r_tmpa", k_), ("r_tmpb", k_)], w=[("r_tmpb", k_)])
                        P.op('dve', lambda e: e.tensor_tensor(out=s[:, t0:t0 + tn], in0=tb_[:, 0:tn], in1=ps[:, 0:tn], op=ALU.add), r=[("r_tmpb", k_), pkey], w=[("r_stg", oc % 2), pkey])
                    if t0 == 2048:
                        P.dma(FM["vT"][oc, :, :], s[:, :], r=[("r_stg", oc % 2)], w=[("fm_vT", oc)])
                        if j == 0:
                            P.dma(FM["vf"][oc, :, :], s[:, :], r=[("r_stg", oc % 2)], w=[("fm_vf", oc)])
                fm_linear(G, xiT, "r_xiT", wb, "r_wv_b", 1024, cons_v)
                P.barrier()
            mk_xi(4)
            for d in range(2):
                with ExitStack() as esw:
                    a1b = load_w_bf16(G, esw, "r_a1", G.Wl("rw_a1", j)[d], 1024, 64)
                    a2b = load_w_bf16(G, esw, "r_a2", G.Wl("rw_a2", j)[d], 64, 1024)
                    a0c = colvec(G, esw, "r_a0c", G.Wl("rw_a0", j)[d], "r_a0c")
                    t1 = sb(nc, esw, "r_t1a", [128, 1, T], BF16)
                    def cons_t(oc, t0, tn, ps, pkey, m_):
                        P.op('act', lambda e: e.copy(out=t1[0:m_, 0, t0:t0 + tn], in_=ps[0:m_, 0:tn]), r=[pkey], w=["r_t1a", pkey])
                    fm_linear(G, xiT, "r_xiT", a1b, "r_a1_b", 64, cons_t, pbanks=(2, 3))
                    def cons_a(oc, t0, tn, ps, pkey, m_):
                        s, s2 = stg[oc % 2], stg2[oc % 2]
                        k_ = tcnt[0] % 2
                        tcnt[0] += 1
                        ta, tb_ = tmpa[k_], tmpb[k_]
                        if t0 == 0:
                            P.dma(ld1[oc % 2][:, :], FM["kT"][oc, :, :], r=[("fm_kT", oc)], w=[("r_ld1", 0)])
                            P.dma(ld2[oc % 2][:, :], FM["kk"][oc, :, :], r=[("fm_kk", oc)], w=[("r_ld2", 0)])
                        kt_, kkt = ld1[oc % 2], ld2[oc % 2]
                        P.op('act', lambda e: e.activation(out=ta[:, 0:tn], in_=ps[:, 0:tn], func=AF.Sigmoid, bias=a0c[:, oc:oc + 1], scale=1.0), r=[pkey, "r_a0c"], w=[("r_tmpa", k_), pkey])
                        P.op('pool', lambda e: e.tensor_tensor(out=s2[:, t0:t0 + tn], in0=kkt[:, t0:t0 + tn], in1=ta[:, 0:tn], op=ALU.mult), r=[("r_ld2", 0), ("r_tmpa", k_)], w=[("r_stg2", 0)])
                        P.op('dve', lambda e: e.tensor_scalar(out=tb_[:, 0:tn], in0=ta[:, 0:tn], scalar1=kac[:, oc:oc + 1], scalar2=omka[:, oc:oc + 1], op0=ALU.mult, op1=ALU.add),
                             r=[("r_tmpa", k_), "r_cv", "r_cv2"], w=[("r_tmpb", k_)])
                        P.op('dve', lambda e: e.tensor_tensor(out=s[:, t0:t0 + tn], in0=tb_[:, 0:tn], in1=kt_[:, t0:t0 + tn], op=ALU.mult), r=[("r_tmpb", k_), ("r_ld1", 0)], w=[("r_stg", oc % 2)])
                        if t0 == 2048:
                            P.dma(FM["kd%d" % d][oc, :, :], s[:, :], r=[("r_stg", oc % 2)], w=[("fm_kd%d" % d, oc)])
                            P.dma(FM["b%d" % d][oc, :, :], s2[:, :], r=[("r_stg2", 0)], w=[("fm_b%d" % d, oc)])
                    fm_linear(G, t1, "r_t1a", a2b, "r_a2_b", 1024, cons_a, kparts=[(0, 64)])
                    P.barrier()
            mk_xi(5)
            with ExitStack() as esw:
                g1b = load_w_bf16(G, esw, "r_g1", G.Wl("rw_g1", j), 1024, 160)
                g2b = load_w_bf16(G, esw, "r_g2", G.Wl("rw_g2", j), 160, 1024)
                t1 = sb(nc, esw, "r_t1g", [128, 2, T], BF16)
                def cons_t(oc, t0, tn, ps, pkey, m_):
                    P.op('act', lambda e: e.activation(out=t1[0:m_, oc, t0:t0 + tn], in_=ps[0:m_, 0:tn], func=AF.Sigmoid), r=[pkey], w=["r_t1g", pkey])
                fm_linear(G, xiT, "r_xiT", g1b, "r_g1_b", 160, cons_t, pbanks=(2, 3))
                def cons_g(oc, t0, tn, ps, pkey, m_):
                    s = stg[oc % 2]
                    P.op('act', lambda e: e.copy(out=s[:, t0:t0 + tn], in_=ps[:, 0:tn]), r=[pkey], w=[("r_stg", oc % 2), pkey])
                    if t0 == 2048:
                        P.dma(FM["gT"][oc, :, :], s[:, :], r=[("r_stg", oc % 2)], w=[("fm_gT", oc)])
                fm_linear(G, t1, "r_t1g", g2b, "r_g2_b", 1024, cons_g, kparts=[(0, 128), (1, 32)])
                P.barrier()
            P.barrier()
        if not SKIP_SCAN:
            scan_stage(G)
        with ExitStack() as esR:
            zT = sb(nc, esR, "o_zT", [128, 8, T], BF16)
            rkc = colvec(G, esR, "o_rkc", G.Wl("rw_rk", j).rearrange("h k -> (h k)"), "o_cv")
            lgc = colvec(G, esR, "o_lgc", G.Wl("rw_lnx_g", j), "o_cv")
            lbc = colvec(G, esR, "o_lbc", G.Wl("rw_lnx_b", j), "o_cv")
            epsg = sb(nc, esR, "o_epsg", [128, 1])
            P.op('dve', lambda e: e.memset(epsg[:], RW_GN_EPS), w=["o_epsg"])
            with ExitStack() as esL:
                L = {nm: [sb(nc, esL, "o_%s" % nm, [128, T])] * 2 for nm in ("y0", "y1", "r", "kd0", "kd1", "v", "g")}
                wa = [sb(nc, esL, "o_wa%d" % i, [128, 512]) for i in range(2)]
                wb_ = [sb(nc, esL, "o_wb%d" % i, [128, 512]) for i in range(2)]
                wc_ = [sb(nc, esL, "o_wc%d" % i, [128, 512]) for i in range(2)]
                it = 0
                for p in range(8):
                    b = 0
                    for d in range(2):
                        for h in range(2):
                            P.dma(L["y%d" % d][b][h * 64:(h + 1) * 64, :], G.y_d[d, :, 2 * p + h, :], r=[("y_d", None)] if False else ["y_d"], w=[("o_L_y%d" % d, b)])
                    for nm, fmn in (("r", "rT"), ("kd0", "kd0"), ("kd1", "kd1"), ("v", "vT"), ("g", "gT")):
                        P.dma(L[nm][b][:, :], FM[fmn][p, :, :], r=[("fm_" + fmn, p)], w=[("o_L_" + nm, b)])
                    for (t0, tn) in TBLK:
                        k_ = it % 2
                        it += 1
                        a_, b2, c_ = wa[k_], wb_[k_], wc_[k_]
                        ka, kb, kc = ("o_wa", k_), ("o_wb", k_), ("o_wc", k_)
                        ts_ = slice(t0, t0 + tn)
                        P.op('dve', lambda e: e.tensor_tensor(out=a_[:, 0:tn], in0=L["y0"][b][:, ts_], in1=L["y1"][b][:, ts_], op=ALU.add), r=[("o_L_y0", b), ("o_L_y1", b)], w=[ka])
                        P.op('pe', lambda e: e.matmul(G.ps[k_][:, 0:tn], lhsT=BO[:, :], rhs=a_[:, 0:tn], start=True, stop=True), r=["r_BO", ka], w=[("ps", k_)])
                        P.op('dve', lambda e: e.scalar_tensor_tensor(out=a_[:, 0:tn], in0=G.ps[k_][:, 0:tn], scalar=-1.0 / 64, in1=a_[:, 0:tn], op0=ALU.mult, op1=ALU.add), r=[("ps", k_), ka], w=[ka, ("ps", k_)])
                        P.op('pool', lambda e: e.tensor_tensor(out=b2[:, 0:tn], in0=a_[:, 0:tn], in1=a_[:, 0:tn], op=ALU.mult), r=[ka], w=[kb])
                        P.op('pe', lambda e: e.matmul(G.ps[2 + k_][:, 0:tn], lhsT=BO[:, :], rhs=b2[:, 0:tn], start=True, stop=True), r=["r_BO", kb], w=[("ps", 2 + k_)])
                        P.op('act', lambda e: e.activation(out=b2[:, 0:tn], in_=G.ps[2 + k_][:, 0:tn], func=AF.Sqrt, scale=1.0 / 64, bias=epsg[:, 0:1]), r=[("ps", 2 + k_), "o_epsg"], w=[kb, ("ps", 2 + k_)])
                        P.op('dve', lambda e: e.reciprocal(out=b2[:, 0:tn], in_=b2[:, 0:tn]), r=[kb], w=[kb])
                        P.op('dve', lambda e: e.tensor_tensor(out=a_[:, 0:tn], in0=a_[:, 0:tn], in1=b2[:, 0:tn], op=ALU.mult), r=[ka, kb], w=[ka])
                        P.op('dve', lambda e: e.tensor_scalar(out=a_[:, 0:tn], in0=a_[:, 0:tn], scalar1=lgc[:, p:p + 1], scalar2=lbc[:, p:p + 1], op0=ALU.mult, op1=ALU.add), r=[ka, "o_cv"], w=[ka])
                        P.op('pool', lambda e: e.tensor_tensor(out=c_[:, 0:tn], in0=L["kd0"][b][:, ts_], in1=L["kd1"][b][:, ts_], op=ALU.add), r=[("o_L_kd0", b), ("o_L_kd1", b)], w=[kc])
                        P.op('dve', lambda e: e.scalar_tensor_tensor(out=c_[:, 0:tn], in0=c_[:, 0:tn], scalar=rkc[:, p:p + 1], in1=L["r"][b][:, ts_], op0=ALU.mult, op1=ALU.mult), r=[kc, "o_cv", ("o_L_r", b)], w=[kc])
                        P.op('pe', lambda e: e.matmul(G.ps[4 + k_][:, 0:tn], lhsT=BO[:, :], rhs=c_[:, 0:tn], start=True, stop=True), r=["r_BO", kc], w=[("ps", 4 + k_)])
                        P.op('dve', lambda e: e.tensor_tensor(out=c_[:, 0:tn], in0=G.ps[4 + k_][:, 0:tn], in1=L["v"][b][:, ts_], op=ALU.mult), r=[("ps", 4 + k_), ("o_L_v", b)], w=[kc, ("ps", 4 + k_)])
                        P.op('dve', lambda e: e.tensor_tensor(out=a_[:, 0:tn], in0=a_[:, 0:tn], in1=c_[:, 0:tn], op=ALU.add), r=[ka, kc], w=[ka])
                        P.op('dve', lambda e: e.tensor_tensor(out=zT[:, p, ts_], in0=a_[:, 0:tn], in1=L["g"][b][:, ts_], op=ALU.mult), r=[ka, ("o_L_g", b)], w=[("o_zT", p)])
                P.barrier()
            wob = load_w_bf16(G, esR, "o_wo", G.Wl("rw_wo", j), 1024, 1024)
            gate = [sb(nc, esR, "o_gate%d" % r_, [128, 1024]) for r_ in range(2)]
            LG = sb(nc, esR, "o_lg", [128, 1024])
            LB = sb(nc, esR, "o_lb", [128, 1024])
            for r_ in range(2):
                P.dma(gate[r_][:], G.modv[l, r_:r_ + 1, 2 * 1024:3 * 1024].partition_broadcast(128), r=[("modv", l)], w=["o_bc"])
            P.dma(LG[:], row(G.Wl("ln1_g", l)).partition_broadcast(128), w=["o_bc"])
            P.dma(LB[:], row(G.Wl("ln1_b", l)).partition_broadcast(128), w=["o_bc"])
            xt = [sb(nc, esR, "o_x%d" % i, [128, 1024]) for i in range(2)]
            tmp = [sb(nc, esR, "o_tmp%d" % i, [128, 1024]) for i in range(2)]
            yt = [sb(nc, esR, "o_y%d" % i, [128, 1024]) for i in range(2)]
            st = [sb(nc, esR, "o_st%d" % i, [128, 16]) for i in range(2)]
            for t in range(NT):
                b = t % 2
                r_ = 0 if t < 16 else 1
                P.dma(xt[b][:], G.xres[t * 128:(t + 1) * 128, :], r=[("xres", t)], w=[("o_x", b)])
                banks = [G.ps[0 + 2 * b], G.ps[1 + 2 * b]]
                okeys = [("ps", 0 + 2 * b), ("ps", 1 + 2 * b)]
                for hh in range(2):
                    for c in range(8):
                        P.op('pe', lambda e_, hh=hh, c=c: e_.matmul(banks[hh][:, :], lhsT=zT[:, c, t * 128:(t + 1) * 128], rhs=wob[:, c, hh * 512:(hh + 1) * 512],
                                                                   start=(c == 0), stop=(c == 7)), r=["o_zT", "o_wo_b"], w=[okeys[hh]])
                ln_epilogue(G, banks, okeys, xt[b], ("o_x", b), gate[r_], LG, LB, ["o_bc"], tmp[b], ("o_tmp", b), st[b], ("o_st", b), yt[b], ("o_y", b))
                P.dma(G.xres[t * 128:(t + 1) * 128, :], yt[b][:], r=[("o_y", b)], w=[("xres", t)])
            P.barrier()
        P.barrier()


def build(layers=(0, 1, 2, 3), stages=None):
    nc = bass.Bass("TRN2", target_bir_lowering=False)
    G = Ctx()
    G.nc = nc

    def din(name, shape):
        return nc.dram_tensor(name, list(shape), F32, kind="ExternalInput").ap()
    G.x_d = din("x", [TL, D])
    G.ctx_d = din("ctx", [TC, D])
    G.c_d = din("c", [1, D])
    G.cc_d = din("c_ctx", [1, D])
    G.used = {}
    specs = dict(WEIGHT_SPECS)

    def Wl(name, l):
        key = "%s_%d" % (name, l)
        if key not in G.used:
            G.used[key] = (name, l, din(key, specs[name][1:]))
        return G.used[key][2]
    G.Wl = Wl
    G.kc = {}

    def KC(name, shape):
        if name not in G.kc:
            G.kc[name] = din(name, shape)
        return G.kc[name]
    G.KC = KC
    G.ident_d = KC("k_ident", [128, 128])
    G.out_d = nc.dram_tensor("out", [TL, D], F32, kind="ExternalOutput").ap()
    G.outc_d = nc.dram_tensor("outc", [TC, D], F32, kind="ExternalOutput").ap()
    G.xres = nc.dram_tensor("xres", [T, D], F32, kind="Internal").ap()
    G.modv = nc.dram_tensor("modv", [4, 2, 6144], F32, kind="Internal").ap()
    G.mo_d = nc.dram_tensor("mo_d", [T, D], BF16, kind="Internal").ap()
    G.fm = {nm: nc.dram_tensor("fm_" + nm, [8, 128, T], F32, kind="Internal").ap()
            for nm in ("rT", "kT", "vT", "vf", "kk", "gT", "lw0", "lw1", "kd0", "kd1", "b0", "b1")}
    G.y_d = nc.dram_tensor("y_d", [2, 64, 16, T], F32, kind="Internal").ap()
    P = Prog(nc)
    G.P = P
    with ExitStack() as es:
        G.ps = [es.enter_context(nc.psum_tensor("ps%d" % i, [128, 512], F32)) for i in range(8)]
        G.identF = sb(nc, es, "identF", [128, 128])
        G.identB = sb(nc, es, "identB", [128, 128], BF16)
        G.eps_ln = sb(nc, es, "eps_ln", [128, 1])
        P.dma(G.identF[:], G.ident_d[:, :], w=["ident"])
        P.op('dve', lambda e: e.tensor_copy(out=G.identB[:], in_=G.identF[:]), r=["ident"], w=["identB"])
        P.op('dve', lambda e: e.memset(G.eps_ln[:], LN_EPS), w=["consts"])
        G.one_c = sb(nc, es, "one_c", [128, 1])
        G.mhalf_c = sb(nc, es, "mhalf_c", [128, 1])
        P.op('dve', lambda e: e.memset(G.one_c[:], 1.0), w=["consts"])
        P.op('dve', lambda e: e.memset(G.mhalf_c[:], -0.5), w=["consts"])
        for t in range(16):
            P.dma(G.xres[t * 128:(t + 1) * 128, :], G.x_d[t * 128:(t + 1) * 128, :], w=[("xres", t)])
        for t in range(2):
            P.dma(G.xres[TL + t * 128:TL + (t + 1) * 128, :], G.ctx_d[t * 128:(t + 1) * 128, :], w=[("xres", 16 + t)])
        layers = tuple(layers)
        defer = stages is None or "ffn" in stages
        modvec_setup(G, es)
        stage_modvec(G, layers[:1] if defer else layers)
        for li, l in enumerate(layers):
            if l % 2 == 0 and (stages is None or "mix" in stages):
                stage_even(G, l)
            if l % 2 == 1 and (stages is None or "mix" in stages):
                stage_rwkv(G, l)
            if stages is None or "ffn" in stages:
                side = None
                if defer and li + 1 < len(layers):
                    side = (lambda es_, nl=layers[li + 1]: modvec_gen(G, nl, es_))
                stage_ffn(G, l, side)
        for t in range(16):
            P.dma(G.out_d[t * 128:(t + 1) * 128, :], G.xres[t * 128:(t + 1) * 128, :], r=[("xres", t)], w=[("out", t)])
        for t in range(2):
            P.dma(G.outc_d[t * 128:(t + 1) * 128, :], G.xres[TL + t * 128:TL + (t + 1) * 128, :], r=[("xres", 16 + t)], w=[("outc", t)])
        P.barrier()
    P.es.close()
    return nc, P, G


def make_consts():
    k = {"k_ident": np.eye(128, dtype=np.float32)}
    t = np.arange(TL)
    rowi = (t // 64).astype(np.float32)
    coli = (t % 64).astype(np.float32)
    for nm, dim, ng in (("A", 64, 8), ("B", 128, 4)):
        nf = dim // 4
        inv = (10000.0 ** (-np.arange(nf, dtype=np.float32) / nf)).astype(np.float32)
        ang = np.concatenate([rowi[:, None] * inv, coli[:, None] * inv], -1).astype(np.float32)
        k["k_cos" + nm] = np.ascontiguousarray(np.tile(np.cos(ang).astype(np.float32), (1, ng)))
        k["k_sin" + nm] = np.ascontiguousarray(np.tile(np.sin(ang).astype(np.float32), (1, ng)))
    p = np.arange(128, dtype=np.float32)
    k["k_cols"] = np.stack([127 - p, p, p + 1, 128 - p], 1).astype(np.float32)
    jj = p[None, :]
    pp = p[:, None]
    k["k_mats"] = np.ascontiguousarray(np.stack([np.maximum(jj - pp, 0), np.maximum(pp - jj, 0), (jj >= pp).astype(np.float32),
                                                 (jj <= pp).astype(np.float32)], 1).astype(np.float32))
    bo = np.zeros((128, 128), np.float32)
    bo[:64, :64] = 1.0
    bo[64:, 64:] = 1.0
    k["k_bo"] = bo
    i64 = np.arange(64)
    strict = (i64[:, None] < i64[None, :]).astype(np.float32)
    incl = (i64[:, None] <= i64[None, :]).astype(np.float32)
    def bdm(m):
        z = np.zeros((128, 128), np.float32)
        z[:64, :64] = m
        z[64:, 64:] = m
        return z
    k["k_smask"] = np.ascontiguousarray(np.stack([bdm(strict), bdm(strict.T), bdm(incl)], 1))
    return k


def kernel(**inputs):
    nc, _, G = build()
    consts = make_consts()
    in_maps = []
    for b in range(8):
        m = {"x": np.ascontiguousarray(inputs["x"][b]), "ctx": np.ascontiguousarray(inputs["ctx"][b]),
             "c": np.ascontiguousarray(inputs["c"][b:b + 1]), "c_ctx": np.ascontiguousarray(inputs["c_ctx"][None, :])}
        for key, (name, l, _ap) in G.used.items():
            m[key] = np.ascontiguousarray(inputs[name][l])
        for key in G.kc:
            m[key] = consts[key]
        in_maps.append(m)
    res = run_bass_kernel_spmd(nc, in_maps, core_ids=list(range(8)))
    return np.stack([r["out"] for r in res.results], axis=0).astype(np.float32)
```

```python
import math
import numpy as np
from contextlib import ExitStack
import concourse.bass as bass
import concourse.mybir as mybir
from concourse.bass_utils import run_bass_kernel_spmd

F32 = mybir.dt.float32
BF16 = mybir.dt.bfloat16
AF = mybir.ActivationFunctionType
ALU = mybir.AluOpType
AX = mybir.AxisListType

NDS = 8
SAME_ENGINE_SYNC = True
EMBED_WAIT = True

D = 1024
TL = 2048
TC = 256
T = TL + TC
NT = T // 128
DFF = 2816
NFC = DFF // 128
DEPTH = 4
ALPHA = (2.0 * DEPTH) ** 0.25
LN_EPS = 1e-6

WEIGHT_SPECS = [
    ("mod_w", (4, 1024, 6144)), ("mod_b", (4, 6144)), ("ln1_g", (4, 1024)), ("ln1_b", (4, 1024)),
    ("ln2_g", (4, 1024)), ("ln2_b", (4, 1024)), ("ffn_w1", (4, 1024, 2816)), ("ffn_w3", (4, 1024, 2816)),
    ("ffn_w2", (4, 2816, 1024)), ("ev_w_in", (2, 1024, 3584)), ("ev_w_out", (2, 1024, 1024)),
    ("da_lam_q1", (2, 64)), ("da_lam_k1", (2, 64)), ("da_lam_q2", (2, 64)), ("da_lam_k2", (2, 64)),
    ("da_gn_g", (2, 512)), ("rt_decay_logit", (2, 2, 4)), ("rw_mu", (2, 6, 1024)),
    ("rw_wr", (2, 1024, 1024)), ("rw_wk", (2, 1024, 1024)), ("rw_wv", (2, 1024, 1024)), ("rw_wo", (2, 1024, 1024)),
    ("rw_w0", (2, 2, 1024)), ("rw_w1", (2, 2, 1024, 64)), ("rw_w2", (2, 2, 64, 1024)),
    ("rw_a0", (2, 2, 1024)), ("rw_a1", (2, 2, 1024, 64)), ("rw_a2", (2, 2, 64, 1024)),
    ("rw_v0", (1, 1024)), ("rw_v1", (1, 1024, 32)), ("rw_v2", (1, 32, 1024)),
    ("rw_g1", (2, 1024, 160)), ("rw_g2", (2, 160, 1024)), ("rw_kk", (2, 1024)), ("rw_ka", (2, 1024)),
    ("rw_rk", (2, 16, 64)), ("rw_lnx_g", (2, 1024)), ("rw_lnx_b", (2, 1024)),
]


class Prog:
    def __init__(self, nc):
        self.nc = nc
        self.engs = {'pe': nc.tensor, 'act': nc.scalar, 'dve': nc.vector, 'pool': nc.gpsimd, 'sp': nc.sync}
        self.es = ExitStack()
        self.sem = {}
        for e in ['pe', 'act', 'dve', 'pool']:
            self.sem[('e', e)] = self.es.enter_context(nc.semaphore('s_' + e))
        for i in range(NDS):
            self.sem[('d', i)] = self.es.enter_context(nc.semaphore('d%d' % i))
        self.cnt = {k: 0 for k in self.sem}
        self.dnext = 0
        self.known = {e: {} for e in self.engs}
        self.res = {}
        self.nops = 0
        self.nwaits = 0

    def _get(self, key):
        name, sub = key if isinstance(key, tuple) else (key, None)
        d = self.res.setdefault(name, {})
        if sub not in d:
            d[sub] = [None, {}]
        return d[sub]

    def _conf(self, key):
        name, sub = key if isinstance(key, tuple) else (key, None)
        d = self.res.setdefault(name, {})
        if sub is None:
            return list(d.values())
        out = []
        if sub in d:
            out.append(d[sub])
        if None in d:
            out.append(d[None])
        return out

    def op(self, eng, fn, r=(), w=(), dma=False, noembed=False):
        deps = {}

        def need(k, v):
            if deps.get(k, 0) < v:
                deps[k] = v
        for key in r:
            for st in self._conf(key):
                if st[0] is not None:
                    need(*st[0])
        for key in w:
            for st in self._conf(key):
                if st[0] is not None:
                    need(*st[0])
                for k, v in st[1].items():
                    need(k, v)
        E = self.engs[eng]
        kn = self.known[eng]
        if dma:
            d = self.dnext
            self.dnext = (d + 1) % NDS
            sk = ('d', d)
            if self.cnt[sk]:
                need(sk, self.cnt[sk])
        else:
            sk = ('e', eng)
        wl = []
        for k, v in deps.items():
            if (not dma) and k == ('e', eng) and (eng == 'pe' or not SAME_ENGINE_SYNC):
                continue
            if kn.get(k, 0) >= v:
                continue
            kn[k] = v
            wl.append((k, v))
        emb = None
        if wl and EMBED_WAIT and not dma and not noembed:
            emb = wl.pop()
        for k, v in wl:
            E.wait_ge(self.sem[k], v)
            self.nwaits += 1
        ins = fn(E)
        if emb is not None:
            ins.wait_op(self.sem[emb[0]], emb[1], "sem-ge")
        inc = 16 if dma else 1
        self.cnt[sk] += inc
        ins.then_inc(self.sem[sk], inc)
        ev = (sk, self.cnt[sk])
        self.nops += 1
        for key in r:
            st = self._get(key)
            if st[1].get(ev[0], 0) < ev[1]:
                st[1][ev[0]] = ev[1]
        for key in w:
            name, sub = key if isinstance(key, tuple) else (key, None)
            if sub is None:
                self.res[name] = {None: [ev, {}]}
            else:
                st = self._get(key)
                st[0] = ev
                st[1] = {}
        return ev

    def dma(self, out, in_, r=(), w=(), eng='sp', **kw):
        return self.op(eng, lambda e: e.dma_start(out=out, in_=in_, **kw), r=r, w=w, dma=True)

    def barrier(self, engines=('pe', 'act', 'dve', 'pool', 'sp')):
        for eng in engines:
            E = self.engs[eng]
            kn = self.known[eng]
            for k, v in self.cnt.items():
                if v and kn.get(k, 0) < v:
                    kn[k] = v
                    E.wait_ge(self.sem[k], v)
                    self.nwaits += 1


class Ctx:
    pass


_SBN = [0]


def sb(nc, es, name, shape, dt=F32):
    _SBN[0] += 1
    return es.enter_context(nc.sbuf_tensor("%s_u%d" % (name, _SBN[0]), list(shape), dt))


_CLN = [0]


def colload(G, es, dst2d, dkey, src_flat, n):
    nc, P = G.nc, G.P
    _CLN[0] += 1
    k = "cl_stg%d" % _CLN[0]
    stg = sb(nc, es, k, [n, 128])
    P.dma(stg[:], src_flat.rearrange("(j p) -> j p", p=128), w=[k])
    P.op('pe', lambda e: e.transpose(out=G.ps[7][:, 0:n], in_=stg[:, :], identity=G.identF[0:n, 0:n]), r=[k, "ident"], w=[("ps", 7)])
    P.op('dve', lambda e: e.tensor_copy(out=dst2d, in_=G.ps[7][:, 0:n]), r=[("ps", 7)], w=[dkey, ("ps", 7)])


def stage_modvec(G, layers):
    nc, P = G.nc, G.P
    with ExitStack() as es:
        craw = sb(nc, es, "mv_craw", [128, 2, 8])
        cT = sb(nc, es, "mv_cT", [128, 8, 2])
        wt = [sb(nc, es, "mv_w%d" % i, [128, 8, 512]) for i in range(2)]
        bt = [sb(nc, es, "mv_b%d" % i, [2, 512]) for i in range(2)]
        ot = [sb(nc, es, "mv_o%d" % i, [2, 512]) for i in range(2)]
        colload(G, es, craw[:, 0, :], "mv_craw", G.c_d[0, :], 8)
        colload(G, es, craw[:, 1, :], "mv_craw", G.cc_d[0, :], 8)
        for r in range(2):
            P.op('act', lambda e, r=r: e.activation(out=cT[:, :, r], in_=craw[:, r, :], func=AF.Silu),
                 r=["mv_craw"], w=[("mv_cT", r)])
        i = 0
        for l in layers:
            for nb in range(12):
                b = i % 2
                i += 1
                P.dma(wt[b][:], G.Wl("mod_w", l)[:, nb * 512:(nb + 1) * 512].rearrange("(c p) n -> p c n", p=128),
                      w=[("mv_w", b)])
                P.dma(bt[b][:], G.Wl("mod_b", l).rearrange("(o n) -> o n", o=1)[:, nb * 512:(nb + 1) * 512].partition_broadcast(2), w=[("mv_b", b)])
                ps = G.ps[b]
                for c in range(8):
                    P.op('pe', lambda e, c=c, b=b, ps=ps: e.matmul(ps[0:2, :], lhsT=cT[:, c, :], rhs=wt[b][:, c, :],
                                                                  start=(c == 0), stop=(c == 7)),
                         r=["mv_cT", ("mv_w", b)], w=[("ps", b)])
                P.op('dve', lambda e, b=b, ps=ps: e.tensor_tensor(out=ot[b][:], in0=ps[0:2, :], in1=bt[b][:], op=ALU.add),
                     r=[("ps", b), ("mv_b", b)], w=[("mv_o", b), ("ps", b)])
                P.dma(G.modv[l, :, nb * 512:(nb + 1) * 512], ot[b][:], r=[("mv_o", b)], w=[("modv", l)])
        P.barrier()


def load_modcols(G, es, l, name):
    nc, P = G.nc, G.P
    mc = sb(nc, es, name, [128, 2, 6, 8])
    for r in range(2):
        _CLN[0] += 1
        k = "cl_stg%d" % _CLN[0]
        stg = sb(nc, es, k, [48, 128])
        P.dma(stg[:], G.modv[l, r, :].rearrange("(j p) -> j p", p=128), r=[("modv", l)], w=[k])
        P.op('pe', lambda e, stg=stg: e.transpose(out=G.ps[7][:, 0:48], in_=stg[:, :], identity=G.identF[0:48, 0:48]), r=[k, "ident"], w=[("ps", 7)])
        P.op('dve', lambda e, r=r: e.tensor_copy(out=mc[:, r, :, :].rearrange("p i c -> p (i c)"), in_=G.ps[7][:, 0:48]), r=[("ps", 7)], w=[name, ("ps", 7)])
    for r in range(2):
        for i in (1, 4):
            P.op('dve', lambda e, r=r, i=i: e.tensor_scalar_add(out=mc[:, r, i, :], in0=mc[:, r, i, :], scalar1=1.0),
                 r=[name], w=[name])
    return mc


def transpose_modulate(G, xt, xkey, hT, hkey, col0, mc, r, ish, isc, pbase):
    P = G.P
    for half in range(2):
        pb = pbase + half
        ps = G.ps[pb]
        for j in range(4):
            c = half * 4 + j
            P.op('pe', lambda e, c=c, j=j, ps=ps: e.transpose(out=ps[:, j * 128:(j + 1) * 128], in_=xt[:, c * 128:(c + 1) * 128],
                                                           identity=G.identF[:]),
                 r=[xkey, "ident"], w=[("ps", pb)])
        for j in range(4):
            c = half * 4 + j
            eng = 'act' if j % 2 == 0 else 'dve'
            if eng == 'act':
                P.op('act', lambda e, c=c, j=j, ps=ps: e.activation(out=hT[:, c, col0:col0 + 128], in_=ps[:, j * 128:(j + 1) * 128],
                                                                   func=AF.Identity, scale=mc[:, r, isc, c:c + 1], bias=mc[:, r, ish, c:c + 1]),
                     r=[("ps", pb), "mc"], w=[hkey, ("ps", pb)])
            else:
                P.op('dve', lambda e, c=c, j=j, ps=ps: e.tensor_scalar(out=hT[:, c, col0:col0 + 128], in0=ps[:, j * 128:(j + 1) * 128],
                                                                      scalar1=mc[:, r, isc, c:c + 1], scalar2=mc[:, r, ish, c:c + 1],
                                                                      op0=ALU.mult, op1=ALU.add),
                     r=[("ps", pb), "mc"], w=[hkey, ("ps", pb)])


def ln_epilogue(G, o_banks, okeys, xt, xkey, Gt, LGt, LBt, bkeys, tmp, tkey, st, skey, yt, ykey):
    P = G.P
    for h in range(2):
        sl = slice(h * 512, (h + 1) * 512)
        P.op('dve', lambda e, h=h, sl=sl: e.tensor_tensor(out=tmp[:, sl], in0=o_banks[h][:, :], in1=Gt[:, sl], op=ALU.mult),
             r=[okeys[h]] + bkeys, w=[tkey, okeys[h]])
    P.op('dve', lambda e: e.scalar_tensor_tensor(out=tmp[:, :], in0=xt[:, :], scalar=ALPHA, in1=tmp[:, :], op0=ALU.mult, op1=ALU.add),
         r=[xkey, tkey], w=[tkey])
    for h in range(2):
        P.op('dve', lambda e, h=h: e.bn_stats(out=st[:, h * 6:(h + 1) * 6], in_=tmp[:, h * 512:(h + 1) * 512]), r=[tkey], w=[skey])
    P.op('dve', lambda e: e.bn_aggr(out=st[:, 12:14], in_=st[:, 0:12]), r=[skey], w=[skey])
    P.op('act', lambda e: e.activation(out=st[:, 14:15], in_=st[:, 13:14], func=AF.Sqrt, bias=G.eps_ln[:, 0:1], scale=1.0), r=[skey, "consts"], w=[skey])
    P.op('dve', lambda e: e.reciprocal(out=st[:, 15:16], in_=st[:, 14:15]), r=[skey], w=[skey])
    P.op('dve', lambda e: e.tensor_scalar(out=tmp[:, :], in0=tmp[:, :], scalar1=st[:, 12:13], scalar2=st[:, 15:16],
                                          op0=ALU.subtract, op1=ALU.mult), r=[skey, tkey], w=[tkey])
    P.op('pool', lambda e: e.tensor_tensor(out=tmp[:, :], in0=tmp[:, :], in1=LGt[:, :], op=ALU.mult), r=[tkey] + bkeys, w=[tkey])
    P.op('pool', lambda e: e.tensor_tensor(out=yt[:, :], in0=tmp[:, :], in1=LBt[:, :], op=ALU.add), r=[tkey] + bkeys, w=[ykey])


def load_bcast(G, tile, key, src_row):
    G.P.dma(tile[:], src_row.partition_broadcast(128), w=[key])


def stage_ffn(G, l):
    nc, P = G.nc, G.P
    W1, W3, W2 = G.Wl("ffn_w1", l), G.Wl("ffn_w3", l), G.Wl("ffn_w2", l)
    with ExitStack() as es:
        mc = load_modcols(G, es, l, "mc")
        gate = [sb(nc, es, "f_gate%d" % r, [128, 1024]) for r in range(2)]
        LG = sb(nc, es, "f_lg", [128, 1024])
        LB = sb(nc, es, "f_lb", [128, 1024])
        for r in range(2):
            P.dma(gate[r][:], G.modv[l, r:r + 1, 5 * 1024:6 * 1024].partition_broadcast(128), r=[("modv", l)], w=["f_bc"])
        P.dma(LG[:], G.Wl("ln2_g", l).rearrange("(o n) -> o n", o=1).partition_broadcast(128), w=["f_bc"])
        P.dma(LB[:], G.Wl("ln2_b", l).rearrange("(o n) -> o n", o=1).partition_broadcast(128), w=["f_bc"])
        w2b = sb(nc, es, "f_w2b", [128, NFC, 1024], BF16)
        w2s = [sb(nc, es, "f_w2s%d" % i, [128, 2, 1024]) for i in range(2)]
        for i in range(NFC // 2):
            b = i % 2
            P.dma(w2s[b][:], W2[i * 256:(i + 1) * 256, :].rearrange("(c p) n -> p c n", p=128), w=[("f_w2s", b)])
            P.op('pool', lambda e, i=i, b=b: e.tensor_copy(out=w2b[:, 2 * i:2 * i + 2, :], in_=w2s[b][:]),
                 r=[("f_w2s", b)], w=[("f_w2b", i)])
        NH = 2
        TPH = NT // NH
        TOKH = TPH * 128
        NTB = TOKH // 384
        hT = sb(nc, es, "f_hT", [128, 8, TOKH], BF16)
        gT = sb(nc, es, "f_gT", [128, NFC, TOKH], BF16)
        xt = [sb(nc, es, "f_x%d" % i, [128, 1024]) for i in range(2)]
        w1s = [sb(nc, es, "f_w1s%d" % i, [128, 8, 128]) for i in range(2)]
        w3s = [sb(nc, es, "f_w3s%d" % i, [128, 8, 128]) for i in range(2)]
        w1b = [sb(nc, es, "f_w1b%d" % i, [128, 8, 128], BF16) for i in range(2)]
        w3b = [sb(nc, es, "f_w3b%d" % i, [128, 8, 128], BF16) for i in range(2)]
        sa = [sb(nc, es, "f_sa%d" % i, [128, 384]) for i in range(2)]
        tmp = [sb(nc, es, "f_tmp%d" % i, [128, 1024]) for i in range(2)]
        yt = [sb(nc, es, "f_y%d" % i, [128, 1024]) for i in range(2)]
        st = [sb(nc, es, "f_st%d" % i, [128, 16]) for i in range(2)]
        xi = 0
        wi = 0
        for hf in range(NH):
            for tt in range(TPH):
                t = hf * TPH + tt
                b = xi % 2
                xi += 1
                r = 0 if t < 16 else 1
                P.dma(xt[b][:], G.xres[t * 128:(t + 1) * 128, :], r=[("xres", t)], w=[("f_x", b)])
                transpose_modulate(G, xt[b], ("f_x", b), hT, ("f_hT", tt), tt * 128, mc, r, 3, 4, 0)
            for cb in range(NFC):
                b = wi % 2
                wi += 1
                P.dma(w1s[b][:], W1[:, cb * 128:(cb + 1) * 128].rearrange("(c p) n -> p c n", p=128), w=[("f_w1s", b)])
                P.dma(w3s[b][:], W3[:, cb * 128:(cb + 1) * 128].rearrange("(c p) n -> p c n", p=128), w=[("f_w3s", b)])
                P.op('pool', lambda e, b=b: e.tensor_copy(out=w1b[b][:], in_=w1s[b][:]), r=[("f_w1s", b)], w=[("f_w1b", b)])
                P.op('pool', lambda e, b=b: e.tensor_copy(out=w3b[b][:], in_=w3s[b][:]), r=[("f_w3s", b)], w=[("f_w3b", b)])
                for sub in range(1):
                    fc = cb
                    for tb in range(NTB):
                        tsl = slice(tb * 384, (tb + 1) * 384)
                        k = (fc * NTB + tb) % 2
                        pa, pb_ = 2 + 2 * k, 3 + 2 * k
                        for c in range(8):
                            P.op('pe', lambda e, c=c, pa=pa, b=b, sub=sub, tsl=tsl: e.matmul(
                                G.ps[pa][:, 0:384], lhsT=w1b[b][:, c, sub * 128:(sub + 1) * 128], rhs=hT[:, c, tsl],
                                start=(c == 0), stop=(c == 7)), r=[("f_w1b", b), "f_hT"], w=[("ps", pa)])
                        for c in range(8):
                            P.op('pe', lambda e, c=c, pb_=pb_, b=b, sub=sub, tsl=tsl: e.matmul(
                                G.ps[pb_][:, 0:384], lhsT=w3b[b][:, c, sub * 128:(sub + 1) * 128], rhs=hT[:, c, tsl],
                                start=(c == 0), stop=(c == 7)), r=[("f_w3b", b), "f_hT"], w=[("ps", pb_)])
                        P.op('act', lambda e, k=k, pa=pa: e.activation(out=sa[k][:, :], in_=G.ps[pa][:, 0:384], func=AF.Silu),
                             r=[("ps", pa)], w=[("f_sa", k), ("ps", pa)])
                        P.op('dve', lambda e, k=k, pb_=pb_, fc=fc, tsl=tsl: e.tensor_tensor(
                            out=gT[:, fc, tsl], in0=G.ps[pb_][:, 0:384], in1=sa[k][:, :], op=ALU.mult),
                            r=[("ps", pb_), ("f_sa", k)], w=[("f_gT", fc), ("ps", pb_)])
            for tt in range(TPH):
                t = hf * TPH + tt
                b = xi % 2
                xi += 1
                r = 0 if t < 16 else 1
                P.dma(xt[b][:], G.xres[t * 128:(t + 1) * 128, :], r=[("xres", t)], w=[("f_x", b)])
                k = tt % 2
                banks = [G.ps[0 + 2 * k], G.ps[1 + 2 * k]]
                okeys = [("ps", 0 + 2 * k), ("ps", 1 + 2 * k)]
                for h in range(2):
                    for fc in range(NFC):
                        P.op('pe', lambda e, h=h, fc=fc, tt=tt, banks=banks: e.matmul(
                            banks[h][:, :], lhsT=gT[:, fc, tt * 128:(tt + 1) * 128], rhs=w2b[:, fc, h * 512:(h + 1) * 512],
                            start=(fc == 0), stop=(fc == NFC - 1)), r=["f_gT", "f_w2b"], w=[okeys[h]])
                ln_epilogue(G, banks, okeys, xt[b], ("f_x", b), gate[r], LG, LB, ["f_bc"], tmp[k], ("f_tmp", k),
                            st[k], ("f_st", k), yt[k], ("f_y", k))
                P.dma(G.xres[t * 128:(t + 1) * 128, :], yt[k][:], r=[("f_y", k)], w=[("xres", t)])
        P.barrier()


def row(ap):
    return ap.rearrange("(o n) -> o n", o=1)


def inproj_block(G, es_w, Wap, j0, ncols, hT, hkey, wtag):
    nc, P = G.nc, G.P
    ws = sb(nc, es_w, wtag + "_s", [128, 8, ncols])
    wb = sb(nc, es_w, wtag + "_b", [128, 8, ncols], BF16)
    P.dma(ws[:], Wap[:, j0:j0 + ncols].rearrange("(c p) n -> p c n", p=128), w=[wtag + "_s"])
    P.op('pool', lambda e: e.tensor_copy(out=wb[:], in_=ws[:]), r=[wtag + "_s"], w=[wtag + "_b"])
    return wb


def rope_evac(G, ps, pkey, t, qr, qkey, cosT, sinT, ckey, tmp, tkey, half, scale=None):
    P = G.P
    ng = 512 // (2 * half)
    if t >= 16:
        if scale is None:
            P.op('act', lambda e: e.copy(out=qr[:, :], in_=ps[:, :]), r=[pkey], w=[qkey, pkey])
        else:
            P.op('act', lambda e: e.mul(out=qr[:, :], in_=ps[:, :], mul=scale), r=[pkey], w=[qkey, pkey])
        return
    pv = ps[:, :].rearrange("p (g two d) -> p g two d", g=ng, two=2)
    qv = qr[:, :].rearrange("p (g two d) -> p g two d", g=ng, two=2)
    x1, x2 = pv[:, :, 0, :], pv[:, :, 1, :]
    cs = cosT[:, :].rearrange("p (g d) -> p g d", g=ng)
    sn = sinT[:, :].rearrange("p (g d) -> p g d", g=ng)
    t1 = tmp[:, 0:256].rearrange("p (g d) -> p g d", g=ng)
    t2 = tmp[:, 256:512].rearrange("p (g d) -> p g d", g=ng)
    P.op('dve', lambda e: e.tensor_tensor(out=t1, in0=x1, in1=cs, op=ALU.mult), r=[pkey, ckey], w=[tkey])
    P.op('dve', lambda e: e.tensor_tensor(out=t2, in0=x2, in1=sn, op=ALU.mult), r=[pkey, ckey], w=[tkey])
    P.op('dve', lambda e: e.tensor_tensor(out=qv[:, :, 0, :], in0=t1, in1=t2, op=ALU.subtract), r=[tkey], w=[qkey])
    P.op('dve', lambda e: e.tensor_tensor(out=t1, in0=x1, in1=sn, op=ALU.mult), r=[pkey, ckey], w=[tkey])
    P.op('dve', lambda e: e.tensor_tensor(out=t2, in0=x2, in1=cs, op=ALU.mult), r=[pkey, ckey], w=[tkey, pkey])
    P.op('dve', lambda e: e.tensor_tensor(out=qv[:, :, 1, :], in0=t1, in1=t2, op=ALU.add), r=[tkey], w=[qkey])
    if scale is not None:
        P.op('act', lambda e: e.mul(out=qr[:, :], in_=qr[:, :], mul=scale), r=[qkey], w=[qkey])


def transpose4(G, src, skey, dstT, dkey, t, pbank):
    P = G.P
    psb = G.ps[pbank][:, 0:256].bitcast(BF16)
    for g in range(4):
        P.op('pe', lambda e, g=g: e.transpose(out=psb[:, g * 128:(g + 1) * 128], in_=src[:, g * 128:(g + 1) * 128], identity=G.identB[:]),
             r=[skey, "identB"], w=[("ps", pbank)])
    P.op('act', lambda e: e.copy(out=dstT[:, :, t * 128:(t + 1) * 128], in_=psb.rearrange("p (g n) -> p g n", g=4)),
         r=[("ps", pbank)], w=[dkey, ("ps", pbank)])


def stage_even(G, l):
    nc, P = G.nc, G.P
    e = l // 2
    lam_init = 0.8 - 0.6 * math.exp(-0.3 * l)
    Win, Wout = G.Wl("ev_w_in", e), G.Wl("ev_w_out", e)
    mo_d = G.mo_d
    with ExitStack() as es:
        mc = load_modcols(G, es, l, "mc")
        hT = sb(nc, es, "e_hT", [128, 8, T], BF16)
        xt = [sb(nc, es, "e_x%d" % i, [128, 1024]) for i in range(2)]
        for t in range(NT):
            b = t % 2
            P.dma(xt[b][:], G.xres[t * 128:(t + 1) * 128, :], r=[("xres", t)], w=[("e_x", b)])
            transpose_modulate(G, xt[b], ("e_x", b), hT, ("e_hT", t), t * 128, mc, 0 if t < 16 else 1, 0, 1, 0)
        cosT = [sb(nc, es, "e_cos%d" % i, [128, 256]) for i in range(2)]
        sinT = [sb(nc, es, "e_sin%d" % i, [128, 256]) for i in range(2)]
        qr = [sb(nc, es, "e_qr%d" % i, [128, 512], BF16) for i in range(2)]
        rtmp = [sb(nc, es, "e_rtmp%d" % i, [128, 512]) for i in range(2)]

        def proj_tile(wb, wkey, t, pbank):
            ps = G.ps[pbank]
            for c in range(8):
                P.op('pe', lambda e_, c=c: e_.matmul(ps[:, :], lhsT=hT[:, c, t * 128:(t + 1) * 128], rhs=wb[:, c, :],
                                                    start=(c == 0), stop=(c == 7)), r=[("e_hT", t), wkey], w=[("ps", pbank)])
            return ps

        def load_tables(t, b, which):
            if t < 16:
                P.dma(cosT[b][:], G.KC("k_cos" + which, [TL, 256])[t * 128:(t + 1) * 128, :], w=[("e_cs", b)])
                P.dma(sinT[b][:], G.KC("k_sin" + which, [TL, 256])[t * 128:(t + 1) * 128, :], w=[("e_cs", b)])

        with ExitStack() as esA:
            aqT = sb(nc, esA, "a_qT", [128, 4, T], BF16)
            akT = sb(nc, esA, "a_kT", [128, 4, T], BF16)
            av = sb(nc, esA, "a_v", [128, NT, 512], BF16)
            for j, dst in ((0, aqT), (1, akT)):
                with ExitStack() as esw:
                    wb = inproj_block(G, esw, Win, j * 512, 512, hT, "e_hT", "a_w%d" % j)
                    for t in range(NT):
                        b = t % 2
                        load_tables(t, b, "A")
                        ps = proj_tile(wb, "a_w%d_b" % j, t, 4 + b)
                        rope_evac(G, ps, ("ps", 4 + b), t, qr[b], ("e_qr", b), cosT[b], sinT[b], ("e_cs", b), rtmp[b], ("e_rtmp", b), 32)
                        transpose4(G, qr[b], ("e_qr", b), dst, ("a_T%d" % j, t), t, 6 + b)
                    P.barrier()
            with ExitStack() as esw:
                wb = inproj_block(G, esw, Win, 2 * 512, 512, hT, "e_hT", "a_w2")
                for t in range(NT):
                    b = t % 2
                    ps = proj_tile(wb, "a_w2_b", t, 4 + b)
                    P.op('act', lambda e_, t=t, ps=ps: e_.copy(out=av[:, t, :], in_=ps[:, :]), r=[("ps", 4 + b)], w=[("a_v", t), ("ps", 4 + b)])
                P.barrier()
            lam4 = sb(nc, esA, "a_lam4", [128, 4, 64])
            lamc = sb(nc, esA, "a_lamc", [128, 8])
            for i, nm in enumerate(("da_lam_q1", "da_lam_k1", "da_lam_q2", "da_lam_k2")):
                P.dma(lam4[:, i, :], row(G.Wl(nm, e)).partition_broadcast(128), w=["a_lam4"])
            for i in range(2):
                P.op('dve', lambda e_, i=i: e_.tensor_tensor(out=lam4[:, 2 * i, :], in0=lam4[:, 2 * i, :], in1=lam4[:, 2 * i + 1, :], op=ALU.mult),
                     r=["a_lam4"], w=["a_lam4"])
                P.op('dve', lambda e_, i=i: e_.reduce_sum(out=lamc[:, i:i + 1], in_=lam4[:, 2 * i, :], axis=AX.X), r=["a_lam4"], w=["a_lamc"])
            P.op('act', lambda e_: e_.activation(out=lamc[:, 2:4], in_=lamc[:, 0:2], func=AF.Exp), r=["a_lamc"], w=["a_lamc"])
            P.op('dve', lambda e_: e_.tensor_tensor(out=lamc[:, 4:5], in0=lamc[:, 3:4], in1=lamc[:, 2:3], op=ALU.subtract), r=["a_lamc"], w=["a_lamc"])
            P.op('dve', lambda e_: e_.tensor_scalar_add(out=lamc[:, 5:6], in0=lamc[:, 4:5], scalar1=-lam_init), r=["a_lamc"], w=["a_lamc"])
            neglam = lamc[:, 5:6]
            Pm = [sb(nc, esA, "a_P%d" % i, [128, T], BF16) for i in range(2)]
            PT = [sb(nc, esA, "a_PT%d" % i, [128, NT, 128], BF16) for i in range(2)]
            sm = [sb(nc, esA, "a_sm%d" % i, [128, 16]) for i in range(2)]
            ao = sb(nc, esA, "a_ao", [128, 512])
            ao2 = sb(nc, esA, "a_ao2", [128, 512])
            aob = [sb(nc, esA, "a_aob%d" % i, [128, 512], BF16) for i in range(2)]
            rs = sb(nc, esA, "a_rs", [128, 16])
            SC = 64 ** -0.5
            def kinfo(qt):
                ktiles = list(range(NT)) if qt < 16 else [16, 17]
                k0 = ktiles[0] * 128
                nk = len(ktiles) * 128
                return ktiles, k0, nk, (nk + 511) // 512

            def stage1(ui, qt, h, m):
                ktiles, k0, nk, nbk = kinfo(qt)
                pb_ = ui % 2
                smt, Pmt = sm[pb_], Pm[pb_]
                psl = slice(m * 64, (m + 1) * 64)
                for jb in range(nbk):
                    w_ = min(512, nk - jb * 512)
                    P.op('pe', lambda e_, jb=jb, w_=w_: e_.matmul(G.ps[jb][:, 0:w_], lhsT=aqT[psl, h, qt * 128:(qt + 1) * 128],
                                                                 rhs=akT[psl, h, k0 + jb * 512:k0 + jb * 512 + w_], start=True, stop=True),
                         r=[("a_T0", qt), "a_T1"], w=[("ps", jb)])
                for jb in range(nbk):
                    w_ = min(512, nk - jb * 512)
                    P.op('dve', lambda e_, jb=jb, w_=w_: e_.reduce_max(out=smt[:, jb:jb + 1], in_=G.ps[jb][:, 0:w_], axis=AX.X),
                         r=[("ps", jb)], w=[("a_sm", pb_), ("ps", jb)])
                P.op('dve', lambda e_: e_.reduce_max(out=smt[:, 6:7], in_=smt[:, 0:nbk], axis=AX.X), r=[("a_sm", pb_)], w=[("a_sm", pb_)])
                P.op('dve', lambda e_: e_.tensor_scalar_mul(out=smt[:, 7:8], in0=smt[:, 6:7], scalar1=-SC), r=[("a_sm", pb_)], w=[("a_sm", pb_)])
                for jb in range(nbk):
                    w_ = min(512, nk - jb * 512)
                    P.op('act', lambda e_, jb=jb, w_=w_: e_.activation(out=Pmt[:, jb * 512:jb * 512 + w_], in_=G.ps[jb][:, 0:w_], func=AF.Exp,
                                                                      scale=SC, bias=smt[:, 7:8], accum_out=smt[:, 8 + jb:9 + jb]),
                         r=[("ps", jb), ("a_sm", pb_)], w=[("a_P", pb_), ("a_sm", pb_), ("ps", jb)], noembed=True)
                P.op('act', lambda e_: e_.copy(out=smt[:, 0:nbk], in_=smt[:, 8:8 + nbk]), r=[("a_sm", pb_)], w=[("a_sm", pb_)])
                P.op('dve', lambda e_: e_.reduce_sum(out=smt[:, 14:15], in_=smt[:, 0:nbk], axis=AX.X), r=[("a_sm", pb_)], w=[("a_sm", pb_)])
                P.op('dve', lambda e_: e_.reciprocal(out=smt[:, 15:16], in_=smt[:, 14:15]), r=[("a_sm", pb_)], w=[("a_sm", pb_)])

            def stage2(ui, qt, h, m):
                ktiles, k0, nk, nbk = kinfo(qt)
                pb_ = ui % 2
                smt, Pmt, PTt = sm[pb_], Pm[pb_], PT[pb_]
                nkt = len(ktiles)
                for g0 in range(0, nkt, 8):
                    gb = 5 + (g0 // 8) % 2
                    psb = G.ps[gb][:, :].bitcast(BF16)
                    n_ = min(8, nkt - g0)
                    for i in range(n_):
                        P.op('pe', lambda e_, i=i, g0=g0, psb=psb: e_.transpose(out=psb[:, i * 128:(i + 1) * 128],
                                                                             in_=Pmt[:, (g0 + i) * 128:(g0 + i + 1) * 128], identity=G.identB[:]),
                             r=[("a_P", pb_), "identB"], w=[("ps", gb)])
                    evac_eng = 'dve' if (g0 // 8) % 2 == 0 else 'act'
                    if evac_eng == 'dve':
                        P.op('dve', lambda e_, g0=g0, n_=n_, psb=psb: e_.tensor_copy(out=PTt[:, g0:g0 + n_, :],
                                                                                     in_=psb[:, 0:n_ * 128].rearrange("p (g n) -> p g n", g=n_)),
                             r=[("ps", gb)], w=[("a_PT", pb_), ("ps", gb)])
                    else:
                        P.op('act', lambda e_, g0=g0, n_=n_, psb=psb: e_.copy(out=PTt[:, g0:g0 + n_, :],
                                                                              in_=psb[:, 0:n_ * 128].rearrange("p (g n) -> p g n", g=n_)),
                             r=[("ps", gb)], w=[("a_PT", pb_), ("ps", gb)])
                osl = slice(m * 128, (m + 1) * 128)
                for i, kt in enumerate(ktiles):
                    P.op('pe', lambda e_, i=i, kt=kt: e_.matmul(G.ps[7][:, osl], lhsT=PTt[:, i, :], rhs=av[:, kt, h * 128:(h + 1) * 128],
                                                               start=(i == 0), stop=(i == nkt - 1)),
                         r=[("a_PT", pb_), "a_v"], w=[("ps", 7)])
                if m == 0:
                    P.op('dve', lambda e_: e_.tensor_scalar(out=ao2[:, 0:128], in0=G.ps[7][:, 0:128], scalar1=smt[:, 15:16], scalar2=None, op0=ALU.mult),
                         r=[("ps", 7), ("a_sm", pb_)], w=["a_ao2", ("ps", 7)])
                else:
                    P.op('dve', lambda e_: e_.tensor_tensor(out=smt[:, 13:14], in0=smt[:, 15:16], in1=neglam, op=ALU.mult),
                         r=[("a_sm", pb_), "a_lamc"], w=[("a_sm", pb_)])
                    P.op('dve', lambda e_: e_.scalar_tensor_tensor(out=ao[:, h * 128:(h + 1) * 128], in0=G.ps[7][:, 128:256], scalar=smt[:, 13:14],
                                                                  in1=ao2[:, 0:128], op0=ALU.mult, op1=ALU.add),
                         r=[("ps", 7), ("a_sm", pb_), "a_ao2"], w=["a_ao", ("ps", 7)])
                if h == 3 and m == 1:
                    P.op('pool', lambda e_: e_.tensor_tensor(out=ao2[:, :], in0=ao[:, :], in1=ao[:, :], op=ALU.mult), r=["a_ao"], w=["a_ao2"])
                    P.op('dve', lambda e_: e_.reduce_sum(out=rs[:, 0:4], in_=ao2[:, :].rearrange("p (h d) -> p h d", h=4), axis=AX.X), r=["a_ao2"], w=["a_rs"])
                    P.op('act', lambda e_: e_.activation(out=rs[:, 4:8], in_=rs[:, 0:4], func=AF.Sqrt, scale=1.0 / 128, bias=G.eps_ln[:, 0:1]), r=["a_rs", "consts"], w=["a_rs"])
                    P.op('dve', lambda e_: e_.reciprocal(out=rs[:, 8:12], in_=rs[:, 4:8]), r=["a_rs"], w=["a_rs"])
                    ab = aob[qt % 2]
                    for hh in range(4):
                        P.op('pool', lambda e_, hh=hh: e_.tensor_scalar(out=ab[:, hh * 128:(hh + 1) * 128], in0=ao[:, hh * 128:(hh + 1) * 128],
                                                                      scalar1=rs[:, 8 + hh:9 + hh], scalar2=None, op0=ALU.mult),
                             r=["a_ao", "a_rs"], w=[("a_aob", qt % 2)])
                    P.dma(mo_d[qt * 128:(qt + 1) * 128, 0:512], ab[:, :], r=[("a_aob", qt % 2)], w=[("mo", (qt, 0))])

            units = [(qt, h, m) for qt in range(NT) for h in range(4) for m in range(2)]
            for ui, u_ in enumerate(units):
                stage1(ui, *u_)
                if ui > 0:
                    stage2(ui - 1, *units[ui - 1])
            stage2(len(units) - 1, *units[-1])
            P.barrier()
        with ExitStack() as esB:
            bqT = sb(nc, esB, "b_qT", [128, 4, T], BF16)
            bkT = sb(nc, esB, "b_kT", [128, 4, T], BF16)
            bk = sb(nc, esB, "b_k", [128, NT, 512], BF16)
            bv = sb(nc, esB, "b_v", [128, NT, 512], BF16)
            for j in (3, 4):
                with ExitStack() as esw:
                    wb = inproj_block(G, esw, Win, j * 512, 512, hT, "e_hT", "b_w%d" % j)
                    for t in range(NT):
                        b = t % 2
                        load_tables(t, b, "B")
                        ps = proj_tile(wb, "b_w%d_b" % j, t, 4 + b)
                        if j == 3:
                            rope_evac(G, ps, ("ps", 4 + b), t, qr[b], ("e_qr", b), cosT[b], sinT[b], ("e_cs", b), rtmp[b], ("e_rtmp", b), 64)
                            transpose4(G, qr[b], ("e_qr", b), bqT, ("b_qT", t), t, 6 + b)
                        else:
                            rope_evac(G, ps, ("ps", 4 + b), t, bk[:, t, :], ("b_k", t), cosT[b], sinT[b], ("e_cs", b), rtmp[b], ("e_rtmp", b), 64,
                                      scale=128 ** -0.5)
                            transpose4(G, bk[:, t, :], ("b_k", t), bkT, ("b_kT", t), t, 6 + b)
                    P.barrier()
            with ExitStack() as esw:
                wb = inproj_block(G, esw, Win, 5 * 512, 512, hT, "e_hT", "b_w5")
                for t in range(NT):
                    b = t % 2
                    ps = proj_tile(wb, "b_w5_b", t, 4 + b)
                    P.op('act', lambda e_, t=t, ps=ps: e_.copy(out=bv[:, t, :], in_=ps[:, :]), r=[("ps", 4 + b)], w=[("b_v", t), ("ps", 4 + b)])
                P.barrier()
            dc = sb(nc, esB, "b_dc", [128, 64])
            kcol = sb(nc, esB, "b_kcol", [128, 4])
            kmat = sb(nc, esB, "b_kmat", [128, 4, 128])
            Mm = sb(nc, esB, "b_M", [128, 4, 128])
            Mt = sb(nc, esB, "b_Mt", [128, 128])
            P.dma(dc[:, 0:8], G.Wl("rt_decay_logit", e).rearrange("(o a) b -> o (a b)", o=1).partition_broadcast(128), w=["b_dc"])
            P.dma(kcol[:], G.KC("k_cols", [128, 4])[:, :], w=["b_kc"])
            P.dma(kmat[:], G.KC("k_mats", [128, 4, 128])[:, :, :], w=["b_kc"])
            P.op('act', lambda e_: e_.activation(out=dc[:, 8:16], in_=dc[:, 0:8], func=AF.Sigmoid), r=["b_dc"], w=["b_dc"])
            P.op('act', lambda e_: e_.activation(out=dc[:, 16:24], in_=dc[:, 8:16], func=AF.Ln), r=["b_dc"], w=["b_dc"])
            lg = lambda d, h: dc[:, 16 + d * 4 + h:17 + d * 4 + h]
            P.op('act', lambda e_: e_.activation(out=dc[:, 24:32], in_=dc[:, 16:24], func=AF.Exp, scale=128.0), r=["b_dc"], w=["b_dc"])
            for h in range(4):
                for (o_, col, d) in ((32, 0, 0), (36, 1, 1), (40, 2, 0), (44, 3, 1)):
                    P.op('act', lambda e_, o_=o_, col=col, d=d, h=h: e_.activation(out=dc[:, o_ + h:o_ + h + 1], in_=kcol[:, col:col + 1], func=AF.Exp, scale=lg(d, h)),
                         r=["b_dc", "b_kc"], w=["b_dc"])
                P.op('act', lambda e_, h=h: e_.activation(out=Mt[:, :], in_=kmat[:, 0, :], func=AF.Exp, scale=lg(0, h)), r=["b_dc", "b_kc"], w=["b_Mt"])
                P.op('dve', lambda e_, h=h: e_.tensor_tensor(out=Mm[:, h, :], in0=Mt[:, :], in1=kmat[:, 2, :], op=ALU.mult), r=["b_Mt", "b_kc"], w=["b_M"])
                P.op('act', lambda e_, h=h: e_.activation(out=Mt[:, :], in_=kmat[:, 1, :], func=AF.Exp, scale=lg(1, h)), r=["b_dc", "b_kc"], w=["b_Mt"])
                P.op('dve', lambda e_, h=h: e_.tensor_tensor(out=Mt[:, :], in0=Mt[:, :], in1=kmat[:, 3, :], op=ALU.mult), r=["b_Mt", "b_kc"], w=["b_Mt"])
                P.op('dve', lambda e_, h=h: e_.tensor_tensor(out=Mm[:, h, :], in0=Mm[:, h, :], in1=Mt[:, :], op=ALU.add), r=["b_Mt", "b_M"], w=["b_M"])
            SfA = sb(nc, esB, "b_SfA", [128, NT, 512], BF16)
            Sst = sb(nc, esB, "b_S", [128, 512])
            Sbb = sb(nc, esB, "b_Sbb", [128, 512], BF16)
            kz = [sb(nc, esB, "b_kz%d" % i, [128, 512], BF16) for i in range(2)]

            def state_update(n, d, i):
                zb = kz[i % 2]
                for h in range(4):
                    P.op('dve', lambda e_, h=h: e_.tensor_scalar(out=zb[:, h * 128:(h + 1) * 128], in0=bk[:, n, h * 128:(h + 1) * 128],
                                                                scalar1=dc[:, 32 + 4 * d + h:33 + 4 * d + h], scalar2=None, op0=ALU.mult),
                         r=[("b_k", n), "b_dc"], w=[("b_kz", i % 2)])
                for h in range(4):
                    P.op('pe', lambda e_, h=h: e_.matmul(G.ps[3][:, h * 128:(h + 1) * 128], lhsT=zb[:, h * 128:(h + 1) * 128],
                                                        rhs=bv[:, n, h * 128:(h + 1) * 128], start=True, stop=True),
                         r=[("b_kz", i % 2), ("b_v", n)], w=[("ps", 3)])
                for h in range(4):
                    P.op('dve', lambda e_, h=h: e_.scalar_tensor_tensor(out=Sst[:, h * 128:(h + 1) * 128], in0=Sst[:, h * 128:(h + 1) * 128],
                                                                       scalar=dc[:, 24 + 4 * d + h:25 + 4 * d + h], in1=G.ps[3][:, h * 128:(h + 1) * 128],
                                                                       op0=ALU.mult, op1=ALU.add),
                         r=["b_S", "b_dc", ("ps", 3)], w=["b_S", ("ps", 3)])
            P.op('dve', lambda e_: e_.memset(Sst[:, :], 0.0), w=["b_S"])
            fwd = [16, 17] + list(range(16))
            for i, n in enumerate(fwd):
                P.op('act', lambda e_, n=n: e_.copy(out=SfA[:, n, :], in_=Sst[:, :]), r=["b_S"], w=[("b_SfA", n)])
                if i < len(fwd) - 1:
                    state_update(n, 0, i)
            P.op('dve', lambda e_: e_.memset(Sst[:, :], 0.0), r=["b_SfA"], w=["b_S"])
            with ExitStack() as esw:
                wg = inproj_block(G, esw, Win, 6 * 512, 512, hT, "e_hT", "b_w6")
                Sm = [sb(nc, esw, "b_Sm%d" % i, [128, 4, 128], BF16) for i in range(2)]
                bo = [sb(nc, esw, "b_o%d" % i, [128, 512]) for i in range(2)]
                bo2 = [sb(nc, esw, "b_o2%d" % i, [128, 512]) for i in range(2)]
                gs = [sb(nc, esw, "b_gs%d" % i, [128, 512]) for i in range(2)]
                bob = [sb(nc, esw, "b_ob%d" % i, [128, 512], BF16) for i in range(2)]
                brs = [sb(nc, esw, "b_rs%d" % i, [128, 16]) for i in range(2)]
                bwd = [17, 16] + list(range(15, -1, -1))
                for i, n in enumerate(bwd):
                    b = i % 2
                    csl = slice(n * 128, (n + 1) * 128)
                    P.op('act', lambda e_: e_.copy(out=Sbb[:, :], in_=Sst[:, :]), r=["b_S"], w=["b_Sbb"])
                    for h in range(4):
                        P.op('pe', lambda e_, h=h: e_.matmul(G.ps[0][:, h * 128:(h + 1) * 128], lhsT=bkT[:, h, csl], rhs=bqT[:, h, csl], start=True, stop=True),
                             r=[("b_kT", n), ("b_qT", n)], w=[("ps", 0)])
                    P.op('dve', lambda e_, b=b: e_.tensor_tensor(out=Sm[b][:, :, :], in0=G.ps[0][:, :].rearrange("p (h n) -> p h n", h=4), in1=Mm[:, :, :], op=ALU.mult),
                         r=[("ps", 0), "b_M"], w=[("b_Sm", b), ("ps", 0)])
                    for h in range(4):
                        hs = slice(h * 128, (h + 1) * 128)
                        P.op('pe', lambda e_, h=h, hs=hs, b=b: e_.matmul(G.ps[1][:, hs], lhsT=Sm[b][:, h, :], rhs=bv[:, n, hs], start=True, stop=True),
                             r=[("b_Sm", b), ("b_v", n)], w=[("ps", 1)])
                    for h in range(4):
                        hs = slice(h * 128, (h + 1) * 128)
                        P.op('pe', lambda e_, h=h, hs=hs: e_.matmul(G.ps[2][:, hs], lhsT=bqT[:, h, csl], rhs=SfA[:, n, hs], start=True, stop=True),
                             r=[("b_qT", n), ("b_SfA", n)], w=[("ps", 2)])
                    for h in range(4):
                        hs = slice(h * 128, (h + 1) * 128)
                        P.op('pe', lambda e_, h=h, hs=hs: e_.matmul(G.ps[4][:, hs], lhsT=bqT[:, h, csl], rhs=Sbb[:, hs], start=True, stop=True),
                             r=[("b_qT", n), "b_Sbb"], w=[("ps", 4)])
                    for c in range(8):
                        P.op('pe', lambda e_, c=c: e_.matmul(G.ps[5][:, :], lhsT=hT[:, c, csl], rhs=wg[:, c, :], start=(c == 0), stop=(c == 7)),
                             r=[("e_hT", n), "b_w6_b"], w=[("ps", 5)])
                    P.op('act', lambda e_, b=b: e_.activation(out=gs[b][:, :], in_=G.ps[5][:, :], func=AF.Silu), r=[("ps", 5)], w=[("b_gs", b), ("ps", 5)])
                    for h in range(4):
                        hs = slice(h * 128, (h + 1) * 128)
                        P.op('dve', lambda e_, h=h, hs=hs, b=b: e_.tensor_scalar(out=bo2[b][:, hs], in0=G.ps[2][:, hs], scalar1=dc[:, 40 + h:41 + h], scalar2=None, op0=ALU.mult),
                             r=[("ps", 2), "b_dc"], w=[("b_o2", b), ("ps", 2)])
                        P.op('dve', lambda e_, h=h, hs=hs, b=b: e_.scalar_tensor_tensor(out=bo2[b][:, hs], in0=G.ps[4][:, hs], scalar=dc[:, 44 + h:45 + h], in1=bo2[b][:, hs],
                                                                                       op0=ALU.mult, op1=ALU.add),
                             r=[("ps", 4), "b_dc", ("b_o2", b)], w=[("b_o2", b), ("ps", 4)])
                    P.op('dve', lambda e_, b=b: e_.tensor_tensor(out=bo[b][:, :], in0=G.ps[1][:, :], in1=bo2[b][:, :], op=ALU.add),
                         r=[("ps", 1), ("b_o2", b)], w=[("b_o", b), ("ps", 1)])
                    P.op('dve', lambda e_, b=b: e_.tensor_tensor(out=bo2[b][:, :], in0=bo[b][:, :], in1=bo[b][:, :], op=ALU.mult), r=[("b_o", b)], w=[("b_o2", b)])
                    P.op('dve', lambda e_, b=b: e_.reduce_sum(out=brs[b][:, 0:4], in_=bo2[b][:, :].rearrange("p (h d) -> p h d", h=4), axis=AX.X), r=[("b_o2", b)], w=[("b_rs", b)])
                    P.op('act', lambda e_, b=b: e_.activation(out=brs[b][:, 4:8], in_=brs[b][:, 0:4], func=AF.Sqrt, scale=1.0 / 128, bias=G.eps_ln[:, 0:1]), r=[("b_rs", b), "consts"], w=[("b_rs", b)])
                    P.op('dve', lambda e_, b=b: e_.reciprocal(out=brs[b][:, 8:12], in_=brs[b][:, 4:8]), r=[("b_rs", b)], w=[("b_rs", b)])
                    for h in range(4):
                        hs = slice(h * 128, (h + 1) * 128)
                        P.op('dve', lambda e_, h=h, hs=hs, b=b: e_.scalar_tensor_tensor(out=bob[b][:, hs], in0=bo[b][:, hs], scalar=brs[b][:, 8 + h:9 + h], in1=gs[b][:, hs],
                                                                                       op0=ALU.mult, op1=ALU.mult),
                             r=[("b_o", b), ("b_rs", b), ("b_gs", b)], w=[("b_ob", b)])
                    P.dma(mo_d[n * 128:(n + 1) * 128, 512:1024], bob[b][:, :], r=[("b_ob", b)], w=[("mo", (n, 1))])
                    if i < len(bwd) - 1:
                        state_update(n, 1, i)
                P.barrier()
            P.barrier()
        with ExitStack() as esO:
            wos = sb(nc, esO, "o_ws", [128, 8, 1024])
            wob = sb(nc, esO, "o_wb", [128, 8, 1024], BF16)
            gn = sb(nc, esO, "o_gn", [128, 4])
            P.dma(wos[:], Wout[:, :].rearrange("(c p) n -> p c n", p=128), w=["o_ws"])
            colload(G, esO, gn[:, :], "o_gn", G.Wl("da_gn_g", e), 4)
            P.op('dve', lambda e_: e_.tensor_scalar_mul(out=gn[:], in0=gn[:], scalar1=1.0 - lam_init), r=["o_gn"], w=["o_gn"])
            for c in range(8):
                if c < 4:
                    P.op('dve', lambda e_, c=c: e_.tensor_scalar(out=wob[:, c, :], in0=wos[:, c, :], scalar1=gn[:, c:c + 1], scalar2=None, op0=ALU.mult),
                         r=["o_ws", "o_gn"], w=[("o_wb", c)])
                else:
                    P.op('pool', lambda e_, c=c: e_.tensor_copy(out=wob[:, c, :], in_=wos[:, c, :]), r=["o_ws"], w=[("o_wb", c)])
            gate = [sb(nc, esO, "o_gate%d" % r_, [128, 1024]) for r_ in range(2)]
            LG = sb(nc, esO, "o_lg", [128, 1024])
            LB = sb(nc, esO, "o_lb", [128, 1024])
            for r_ in range(2):
                P.dma(gate[r_][:], G.modv[l, r_:r_ + 1, 2 * 1024:3 * 1024].partition_broadcast(128), r=[("modv", l)], w=["o_bc"])
            P.dma(LG[:], row(G.Wl("ln1_g", l)).partition_broadcast(128), w=["o_bc"])
            P.dma(LB[:], row(G.Wl("ln1_b", l)).partition_broadcast(128), w=["o_bc"])
            mot = [sb(nc, esO, "o_mo%d" % i, [128, 1024], BF16) for i in range(2)]
            moT = [sb(nc, esO, "o_moT%d" % i, [128, 8, 128], BF16) for i in range(2)]
            tmp = [sb(nc, esO, "o_tmp%d" % i, [128, 1024]) for i in range(2)]
            yt = [sb(nc, esO, "o_y%d" % i, [128, 1024]) for i in range(2)]
            st = [sb(nc, esO, "o_st%d" % i, [128, 16]) for i in range(2)]
            for t in range(NT):
                b = t % 2
                r_ = 0 if t < 16 else 1
                P.dma(mot[b][:], mo_d[t * 128:(t + 1) * 128, :], r=[("mo", (t, 0)), ("mo", (t, 1))], w=[("o_mo", b)])
                P.dma(xt[b][:], G.xres[t * 128:(t + 1) * 128, :], r=[("xres", t)], w=[("e_x", b)])
                for hh in range(2):
                    pbk = 4 + hh
                    psb = G.ps[pbk][:, 0:256].bitcast(BF16)
                    for g in range(4):
                        c = hh * 4 + g
                        P.op('pe', lambda e_, g=g, c=c, psb=psb: e_.transpose(out=psb[:, g * 128:(g + 1) * 128], in_=mot[b][:, c * 128:(c + 1) * 128], identity=G.identB[:]),
                             r=[("o_mo", b), "identB"], w=[("ps", pbk)])
                    P.op('act', lambda e_, hh=hh, psb=psb: e_.copy(out=moT[b][:, hh * 4:hh * 4 + 4, :], in_=psb.rearrange("p (g n) -> p g n", g=4)),
                         r=[("ps", pbk)], w=[("o_moT", b), ("ps", pbk)])
                banks = [G.ps[0 + 2 * b], G.ps[1 + 2 * b]]
                okeys = [("ps", 0 + 2 * b), ("ps", 1 + 2 * b)]
                for hh in range(2):
                    for c in range(8):
                        P.op('pe', lambda e_, hh=hh, c=c: e_.matmul(banks[hh][:, :], lhsT=moT[b][:, c, :], rhs=wob[:, c, hh * 512:(hh + 1) * 512],
                                                                   start=(c == 0), stop=(c == 7)), r=[("o_moT", b), "o_wb"], w=[okeys[hh]])
                ln_epilogue(G, banks, okeys, xt[b], ("e_x", b), gate[r_], LG, LB, ["o_bc"], tmp[b], ("o_tmp", b), st[b], ("o_st", b), yt[b], ("o_y", b))
                P.dma(G.xres[t * 128:(t + 1) * 128, :], yt[b][:], r=[("o_y", b)], w=[("xres", t)])
            P.barrier()
        P.barrier()


TBLK = [(0, 512), (512, 512), (1024, 512), (1536, 512), (2048, 256)]
RW_GN_EPS = 64e-5


def colvec(G, es, name, ap1024, key):
    t = sb(G.nc, es, name, [128, 8])
    colload(G, es, t[:, :], key, ap1024, 8)
    return t


def load_w_bf16(G, es, name, wap, rows, cols, eng='pool'):
    nc, P = G.nc, G.P
    nck = (rows + 127) // 128
    wb = sb(nc, es, name + "_b", [128, nck, cols], BF16)
    if rows % 128 == 0 and cols > 512:
        with ExitStack() as es2:
            ws = sb(nc, es2, name + "_s", [128, nck, 512])
            for h0 in range(0, cols, 512):
                P.dma(ws[:], wap[:, h0:h0 + 512].rearrange("(c p) n -> p c n", p=128), w=[name + "_s"])
                P.op(eng, lambda e, h0=h0: e.tensor_copy(out=wb[:, :, h0:h0 + 512], in_=ws[:]), r=[name + "_s"], w=[name + "_b"])
            P.barrier()
        return wb
    ws = sb(nc, es, name + "_s", [128, nck, cols])
    if rows % 128 == 0:
        P.dma(ws[:], wap.rearrange("(c p) n -> p c n", p=128), w=[name + "_s"])
        P.op(eng, lambda e: e.tensor_copy(out=wb[:], in_=ws[:]), r=[name + "_s"], w=[name + "_b"])
    else:
        for c in range(nck):
            n_ = min(128, rows - c * 128)
            P.dma(ws[0:n_, c, :], wap[c * 128:c * 128 + n_, :], w=[name + "_s"])
        for c in range(nck):
            n_ = min(128, rows - c * 128)
            P.op(eng, lambda e, c=c, n_=n_: e.tensor_copy(out=wb[0:n_, c, :], in_=ws[0:n_, c, :]), r=[name + "_s"], w=[name + "_b"])
    return wb


def fm_linear(G, xT, xkey, wb, wkey, nout, consumer, kparts=None, pbanks=(0, 1)):
    P = G.P
    if kparts is None:
        kparts = [(c, 128) for c in range(8)]
    it = 0
    for oc in range((nout + 127) // 128):
        m_ = min(128, nout - oc * 128)
        for (t0, tn) in TBLK:
            pb = pbanks[it % len(pbanks)]
            it += 1
            for i, (c, kn) in enumerate(kparts):
                P.op('pe', lambda e, c=c, kn=kn, i=i, pb=pb: e.matmul(G.ps[pb][0:m_, 0:tn], lhsT=wb[0:kn, c, oc * 128:oc * 128 + m_], rhs=xT[0:kn, c, t0:t0 + tn],
                                                                   start=(i == 0), stop=(i == len(kparts) - 1)), r=[xkey, wkey], w=[("ps", pb)])
            consumer(oc, t0, tn, G.ps[pb], ("ps", pb), m_)


F32R = mybir.dt.float32r
NCHAIN = 4
SKIP_SCAN = False
SCAN_F32R = False
STAGGER = 6


def rr(ap):
    return ap.bitcast(F32R) if SCAN_F32R else ap


def scan_stage(G):
    nc, P = G.nc, G.P
    FM = G.fm
    C = 64
    NCH = T // C
    with ExitStack() as esS:
        msk = sb(nc, esS, "s_msk", [128, 3, 128])
        P.dma(msk[:], G.KC("k_smask", [128, 3, 128])[:, :, :], w=["s_msk"])
        ones = sb(nc, esS, "s_ones", [128, 64])
        P.op('dve', lambda e: e.memset(ones[:], 1.0), w=["s_ones"])
        identR = sb(nc, esS, "s_identR", [128, 128])
        P.op('dve', lambda e: e.tensor_copy(out=rr(identR[:]), in_=G.identF[:]), r=["ident"], w=["s_identR"])
        NAMES = ("r", "v", "kk", "lw", "kd", "b")
        bufs = []
        for ci in range(NCHAIN):
            B = Ctx()
            B.src = [sb(nc, esS, "s_src%d_%d" % (ci, i), [128, 6, 256]) for i in range(2)]
            B.bd = sb(nc, esS, "s_bd%d" % ci, [128, 5, 128])
            P.op('pool', lambda e, B=B: e.memset(B.bd[:], 0.0), w=[("s_bd", ci)])
            B.cum = sb(nc, esS, "s_cum%d" % ci, [128, 4, 64])
            B.Am = sb(nc, esS, "s_Am%d" % ci, [128, 5, 128])
            B.Ap = [sb(nc, esS, "s_Ap%d_%d" % (ci, i), [128, 2, 128]) for i in range(2)]
            B.VT = sb(nc, esS, "s_VT%d" % ci, [128, 64])
            B.Z = sb(nc, esS, "s_Z%d" % ci, [128, 2, 64])
            B.BT = sb(nc, esS, "s_BT%d" % ci, [128, 2, 128])
            B.S = sb(nc, esS, "s_S%d" % ci, [128, 64])
            B.Sr = sb(nc, esS, "s_Sr%d" % ci, [128, 64])
            B.yo = [sb(nc, esS, "s_yo%d_%d" % (ci, i), [64, 2, 64]) for i in range(2)]
            bufs.append(B)

        def chain(ci, p, d):
            B = bufs[ci]
            P0, P1 = G.ps[2 * ci], G.ps[2 * ci + 1]
            k0, k1 = ("ps", 2 * ci), ("ps", 2 * ci + 1)
            K = lambda nm, sub=None: ("s%d_%s" % (ci, nm), sub)
            fmn = {"r": "rT", "v": "vT", "kk": "kk", "lw": "lw%d" % d, "kd": "kd%d" % d, "b": "b%d" % d}
            P.op('dve', lambda e: e.memset(B.S[:], 0.0), w=[K("S")])
            P.op('act', lambda e: e.copy(out=rr(B.Sr[:, :]), in_=B.S[:, :]), r=[K("S")], w=[K("Sr")])
            R_, K_, B_, A_, V_ = (B.bd[:, i, :] for i in range(5))
            for n in range(NCH):
                if n < TC // C:
                    t0 = TL + n * C if d == 0 else T - (n + 1) * C
                else:
                    m_ = n - TC // C
                    t0 = m_ * C if d == 0 else TL - (m_ + 1) * C
                blk0 = (t0 // 256) * 256
                sbi = (n // 4) % 2
                if n % 4 == 0:
                    for i, nm in enumerate(NAMES):
                        P.dma(B.src[sbi][:, i, :], FM[fmn[nm]][p, :, blk0:blk0 + 256], r=[("fm_" + fmn[nm], p)], w=[K("src", sbi)])
                off = t0 - blk0
                sk = K("src", sbi)

                def tsl(i):
                    a_ = B.src[sbi][:, i, off:off + C]
                    return a_ if d == 0 else a_[:, ::-1]
                cm = B.cum
                ck = K("cum")
                P.op('dve', lambda e: e.tensor_tensor_scan(out=cm[:, 0, :], data0=ones[:, :], data1=tsl(3), initial=0.0, op0=ALU.mult, op1=ALU.add), r=[sk, "s_ones"], w=[ck])
                P.op('act', lambda e: e.activation(out=cm[:, 1, :], in_=cm[:, 0, :], func=AF.Exp), r=[ck], w=[ck])
                P.op('act', lambda e: e.activation(out=cm[:, 2, :], in_=cm[:, 0, :], func=AF.Exp, scale=-1.0), r=[ck], w=[ck])
                P.op('pool', lambda e: e.tensor_tensor(out=cm[:, 3, :], in0=cm[:, 0, :], in1=tsl(3), op=ALU.subtract), r=[ck, sk], w=[ck])
                P.op('act', lambda e: e.activation(out=cm[:, 3, :], in_=cm[:, 3, :], func=AF.Exp), r=[ck], w=[ck])
                yield
                bk = K("bd")
                for h in range(2):
                    hp = slice(h * 64, (h + 1) * 64)
                    P.op('dve', lambda e, hp=hp: e.tensor_tensor(out=rr(R_[hp, hp]), in0=tsl(0)[hp, :], in1=cm[hp, 1, :], op=ALU.mult), r=[sk, ck], w=[K("bd", 0)])
                    P.op('dve', lambda e, hp=hp: e.tensor_tensor(out=rr(K_[hp, hp]), in0=tsl(4)[hp, :], in1=cm[hp, 2, :], op=ALU.mult), r=[sk, ck], w=[K("bd", 1)])
                    P.op('dve', lambda e, hp=hp: e.tensor_tensor(out=rr(B_[hp, hp]), in0=tsl(5)[hp, :], in1=cm[hp, 2, :], op=ALU.mult), r=[sk, ck], w=[K("bd", 2)])
                    P.op('dve', lambda e, hp=hp: e.scalar_tensor_tensor(out=rr(A_[hp, hp]), in0=tsl(2)[hp, :], scalar=-1.0, in1=cm[hp, 3, :], op0=ALU.mult, op1=ALU.mult),
                         r=[sk, ck], w=[K("bd", 3)])
                    P.op('act', lambda e, hp=hp: e.copy(out=rr(V_[hp, hp]), in_=tsl(1)[hp, :]), r=[sk], w=[K("bd", 4)])
                yield
                A = B.Am
                specs = [(0, 2, 3, 0), (1, 3, 2, 1), (2, 1, 3, 0), (3, 2, 0, 2), (4, 1, 0, 2)]
                for (ai, li, ri, mi) in specs:
                    pt, pk = (P0, k0) if ai < 4 else (P1, k1)
                    sl = slice((ai % 4) * 128, (ai % 4 + 1) * 128)
                    P.op('pe', lambda e, li=li, ri=ri, pt=pt, sl=sl: e.matmul(pt[:, sl], lhsT=rr(B.bd[:, li, :]), rhs=rr(B.bd[:, ri, :]), start=True, stop=True),
                         r=[K("bd", li), K("bd", ri)], w=[pk])
                P.op('pe', lambda e: e.transpose(out=P1[:, 128:256], in_=V_, identity=G.identF[:]), r=[K("bd", 4), "ident"], w=[k1])
                yield
                for (ai, li, ri, mi) in specs:
                    pt, pk = (P0, k0) if ai < 4 else (P1, k1)
                    sl = slice((ai % 4) * 128, (ai % 4 + 1) * 128)
                    P.op('dve', lambda e, ai=ai, pt=pt, sl=sl, mi=mi: e.tensor_tensor(out=rr(A[:, ai, :]), in0=pt[:, sl], in1=msk[:, mi, :], op=ALU.mult),
                         r=[pk, "s_msk"], w=[K("Am", ai), pk])
                for h in range(2):
                    hp = slice(h * 64, (h + 1) * 64)
                    P.op('act', lambda e, hp=hp, h=h: e.copy(out=rr(B.VT[hp, :]), in_=P1[hp, 128 + h * 64:128 + (h + 1) * 64]), r=[k1], w=[K("VT"), k1])
                yield
                zs = slice(256, 320)
                P.op('pe', lambda e: e.matmul(P1[:, zs], lhsT=rr(A_), rhs=rr(B.Sr[:, :]), start=True, stop=False), r=[K("bd", 3), K("Sr")], w=[k1])
                P.op('pe', lambda e: e.matmul(P1[:, zs], lhsT=rr(A[:, 2, :]), rhs=rr(B.VT[:, :]), start=False, stop=True), r=[K("Am", 2), K("VT")], w=[k1])
                yield
                P.op('act', lambda e: e.copy(out=rr(B.Z[:, 0, :]), in_=P1[:, zs]), r=[k1], w=[K("Z", 0), k1])
                yield
                zc = 0
                curA, curAT = (A[:, 0, :], K("Am", 0)), (A[:, 1, :], K("Am", 1))
                for step in range(6):
                    P.op('pe', lambda e, zc=zc, curA=curA: e.matmul(P1[:, zs], lhsT=rr(curA[0]), rhs=rr(B.Z[:, zc, :]), start=True, stop=True), r=[curA[1], K("Z", zc)], w=[k1])
                    if step < 5:
                        dstt = B.Ap[step % 2]
                        dn = "Ap%d" % (step % 2)
                        P.op('pe', lambda e, curA=curA, curAT=curAT: e.matmul(P0[:, 0:128], lhsT=rr(curAT[0]), rhs=rr(curA[0]), start=True, stop=True), r=[curA[1], curAT[1]], w=[k0])
                        if step < 4:
                            P.op('pe', lambda e, curA=curA, curAT=curAT: e.matmul(P0[:, 128:256], lhsT=rr(curA[0]), rhs=rr(curAT[0]), start=True, stop=True), r=[curA[1], curAT[1]], w=[k0])
                    yield
                    P.op('dve', lambda e, zc=zc: e.tensor_tensor(out=rr(B.Z[:, 1 - zc, :]), in0=P1[:, zs], in1=B.Z[:, zc, :], op=ALU.add), r=[k1, K("Z", zc)], w=[K("Z", 1 - zc), k1])
                    zc = 1 - zc
                    if step < 5:
                        if step < 4:
                            P.op('act', lambda e, dstt=dstt: e.copy(out=rr(dstt[:, :, :]), in_=P0[:, 0:256].rearrange("p (a n) -> p a n", a=2)), r=[k0], w=[K(dn), k0])
                        else:
                            P.op('act', lambda e, dstt=dstt: e.copy(out=rr(dstt[:, 0, :]), in_=P0[:, 0:128]), r=[k0], w=[K(dn), k0])
                        curA, curAT = (dstt[:, 0, :], K(dn)), (dstt[:, 1, :], K(dn))
                    yield
                UT = B.Z[:, zc, :]
                uk = K("Z", zc)
                ys = slice(384, 512)
                P.op('pe', lambda e: e.matmul(P1[0:64, ys], lhsT=rr(B.Sr[:, :]), rhs=rr(R_), start=True, stop=False), r=[K("Sr"), K("bd", 0)], w=[k1])
                P.op('pe', lambda e: e.matmul(P1[0:64, ys], lhsT=rr(UT), rhs=rr(A[:, 3, :]), start=False, stop=False), r=[uk, K("Am", 3)], w=[k1])
                P.op('pe', lambda e: e.matmul(P1[0:64, ys], lhsT=rr(B.VT[:, :]), rhs=rr(A[:, 4, :]), start=False, stop=True), r=[K("VT"), K("Am", 4)], w=[k1])
                if n < NCH - 1:
                    P.op('pe', lambda e: e.transpose(out=P0[:, 256:384], in_=B_, identity=G.identF[:]), r=[K("bd", 2), "ident"], w=[k0])
                    P.op('pe', lambda e: e.transpose(out=P0[:, 384:512], in_=K_, identity=G.identF[:]), r=[K("bd", 1), "ident"], w=[k0])
                yield
                yq = B.yo[n % 2]
                yv = P1[0:64, ys].rearrange("p (h t) -> p h t", h=2)
                yov = yq[:, :, :] if d == 0 else yq[:, :, ::-1]
                P.op('act', lambda e: e.copy(out=yov, in_=yv), r=[k1], w=[K("yo", n % 2), k1])
                P.dma(G.y_d[d, :, 2 * p:2 * p + 2, t0:t0 + C], yq[:, :, :], r=[K("yo", n % 2)], w=[("y_d", (d, p, t0 // 128))])
                if n < NCH - 1:
                    P.op('dve', lambda e: e.tensor_copy(out=rr(B.BT[:, :, :]), in_=P0[:, 256:512].rearrange("p (a n) -> p a n", a=2)), r=[k0], w=[K("BT"), k0])
                    yield
                    P.op('pe', lambda e: e.matmul(P1[:, 0:64], lhsT=rr(B.BT[:, 0, :]), rhs=rr(UT), start=True, stop=False), r=[K("BT"), uk], w=[k1])
                    P.op('pe', lambda e: e.matmul(P1[:, 0:64], lhsT=rr(B.BT[:, 1, :]), rhs=rr(B.VT[:, :]), start=False, stop=True), r=[K("BT"), K("VT")], w=[k1])
                    yield
                    P.op('dve', lambda e: e.tensor_tensor(out=B.S[:, :], in0=B.S[:, :], in1=P1[:, 0:64], op=ALU.add), r=[K("S"), k1], w=[K("S"), k1])
                    P.op('dve', lambda e: e.tensor_scalar(out=B.S[:, :], in0=B.S[:, :], scalar1=cm[:, 1, 63:64], scalar2=None, op0=ALU.mult), r=[K("S"), ck], w=[K("S")])
                    P.op('act', lambda e: e.copy(out=rr(B.Sr[:, :]), in_=B.S[:, :]), r=[K("S")], w=[K("Sr")])
                yield

        todo = [(p, d) for p in range(8) for d in range(2)]
        active = [None] * NCHAIN
        for ci in range(NCHAIN):
            p, d = todo.pop(0)
            active[ci] = chain(ci, p, d)
            for _ in range(ci * STAGGER):
                next(active[ci])
        while todo or any(a is not None for a in active):
            for ci in range(NCHAIN):
                if active[ci] is None and todo:
                    p, d = todo.pop(0)
                    active[ci] = chain(ci, p, d)
                if active[ci] is not None:
                    try:
                        next(active[ci])
                    except StopIteration:
                        active[ci] = None
        P.barrier()


def stage_rwkv(G, l):
    nc, P = G.nc, G.P
    j = l // 2
    FM = G.fm
    with ExitStack() as es:
        mc = load_modcols(G, es, l, "mc")
        BO = sb(nc, es, "r_BO", [128, 128])
        P.dma(BO[:], G.KC("k_bo", [128, 128])[:, :], w=["r_BO"])
        with ExitStack() as esP:
            hT = sb(nc, esP, "r_hT", [128, 8, T], BF16)
            dT = sb(nc, esP, "r_dT", [128, 8, T], BF16)
            xiT = sb(nc, esP, "r_xiT", [128, 8, T], BF16)
            with ExitStack() as esx:
                xt = [sb(nc, esx, "r_x%d" % i, [128, 1024]) for i in range(2)]
                for t in range(NT):
                    b = t % 2
                    P.dma(xt[b][:], G.xres[t * 128:(t + 1) * 128, :], r=[("xres", t)], w=[("r_x", b)])
                    transpose_modulate(G, xt[b], ("r_x", b), hT, ("r_hT", t), t * 128, mc, 0 if t < 16 else 1, 0, 1, 0)
                P.barrier()
            def lat(tile_, c0, c1):
                return tile_[:, c0:c1, 0:TL].rearrange("p c (r w) -> p c r w", w=64)
            hl = lambda c0, c1: lat(hT, c0, c1)
            dl = lambda c0, c1: lat(dT, c0, c1)
            sub = ALU.subtract
            ops = [
                (dl(0, 2)[:, :, :, 1:64], hl(0, 2)[:, :, :, 0:63], hl(0, 2)[:, :, :, 1:64]),
                (dl(2, 4)[:, :, :, 0:63], hl(2, 4)[:, :, :, 1:64], hl(2, 4)[:, :, :, 0:63]),
                (dl(4, 6)[:, :, 1:32, :], hl(4, 6)[:, :, 0:31, :], hl(4, 6)[:, :, 1:32, :]),
                (dl(6, 8)[:, :, 0:31, :], hl(6, 8)[:, :, 1:32, :], hl(6, 8)[:, :, 0:31, :]),
                (dT[:, 0:4, TL + 1:T], hT[:, 0:4, TL:T - 1], hT[:, 0:4, TL + 1:T]),
                (dT[:, 4:8, TL:T - 1], hT[:, 4:8, TL + 1:T], hT[:, 4:8, TL:T - 1]),
            ]
            for (o_, a_, b_) in ops:
                for cc in range(o_.shape[1]):
                    P.op('dve', lambda e, o_=o_, a_=a_, b_=b_, cc=cc: e.tensor_tensor(out=o_[:, cc], in0=a_[:, cc], in1=b_[:, cc], op=sub), r=["r_hT"], w=["r_dT"])
            bnd = [
                (dl(0, 2)[:, :, :, 0:1], hl(0, 2)[:, :, :, 0:1]), (dl(2, 4)[:, :, :, 63:64], hl(2, 4)[:, :, :, 63:64]),
                (dl(4, 6)[:, :, 0:1, :], hl(4, 6)[:, :, 0:1, :]), (dl(6, 8)[:, :, 31:32, :], hl(6, 8)[:, :, 31:32, :]),
                (dT[:, 0:4, TL:TL + 1], hT[:, 0:4, TL:TL + 1]), (dT[:, 4:8, T - 1:T], hT[:, 4:8, T - 1:T]),
            ]
            for (o_, a_) in bnd:
                for cc in range(o_.shape[1]):
                    P.op('dve', lambda e, o_=o_, a_=a_, cc=cc: e.tensor_scalar_mul(out=o_[:, cc], in0=a_[:, cc], scalar1=-1.0), r=["r_hT"], w=["r_dT"])
            mu = sb(nc, esP, "r_mu", [128, 6, 8])
            colload(G, esP, mu[:, :, :].rearrange("p i c -> p (i c)"), "r_mu", G.Wl("rw_mu", j).rearrange("i n -> (i n)"), 48)
            kkc = colvec(G, esP, "r_kkc", G.Wl("rw_kk", j), "r_cv")
            kac = colvec(G, esP, "r_kac", G.Wl("rw_ka", j), "r_cv")
            omka = sb(nc, esP, "r_omka", [128, 8])
            P.op('dve', lambda e: e.tensor_scalar(out=omka[:], in0=kac[:], scalar1=-1.0, scalar2=1.0, op0=ALU.mult, op1=ALU.add), r=["r_cv"], w=["r_cv2"])
            stg = [sb(nc, esP, "r_stg%d" % i, [128, T]) for i in range(2)]
            stg2 = [sb(nc, esP, "r_stg2_0", [128, T])] * 2
            ld1 = [sb(nc, esP, "r_ld1_0", [128, T])] * 2
            ld2 = [sb(nc, esP, "r_ld2_0", [128, T])] * 2
            tmpa = [sb(nc, esP, "r_tmpa%d" % i, [128, 512]) for i in range(2)]
            tmpb = [sb(nc, esP, "r_tmpb%d" % i, [128, 512]) for i in range(2)]
            tcnt = [0]

            def mk_xi(i):
                for c in range(8):
                    P.op('dve', lambda e, c=c: e.scalar_tensor_tensor(out=xiT[:, c, :], in0=dT[:, c, :], scalar=mu[:, i, c:c + 1], in1=hT[:, c, :], op0=ALU.mult, op1=ALU.add),
                         r=["r_dT", "r_hT", "r_mu"], w=["r_xiT"])

            def store(oc, dst, src, skey):
                P.dma(dst[oc, :, :], src[:, :], r=[skey], w=[(dst.name if hasattr(dst, "name") else "fm", oc)])

            mk_xi(0)
            with ExitStack() as esw:
                wb = load_w_bf16(G, esw, "r_wr", G.Wl("rw_wr", j), 1024, 1024)
                def cons_r(oc, t0, tn, ps, pkey, m_):
                    s = stg[oc % 2]
                    P.op('act', lambda e: e.copy(out=s[:, t0:t0 + tn], in_=ps[:, 0:tn]), r=[pkey], w=[("r_stg", oc % 2), pkey])
                    if t0 == 2048:
                        P.dma(FM["rT"][oc, :, :], s[:, :], r=[("r_stg", oc % 2)], w=[("fm_rT", oc)])
                fm_linear(G, xiT, "r_xiT", wb, "r_wr_b", 1024, cons_r)
                P.barrier()
            mk_xi(1)
            for d in range(2):
                with ExitStack() as esw:
                    w1b = load_w_bf16(G, esw, "r_w1", G.Wl("rw_w1", j)[d], 1024, 64)
                    w2b = load_w_bf16(G, esw, "r_w2", G.Wl("rw_w2", j)[d], 64, 1024)
                    w0c = colvec(G, esw, "r_w0c", G.Wl("rw_w0", j)[d], "r_w0c")
                    P.op('dve', lambda e: e.tensor_scalar_mul(out=w0c[:], in0=w0c[:], scalar1=-1.0), r=["r_w0c"], w=["r_w0c"])
                    t1 = sb(nc, esw, "r_t1", [128, 1, T], BF16)
                    def cons_t(oc, t0, tn, ps, pkey, m_):
                        P.op('act', lambda e: e.activation(out=t1[0:m_, 0, t0:t0 + tn], in_=ps[0:m_, 0:tn], func=AF.Tanh), r=[pkey], w=["r_t1", pkey])
                    fm_linear(G, xiT, "r_xiT", w1b, "r_w1_b", 64, cons_t, pbanks=(2, 3))
                    def cons_w(oc, t0, tn, ps, pkey, m_):
                        s = stg[oc % 2]
                        k_ = tcnt[0] % 2
                        tcnt[0] += 1
                        ta, tb_ = tmpa[k_], tmpb[k_]
                        P.op('act', lambda e: e.activation(out=ta[:, 0:tn], in_=ps[:, 0:tn], func=AF.Exp, scale=-1.0, bias=w0c[:, oc:oc + 1]), r=[pkey, "r_w0c"], w=[("r_tmpa", k_), pkey])
                        P.op('act', lambda e: e.activation(out=tb_[:, 0:tn], in_=ta[:, 0:tn], func=AF.Ln, bias=G.one_c[:, 0:1], scale=1.0), r=[("r_tmpa", k_), "consts"], w=[("r_tmpb", k_)])
                        P.op('act', lambda e: e.activation(out=ta[:, 0:tn], in_=tb_[:, 0:tn], func=AF.Exp, scale=-1.0, bias=G.mhalf_c[:, 0:1]), r=[("r_tmpb", k_), "consts"], w=[("r_tmpa", k_)])
                        P.op('dve', lambda e: e.tensor_scalar_mul(out=s[:, t0:t0 + tn], in0=ta[:, 0:tn], scalar1=-1.0), r=[("r_tmpa", k_)], w=[("r_stg", oc % 2)])
                        if t0 == 2048:
                            P.dma(FM["lw%d" % d][oc, :, :], s[:, :], r=[("r_stg", oc % 2)], w=[("fm_lw%d" % d, oc)])
                    fm_linear(G, t1, "r_t1", w2b, "r_w2_b", 1024, cons_w, kparts=[(0, 64)])
                    P.barrier()
            mk_xi(2)
            with ExitStack() as esw:
                wb = load_w_bf16(G, esw, "r_wk", G.Wl("rw_wk", j), 1024, 1024)
                def cons_k(oc, t0, tn, ps, pkey, m_):
                    s, s2 = stg[oc % 2], stg2[oc % 2]
                    k_ = tcnt[0] % 2
                    tcnt[0] += 1
                    ta, tb_ = tmpa[k_], tmpb[k_]
                    P.op('act', lambda e: e.copy(out=s[:, t0:t0 + tn], in_=ps[:, 0:tn]), r=[pkey], w=[("r_stg", oc % 2), pkey])
                    P.op('dve', lambda e: e.tensor_scalar(out=ta[:, 0:tn], in0=s[:, t0:t0 + tn], scalar1=kkc[:, oc:oc + 1], scalar2=None, op0=ALU.mult), r=[("r_stg", oc % 2), "r_cv"], w=[("r_tmpa", k_)])
                    P.op('pool', lambda e: e.tensor_tensor(out=tb_[:, 0:tn], in0=ta[:, 0:tn], in1=ta[:, 0:tn], op=ALU.mult), r=[("r_tmpa", k_)], w=[("r_tmpb", k_)])
                    P.op('pe', lambda e: e.matmul(G.ps[4 + k_][:, 0:tn], lhsT=BO[:, :], rhs=tb_[:, 0:tn], start=True, stop=True), r=["r_BO", ("r_tmpb", k_)], w=[("ps", 4 + k_)])
                    P.op('act', lambda e: e.activation(out=tb_[:, 0:tn], in_=G.ps[4 + k_][:, 0:tn], func=AF.Sqrt), r=[("ps", 4 + k_)], w=[("r_tmpb", k_), ("ps", 4 + k_)])
                    P.op('dve', lambda e: e.tensor_scalar_max(out=tb_[:, 0:tn], in0=tb_[:, 0:tn], scalar1=1e-12), r=[("r_tmpb", k_)], w=[("r_tmpb", k_)])
                    P.op('dve', lambda e: e.reciprocal(out=tb_[:, 0:tn], in_=tb_[:, 0:tn]), r=[("r_tmpb", k_)], w=[("r_tmpb", k_)])
                    P.op('dve', lambda e: e.tensor_tensor(out=s2[:, t0:t0 + tn], in0=ta[:, 0:tn], in1=tb_[:, 0:tn], op=ALU.mult), r=[("r_tmpa", k_), ("r_tmpb", k_)], w=[("r_stg2", 0)])
                    if t0 == 2048:
                        P.dma(FM["kT"][oc, :, :], s[:, :], r=[("r_stg", oc % 2)], w=[("fm_kT", oc)])
                        P.dma(FM["kk"][oc, :, :], s2[:, :], r=[("r_stg2", 0)], w=[("fm_kk", oc)])
                fm_linear(G, xiT, "r_xiT", wb, "r_wk_b", 1024, cons_k)
                P.barrier()
            mk_xi(3)
            with ExitStack() as esw:
                wb = load_w_bf16(G, esw, "r_wv", G.Wl("rw_wv", j), 1024, 1024)
                if j > 0:
                    v1b = load_w_bf16(G, esw, "r_v1", G.Wl("rw_v1", j - 1), 1024, 32)
                    v2b = load_w_bf16(G, esw, "r_v2", G.Wl("rw_v2", j - 1), 32, 1024)
                    v0c = colvec(G, esw, "r_v0c", G.Wl("rw_v0", j - 1), "r_v0c")
                    t1 = sb(nc, esw, "r_t1v", [128, 1, T], BF16)
                    def cons_t(oc, t0, tn, ps, pkey, m_):
                        P.op('act', lambda e: e.copy(out=t1[0:m_, 0, t0:t0 + tn], in_=ps[0:m_, 0:tn]), r=[pkey], w=["r_t1v", pkey])
                    fm_linear(G, xiT, "r_xiT", v1b, "r_v1_b", 32, cons_t, pbanks=(2, 3))
                def cons_v(oc, t0, tn, ps, pkey, m_):
                    s = stg[oc % 2]
                    if j == 0:
                        P.op('act', lambda e: e.copy(out=s[:, t0:t0 + tn], in_=ps[:, 0:tn]), r=[pkey], w=[("r_stg", oc % 2), pkey])
                    else:
                        k_ = tcnt[0] % 2
                        tcnt[0] += 1
                        ta, tb_ = tmpa[k_], tmpb[k_]
                        if t0 == 0:
                            P.dma(ld1[oc % 2][:, :], FM["vf"][oc, :, :], r=[("fm_vf", oc)], w=[("r_ld1", 0)])
                        vf = ld1[oc % 2]
                        pb2 = 4 + k_
                        P.op('pe', lambda e: e.matmul(G.ps[pb2][:, 0:tn], lhsT=v2b[0:32, 0, oc * 128:(oc + 1) * 128], rhs=t1[0:32, 0, t0:t0 + tn], start=True, stop=True),
                             r=["r_t1v", "r_v2_b"], w=[("ps", pb2)])
                        P.op('act', lambda e: e.activation(out=ta[:, 0:tn], in_=G.ps[pb2][:, 0:tn], func=AF.Sigmoid, bias=v0c[:, oc:oc + 1], scale=1.0), r=[("ps", pb2), "r_v0c"], w=[("r_tmpa", k_), ("ps", pb2)])
                        P.op('dve', lambda e: e.tensor_tensor(out=tb_[:, 0:tn], in0=vf[:, t0:t0 + tn], in1=ps[:, 0:tn], op=ALU.subtract), r=[("r_ld1", 0), pkey], w=[("r_tmpb", k_)])
                        P.op('dve', lambda e: e.tensor_tensor(out=tb_[:, 0:tn], in0=tb_[:, 0:tn], in1=ta[:, 0:tn], op=ALU.mult), r=[("r_tmpa", k_), ("r_tmpb", k_)], w=[("r_tmpb", k_)])
                        P.op('dve', lambda e: e.tensor_tensor(out=s[:, t0:t0 + tn], in0=tb_[:, 0:tn], in1=ps[:, 0:tn], op=ALU.add), r=[("r_tmpb", k_), pkey], w=[("r_stg", oc % 2), pkey])
                    if t0 == 2048:
                        P.dma(FM["vT"][oc, :, :], s[:, :], r=[("r_stg", oc % 2)], w=[("fm_vT", oc)])
                        if j == 0:
                            P.dma(FM["vf"][oc, :, :], s[:, :], r=[("r_stg", oc % 2)], w=[("fm_vf", oc)])
                fm_linear(G, xiT, "r_xiT", wb, "r_wv_b", 1024, cons_v)
                P.barrier()
            mk_xi(4)
            for d in range(2):
                with ExitStack() as esw:
                    a1b = load_w_bf16(G, esw, "r_a1", G.Wl("rw_a1", j)[d], 1024, 64)
                    a2b = load_w_bf16(G, esw, "r_a2", G.Wl("rw_a2", j)[d], 64, 1024)
                    a0c = colvec(G, esw, "r_a0c", G.Wl("rw_a0", j)[d], "r_a0c")
                    t1 = sb(nc, esw, "r_t1a", [128, 1, T], BF16)
                    def cons_t(oc, t0, tn, ps, pkey, m_):
                        P.op('act', lambda e: e.copy(out=t1[0:m_, 0, t0:t0 + tn], in_=ps[0:m_, 0:tn]), r=[pkey], w=["r_t1a", pkey])
                    fm_linear(G, xiT, "r_xiT", a1b, "r_a1_b", 64, cons_t, pbanks=(2, 3))
                    def cons_a(oc, t0, tn, ps, pkey, m_):
                        s, s2 = stg[oc % 2], stg2[oc % 2]
                        k_ = tcnt[0] % 2
                        tcnt[0] += 1
                        ta, tb_ = tmpa[k_], tmpb[k_]
                        if t0 == 0:
                            P.dma(ld1[oc % 2][:, :], FM["kT"][oc, :, :], r=[("fm_kT", oc)], w=[("r_ld1", 0)])
                            P.dma(ld2[oc % 2][:, :], FM["kk"][oc, :, :], r=[("fm_kk", oc)], w=[("r_ld2", 0)])
                        kt_, kkt = ld1[oc % 2], ld2[oc % 2]
                        P.op('act', lambda e: e.activation(out=ta[:, 0:tn], in_=ps[:, 0:tn], func=AF.Sigmoid, bias=a0c[:, oc:oc + 1], scale=1.0), r=[pkey, "r_a0c"], w=[("r_tmpa", k_), pkey])
                        P.op('pool', lambda e: e.tensor_tensor(out=s2[:, t0:t0 + tn], in0=kkt[:, t0:t0 + tn], in1=ta[:, 0:tn], op=ALU.mult), r=[("r_ld2", 0), ("r_tmpa", k_)], w=[("r_stg2", 0)])
                        P.op('dve', lambda e: e.tensor_scalar(out=tb_[:, 0:tn], in0=ta[:, 0:tn], scalar1=kac[:, oc:oc + 1], scalar2=omka[:, oc:oc + 1], op0=ALU.mult, op1=ALU.add),
                             r=[("r_tmpa", k_), "r_cv", "r_cv2"], w=[("r_tmpb", k_)])
                        P.op('dve', lambda e: e.tensor_tensor(out=s[:, t0:t0 + tn], in0=tb_[:, 0:tn], in1=kt_[:, t0:t0 + tn], op=ALU.mult), r=[("r_tmpb", k_), ("r_ld1", 0)], w=[("r_stg", oc % 2)])
                        if t0 == 2048:
                            P.dma(FM["kd%d" % d][oc, :, :], s[:, :], r=[("r_stg", oc % 2)], w=[("fm_kd%d" % d, oc)])
                            P.dma(FM["b%d" % d][oc, :, :], s2[:, :], r=[("r_stg2", 0)], w=[("fm_b%d" % d, oc)])
                    fm_linear(G, t1, "r_t1a", a2b, "r_a2_b", 1024, cons_a, kparts=[(0, 64)])
                    P.barrier()
            mk_xi(5)
            with ExitStack() as esw:
                g1b = load_w_bf16(G, esw, "r_g1", G.Wl("rw_g1", j), 1024, 160)
                g2b = load_w_bf16(G, esw, "r_g2", G.Wl("rw_g2", j), 160, 1024)
                t1 = sb(nc, esw, "r_t1g", [128, 2, T], BF16)
                def cons_t(oc, t0, tn, ps, pkey, m_):
                    P.op('act', lambda e: e.activation(out=t1[0:m_, oc, t0:t0 + tn], in_=ps[0:m_, 0:tn], func=AF.Sigmoid), r=[pkey], w=["r_t1g", pkey])
                fm_linear(G, xiT, "r_xiT", g1b, "r_g1_b", 160, cons_t, pbanks=(2, 3))
                def cons_g(oc, t0, tn, ps, pkey, m_):
                    s = stg[oc % 2]
                    P.op('act', lambda e: e.copy(out=s[:, t0:t0 + tn], in_=ps[:, 0:tn]), r=[pkey], w=[("r_stg", oc % 2), pkey])
                    if t0 == 2048:
                        P.dma(FM["gT"][oc, :, :], s[:, :], r=[("r_stg", oc % 2)], w=[("fm_gT", oc)])
                fm_linear(G, t1, "r_t1g", g2b, "r_g2_b", 1024, cons_g, kparts=[(0, 128), (1, 32)])
                P.barrier()
            P.barrier()
        if not SKIP_SCAN:
            scan_stage(G)
        with ExitStack() as esR:
            zT = sb(nc, esR, "o_zT", [128, 8, T], BF16)
            rkc = colvec(G, esR, "o_rkc", G.Wl("rw_rk", j).rearrange("h k -> (h k)"), "o_cv")
            lgc = colvec(G, esR, "o_lgc", G.Wl("rw_lnx_g", j), "o_cv")
            lbc = colvec(G, esR, "o_lbc", G.Wl("rw_lnx_b", j), "o_cv")
            epsg = sb(nc, esR, "o_epsg", [128, 1])
            P.op('dve', lambda e: e.memset(epsg[:], RW_GN_EPS), w=["o_epsg"])
            with ExitStack() as esL:
                L = {nm: [sb(nc, esL, "o_%s" % nm, [128, T])] * 2 for nm in ("y0", "y1", "r", "kd0", "kd1", "v", "g")}
                wa = [sb(nc, esL, "o_wa%d" % i, [128, 512]) for i in range(2)]
                wb_ = [sb(nc, esL, "o_wb%d" % i, [128, 512]) for i in range(2)]
                wc_ = [sb(nc, esL, "o_wc%d" % i, [128, 512]) for i in range(2)]
                it = 0
                for p in range(8):
                    b = 0
                    for d in range(2):
                        for h in range(2):
                            P.dma(L["y%d" % d][b][h * 64:(h + 1) * 64, :], G.y_d[d, :, 2 * p + h, :], r=[("y_d", None)] if False else ["y_d"], w=[("o_L_y%d" % d, b)])
                    for nm, fmn in (("r", "rT"), ("kd0", "kd0"), ("kd1", "kd1"), ("v", "vT"), ("g", "gT")):
                        P.dma(L[nm][b][:, :], FM[fmn][p, :, :], r=[("fm_" + fmn, p)], w=[("o_L_" + nm, b)])
                    for (t0, tn) in TBLK:
                        k_ = it % 2
                        it += 1
                        a_, b2, c_ = wa[k_], wb_[k_], wc_[k_]
                        ka, kb, kc = ("o_wa", k_), ("o_wb", k_), ("o_wc", k_)
                        ts_ = slice(t0, t0 + tn)
                        P.op('dve', lambda e: e.tensor_tensor(out=a_[:, 0:tn], in0=L["y0"][b][:, ts_], in1=L["y1"][b][:, ts_], op=ALU.add), r=[("o_L_y0", b), ("o_L_y1", b)], w=[ka])
                        P.op('pe', lambda e: e.matmul(G.ps[k_][:, 0:tn], lhsT=BO[:, :], rhs=a_[:, 0:tn], start=True, stop=True), r=["r_BO", ka], w=[("ps", k_)])
                        P.op('dve', lambda e: e.scalar_tensor_tensor(out=a_[:, 0:tn], in0=G.ps[k_][:, 0:tn], scalar=-1.0 / 64, in1=a_[:, 0:tn], op0=ALU.mult, op1=ALU.add), r=[("ps", k_), ka], w=[ka, ("ps", k_)])
                        P.op('pool', lambda e: e.tensor_tensor(out=b2[:, 0:tn], in0=a_[:, 0:tn], in1=a_[:, 0:tn], op=ALU.mult), r=[ka], w=[kb])
                        P.op('pe', lambda e: e.matmul(G.ps[2 + k_][:, 0:tn], lhsT=BO[:, :], rhs=b2[:, 0:tn], start=True, stop=True), r=["r_BO", kb], w=[("ps", 2 + k_)])
                        P.op('act', lambda e: e.activation(out=b2[:, 0:tn], in_=G.ps[2 + k_][:, 0:tn], func=AF.Sqrt, scale=1.0 / 64, bias=epsg[:, 0:1]), r=[("ps", 2 + k_), "o_epsg"], w=[kb, ("ps", 2 + k_)])
                        P.op('dve', lambda e: e.reciprocal(out=b2[:, 0:tn], in_=b2[:, 0:tn]), r=[kb], w=[kb])
                        P.op('dve', lambda e: e.tensor_tensor(out=a_[:, 0:tn], in0=a_[:, 0:tn], in1=b2[:, 0:tn], op=ALU.mult), r=[ka, kb], w=[ka])
                        P.op('dve', lambda e: e.tensor_scalar(out=a_[:, 0:tn], in0=a_[:, 0:tn], scalar1=lgc[:, p:p + 1], scalar2=lbc[:, p:p + 1], op0=ALU.mult, op1=ALU.add), r=[ka, "o_cv"], w=[ka])
                        P.op('pool', lambda e: e.tensor_tensor(out=c_[:, 0:tn], in0=L["kd0"][b][:, ts_], in1=L["kd1"][b][:, ts_], op=ALU.add), r=[("o_L_kd0", b), ("o_L_kd1", b)], w=[kc])
                        P.op('dve', lambda e: e.scalar_tensor_tensor(out=c_[:, 0:tn], in0=c_[:, 0:tn], scalar=rkc[:, p:p + 1], in1=L["r"][b][:, ts_], op0=ALU.mult, op1=ALU.mult), r=[kc, "o_cv", ("o_L_r", b)], w=[kc])
                        P.op('pe', lambda e: e.matmul(G.ps[4 + k_][:, 0:tn], lhsT=BO[:, :], rhs=c_[:, 0:tn], start=True, stop=True), r=["r_BO", kc], w=[("ps", 4 + k_)])
                        P.op('dve', lambda e: e.tensor_tensor(out=c_[:, 0:tn], in0=G.ps[4 + k_][:, 0:tn], in1=L["v"][b][:, ts_], op=ALU.mult), r=[("ps", 4 + k_), ("o_L_v", b)], w=[kc, ("ps", 4 + k_)])
                        P.op('dve', lambda e: e.tensor_tensor(out=a_[:, 0:tn], in0=a_[:, 0:tn], in1=c_[:, 0:tn], op=ALU.add), r=[ka, kc], w=[ka])
                        P.op('dve', lambda e: e.tensor_tensor(out=zT[:, p, ts_], in0=a_[:, 0:tn], in1=L["g"][b][:, ts_], op=ALU.mult), r=[ka, ("o_L_g", b)], w=[("o_zT", p)])
                P.barrier()
            wob = load_w_bf16(G, esR, "o_wo", G.Wl("rw_wo", j), 1024, 1024)
            gate = [sb(nc, esR, "o_gate%d" % r_, [128, 1024]) for r_ in range(2)]
            LG = sb(nc, esR, "o_lg", [128, 1024])
            LB = sb(nc, esR, "o_lb", [128, 1024])
            for r_ in range(2):
                P.dma(gate[r_][:], G.modv[l, r_:r_ + 1, 2 * 1024:3 * 1024].partition_broadcast(128), r=[("modv", l)], w=["o_bc"])
            P.dma(LG[:], row(G.Wl("ln1_g", l)).partition_broadcast(128), w=["o_bc"])
            P.dma(LB[:], row(G.Wl("ln1_b", l)).partition_broadcast(128), w=["o_bc"])
            xt = [sb(nc, esR, "o_x%d" % i, [128, 1024]) for i in range(2)]
            tmp = [sb(nc, esR, "o_tmp%d" % i, [128, 1024]) for i in range(2)]
            yt = [sb(nc, esR, "o_y%d" % i, [128, 1024]) for i in range(2)]
            st = [sb(nc, esR, "o_st%d" % i, [128, 16]) for i in range(2)]
            for t in range(NT):
                b = t % 2
                r_ = 0 if t < 16 else 1
                P.dma(xt[b][:], G.xres[t * 128:(t + 1) * 128, :], r=[("xres", t)], w=[("o_x", b)])
                banks = [G.ps[0 + 2 * b], G.ps[1 + 2 * b]]
                okeys = [("ps", 0 + 2 * b), ("ps", 1 + 2 * b)]
                for hh in range(2):
                    for c in range(8):
                        P.op('pe', lambda e_, hh=hh, c=c: e_.matmul(banks[hh][:, :], lhsT=zT[:, c, t * 128:(t + 1) * 128], rhs=wob[:, c, hh * 512:(hh + 1) * 512],
                                                                   start=(c == 0), stop=(c == 7)), r=["o_zT", "o_wo_b"], w=[okeys[hh]])
                ln_epilogue(G, banks, okeys, xt[b], ("o_x", b), gate[r_], LG, LB, ["o_bc"], tmp[b], ("o_tmp", b), st[b], ("o_st", b), yt[b], ("o_y", b))
                P.dma(G.xres[t * 128:(t + 1) * 128, :], yt[b][:], r=[("o_y", b)], w=[("xres", t)])
            P.barrier()
        P.barrier()


def build(layers=(0, 1, 2, 3), stages=None):
    nc = bass.Bass("TRN2", target_bir_lowering=False)
    G = Ctx()
    G.nc = nc

    def din(name, shape):
        return nc.dram_tensor(name, list(shape), F32, kind="ExternalInput").ap()
    G.x_d = din("x", [TL, D])
    G.ctx_d = din("ctx", [TC, D])
    G.c_d = din("c", [1, D])
    G.cc_d = din("c_ctx", [1, D])
    G.used = {}
    specs = dict(WEIGHT_SPECS)

    def Wl(name, l):
        key = "%s_%d" % (name, l)
        if key not in G.used:
            G.used[key] = (name, l, din(key, specs[name][1:]))
        return G.used[key][2]
    G.Wl = Wl
    G.kc = {}

    def KC(name, shape):
        if name not in G.kc:
            G.kc[name] = din(name, shape)
        return G.kc[name]
    G.KC = KC
    G.ident_d = KC("k_ident", [128, 128])
    G.out_d = nc.dram_tensor("out", [TL, D], F32, kind="ExternalOutput").ap()
    G.outc_d = nc.dram_tensor("outc", [TC, D], F32, kind="ExternalOutput").ap()
    G.xres = nc.dram_tensor("xres", [T, D], F32, kind="Internal").ap()
    G.modv = nc.dram_tensor("modv", [4, 2, 6144], F32, kind="Internal").ap()
    G.mo_d = nc.dram_tensor("mo_d", [T, D], BF16, kind="Internal").ap()
    G.fm = {nm: nc.dram_tensor("fm_" + nm, [8, 128, T], F32, kind="Internal").ap()
            for nm in ("rT", "kT", "vT", "vf", "kk", "gT", "lw0", "lw1", "kd0", "kd1", "b0", "b1")}
    G.y_d = nc.dram_tensor("y_d", [2, 64, 16, T], F32, kind="Internal").ap()
    P = Prog(nc)
    G.P = P
    with ExitStack() as es:
        G.ps = [es.enter_context(nc.psum_tensor("ps%d" % i, [128, 512], F32)) for i in range(8)]
        G.identF = sb(nc, es, "identF", [128, 128])
        G.identB = sb(nc, es, "identB", [128, 128], BF16)
        G.eps_ln = sb(nc, es, "eps_ln", [128, 1])
        P.dma(G.identF[:], G.ident_d[:, :], w=["ident"])
        P.op('dve', lambda e: e.tensor_copy(out=G.identB[:], in_=G.identF[:]), r=["ident"], w=["identB"])
        P.op('dve', lambda e: e.memset(G.eps_ln[:], LN_EPS), w=["consts"])
        G.one_c = sb(nc, es, "one_c", [128, 1])
        G.mhalf_c = sb(nc, es, "mhalf_c", [128, 1])
        P.op('dve', lambda e: e.memset(G.one_c[:], 1.0), w=["consts"])
        P.op('dve', lambda e: e.memset(G.mhalf_c[:], -0.5), w=["consts"])
        for t in range(16):
            P.dma(G.xres[t * 128:(t + 1) * 128, :], G.x_d[t * 128:(t + 1) * 128, :], w=[("xres", t)])
        for t in range(2):
            P.dma(G.xres[TL + t * 128:TL + (t + 1) * 128, :], G.ctx_d[t * 128:(t + 1) * 128, :], w=[("xres", 16 + t)])
        stage_modvec(G, layers)
        for l in layers:
            if l % 2 == 0 and (stages is None or "mix" in stages):
                stage_even(G, l)
            if l % 2 == 1 and (stages is None or "mix" in stages):
                stage_rwkv(G, l)
            if stages is None or "ffn" in stages:
                stage_ffn(G, l)
        for t in range(16):
            P.dma(G.out_d[t * 128:(t + 1) * 128, :], G.xres[t * 128:(t + 1) * 128, :], r=[("xres", t)], w=[("out", t)])
        for t in range(2):
            P.dma(G.outc_d[t * 128:(t + 1) * 128, :], G.xres[TL + t * 128:TL + (t + 1) * 128, :], r=[("xres", 16 + t)], w=[("outc", t)])
        P.barrier()
    P.es.close()
    return nc, P, G


def make_consts():
    k = {"k_ident": np.eye(128, dtype=np.float32)}
    t = np.arange(TL)
    rowi = (t // 64).astype(np.float32)
    coli = (t % 64).astype(np.float32)
    for nm, dim, ng in (("A", 64, 8), ("B", 128, 4)):
        nf = dim // 4
        inv = (10000.0 ** (-np.arange(nf, dtype=np.float32) / nf)).astype(np.float32)
        ang = np.concatenate([rowi[:, None] * inv, coli[:, None] * inv], -1).astype(np.float32)
        k["k_cos" + nm] = np.ascontiguousarray(np.tile(np.cos(ang).astype(np.float32), (1, ng)))
        k["k_sin" + nm] = np.ascontiguousarray(np.tile(np.sin(ang).astype(np.float32), (1, ng)))
    p = np.arange(128, dtype=np.float32)
    k["k_cols"] = np.stack([127 - p, p, p + 1, 128 - p], 1).astype(np.float32)
    jj = p[None, :]
    pp = p[:, None]
    k["k_mats"] = np.ascontiguousarray(np.stack([np.maximum(jj - pp, 0), np.maximum(pp - jj, 0), (jj >= pp).astype(np.float32),
                                                 (jj <= pp).astype(np.float32)], 1).astype(np.float32))
    bo = np.zeros((128, 128), np.float32)
    bo[:64, :64] = 1.0
    bo[64:, 64:] = 1.0
    k["k_bo"] = bo
    i64 = np.arange(64)
    strict = (i64[:, None] < i64[None, :]).astype(np.float32)
    incl = (i64[:, None] <= i64[None, :]).astype(np.float32)
    def bdm(m):
        z = np.zeros((128, 128), np.float32)
        z[:64, :64] = m
        z[64:, 64:] = m
        return z
    k["k_smask"] = np.ascontiguousarray(np.stack([bdm(strict), bdm(strict.T), bdm(incl)], 1))
    return k


def kernel(**inputs):
    nc, _, G = build()
    consts = make_consts()
    in_maps = []
    for b in range(8):
        m = {"x": np.ascontiguousarray(inputs["x"][b]), "ctx": np.ascontiguousarray(inputs["ctx"][b]),
             "c": np.ascontiguousarray(inputs["c"][b:b + 1]), "c_ctx": np.ascontiguousarray(inputs["c_ctx"][None, :])}
        for key, (name, l, _ap) in G.used.items():
            m[key] = np.ascontiguousarray(inputs[name][l])
        for key in G.kc:
            m[key] = consts[key]
        in_maps.append(m)
    res = run_bass_kernel_spmd(nc, in_maps, core_ids=list(range(8)))
    return np.stack([r["out"] for r in res.results], axis=0).astype(np.float32)
```

```python
import math
import numpy as np
from contextlib import ExitStack
import concourse.bass as bass
import concourse.mybir as mybir
from concourse.bass_utils import run_bass_kernel_spmd

F32 = mybir.dt.float32
BF16 = mybir.dt.bfloat16
AF = mybir.ActivationFunctionType
ALU = mybir.AluOpType
AX = mybir.AxisListType

NDS = 8
SAME_ENGINE_SYNC = True
EMBED_WAIT = True

D = 1024
TL = 2048
TC = 256
T = TL + TC
NT = T // 128
DFF = 2816
NFC = DFF // 128
DEPTH = 4
ALPHA = (2.0 * DEPTH) ** 0.25
LN_EPS = 1e-6

WEIGHT_SPECS = [
    ("mod_w", (4, 1024, 6144)), ("mod_b", (4, 6144)), ("ln1_g", (4, 1024)), ("ln1_b", (4, 1024)),
    ("ln2_g", (4, 1024)), ("ln2_b", (4, 1024)), ("ffn_w1", (4, 1024, 2816)), ("ffn_w3", (4, 1024, 2816)),
    ("ffn_w2", (4, 2816, 1024)), ("ev_w_in", (2, 1024, 3584)), ("ev_w_out", (2, 1024, 1024)),
    ("da_lam_q1", (2, 64)), ("da_lam_k1", (2, 64)), ("da_lam_q2", (2, 64)), ("da_lam_k2", (2, 64)),
    ("da_gn_g", (2, 512)), ("rt_decay_logit", (2, 2, 4)), ("rw_mu", (2, 6, 1024)),
    ("rw_wr", (2, 1024, 1024)), ("rw_wk", (2, 1024, 1024)), ("rw_wv", (2, 1024, 1024)), ("rw_wo", (2, 1024, 1024)),
    ("rw_w0", (2, 2, 1024)), ("rw_w1", (2, 2, 1024, 64)), ("rw_w2", (2, 2, 64, 1024)),
    ("rw_a0", (2, 2, 1024)), ("rw_a1", (2, 2, 1024, 64)), ("rw_a2", (2, 2, 64, 1024)),
    ("rw_v0", (1, 1024)), ("rw_v1", (1, 1024, 32)), ("rw_v2", (1, 32, 1024)),
    ("rw_g1", (2, 1024, 160)), ("rw_g2", (2, 160, 1024)), ("rw_kk", (2, 1024)), ("rw_ka", (2, 1024)),
    ("rw_rk", (2, 16, 64)), ("rw_lnx_g", (2, 1024)), ("rw_lnx_b", (2, 1024)),
]


class Prog:
    def __init__(self, nc):
        self.nc = nc
        self.engs = {'pe': nc.tensor, 'act': nc.scalar, 'dve': nc.vector, 'pool': nc.gpsimd, 'sp': nc.sync}
        self.es = ExitStack()
        self.sem = {}
        for e in ['pe', 'act', 'dve', 'pool']:
            self.sem[('e', e)] = self.es.enter_context(nc.semaphore('s_' + e))
        for i in range(NDS):
            self.sem[('d', i)] = self.es.enter_context(nc.semaphore('d%d' % i))
        self.cnt = {k: 0 for k in self.sem}
        self.dnext = 0
        self.known = {e: {} for e in self.engs}
        self.res = {}
        self.nops = 0
        self.nwaits = 0

    def _get(self, key):
        name, sub = key if isinstance(key, tuple) else (key, None)
        d = self.res.setdefault(name, {})
        if sub not in d:
            d[sub] = [None, {}]
        return d[sub]

    def _conf(self, key):
        name, sub = key if isinstance(key, tuple) else (key, None)
        d = self.res.setdefault(name, {})
        if sub is None:
            return list(d.values())
        out = []
        if sub in d:
            out.append(d[sub])
        if None in d:
            out.append(d[None])
        return out

    def op(self, eng, fn, r=(), w=(), dma=False, noembed=False):
        deps = {}

        def need(k, v):
            if deps.get(k, 0) < v:
                deps[k] = v
        for key in r:
            for st in self._conf(key):
                if st[0] is not None:
                    need(*st[0])
        for key in w:
            for st in self._conf(key):
                if st[0] is not None:
                    need(*st[0])
                for k, v in st[1].items():
                    need(k, v)
        E = self.engs[eng]
        kn = self.known[eng]
        if dma:
            d = self.dnext
            self.dnext = (d + 1) % NDS
            sk = ('d', d)
            if self.cnt[sk]:
                need(sk, self.cnt[sk])
        else:
            sk = ('e', eng)
        wl = []
        for k, v in deps.items():
            if (not dma) and k == ('e', eng) and (eng == 'pe' or not SAME_ENGINE_SYNC):
                continue
            if kn.get(k, 0) >= v:
                continue
            kn[k] = v
            wl.append((k, v))
        emb = None
        if wl and EMBED_WAIT and not dma and not noembed:
            emb = wl.pop()
        for k, v in wl:
            E.wait_ge(self.sem[k], v)
            self.nwaits += 1
        ins = fn(E)
        if emb is not None:
            ins.wait_op(self.sem[emb[0]], emb[1], "sem-ge")
        inc = 16 if dma else 1
        self.cnt[sk] += inc
        ins.then_inc(self.sem[sk], inc)
        ev = (sk, self.cnt[sk])
        self.nops += 1
        for key in r:
            st = self._get(key)
            if st[1].get(ev[0], 0) < ev[1]:
                st[1][ev[0]] = ev[1]
        for key in w:
            name, sub = key if isinstance(key, tuple) else (key, None)
            if sub is None:
                self.res[name] = {None: [ev, {}]}
            else:
                st = self._get(key)
                st[0] = ev
                st[1] = {}
        return ev

    def dma(self, out, in_, r=(), w=(), eng='sp', **kw):
        return self.op(eng, lambda e: e.dma_start(out=out, in_=in_, **kw), r=r, w=w, dma=True)

    def barrier(self, engines=('pe', 'act', 'dve', 'pool', 'sp')):
        for eng in engines:
            E = self.engs[eng]
            kn = self.known[eng]
            for k, v in self.cnt.items():
                if v and kn.get(k, 0) < v:
                    kn[k] = v
                    E.wait_ge(self.sem[k], v)
                    self.nwaits += 1


class Ctx:
    pass


_SBN = [0]


def sb(nc, es, name, shape, dt=F32):
    _SBN[0] += 1
    return es.enter_context(nc.sbuf_tensor("%s_u%d" % (name, _SBN[0]), list(shape), dt))


_CLN = [0]


def colload(G, es, dst2d, dkey, src_flat, n):
    nc, P = G.nc, G.P
    _CLN[0] += 1
    k = "cl_stg%d" % _CLN[0]
    stg = sb(nc, es, k, [n, 128])
    P.dma(stg[:], src_flat.rearrange("(j p) -> j p", p=128), w=[k])
    P.op('pe', lambda e: e.transpose(out=G.ps[7][:, 0:n], in_=stg[:, :], identity=G.identF[0:n, 0:n]), r=[k, "ident"], w=[("ps", 7)])
    P.op('dve', lambda e: e.tensor_copy(out=dst2d, in_=G.ps[7][:, 0:n]), r=[("ps", 7)], w=[dkey, ("ps", 7)])


def stage_modvec(G, layers):
    nc, P = G.nc, G.P
    with ExitStack() as es:
        craw = sb(nc, es, "mv_craw", [128, 2, 8])
        cT = sb(nc, es, "mv_cT", [128, 8, 2])
        wt = [sb(nc, es, "mv_w%d" % i, [128, 8, 512]) for i in range(2)]
        bt = [sb(nc, es, "mv_b%d" % i, [2, 512]) for i in range(2)]
        ot = [sb(nc, es, "mv_o%d" % i, [2, 512]) for i in range(2)]
        colload(G, es, craw[:, 0, :], "mv_craw", G.c_d[0, :], 8)
        colload(G, es, craw[:, 1, :], "mv_craw", G.cc_d[0, :], 8)
        for r in range(2):
            P.op('act', lambda e, r=r: e.activation(out=cT[:, :, r], in_=craw[:, r, :], func=AF.Silu),
                 r=["mv_craw"], w=[("mv_cT", r)])
        i = 0
        for l in layers:
            for nb in range(12):
                b = i % 2
                i += 1
                P.dma(wt[b][:], G.Wl("mod_w", l)[:, nb * 512:(nb + 1) * 512].rearrange("(c p) n -> p c n", p=128),
                      w=[("mv_w", b)])
                P.dma(bt[b][:], G.Wl("mod_b", l).rearrange("(o n) -> o n", o=1)[:, nb * 512:(nb + 1) * 512].partition_broadcast(2), w=[("mv_b", b)])
                ps = G.ps[b]
                for c in range(8):
                    P.op('pe', lambda e, c=c, b=b, ps=ps: e.matmul(ps[0:2, :], lhsT=cT[:, c, :], rhs=wt[b][:, c, :],
                                                                  start=(c == 0), stop=(c == 7)),
                         r=["mv_cT", ("mv_w", b)], w=[("ps", b)])
                P.op('dve', lambda e, b=b, ps=ps: e.tensor_tensor(out=ot[b][:], in0=ps[0:2, :], in1=bt[b][:], op=ALU.add),
                     r=[("ps", b), ("mv_b", b)], w=[("mv_o", b), ("ps", b)])
                P.dma(G.modv[l, :, nb * 512:(nb + 1) * 512], ot[b][:], r=[("mv_o", b)], w=[("modv", l)])
        P.barrier()


def load_modcols(G, es, l, name):
    nc, P = G.nc, G.P
    mc = sb(nc, es, name, [128, 2, 6, 8])
    for r in range(2):
        _CLN[0] += 1
        k = "cl_stg%d" % _CLN[0]
        stg = sb(nc, es, k, [48, 128])
        P.dma(stg[:], G.modv[l, r, :].rearrange("(j p) -> j p", p=128), r=[("modv", l)], w=[k])
        P.op('pe', lambda e, stg=stg: e.transpose(out=G.ps[7][:, 0:48], in_=stg[:, :], identity=G.identF[0:48, 0:48]), r=[k, "ident"], w=[("ps", 7)])
        P.op('dve', lambda e, r=r: e.tensor_copy(out=mc[:, r, :, :].rearrange("p i c -> p (i c)"), in_=G.ps[7][:, 0:48]), r=[("ps", 7)], w=[name, ("ps", 7)])
    for r in range(2):
        for i in (1, 4):
            P.op('dve', lambda e, r=r, i=i: e.tensor_scalar_add(out=mc[:, r, i, :], in0=mc[:, r, i, :], scalar1=1.0),
                 r=[name], w=[name])
    return mc


def transpose_modulate(G, xt, xkey, hT, hkey, col0, mc, r, ish, isc, pbase):
    P = G.P
    for half in range(2):
        pb = pbase + half
        ps = G.ps[pb]
        for j in range(4):
            c = half * 4 + j
            P.op('pe', lambda e, c=c, j=j, ps=ps: e.transpose(out=ps[:, j * 128:(j + 1) * 128], in_=xt[:, c * 128:(c + 1) * 128],
                                                           identity=G.identF[:]),
                 r=[xkey, "ident"], w=[("ps", pb)])
        for j in range(4):
            c = half * 4 + j
            eng = 'act' if j % 2 == 0 else 'dve'
            if eng == 'act':
                P.op('act', lambda e, c=c, j=j, ps=ps: e.activation(out=hT[:, c, col0:col0 + 128], in_=ps[:, j * 128:(j + 1) * 128],
                                                                   func=AF.Identity, scale=mc[:, r, isc, c:c + 1], bias=mc[:, r, ish, c:c + 1]),
                     r=[("ps", pb), "mc"], w=[hkey, ("ps", pb)])
            else:
                P.op('dve', lambda e, c=c, j=j, ps=ps: e.tensor_scalar(out=hT[:, c, col0:col0 + 128], in0=ps[:, j * 128:(j + 1) * 128],
                                                                      scalar1=mc[:, r, isc, c:c + 1], scalar2=mc[:, r, ish, c:c + 1],
                                                                      op0=ALU.mult, op1=ALU.add),
                     r=[("ps", pb), "mc"], w=[hkey, ("ps", pb)])


def ln_epilogue(G, o_banks, okeys, xt, xkey, Gt, LGt, LBt, bkeys, tmp, tkey, st, skey, yt, ykey):
    P = G.P
    for h in range(2):
        sl = slice(h * 512, (h + 1) * 512)
        P.op('dve', lambda e, h=h, sl=sl: e.tensor_tensor(out=tmp[:, sl], in0=o_banks[h][:, :], in1=Gt[:, sl], op=ALU.mult),
             r=[okeys[h]] + bkeys, w=[tkey, okeys[h]])
    P.op('dve', lambda e: e.scalar_tensor_tensor(out=tmp[:, :], in0=xt[:, :], scalar=ALPHA, in1=tmp[:, :], op0=ALU.mult, op1=ALU.add),
         r=[xkey, tkey], w=[tkey])
    for h in range(2):
        P.op('dve', lambda e, h=h: e.bn_stats(out=st[:, h * 6:(h + 1) * 6], in_=tmp[:, h * 512:(h + 1) * 512]), r=[tkey], w=[skey])
    P.op('dve', lambda e: e.bn_aggr(out=st[:, 12:14], in_=st[:, 0:12]), r=[skey], w=[skey])
    P.op('act', lambda e: e.activation(out=st[:, 14:15], in_=st[:, 13:14], func=AF.Sqrt, bias=G.eps_ln[:, 0:1], scale=1.0), r=[skey, "consts"], w=[skey])
    P.op('dve', lambda e: e.reciprocal(out=st[:, 15:16], in_=st[:, 14:15]), r=[skey], w=[skey])
    P.op('dve', lambda e: e.tensor_scalar(out=tmp[:, :], in0=tmp[:, :], scalar1=st[:, 12:13], scalar2=st[:, 15:16],
                                          op0=ALU.subtract, op1=ALU.mult), r=[skey, tkey], w=[tkey])
    P.op('pool', lambda e: e.tensor_tensor(out=tmp[:, :], in0=tmp[:, :], in1=LGt[:, :], op=ALU.mult), r=[tkey] + bkeys, w=[tkey])
    P.op('pool', lambda e: e.tensor_tensor(out=yt[:, :], in0=tmp[:, :], in1=LBt[:, :], op=ALU.add), r=[tkey] + bkeys, w=[ykey])


def load_bcast(G, tile, key, src_row):
    G.P.dma(tile[:], src_row.partition_broadcast(128), w=[key])


def stage_ffn(G, l):
    nc, P = G.nc, G.P
    W1, W3, W2 = G.Wl("ffn_w1", l), G.Wl("ffn_w3", l), G.Wl("ffn_w2", l)
    with ExitStack() as es:
        mc = load_modcols(G, es, l, "mc")
        gate = [sb(nc, es, "f_gate%d" % r, [128, 1024]) for r in range(2)]
        LG = sb(nc, es, "f_lg", [128, 1024])
        LB = sb(nc, es, "f_lb", [128, 1024])
        for r in range(2):
            P.dma(gate[r][:], G.modv[l, r:r + 1, 5 * 1024:6 * 1024].partition_broadcast(128), r=[("modv", l)], w=["f_bc"])
        P.dma(LG[:], G.Wl("ln2_g", l).rearrange("(o n) -> o n", o=1).partition_broadcast(128), w=["f_bc"])
        P.dma(LB[:], G.Wl("ln2_b", l).rearrange("(o n) -> o n", o=1).partition_broadcast(128), w=["f_bc"])
        w2b = sb(nc, es, "f_w2b", [128, NFC, 1024], BF16)
        w2s = [sb(nc, es, "f_w2s%d" % i, [128, 2, 1024]) for i in range(2)]
        for i in range(NFC // 2):
            b = i % 2
            P.dma(w2s[b][:], W2[i * 256:(i + 1) * 256, :].rearrange("(c p) n -> p c n", p=128), w=[("f_w2s", b)])
            P.op('pool', lambda e, i=i, b=b: e.tensor_copy(out=w2b[:, 2 * i:2 * i + 2, :], in_=w2s[b][:]),
                 r=[("f_w2s", b)], w=[("f_w2b", i)])
        NH = 2
        TPH = NT // NH
        TOKH = TPH * 128
        NTB = TOKH // 384
        hT = sb(nc, es, "f_hT", [128, 8, TOKH], BF16)
        gT = sb(nc, es, "f_gT", [128, NFC, TOKH], BF16)
        xt = [sb(nc, es, "f_x%d" % i, [128, 1024]) for i in range(2)]
        w1s = [sb(nc, es, "f_w1s%d" % i, [128, 8, 128]) for i in range(2)]
        w3s = [sb(nc, es, "f_w3s%d" % i, [128, 8, 128]) for i in range(2)]
        w1b = [sb(nc, es, "f_w1b%d" % i, [128, 8, 128], BF16) for i in range(2)]
        w3b = [sb(nc, es, "f_w3b%d" % i, [128, 8, 128], BF16) for i in range(2)]
        sa = [sb(nc, es, "f_sa%d" % i, [128, 384]) for i in range(2)]
        tmp = [sb(nc, es, "f_tmp%d" % i, [128, 1024]) for i in range(2)]
        yt = [sb(nc, es, "f_y%d" % i, [128, 1024]) for i in range(2)]
        st = [sb(nc, es, "f_st%d" % i, [128, 16]) for i in range(2)]
        xi = 0
        wi = 0
        for hf in range(NH):
            for tt in range(TPH):
                t = hf * TPH + tt
                b = xi % 2
                xi += 1
                r = 0 if t < 16 else 1
                P.dma(xt[b][:], G.xres[t * 128:(t + 1) * 128, :], r=[("xres", t)], w=[("f_x", b)])
                transpose_modulate(G, xt[b], ("f_x", b), hT, ("f_hT", tt), tt * 128, mc, r, 3, 4, 0)
            for cb in range(NFC):
                b = wi % 2
                wi += 1
                P.dma(w1s[b][:], W1[:, cb * 128:(cb + 1) * 128].rearrange("(c p) n -> p c n", p=128), w=[("f_w1s", b)])
                P.dma(w3s[b][:], W3[:, cb * 128:(cb + 1) * 128].rearrange("(c p) n -> p c n", p=128), w=[("f_w3s", b)])
                P.op('pool', lambda e, b=b: e.tensor_copy(out=w1b[b][:], in_=w1s[b][:]), r=[("f_w1s", b)], w=[("f_w1b", b)])
                P.op('pool', lambda e, b=b: e.tensor_copy(out=w3b[b][:], in_=w3s[b][:]), r=[("f_w3s", b)], w=[("f_w3b", b)])
                for sub in range(1):
                    fc = cb
                    for tb in range(NTB):
                        tsl = slice(tb * 384, (tb + 1) * 384)
                        k = (fc * NTB + tb) % 2
                        pa, pb_ = 2 + 2 * k, 3 + 2 * k
                        for c in range(8):
                            P.op('pe', lambda e, c=c, pa=pa, b=b, sub=sub, tsl=tsl: e.matmul(
                                G.ps[pa][:, 0:384], lhsT=w1b[b][:, c, sub * 128:(sub + 1) * 128], rhs=hT[:, c, tsl],
                                start=(c == 0), stop=(c == 7)), r=[("f_w1b", b), "f_hT"], w=[("ps", pa)])
                        for c in range(8):
                            P.op('pe', lambda e, c=c, pb_=pb_, b=b, sub=sub, tsl=tsl: e.matmul(
                                G.ps[pb_][:, 0:384], lhsT=w3b[b][:, c, sub * 128:(sub + 1) * 128], rhs=hT[:, c, tsl],
                                start=(c == 0), stop=(c == 7)), r=[("f_w3b", b), "f_hT"], w=[("ps", pb_)])
                        P.op('act', lambda e, k=k, pa=pa: e.activation(out=sa[k][:, :], in_=G.ps[pa][:, 0:384], func=AF.Silu),
                             r=[("ps", pa)], w=[("f_sa", k), ("ps", pa)])
                        P.op('dve', lambda e, k=k, pb_=pb_, fc=fc, tsl=tsl: e.tensor_tensor(
                            out=gT[:, fc, tsl], in0=G.ps[pb_][:, 0:384], in1=sa[k][:, :], op=ALU.mult),
                            r=[("ps", pb_), ("f_sa", k)], w=[("f_gT", fc), ("ps", pb_)])
            for tt in range(TPH):
                t = hf * TPH + tt
                b = xi % 2
                xi += 1
                r = 0 if t < 16 else 1
                P.dma(xt[b][:], G.xres[t * 128:(t + 1) * 128, :], r=[("xres", t)], w=[("f_x", b)])
                k = tt % 2
                banks = [G.ps[0 + 2 * k], G.ps[1 + 2 * k]]
                okeys = [("ps", 0 + 2 * k), ("ps", 1 + 2 * k)]
                for h in range(2):
                    for fc in range(NFC):
                        P.op('pe', lambda e, h=h, fc=fc, tt=tt, banks=banks: e.matmul(
                            banks[h][:, :], lhsT=gT[:, fc, tt * 128:(tt + 1) * 128], rhs=w2b[:, fc, h * 512:(h + 1) * 512],
                            start=(fc == 0), stop=(fc == NFC - 1)), r=["f_gT", "f_w2b"], w=[okeys[h]])
                ln_epilogue(G, banks, okeys, xt[b], ("f_x", b), gate[r], LG, LB, ["f_bc"], tmp[k], ("f_tmp", k),
                            st[k], ("f_st", k), yt[k], ("f_y", k))
                P.dma(G.xres[t * 128:(t + 1) * 128, :], yt[k][:], r=[("f_y", k)], w=[("xres", t)])
        P.barrier()


def row(ap):
    return ap.rearrange("(o n) -> o n", o=1)


def inproj_block(G, es_w, Wap, j0, ncols, hT, hkey, wtag):
    nc, P = G.nc, G.P
    ws = sb(nc, es_w, wtag + "_s", [128, 8, ncols])
    wb = sb(nc, es_w, wtag + "_b", [128, 8, ncols], BF16)
    P.dma(ws[:], Wap[:, j0:j0 + ncols].rearrange("(c p) n -> p c n", p=128), w=[wtag + "_s"])
    P.op('pool', lambda e: e.tensor_copy(out=wb[:], in_=ws[:]), r=[wtag + "_s"], w=[wtag + "_b"])
    return wb


def rope_evac(G, ps, pkey, t, qr, qkey, cosT, sinT, ckey, tmp, tkey, half, scale=None):
    P = G.P
    ng = 512 // (2 * half)
    if t >= 16:
        if scale is None:
            P.op('act', lambda e: e.copy(out=qr[:, :], in_=ps[:, :]), r=[pkey], w=[qkey, pkey])
        else:
            P.op('act', lambda e: e.mul(out=qr[:, :], in_=ps[:, :], mul=scale), r=[pkey], w=[qkey, pkey])
        return
    pv = ps[:, :].rearrange("p (g two d) -> p g two d", g=ng, two=2)
    qv = qr[:, :].rearrange("p (g two d) -> p g two d", g=ng, two=2)
    x1, x2 = pv[:, :, 0, :], pv[:, :, 1, :]
    cs = cosT[:, :].rearrange("p (g d) -> p g d", g=ng)
    sn = sinT[:, :].rearrange("p (g d) -> p g d", g=ng)
    t1 = tmp[:, 0:256].rearrange("p (g d) -> p g d", g=ng)
    t2 = tmp[:, 256:512].rearrange("p (g d) -> p g d", g=ng)
    P.op('dve', lambda e: e.tensor_tensor(out=t1, in0=x1, in1=cs, op=ALU.mult), r=[pkey, ckey], w=[tkey])
    P.op('dve', lambda e: e.tensor_tensor(out=t2, in0=x2, in1=sn, op=ALU.mult), r=[pkey, ckey], w=[tkey])
    P.op('dve', lambda e: e.tensor_tensor(out=qv[:, :, 0, :], in0=t1, in1=t2, op=ALU.subtract), r=[tkey], w=[qkey])
    P.op('dve', lambda e: e.tensor_tensor(out=t1, in0=x1, in1=sn, op=ALU.mult), r=[pkey, ckey], w=[tkey])
    P.op('dve', lambda e: e.tensor_tensor(out=t2, in0=x2, in1=cs, op=ALU.mult), r=[pkey, ckey], w=[tkey, pkey])
    P.op('dve', lambda e: e.tensor_tensor(out=qv[:, :, 1, :], in0=t1, in1=t2, op=ALU.add), r=[tkey], w=[qkey])
    if scale is not None:
        P.op('act', lambda e: e.mul(out=qr[:, :], in_=qr[:, :], mul=scale), r=[qkey], w=[qkey])


def transpose4(G, src, skey, dstT, dkey, t, pbank):
    P = G.P
    psb = G.ps[pbank][:, 0:256].bitcast(BF16)
    for g in range(4):
        P.op('pe', lambda e, g=g: e.transpose(out=psb[:, g * 128:(g + 1) * 128], in_=src[:, g * 128:(g + 1) * 128], identity=G.identB[:]),
             r=[skey, "identB"], w=[("ps", pbank)])
    P.op('act', lambda e: e.copy(out=dstT[:, :, t * 128:(t + 1) * 128], in_=psb.rearrange("p (g n) -> p g n", g=4)),
         r=[("ps", pbank)], w=[dkey, ("ps", pbank)])


def stage_even(G, l):
    nc, P = G.nc, G.P
    e = l // 2
    lam_init = 0.8 - 0.6 * math.exp(-0.3 * l)
    Win, Wout = G.Wl("ev_w_in", e), G.Wl("ev_w_out", e)
    mo_d = G.mo_d
    with ExitStack() as es:
        mc = load_modcols(G, es, l, "mc")
        hT = sb(nc, es, "e_hT", [128, 8, T], BF16)
        xt = [sb(nc, es, "e_x%d" % i, [128, 1024]) for i in range(2)]
        for t in range(NT):
            b = t % 2
            P.dma(xt[b][:], G.xres[t * 128:(t + 1) * 128, :], r=[("xres", t)], w=[("e_x", b)])
            transpose_modulate(G, xt[b], ("e_x", b), hT, ("e_hT", t), t * 128, mc, 0 if t < 16 else 1, 0, 1, 0)
        cosT = [sb(nc, es, "e_cos%d" % i, [128, 256]) for i in range(2)]
        sinT = [sb(nc, es, "e_sin%d" % i, [128, 256]) for i in range(2)]
        qr = [sb(nc, es, "e_qr%d" % i, [128, 512], BF16) for i in range(2)]
        rtmp = [sb(nc, es, "e_rtmp%d" % i, [128, 512]) for i in range(2)]

        def proj_tile(wb, wkey, t, pbank):
            ps = G.ps[pbank]
            for c in range(8):
                P.op('pe', lambda e_, c=c: e_.matmul(ps[:, :], lhsT=hT[:, c, t * 128:(t + 1) * 128], rhs=wb[:, c, :],
                                                    start=(c == 0), stop=(c == 7)), r=[("e_hT", t), wkey], w=[("ps", pbank)])
            return ps

        def load_tables(t, b, which):
            if t < 16:
                P.dma(cosT[b][:], G.KC("k_cos" + which, [TL, 256])[t * 128:(t + 1) * 128, :], w=[("e_cs", b)])
                P.dma(sinT[b][:], G.KC("k_sin" + which, [TL, 256])[t * 128:(t + 1) * 128, :], w=[("e_cs", b)])

        with ExitStack() as esA:
            aqT = sb(nc, esA, "a_qT", [128, 4, T], BF16)
            akT = sb(nc, esA, "a_kT", [128, 4, T], BF16)
            av = sb(nc, esA, "a_v", [128, NT, 512], BF16)
            for j, dst in ((0, aqT), (1, akT)):
                with ExitStack() as esw:
                    wb = inproj_block(G, esw, Win, j * 512, 512, hT, "e_hT", "a_w%d" % j)
                    for t in range(NT):
                        b = t % 2
                        load_tables(t, b, "A")
                        ps = proj_tile(wb, "a_w%d_b" % j, t, 4 + b)
                        rope_evac(G, ps, ("ps", 4 + b), t, qr[b], ("e_qr", b), cosT[b], sinT[b], ("e_cs", b), rtmp[b], ("e_rtmp", b), 32)
                        transpose4(G, qr[b], ("e_qr", b), dst, ("a_T%d" % j, t), t, 6 + b)
                    P.barrier()
            with ExitStack() as esw:
                wb = inproj_block(G, esw, Win, 2 * 512, 512, hT, "e_hT", "a_w2")
                for t in range(NT):
                    b = t % 2
                    ps = proj_tile(wb, "a_w2_b", t, 4 + b)
                    P.op('act', lambda e_, t=t, ps=ps: e_.copy(out=av[:, t, :], in_=ps[:, :]), r=[("ps", 4 + b)], w=[("a_v", t), ("ps", 4 + b)])
                P.barrier()
            lam4 = sb(nc, esA, "a_lam4", [128, 4, 64])
            lamc = sb(nc, esA, "a_lamc", [128, 8])
            for i, nm in enumerate(("da_lam_q1", "da_lam_k1", "da_lam_q2", "da_lam_k2")):
                P.dma(lam4[:, i, :], row(G.Wl(nm, e)).partition_broadcast(128), w=["a_lam4"])
            for i in range(2):
                P.op('dve', lambda e_, i=i: e_.tensor_tensor(out=lam4[:, 2 * i, :], in0=lam4[:, 2 * i, :], in1=lam4[:, 2 * i + 1, :], op=ALU.mult),
                     r=["a_lam4"], w=["a_lam4"])
                P.op('dve', lambda e_, i=i: e_.reduce_sum(out=lamc[:, i:i + 1], in_=lam4[:, 2 * i, :], axis=AX.X), r=["a_lam4"], w=["a_lamc"])
            P.op('act', lambda e_: e_.activation(out=lamc[:, 2:4], in_=lamc[:, 0:2], func=AF.Exp), r=["a_lamc"], w=["a_lamc"])
            P.op('dve', lambda e_: e_.tensor_tensor(out=lamc[:, 4:5], in0=lamc[:, 3:4], in1=lamc[:, 2:3], op=ALU.subtract), r=["a_lamc"], w=["a_lamc"])
            P.op('dve', lambda e_: e_.tensor_scalar_add(out=lamc[:, 5:6], in0=lamc[:, 4:5], scalar1=-lam_init), r=["a_lamc"], w=["a_lamc"])
            neglam = lamc[:, 5:6]
            Pm = [sb(nc, esA, "a_P%d" % i, [128, T], BF16) for i in range(2)]
            PT = [sb(nc, esA, "a_PT%d" % i, [128, NT, 128], BF16) for i in range(2)]
            sm = [sb(nc, esA, "a_sm%d" % i, [128, 16]) for i in range(2)]
            ao = sb(nc, esA, "a_ao", [128, 512])
            ao2 = sb(nc, esA, "a_ao2", [128, 512])
            aob = [sb(nc, esA, "a_aob%d" % i, [128, 512], BF16) for i in range(2)]
            rs = sb(nc, esA, "a_rs", [128, 16])
            SC = 64 ** -0.5
            def kinfo(qt):
                ktiles = list(range(NT)) if qt < 16 else [16, 17]
                k0 = ktiles[0] * 128
                nk = len(ktiles) * 128
                return ktiles, k0, nk, (nk + 511) // 512

            def stage1(ui, qt, h, m):
                ktiles, k0, nk, nbk = kinfo(qt)
                pb_ = ui % 2
                smt, Pmt = sm[pb_], Pm[pb_]
                psl = slice(m * 64, (m + 1) * 64)
                for jb in range(nbk):
                    w_ = min(512, nk - jb * 512)
                    P.op('pe', lambda e_, jb=jb, w_=w_: e_.matmul(G.ps[jb][:, 0:w_], lhsT=aqT[psl, h, qt * 128:(qt + 1) * 128],
                                                                 rhs=akT[psl, h, k0 + jb * 512:k0 + jb * 512 + w_], start=True, stop=True),
                         r=[("a_T0", qt), "a_T1"], w=[("ps", jb)])
                for jb in range(nbk):
                    w_ = min(512, nk - jb * 512)
                    P.op('dve', lambda e_, jb=jb, w_=w_: e_.reduce_max(out=smt[:, jb:jb + 1], in_=G.ps[jb][:, 0:w_], axis=AX.X),
                         r=[("ps", jb)], w=[("a_sm", pb_), ("ps", jb)])
                P.op('dve', lambda e_: e_.reduce_max(out=smt[:, 6:7], in_=smt[:, 0:nbk], axis=AX.X), r=[("a_sm", pb_)], w=[("a_sm", pb_)])
                P.op('dve', lambda e_: e_.tensor_scalar_mul(out=smt[:, 7:8], in0=smt[:, 6:7], scalar1=-SC), r=[("a_sm", pb_)], w=[("a_sm", pb_)])
                for jb in range(nbk):
                    w_ = min(512, nk - jb * 512)
                    P.op('act', lambda e_, jb=jb, w_=w_: e_.activation(out=Pmt[:, jb * 512:jb * 512 + w_], in_=G.ps[jb][:, 0:w_], func=AF.Exp,
                                                                      scale=SC, bias=smt[:, 7:8], accum_out=smt[:, 8 + jb:9 + jb]),
                         r=[("ps", jb), ("a_sm", pb_)], w=[("a_P", pb_), ("a_sm", pb_), ("ps", jb)], noembed=True)
                P.op('act', lambda e_: e_.copy(out=smt[:, 0:nbk], in_=smt[:, 8:8 + nbk]), r=[("a_sm", pb_)], w=[("a_sm", pb_)])
                P.op('dve', lambda e_: e_.reduce_sum(out=smt[:, 14:15], in_=smt[:, 0:nbk], axis=AX.X), r=[("a_sm", pb_)], w=[("a_sm", pb_)])
                P.op('dve', lambda e_: e_.reciprocal(out=smt[:, 15:16], in_=smt[:, 14:15]), r=[("a_sm", pb_)], w=[("a_sm", pb_)])

            def stage2(ui, qt, h, m):
                ktiles, k0, nk, nbk = kinfo(qt)
                pb_ = ui % 2
                smt, Pmt, PTt = sm[pb_], Pm[pb_], PT[pb_]
                nkt = len(ktiles)
                for g0 in range(0, nkt, 8):
                    gb = 5 + (g0 // 8) % 2
                    psb = G.ps[gb][:, :].bitcast(BF16)
                    n_ = min(8, nkt - g0)
                    for i in range(n_):
                        P.op('pe', lambda e_, i=i, g0=g0, psb=psb: e_.transpose(out=psb[:, i * 128:(i + 1) * 128],
                                                                             in_=Pmt[:, (g0 + i) * 128:(g0 + i + 1) * 128], identity=G.identB[:]),
                             r=[("a_P", pb_), "identB"], w=[("ps", gb)])
                    evac_eng = 'dve' if (g0 // 8) % 2 == 0 else 'act'
                    if evac_eng == 'dve':
                        P.op('dve', lambda e_, g0=g0, n_=n_, psb=psb: e_.tensor_copy(out=PTt[:, g0:g0 + n_, :],
                                                                                     in_=psb[:, 0:n_ * 128].rearrange("p (g n) -> p g n", g=n_)),
                             r=[("ps", gb)], w=[("a_PT", pb_), ("ps", gb)])
                    else:
                        P.op('act', lambda e_, g0=g0, n_=n_, psb=psb: e_.copy(out=PTt[:, g0:g0 + n_, :],
                                                                              in_=psb[:, 0:n_ * 128].rearrange("p (g n) -> p g n", g=n_)),
                             r=[("ps", gb)], w=[("a_PT", pb_), ("ps", gb)])
                osl = slice(m * 128, (m + 1) * 128)
                for i, kt in enumerate(ktiles):
                    P.op('pe', lambda e_, i=i, kt=kt: e_.matmul(G.ps[7][:, osl], lhsT=PTt[:, i, :], rhs=av[:, kt, h * 128:(h + 1) * 128],
                                                               start=(i == 0), stop=(i == nkt - 1)),
                         r=[("a_PT", pb_), "a_v"], w=[("ps", 7)])
                if m == 0:
                    P.op('dve', lambda e_: e_.tensor_scalar(out=ao2[:, 0:128], in0=G.ps[7][:, 0:128], scalar1=smt[:, 15:16], scalar2=None, op0=ALU.mult),
                         r=[("ps", 7), ("a_sm", pb_)], w=["a_ao2", ("ps", 7)])
                else:
                    P.op('dve', lambda e_: e_.tensor_tensor(out=smt[:, 13:14], in0=smt[:, 15:16], in1=neglam, op=ALU.mult),
                         r=[("a_sm", pb_), "a_lamc"], w=[("a_sm", pb_)])
                    P.op('dve', lambda e_: e_.scalar_tensor_tensor(out=ao[:, h * 128:(h + 1) * 128], in0=G.ps[7][:, 128:256], scalar=smt[:, 13:14],
                                                                  in1=ao2[:, 0:128], op0=ALU.mult, op1=ALU.add),
                         r=[("ps", 7), ("a_sm", pb_), "a_ao2"], w=["a_ao", ("ps", 7)])
                if h == 3 and m == 1:
                    P.op('pool', lambda e_: e_.tensor_tensor(out=ao2[:, :], in0=ao[:, :], in1=ao[:, :], op=ALU.mult), r=["a_ao"], w=["a_ao2"])
                    P.op('dve', lambda e_: e_.reduce_sum(out=rs[:, 0:4], in_=ao2[:, :].rearrange("p (h d) -> p h d", h=4), axis=AX.X), r=["a_ao2"], w=["a_rs"])
                    P.op('act', lambda e_: e_.activation(out=rs[:, 4:8], in_=rs[:, 0:4], func=AF.Sqrt, scale=1.0 / 128, bias=G.eps_ln[:, 0:1]), r=["a_rs", "consts"], w=["a_rs"])
                    P.op('dve', lambda e_: e_.reciprocal(out=rs[:, 8:12], in_=rs[:, 4:8]), r=["a_rs"], w=["a_rs"])
                    ab = aob[qt % 2]
                    for hh in range(4):
                        P.op('pool', lambda e_, hh=hh: e_.tensor_scalar(out=ab[:, hh * 128:(hh + 1) * 128], in0=ao[:, hh * 128:(hh + 1) * 128],
                                                                      scalar1=rs[:, 8 + hh:9 + hh], scalar2=None, op0=ALU.mult),
                             r=["a_ao", "a_rs"], w=[("a_aob", qt % 2)])
                    P.dma(mo_d[qt * 128:(qt + 1) * 128, 0:512], ab[:, :], r=[("a_aob", qt % 2)], w=[("mo", (qt, 0))])

            units = [(qt, h, m) for qt in range(NT) for h in range(4) for m in range(2)]
            for ui, u_ in enumerate(units):
                stage1(ui, *u_)
                if ui > 0:
                    stage2(ui - 1, *units[ui - 1])
            stage2(len(units) - 1, *units[-1])
            P.barrier()
        with ExitStack() as esB:
            bqT = sb(nc, esB, "b_qT", [128, 4, T], BF16)
            bkT = sb(nc, esB, "b_kT", [128, 4, T], BF16)
            bk = sb(nc, esB, "b_k", [128, NT, 512], BF16)
            bv = sb(nc, esB, "b_v", [128, NT, 512], BF16)
            for j in (3, 4):
                with ExitStack() as esw:
                    wb = inproj_block(G, esw, Win, j * 512, 512, hT, "e_hT", "b_w%d" % j)
                    for t in range(NT):
                        b = t % 2
                        load_tables(t, b, "B")
                        ps = proj_tile(wb, "b_w%d_b" % j, t, 4 + b)
                        if j == 3:
                            rope_evac(G, ps, ("ps", 4 + b), t, qr[b], ("e_qr", b), cosT[b], sinT[b], ("e_cs", b), rtmp[b], ("e_rtmp", b), 64)
                            transpose4(G, qr[b], ("e_qr", b), bqT, ("b_qT", t), t, 6 + b)
                        else:
                            rope_evac(G, ps, ("ps", 4 + b), t, bk[:, t, :], ("b_k", t), cosT[b], sinT[b], ("e_cs", b), rtmp[b], ("e_rtmp", b), 64,
                                      scale=128 ** -0.5)
                            transpose4(G, bk[:, t, :], ("b_k", t), bkT, ("b_kT", t), t, 6 + b)
                    P.barrier()
            with ExitStack() as esw:
                wb = inproj_block(G, esw, Win, 5 * 512, 512, hT, "e_hT", "b_w5")
                for t in range(NT):
                    b = t % 2
                    ps = proj_tile(wb, "b_w5_b", t, 4 + b)
                    P.op('act', lambda e_, t=t, ps=ps: e_.copy(out=bv[:, t, :], in_=ps[:, :]), r=[("ps", 4 + b)], w=[("b_v", t), ("ps", 4 + b)])
                P.barrier()
            dc = sb(nc, esB, "b_dc", [128, 64])
            kcol = sb(nc, esB, "b_kcol", [128, 4])
            kmat = sb(nc, esB, "b_kmat", [128, 4, 128])
            Mm = sb(nc, esB, "b_M", [128, 4, 128])
            Mt = sb(nc, esB, "b_Mt", [128, 128])
            P.dma(dc[:, 0:8], G.Wl("rt_decay_logit", e).rearrange("(o a) b -> o (a b)", o=1).partition_broadcast(128), w=["b_dc"])
            P.dma(kcol[:], G.KC("k_cols", [128, 4])[:, :], w=["b_kc"])
            P.dma(kmat[:], G.KC("k_mats", [128, 4, 128])[:, :, :], w=["b_kc"])
            P.op('act', lambda e_: e_.activation(out=dc[:, 8:16], in_=dc[:, 0:8], func=AF.Sigmoid), r=["b_dc"], w=["b_dc"])
            P.op('act', lambda e_: e_.activation(out=dc[:, 16:24], in_=dc[:, 8:16], func=AF.Ln), r=["b_dc"], w=["b_dc"])
            lg = lambda d, h: dc[:, 16 + d * 4 + h:17 + d * 4 + h]
            P.op('act', lambda e_: e_.activation(out=dc[:, 24:32], in_=dc[:, 16:24], func=AF.Exp, scale=128.0), r=["b_dc"], w=["b_dc"])
            for h in range(4):
                for (o_, col, d) in ((32, 0, 0), (36, 1, 1), (40, 2, 0), (44, 3, 1)):
                    P.op('act', lambda e_, o_=o_, col=col, d=d, h=h: e_.activation(out=dc[:, o_ + h:o_ + h + 1], in_=kcol[:, col:col + 1], func=AF.Exp, scale=lg(d, h)),
                         r=["b_dc", "b_kc"], w=["b_dc"])
                P.op('act', lambda e_, h=h: e_.activation(out=Mt[:, :], in_=kmat[:, 0, :], func=AF.Exp, scale=lg(0, h)), r=["b_dc", "b_kc"], w=["b_Mt"])
                P.op('dve', lambda e_, h=h: e_.tensor_tensor(out=Mm[:, h, :], in0=Mt[:, :], in1=kmat[:, 2, :], op=ALU.mult), r=["b_Mt", "b_kc"], w=["b_M"])
                P.op('act', lambda e_, h=h: e_.activation(out=Mt[:, :], in_=kmat[:, 1, :], func=AF.Exp, scale=lg(1, h)), r=["b_dc", "b_kc"], w=["b_Mt"])
                P.op('dve', lambda e_, h=h: e_.tensor_tensor(out=Mt[:, :], in0=Mt[:, :], in1=kmat[:, 3, :], op=ALU.mult), r=["b_Mt", "b_kc"], w=["b_Mt"])
                P.op('dve', lambda e_, h=h: e_.tensor_tensor(out=Mm[:, h, :], in0=Mm[:, h, :], in1=Mt[:, :], op=ALU.add), r=["b_Mt", "b_M"], w=["b_M"])
            SfA = sb(nc, esB, "b_SfA", [128, NT, 512], BF16)
            Sst = sb(nc, esB, "b_S", [128, 512])
            Sbb = sb(nc, esB, "b_Sbb", [128, 512], BF16)
            kz = [sb(nc, esB, "b_kz%d" % i, [128, 512], BF16) for i in range(2)]

            def state_update(n, d, i):
                zb = kz[i % 2]
                for h in range(4):
                    P.op('dve', lambda e_, h=h: e_.tensor_scalar(out=zb[:, h * 128:(h + 1) * 128], in0=bk[:, n, h * 128:(h + 1) * 128],
                                                                scalar1=dc[:, 32 + 4 * d + h:33 + 4 * d + h], scalar2=None, op0=ALU.mult),
                         r=[("b_k", n), "b_dc"], w=[("b_kz", i % 2)])
                for h in range(4):
                    P.op('pe', lambda e_, h=h: e_.matmul(G.ps[3][:, h * 128:(h + 1) * 128], lhsT=zb[:, h * 128:(h + 1) * 128],
                                                        rhs=bv[:, n, h * 128:(h + 1) * 128], start=True, stop=True),
                         r=[("b_kz", i % 2), ("b_v", n)], w=[("ps", 3)])
                for h in range(4):
                    P.op('dve', lambda e_, h=h: e_.scalar_tensor_tensor(out=Sst[:, h * 128:(h + 1) * 128], in0=Sst[:, h * 128:(h + 1) * 128],
                                                                       scalar=dc[:, 24 + 4 * d + h:25 + 4 * d + h], in1=G.ps[3][:, h * 128:(h + 1) * 128],
                                                                       op0=ALU.mult, op1=ALU.add),
                         r=["b_S", "b_dc", ("ps", 3)], w=["b_S", ("ps", 3)])
            P.op('dve', lambda e_: e_.memset(Sst[:, :], 0.0), w=["b_S"])
            fwd = [16, 17] + list(range(16))
            for i, n in enumerate(fwd):
                P.op('act', lambda e_, n=n: e_.copy(out=SfA[:, n, :], in_=Sst[:, :]), r=["b_S"], w=[("b_SfA", n)])
                if i < len(fwd) - 1:
                    state_update(n, 0, i)
            P.op('dve', lambda e_: e_.memset(Sst[:, :], 0.0), r=["b_SfA"], w=["b_S"])
            with ExitStack() as esw:
                wg = inproj_block(G, esw, Win, 6 * 512, 512, hT, "e_hT", "b_w6")
                Sm = [sb(nc, esw, "b_Sm%d" % i, [128, 4, 128], BF16) for i in range(2)]
                bo = [sb(nc, esw, "b_o%d" % i, [128, 512]) for i in range(2)]
                bo2 = [sb(nc, esw, "b_o2%d" % i, [128, 512]) for i in range(2)]
                gs = [sb(nc, esw, "b_gs%d" % i, [128, 512]) for i in range(2)]
                bob = [sb(nc, esw, "b_ob%d" % i, [128, 512], BF16) for i in range(2)]
                brs = [sb(nc, esw, "b_rs%d" % i, [128, 16]) for i in range(2)]
                bwd = [17, 16] + list(range(15, -1, -1))
                for i, n in enumerate(bwd):
                    b = i % 2
                    csl = slice(n * 128, (n + 1) * 128)
                    P.op('act', lambda e_: e_.copy(out=Sbb[:, :], in_=Sst[:, :]), r=["b_S"], w=["b_Sbb"])
                    for h in range(4):
                        P.op('pe', lambda e_, h=h: e_.matmul(G.ps[0][:, h * 128:(h + 1) * 128], lhsT=bkT[:, h, csl], rhs=bqT[:, h, csl], start=True, stop=True),
                             r=[("b_kT", n), ("b_qT", n)], w=[("ps", 0)])
                    P.op('dve', lambda e_, b=b: e_.tensor_tensor(out=Sm[b][:, :, :], in0=G.ps[0][:, :].rearrange("p (h n) -> p h n", h=4), in1=Mm[:, :, :], op=ALU.mult),
                         r=[("ps", 0), "b_M"], w=[("b_Sm", b), ("ps", 0)])
                    for h in range(4):
                        hs = slice(h * 128, (h + 1) * 128)
                        P.op('pe', lambda e_, h=h, hs=hs, b=b: e_.matmul(G.ps[1][:, hs], lhsT=Sm[b][:, h, :], rhs=bv[:, n, hs], start=True, stop=True),
                             r=[("b_Sm", b), ("b_v", n)], w=[("ps", 1)])
                    for h in range(4):
                        hs = slice(h * 128, (h + 1) * 128)
                        P.op('pe', lambda e_, h=h, hs=hs: e_.matmul(G.ps[2][:, hs], lhsT=bqT[:, h, csl], rhs=SfA[:, n, hs], start=True, stop=True),
                             r=[("b_qT", n), ("b_SfA", n)], w=[("ps", 2)])
                    for h in range(4):
                        hs = slice(h * 128, (h + 1) * 128)
                        P.op('pe', lambda e_, h=h, hs=hs: e_.matmul(G.ps[4][:, hs], lhsT=bqT[:, h, csl], rhs=Sbb[:, hs], start=True, stop=True),
                             r=[("b_qT", n), "b_Sbb"], w=[("ps", 4)])
                    for c in range(8):
                        P.op('pe', lambda e_, c=c: e_.matmul(G.ps[5][:, :], lhsT=hT[:, c, csl], rhs=wg[:, c, :], start=(c == 0), stop=(c == 7)),
                             r=[("e_hT", n), "b_w6_b"], w=[("ps", 5)])
                    P.op('act', lambda e_, b=b: e_.activation(out=gs[b][:, :], in_=G.ps[5][:, :], func=AF.Silu), r=[("ps", 5)], w=[("b_gs", b), ("ps", 5)])
                    for h in range(4):
                        hs = slice(h * 128, (h + 1) * 128)
                        P.op('dve', lambda e_, h=h, hs=hs, b=b: e_.tensor_scalar(out=bo2[b][:, hs], in0=G.ps[2][:, hs], scalar1=dc[:, 40 + h:41 + h], scalar2=None, op0=ALU.mult),
                             r=[("ps", 2), "b_dc"], w=[("b_o2", b), ("ps", 2)])
                        P.op('dve', lambda e_, h=h, hs=hs, b=b: e_.scalar_tensor_tensor(out=bo2[b][:, hs], in0=G.ps[4][:, hs], scalar=dc[:, 44 + h:45 + h], in1=bo2[b][:, hs],
                                                                                       op0=ALU.mult, op1=ALU.add),
                             r=[("ps", 4), "b_dc", ("b_o2", b)], w=[("b_o2", b), ("ps", 4)])
                    P.op('dve', lambda e_, b=b: e_.tensor_tensor(out=bo[b][:, :], in0=G.ps[1][:, :], in1=bo2[b][:, :], op=ALU.add),
                         r=[("ps", 1), ("b_o2", b)], w=[("b_o", b), ("ps", 1)])
                    P.op('dve', lambda e_, b=b: e_.tensor_tensor(out=bo2[b][:, :], in0=bo[b][:, :], in1=bo[b][:, :], op=ALU.mult), r=[("b_o", b)], w=[("b_o2", b)])
                    P.op('dve', lambda e_, b=b: e_.reduce_sum(out=brs[b][:, 0:4], in_=bo2[b][:, :].rearrange("p (h d) -> p h d", h=4), axis=AX.X), r=[("b_o2", b)], w=[("b_rs", b)])
                    P.op('act', lambda e_, b=b: e_.activation(out=brs[b][:, 4:8], in_=brs[b][:, 0:4], func=AF.Sqrt, scale=1.0 / 128, bias=G.eps_ln[:, 0:1]), r=[("b_rs", b), "consts"], w=[("b_rs", b)])
                    P.op('dve', lambda e_, b=b: e_.reciprocal(out=brs[b][:, 8:12], in_=brs[b][:, 4:8]), r=[("b_rs", b)], w=[("b_rs", b)])
                    for h in range(4):
                        hs = slice(h * 128, (h + 1) * 128)
                        P.op('dve', lambda e_, h=h, hs=hs, b=b: e_.scalar_tensor_tensor(out=bob[b][:, hs], in0=bo[b][:, hs], scalar=brs[b][:, 8 + h:9 + h], in1=gs[b][:, hs],
                                                                                       op0=ALU.mult, op1=ALU.mult),
                             r=[("b_o", b), ("b_rs", b), ("b_gs", b)], w=[("b_ob", b)])
                    P.dma(mo_d[n * 128:(n + 1) * 128, 512:1024], bob[b][:, :], r=[("b_ob", b)], w=[("mo", (n, 1))])
                    if i < len(bwd) - 1:
                        state_update(n, 1, i)
                P.barrier()
            P.barrier()
        with ExitStack() as esO:
            wos = sb(nc, esO, "o_ws", [128, 8, 1024])
            wob = sb(nc, esO, "o_wb", [128, 8, 1024], BF16)
            gn = sb(nc, esO, "o_gn", [128, 4])
            P.dma(wos[:], Wout[:, :].rearrange("(c p) n -> p c n", p=128), w=["o_ws"])
            colload(G, esO, gn[:, :], "o_gn", G.Wl("da_gn_g", e), 4)
            P.op('dve', lambda e_: e_.tensor_scalar_mul(out=gn[:], in0=gn[:], scalar1=1.0 - lam_init), r=["o_gn"], w=["o_gn"])
            for c in range(8):
                if c < 4:
                    P.op('dve', lambda e_, c=c: e_.tensor_scalar(out=wob[:, c, :], in0=wos[:, c, :], scalar1=gn[:, c:c + 1], scalar2=None, op0=ALU.mult),
                         r=["o_ws", "o_gn"], w=[("o_wb", c)])
                else:
                    P.op('pool', lambda e_, c=c: e_.tensor_copy(out=wob[:, c, :], in_=wos[:, c, :]), r=["o_ws"], w=[("o_wb", c)])
            gate = [sb(nc, esO, "o_gate%d" % r_, [128, 1024]) for r_ in range(2)]
            LG = sb(nc, esO, "o_lg", [128, 1024])
            LB = sb(nc, esO, "o_lb", [128, 1024])
            for r_ in range(2):
                P.dma(gate[r_][:], G.modv[l, r_:r_ + 1, 2 * 1024:3 * 1024].partition_broadcast(128), r=[("modv", l)], w=["o_bc"])
            P.dma(LG[:], row(G.Wl("ln1_g", l)).partition_broadcast(128), w=["o_bc"])
            P.dma(LB[:], row(G.Wl("ln1_b", l)).partition_broadcast(128), w=["o_bc"])
            mot = [sb(nc, esO, "o_mo%d" % i, [128, 1024], BF16) for i in range(2)]
            moT = [sb(nc, esO, "o_moT%d" % i, [128, 8, 128], BF16) for i in range(2)]
            tmp = [sb(nc, esO, "o_tmp%d" % i, [128, 1024]) for i in range(2)]
            yt = [sb(nc, esO, "o_y%d" % i, [128, 1024]) for i in range(2)]
            st = [sb(nc, esO, "o_st%d" % i, [128, 16]) for i in range(2)]
            for t in range(NT):
                b = t % 2
                r_ = 0 if t < 16 else 1
                P.dma(mot[b][:], mo_d[t * 128:(t + 1) * 128, :], r=[("mo", (t, 0)), ("mo", (t, 1))], w=[("o_mo", b)])
                P.dma(xt[b][:], G.xres[t * 128:(t + 1) * 128, :], r=[("xres", t)], w=[("e_x", b)])
                for hh in range(2):
                    pbk = 4 + hh
                    psb = G.ps[pbk][:, 0:256].bitcast(BF16)
                    for g in range(4):
                        c = hh * 4 + g
                        P.op('pe', lambda e_, g=g, c=c, psb=psb: e_.transpose(out=psb[:, g * 128:(g + 1) * 128], in_=mot[b][:, c * 128:(c + 1) * 128], identity=G.identB[:]),
                             r=[("o_mo", b), "identB"], w=[("ps", pbk)])
                    P.op('act', lambda e_, hh=hh, psb=psb: e_.copy(out=moT[b][:, hh * 4:hh * 4 + 4, :], in_=psb.rearrange("p (g n) -> p g n", g=4)),
                         r=[("ps", pbk)], w=[("o_moT", b), ("ps", pbk)])
                banks = [G.ps[0 + 2 * b], G.ps[1 + 2 * b]]
                okeys = [("ps", 0 + 2 * b), ("ps", 1 + 2 * b)]
                for hh in range(2):
                    for c in range(8):
                        P.op('pe', lambda e_, hh=hh, c=c: e_.matmul(banks[hh][:, :], lhsT=moT[b][:, c, :], rhs=wob[:, c, hh * 512:(hh + 1) * 512],
                                                                   start=(c == 0), stop=(c == 7)), r=[("o_moT", b), "o_wb"], w=[okeys[hh]])
                ln_epilogue(G, banks, okeys, xt[b], ("e_x", b), gate[r_], LG, LB, ["o_bc"], tmp[b], ("o_tmp", b), st[b], ("o_st", b), yt[b], ("o_y", b))
                P.dma(G.xres[t * 128:(t + 1) * 128, :], yt[b][:], r=[("o_y", b)], w=[("xres", t)])
            P.barrier()
        P.barrier()


TBLK = [(0, 512), (512, 512), (1024, 512), (1536, 512), (2048, 256)]
RW_GN_EPS = 64e-5


def colvec(G, es, name, ap1024, key):
    t = sb(G.nc, es, name, [128, 8])
    colload(G, es, t[:, :], key, ap1024, 8)
    return t


def load_w_bf16(G, es, name, wap, rows, cols, eng='pool'):
    nc, P = G.nc, G.P
    nck = (rows + 127) // 128
    wb = sb(nc, es, name + "_b", [128, nck, cols], BF16)
    if rows % 128 == 0 and cols > 512:
        with ExitStack() as es2:
            ws = sb(nc, es2, name + "_s", [128, nck, 512])
            for h0 in range(0, cols, 512):
                P.dma(ws[:], wap[:, h0:h0 + 512].rearrange("(c p) n -> p c n", p=128), w=[name + "_s"])
                P.op(eng, lambda e, h0=h0: e.tensor_copy(out=wb[:, :, h0:h0 + 512], in_=ws[:]), r=[name + "_s"], w=[name + "_b"])
            P.barrier()
        return wb
    ws = sb(nc, es, name + "_s", [128, nck, cols])
    if rows % 128 == 0:
        P.dma(ws[:], wap.rearrange("(c p) n -> p c n", p=128), w=[name + "_s"])
        P.op(eng, lambda e: e.tensor_copy(out=wb[:], in_=ws[:]), r=[name + "_s"], w=[name + "_b"])
    else:
        for c in range(nck):
            n_ = min(128, rows - c * 128)
            P.dma(ws[0:n_, c, :], wap[c * 128:c * 128 + n_, :], w=[name + "_s"])
        for c in range(nck):
            n_ = min(128, rows - c * 128)
            P.op(eng, lambda e, c=c, n_=n_: e.tensor_copy(out=wb[0:n_, c, :], in_=ws[0:n_, c, :]), r=[name + "_s"], w=[name + "_b"])
    return wb


def fm_linear(G, xT, xkey, wb, wkey, nout, consumer, kparts=None, pbanks=(0, 1)):
    P = G.P
    if kparts is None:
        kparts = [(c, 128) for c in range(8)]
    it = 0
    for oc in range((nout + 127) // 128):
        m_ = min(128, nout - oc * 128)
        for (t0, tn) in TBLK:
            pb = pbanks[it % len(pbanks)]
            it += 1
            for i, (c, kn) in enumerate(kparts):
                P.op('pe', lambda e, c=c, kn=kn, i=i, pb=pb: e.matmul(G.ps[pb][0:m_, 0:tn], lhsT=wb[0:kn, c, oc * 128:oc * 128 + m_], rhs=xT[0:kn, c, t0:t0 + tn],
                                                                   start=(i == 0), stop=(i == len(kparts) - 1)), r=[xkey, wkey], w=[("ps", pb)])
            consumer(oc, t0, tn, G.ps[pb], ("ps", pb), m_)


F32R = mybir.dt.float32r
NCHAIN = 4
SKIP_SCAN = False
SCAN_F32R = False
STAGGER = 6


def rr(ap):
    return ap.bitcast(F32R) if SCAN_F32R else ap


def scan_stage(G):
    nc, P = G.nc, G.P
    FM = G.fm
    C = 64
    NCH = T // C
    with ExitStack() as esS:
        msk = sb(nc, esS, "s_msk", [128, 3, 128])
        P.dma(msk[:], G.KC("k_smask", [128, 3, 128])[:, :, :], w=["s_msk"])
        ones = sb(nc, esS, "s_ones", [128, 64])
        P.op('dve', lambda e: e.memset(ones[:], 1.0), w=["s_ones"])
        identR = sb(nc, esS, "s_identR", [128, 128])
        P.op('dve', lambda e: e.tensor_copy(out=rr(identR[:]), in_=G.identF[:]), r=["ident"], w=["s_identR"])
        NAMES = ("r", "v", "kk", "lw", "kd", "b")
        bufs = []
        for ci in range(NCHAIN):
            B = Ctx()
            B.src = [sb(nc, esS, "s_src%d_%d" % (ci, i), [128, 6, 256]) for i in range(2)]
            B.bd = sb(nc, esS, "s_bd%d" % ci, [128, 5, 128])
            P.op('pool', lambda e, B=B: e.memset(B.bd[:], 0.0), w=[("s_bd", ci)])
            B.cum = sb(nc, esS, "s_cum%d" % ci, [128, 4, 64])
            B.Am = sb(nc, esS, "s_Am%d" % ci, [128, 5, 128])
            B.Ap = [sb(nc, esS, "s_Ap%d_%d" % (ci, i), [128, 2, 128]) for i in range(2)]
            B.VT = sb(nc, esS, "s_VT%d" % ci, [128, 64])
            B.Z = sb(nc, esS, "s_Z%d" % ci, [128, 2, 64])
            B.BT = sb(nc, esS, "s_BT%d" % ci, [128, 2, 128])
            B.S = sb(nc, esS, "s_S%d" % ci, [128, 64])
            B.Sr = sb(nc, esS, "s_Sr%d" % ci, [128, 64])
            B.yo = [sb(nc, esS, "s_yo%d_%d" % (ci, i), [64, 2, 64]) for i in range(4)]
            bufs.append(B)

        def chain(ci, p, d):
            B = bufs[ci]
            P0, P1 = G.ps[2 * ci], G.ps[2 * ci + 1]
            k0, k1 = ("ps", 2 * ci), ("ps", 2 * ci + 1)
            K = lambda nm, sub=None: ("s%d_%s" % (ci, nm), sub)
            fmn = {"r": "rT", "v": "vT", "kk": "kk", "lw": "lw%d" % d, "kd": "kd%d" % d, "b": "b%d" % d}
            P.op('dve', lambda e: e.memset(B.S[:], 0.0), w=[K("S")])
            P.op('act', lambda e: e.copy(out=rr(B.Sr[:, :]), in_=B.S[:, :]), r=[K("S")], w=[K("Sr")])
            R_, K_, B_, A_, V_ = (B.bd[:, i, :] for i in range(5))
            for n in range(NCH):
                if n < TC // C:
                    t0 = TL + n * C if d == 0 else T - (n + 1) * C
                else:
                    m_ = n - TC // C
                    t0 = m_ * C if d == 0 else TL - (m_ + 1) * C
                blk0 = (t0 // 256) * 256
                sbi = (n // 4) % 2

                def blk_start(nb_):
                    n_ = 4 * nb_
                    if n_ < TC // C:
                        t_ = TL + n_ * C if d == 0 else T - (n_ + 1) * C
                    else:
                        mm_ = n_ - TC // C
                        t_ = mm_ * C if d == 0 else TL - (mm_ + 1) * C
                    return (t_ // 256) * 256

                def load_blk(nb_):
                    b0 = blk_start(nb_)
                    for i, nm in enumerate(NAMES):
                        P.dma(B.src[nb_ % 2][:, i, :], FM[fmn[nm]][p, :, b0:b0 + 256], r=[("fm_" + fmn[nm], p)], w=[K("src", nb_ % 2)])
                if n % 4 == 0:
                    nb = n // 4
                    if nb == 0:
                        load_blk(0)
                    if nb + 1 < NCH // 4:
                        load_blk(nb + 1)
                off = t0 - blk0
                sk = K("src", sbi)

                def tsl(i):
                    a_ = B.src[sbi][:, i, off:off + C]
                    return a_ if d == 0 else a_[:, ::-1]
                cm = B.cum
                ck = K("cum")
                P.op('dve', lambda e: e.tensor_tensor_scan(out=cm[:, 0, :], data0=ones[:, :], data1=tsl(3), initial=0.0, op0=ALU.mult, op1=ALU.add), r=[sk, "s_ones"], w=[ck])
                P.op('act', lambda e: e.activation(out=cm[:, 1, :], in_=cm[:, 0, :], func=AF.Exp), r=[ck], w=[ck])
                P.op('act', lambda e: e.activation(out=cm[:, 2, :], in_=cm[:, 0, :], func=AF.Exp, scale=-1.0), r=[ck], w=[ck])
                P.op('pool', lambda e: e.tensor_tensor(out=cm[:, 3, :], in0=cm[:, 0, :], in1=tsl(3), op=ALU.subtract), r=[ck, sk], w=[ck])
                P.op('act', lambda e: e.activation(out=cm[:, 3, :], in_=cm[:, 3, :], func=AF.Exp), r=[ck], w=[ck])
                yield
                bk = K("bd")
                for h in range(2):
                    hp = slice(h * 64, (h + 1) * 64)
                    P.op('dve', lambda e, hp=hp: e.tensor_tensor(out=rr(R_[hp, hp]), in0=tsl(0)[hp, :], in1=cm[hp, 1, :], op=ALU.mult), r=[sk, ck], w=[K("bd", 0)])
                    P.op('dve', lambda e, hp=hp: e.tensor_tensor(out=rr(K_[hp, hp]), in0=tsl(4)[hp, :], in1=cm[hp, 2, :], op=ALU.mult), r=[sk, ck], w=[K("bd", 1)])
                    P.op('dve', lambda e, hp=hp: e.tensor_tensor(out=rr(B_[hp, hp]), in0=tsl(5)[hp, :], in1=cm[hp, 2, :], op=ALU.mult), r=[sk, ck], w=[K("bd", 2)])
                    P.op('dve', lambda e, hp=hp: e.scalar_tensor_tensor(out=rr(A_[hp, hp]), in0=tsl(2)[hp, :], scalar=-1.0, in1=cm[hp, 3, :], op0=ALU.mult, op1=ALU.mult),
                         r=[sk, ck], w=[K("bd", 3)])
                    P.op('act', lambda e, hp=hp: e.copy(out=rr(V_[hp, hp]), in_=tsl(1)[hp, :]), r=[sk], w=[K("bd", 4)])
                yield
                A = B.Am
                specs = [(0, 2, 3, 0), (1, 3, 2, 1), (2, 1, 3, 0), (3, 2, 0, 2), (4, 1, 0, 2)]
                for (ai, li, ri, mi) in specs:
                    pt, pk = (P0, k0) if ai < 4 else (P1, k1)
                    sl = slice((ai % 4) * 128, (ai % 4 + 1) * 128)
                    P.op('pe', lambda e, li=li, ri=ri, pt=pt, sl=sl: e.matmul(pt[:, sl], lhsT=rr(B.bd[:, li, :]), rhs=rr(B.bd[:, ri, :]), start=True, stop=True),
                         r=[K("bd", li), K("bd", ri)], w=[pk])
                P.op('pe', lambda e: e.transpose(out=P1[:, 128:256], in_=V_, identity=G.identF[:]), r=[K("bd", 4), "ident"], w=[k1])
                yield
                for (ai, li, ri, mi) in specs:
                    pt, pk = (P0, k0) if ai < 4 else (P1, k1)
                    sl = slice((ai % 4) * 128, (ai % 4 + 1) * 128)
                    P.op('dve', lambda e, ai=ai, pt=pt, sl=sl, mi=mi: e.tensor_tensor(out=rr(A[:, ai, :]), in0=pt[:, sl], in1=msk[:, mi, :], op=ALU.mult),
                         r=[pk, "s_msk"], w=[K("Am", ai), pk])
                for h in range(2):
                    hp = slice(h * 64, (h + 1) * 64)
                    P.op('act', lambda e, hp=hp, h=h: e.copy(out=rr(B.VT[hp, :]), in_=P1[hp, 128 + h * 64:128 + (h + 1) * 64]), r=[k1], w=[K("VT"), k1])
                yield
                zs = slice(256, 320)
                P.op('pe', lambda e: e.matmul(P1[:, zs], lhsT=rr(A_), rhs=rr(B.Sr[:, :]), start=True, stop=False), r=[K("bd", 3), K("Sr")], w=[k1])
                P.op('pe', lambda e: e.matmul(P1[:, zs], lhsT=rr(A[:, 2, :]), rhs=rr(B.VT[:, :]), start=False, stop=True), r=[K("Am", 2), K("VT")], w=[k1])
                yield
                P.op('act', lambda e: e.copy(out=rr(B.Z[:, 0, :]), in_=P1[:, zs]), r=[k1], w=[K("Z", 0), k1])
                yield
                zc = 0
                curA, curAT = (A[:, 0, :], K("Am", 0)), (A[:, 1, :], K("Am", 1))
                for step in range(6):
                    P.op('pe', lambda e, zc=zc, curA=curA: e.matmul(P1[:, zs], lhsT=rr(curA[0]), rhs=rr(B.Z[:, zc, :]), start=True, stop=True), r=[curA[1], K("Z", zc)], w=[k1])
                    if step < 5:
                        dstt = B.Ap[step % 2]
                        dn = "Ap%d" % (step % 2)
                        P.op('pe', lambda e, curA=curA, curAT=curAT: e.matmul(P0[:, 0:128], lhsT=rr(curAT[0]), rhs=rr(curA[0]), start=True, stop=True), r=[curA[1], curAT[1]], w=[k0])
                        if step < 4:
                            P.op('pe', lambda e, curA=curA, curAT=curAT: e.matmul(P0[:, 128:256], lhsT=rr(curA[0]), rhs=rr(curAT[0]), start=True, stop=True), r=[curA[1], curAT[1]], w=[k0])
                    yield
                    P.op('dve', lambda e, zc=zc: e.tensor_tensor(out=rr(B.Z[:, 1 - zc, :]), in0=P1[:, zs], in1=B.Z[:, zc, :], op=ALU.add), r=[k1, K("Z", zc)], w=[K("Z", 1 - zc), k1])
                    zc = 1 - zc
                    if step < 5:
                        if step < 4:
                            P.op('act', lambda e, dstt=dstt: e.copy(out=rr(dstt[:, :, :]), in_=P0[:, 0:256].rearrange("p (a n) -> p a n", a=2)), r=[k0], w=[K(dn), k0])
                        else:
                            P.op('act', lambda e, dstt=dstt: e.copy(out=rr(dstt[:, 0, :]), in_=P0[:, 0:128]), r=[k0], w=[K(dn), k0])
                        curA, curAT = (dstt[:, 0, :], K(dn)), (dstt[:, 1, :], K(dn))
                    yield
                UT = B.Z[:, zc, :]
                uk = K("Z", zc)
                ys = slice(384, 512)
                P.op('pe', lambda e: e.matmul(P1[0:64, ys], lhsT=rr(B.Sr[:, :]), rhs=rr(R_), start=True, stop=False), r=[K("Sr"), K("bd", 0)], w=[k1])
                P.op('pe', lambda e: e.matmul(P1[0:64, ys], lhsT=rr(UT), rhs=rr(A[:, 3, :]), start=False, stop=False), r=[uk, K("Am", 3)], w=[k1])
                P.op('pe', lambda e: e.matmul(P1[0:64, ys], lhsT=rr(B.VT[:, :]), rhs=rr(A[:, 4, :]), start=False, stop=True), r=[K("VT"), K("Am", 4)], w=[k1])
                if n < NCH - 1:
                    P.op('pe', lambda e: e.transpose(out=P0[:, 256:384], in_=B_, identity=G.identF[:]), r=[K("bd", 2), "ident"], w=[k0])
                    P.op('pe', lambda e: e.transpose(out=P0[:, 384:512], in_=K_, identity=G.identF[:]), r=[K("bd", 1), "ident"], w=[k0])
                yield
                yq = B.yo[n % 4]
                yv = P1[0:64, ys].rearrange("p (h t) -> p h t", h=2)
                yov = yq[:, :, :] if d == 0 else yq[:, :, ::-1]
                P.op('act', lambda e: e.copy(out=yov, in_=yv), r=[k1], w=[K("yo", n % 4), k1])
                P.dma(G.y_d[d, :, 2 * p:2 * p + 2, t0:t0 + C], yq[:, :, :], r=[K("yo", n % 4)], w=[("y_d", (d, p, t0 // 128))])
                if n < NCH - 1:
                    P.op('dve', lambda e: e.tensor_copy(out=rr(B.BT[:, :, :]), in_=P0[:, 256:512].rearrange("p (a n) -> p a n", a=2)), r=[k0], w=[K("BT"), k0])
                    yield
                    P.op('pe', lambda e: e.matmul(P1[:, 0:64], lhsT=rr(B.BT[:, 0, :]), rhs=rr(UT), start=True, stop=False), r=[K("BT"), uk], w=[k1])
                    P.op('pe', lambda e: e.matmul(P1[:, 0:64], lhsT=rr(B.BT[:, 1, :]), rhs=rr(B.VT[:, :]), start=False, stop=True), r=[K("BT"), K("VT")], w=[k1])
                    yield
                    P.op('dve', lambda e: e.tensor_tensor(out=B.S[:, :], in0=B.S[:, :], in1=P1[:, 0:64], op=ALU.add), r=[K("S"), k1], w=[K("S"), k1])
                    P.op('dve', lambda e: e.tensor_scalar(out=B.S[:, :], in0=B.S[:, :], scalar1=cm[:, 1, 63:64], scalar2=None, op0=ALU.mult), r=[K("S"), ck], w=[K("S")])
                    P.op('act', lambda e: e.copy(out=rr(B.Sr[:, :]), in_=B.S[:, :]), r=[K("S")], w=[K("Sr")])
                yield

        todo = [(p, d) for p in range(8) for d in range(2)]
        active = [None] * NCHAIN
        for ci in range(NCHAIN):
            p, d = todo.pop(0)
            active[ci] = chain(ci, p, d)
            for _ in range(ci * STAGGER):
                next(active[ci])
        while todo or any(a is not None for a in active):
            for ci in range(NCHAIN):
                if active[ci] is None and todo:
                    p, d = todo.pop(0)
                    active[ci] = chain(ci, p, d)
                if active[ci] is not None:
                    try:
                        next(active[ci])
                    except StopIteration:
                        active[ci] = None
        P.barrier()


def stage_rwkv(G, l):
    nc, P = G.nc, G.P
    j = l // 2
    FM = G.fm
    with ExitStack() as es:
        mc = load_modcols(G, es, l, "mc")
        BO = sb(nc, es, "r_BO", [128, 128])
        P.dma(BO[:], G.KC("k_bo", [128, 128])[:, :], w=["r_BO"])
        with ExitStack() as esP:
            hT = sb(nc, esP, "r_hT", [128, 8, T], BF16)
            dT = sb(nc, esP, "r_dT", [128, 8, T], BF16)
            xiT = sb(nc, esP, "r_xiT", [128, 8, T], BF16)
            with ExitStack() as esx:
                xt = [sb(nc, esx, "r_x%d" % i, [128, 1024]) for i in range(2)]
                for t in range(NT):
                    b = t % 2
                    P.dma(xt[b][:], G.xres[t * 128:(t + 1) * 128, :], r=[("xres", t)], w=[("r_x", b)])
                    transpose_modulate(G, xt[b], ("r_x", b), hT, ("r_hT", t), t * 128, mc, 0 if t < 16 else 1, 0, 1, 0)
                P.barrier()
            def lat(tile_, c0, c1):
                return tile_[:, c0:c1, 0:TL].rearrange("p c (r w) -> p c r w", w=64)
            hl = lambda c0, c1: lat(hT, c0, c1)
            dl = lambda c0, c1: lat(dT, c0, c1)
            sub = ALU.subtract
            ops = [
                (dl(0, 2)[:, :, :, 1:64], hl(0, 2)[:, :, :, 0:63], hl(0, 2)[:, :, :, 1:64]),
                (dl(2, 4)[:, :, :, 0:63], hl(2, 4)[:, :, :, 1:64], hl(2, 4)[:, :, :, 0:63]),
                (dl(4, 6)[:, :, 1:32, :], hl(4, 6)[:, :, 0:31, :], hl(4, 6)[:, :, 1:32, :]),
                (dl(6, 8)[:, :, 0:31, :], hl(6, 8)[:, :, 1:32, :], hl(6, 8)[:, :, 0:31, :]),
                (dT[:, 0:4, TL + 1:T], hT[:, 0:4, TL:T - 1], hT[:, 0:4, TL + 1:T]),
                (dT[:, 4:8, TL:T - 1], hT[:, 4:8, TL + 1:T], hT[:, 4:8, TL:T - 1]),
            ]
            for (o_, a_, b_) in ops:
                for cc in range(o_.shape[1]):
                    P.op('dve', lambda e, o_=o_, a_=a_, b_=b_, cc=cc: e.tensor_tensor(out=o_[:, cc], in0=a_[:, cc], in1=b_[:, cc], op=sub), r=["r_hT"], w=["r_dT"])
            bnd = [
                (dl(0, 2)[:, :, :, 0:1], hl(0, 2)[:, :, :, 0:1]), (dl(2, 4)[:, :, :, 63:64], hl(2, 4)[:, :, :, 63:64]),
                (dl(4, 6)[:, :, 0:1, :], hl(4, 6)[:, :, 0:1, :]), (dl(6, 8)[:, :, 31:32, :], hl(6, 8)[:, :, 31:32, :]),
                (dT[:, 0:4, TL:TL + 1], hT[:, 0:4, TL:TL + 1]), (dT[:, 4:8, T - 1:T], hT[:, 4:8, T - 1:T]),
            ]
            for (o_, a_) in bnd:
                for cc in range(o_.shape[1]):
                    P.op('dve', lambda e, o_=o_, a_=a_, cc=cc: e.tensor_scalar_mul(out=o_[:, cc], in0=a_[:, cc], scalar1=-1.0), r=["r_hT"], w=["r_dT"])
            mu = sb(nc, esP, "r_mu", [128, 6, 8])
            colload(G, esP, mu[:, :, :].rearrange("p i c -> p (i c)"), "r_mu", G.Wl("rw_mu", j).rearrange("i n -> (i n)"), 48)
            kkc = colvec(G, esP, "r_kkc", G.Wl("rw_kk", j), "r_cv")
            kac = colvec(G, esP, "r_kac", G.Wl("rw_ka", j), "r_cv")
            omka = sb(nc, esP, "r_omka", [128, 8])
            P.op('dve', lambda e: e.tensor_scalar(out=omka[:], in0=kac[:], scalar1=-1.0, scalar2=1.0, op0=ALU.mult, op1=ALU.add), r=["r_cv"], w=["r_cv2"])
            stg = [sb(nc, esP, "r_stg%d" % i, [128, T]) for i in range(2)]
            stg2 = [sb(nc, esP, "r_stg2_0", [128, T])] * 2
            ld1 = [sb(nc, esP, "r_ld1_0", [128, T])] * 2
            ld2 = [sb(nc, esP, "r_ld2_0", [128, T])] * 2
            tmpa = [sb(nc, esP, "r_tmpa%d" % i, [128, 512]) for i in range(2)]
            tmpb = [sb(nc, esP, "r_tmpb%d" % i, [128, 512]) for i in range(2)]
            tcnt = [0]

            def mk_xi(i):
                for c in range(8):
                    P.op('dve', lambda e, c=c: e.scalar_tensor_tensor(out=xiT[:, c, :], in0=dT[:, c, :], scalar=mu[:, i, c:c + 1], in1=hT[:, c, :], op0=ALU.mult, op1=ALU.add),
                         r=["r_dT", "r_hT", "r_mu"], w=["r_xiT"])

            def store(oc, dst, src, skey):
                P.dma(dst[oc, :, :], src[:, :], r=[skey], w=[(dst.name if hasattr(dst, "name") else "fm", oc)])

            mk_xi(0)
            with ExitStack() as esw:
                wb = load_w_bf16(G, esw, "r_wr", G.Wl("rw_wr", j), 1024, 1024)
                def cons_r(oc, t0, tn, ps, pkey, m_):
                    s = stg[oc % 2]
                    P.op('act', lambda e: e.copy(out=s[:, t0:t0 + tn], in_=ps[:, 0:tn]), r=[pkey], w=[("r_stg", oc % 2), pkey])
                    if t0 == 2048:
                        P.dma(FM["rT"][oc, :, :], s[:, :], r=[("r_stg", oc % 2)], w=[("fm_rT", oc)])
                fm_linear(G, xiT, "r_xiT", wb, "r_wr_b", 1024, cons_r)
                P.barrier()
            mk_xi(1)
            for d in range(2):
                with ExitStack() as esw:
                    w1b = load_w_bf16(G, esw, "r_w1", G.Wl("rw_w1", j)[d], 1024, 64)
                    w2b = load_w_bf16(G, esw, "r_w2", G.Wl("rw_w2", j)[d], 64, 1024)
                    w0c = colvec(G, esw, "r_w0c", G.Wl("rw_w0", j)[d], "r_w0c")
                    P.op('dve', lambda e: e.tensor_scalar_mul(out=w0c[:], in0=w0c[:], scalar1=-1.0), r=["r_w0c"], w=["r_w0c"])
                    t1 = sb(nc, esw, "r_t1", [128, 1, T], BF16)
                    def cons_t(oc, t0, tn, ps, pkey, m_):
                        P.op('act', lambda e: e.activation(out=t1[0:m_, 0, t0:t0 + tn], in_=ps[0:m_, 0:tn], func=AF.Tanh), r=[pkey], w=["r_t1", pkey])
                    fm_linear(G, xiT, "r_xiT", w1b, "r_w1_b", 64, cons_t, pbanks=(2, 3))
                    def cons_w(oc, t0, tn, ps, pkey, m_):
                        s = stg[oc % 2]
                        k_ = tcnt[0] % 2
                        tcnt[0] += 1
                        ta, tb_ = tmpa[k_], tmpb[k_]
                        P.op('act', lambda e: e.activation(out=ta[:, 0:tn], in_=ps[:, 0:tn], func=AF.Exp, scale=-1.0, bias=w0c[:, oc:oc + 1]), r=[pkey, "r_w0c"], w=[("r_tmpa", k_), pkey])
                        P.op('act', lambda e: e.activation(out=tb_[:, 0:tn], in_=ta[:, 0:tn], func=AF.Ln, bias=G.one_c[:, 0:1], scale=1.0), r=[("r_tmpa", k_), "consts"], w=[("r_tmpb", k_)])
                        P.op('act', lambda e: e.activation(out=ta[:, 0:tn], in_=tb_[:, 0:tn], func=AF.Exp, scale=-1.0, bias=G.mhalf_c[:, 0:1]), r=[("r_tmpb", k_), "consts"], w=[("r_tmpa", k_)])
                        P.op('dve', lambda e: e.tensor_scalar_mul(out=s[:, t0:t0 + tn], in0=ta[:, 0:tn], scalar1=-1.0), r=[("r_tmpa", k_)], w=[("r_stg", oc % 2)])
                        if t0 == 2048:
                            P.dma(FM["lw%d" % d][oc, :, :], s[:, :], r=[("r_stg", oc % 2)], w=[("fm_lw%d" % d, oc)])
                    fm_linear(G, t1, "r_t1", w2b, "r_w2_b", 1024, cons_w, kparts=[(0, 64)])
                    P.barrier()
            mk_xi(2)
            with ExitStack() as esw:
                wb = load_w_bf16(G, esw, "r_wk", G.Wl("rw_wk", j), 1024, 1024)
                def cons_k(oc, t0, tn, ps, pkey, m_):
                    s, s2 = stg[oc % 2], stg2[oc % 2]
                    k_ = tcnt[0] % 2
                    tcnt[0] += 1
                    ta, tb_ = tmpa[k_], tmpb[k_]
                    P.op('act', lambda e: e.copy(out=s[:, t0:t0 + tn], in_=ps[:, 0:tn]), r=[pkey], w=[("r_stg", oc % 2), pkey])
                    P.op('dve', lambda e: e.tensor_scalar(out=ta[:, 0:tn], in0=s[:, t0:t0 + tn], scalar1=kkc[:, oc:oc + 1], scalar2=None, op0=ALU.mult), r=[("r_stg", oc % 2), "r_cv"], w=[("r_tmpa", k_)])
                    P.op('pool', lambda e: e.tensor_tensor(out=tb_[:, 0:tn], in0=ta[:, 0:tn], in1=ta[:, 0:tn], op=ALU.mult), r=[("r_tmpa", k_)], w=[("r_tmpb", k_)])
                    P.op('pe', lambda e: e.matmul(G.ps[4 + k_][:, 0:tn], lhsT=BO[:, :], rhs=tb_[:, 0:tn], start=True, stop=True), r=["r_BO", ("r_tmpb", k_)], w=[("ps", 4 + k_)])
                    P.op('act', lambda e: e.activation(out=tb_[:, 0:tn], in_=G.ps[4 + k_][:, 0:tn], func=AF.Sqrt), r=[("ps", 4 + k_)], w=[("r_tmpb", k_), ("ps", 4 + k_)])
                    P.op('dve', lambda e: e.tensor_scalar_max(out=tb_[:, 0:tn], in0=tb_[:, 0:tn], scalar1=1e-12), r=[("r_tmpb", k_)], w=[("r_tmpb", k_)])
                    P.op('dve', lambda e: e.reciprocal(out=tb_[:, 0:tn], in_=tb_[:, 0:tn]), r=[("r_tmpb", k_)], w=[("r_tmpb", k_)])
                    P.op('dve', lambda e: e.tensor_tensor(out=s2[:, t0:t0 + tn], in0=ta[:, 0:tn], in1=tb_[:, 0:tn], op=ALU.mult), r=[("r_tmpa", k_), ("r_tmpb", k_)], w=[("r_stg2", 0)])
                    if t0 == 2048:
                        P.dma(FM["kT"][oc, :, :], s[:, :], r=[("r_stg", oc % 2)], w=[("fm_kT", oc)])
                        P.dma(FM["kk"][oc, :, :], s2[:, :], r=[("r_stg2", 0)], w=[("fm_kk", oc)])
                fm_linear(G, xiT, "r_xiT", wb, "r_wk_b", 1024, cons_k)
                P.barrier()
            mk_xi(3)
            with ExitStack() as esw:
                wb = load_w_bf16(G, esw, "r_wv", G.Wl("rw_wv", j), 1024, 1024)
                if j > 0:
                    v1b = load_w_bf16(G, esw, "r_v1", G.Wl("rw_v1", j - 1), 1024, 32)
                    v2b = load_w_bf16(G, esw, "r_v2", G.Wl("rw_v2", j - 1), 32, 1024)
                    v0c = colvec(G, esw, "r_v0c", G.Wl("rw_v0", j - 1), "r_v0c")
                    t1 = sb(nc, esw, "r_t1v", [128, 1, T], BF16)
                    def cons_t(oc, t0, tn, ps, pkey, m_):
                        P.op('act', lambda e: e.copy(out=t1[0:m_, 0, t0:t0 + tn], in_=ps[0:m_, 0:tn]), r=[pkey], w=["r_t1v", pkey])
                    fm_linear(G, xiT, "r_xiT", v1b, "r_v1_b", 32, cons_t, pbanks=(2, 3))
                def cons_v(oc, t0, tn, ps, pkey, m_):
                    s = stg[oc % 2]
                    if j == 0:
                        P.op('act', lambda e: e.copy(out=s[:, t0:t0 + tn], in_=ps[:, 0:tn]), r=[pkey], w=[("r_stg", oc % 2), pkey])
                    else:
                        k_ = tcnt[0] % 2
                        tcnt[0] += 1
                        ta, tb_ = tmpa[k_], tmpb[k_]
                        if t0 == 0:
                            P.dma(ld1[oc % 2][:, :], FM["vf"][oc, :, :], r=[("fm_vf", oc)], w=[("r_ld1", 0)])
                        vf = ld1[oc % 2]
                        pb2 = 4 + k_
                        P.op('pe', lambda e: e.matmul(G.ps[pb2][:, 0:tn], lhsT=v2b[0:32, 0, oc * 128:(oc + 1) * 128], rhs=t1[0:32, 0, t0:t0 + tn], start=True, stop=True),
                             r=["r_t1v", "r_v2_b"], w=[("ps", pb2)])
                        P.op('act', lambda e: e.activation(out=ta[:, 0:tn], in_=G.ps[pb2][:, 0:tn], func=AF.Sigmoid, bias=v0c[:, oc:oc + 1], scale=1.0), r=[("ps", pb2), "r_v0c"], w=[("r_tmpa", k_), ("ps", pb2)])
                        P.op('dve', lambda e: e.tensor_tensor(out=tb_[:, 0:tn], in0=vf[:, t0:t0 + tn], in1=ps[:, 0:tn], op=ALU.subtract), r=[("r_ld1", 0), pkey], w=[("r_tmpb", k_)])
                        P.op('dve', lambda e: e.tensor_tensor(out=tb_[:, 0:tn], in0=tb_[:, 0:tn], in1=ta[:, 0:tn], op=ALU.mult), r=[("r_tmpa", k_), ("r_tmpb", k_)], w=[("r_tmpb", k_)])
                        P.op('dve', lambda e: e.tensor_tensor(out=s[:, t0:t0 + tn], in0=tb_[:, 0:tn], in1=ps[:, 0:tn], op=ALU.add), r=[("r_tmpb", k_), pkey], w=[("r_stg", oc % 2), pkey])
                    if t0 == 2048:
                        P.dma(FM["vT"][oc, :, :], s[:, :], r=[("r_stg", oc % 2)], w=[("fm_vT", oc)])
                        if j == 0:
                            P.dma(FM["vf"][oc, :, :], s[:, :], r=[("r_stg", oc % 2)], w=[("fm_vf", oc)])
                fm_linear(G, xiT, "r_xiT", wb, "r_wv_b", 1024, cons_v)
                P.barrier()
            mk_xi(4)
            for d in range(2):
                with ExitStack() as esw:
                    a1b = load_w_bf16(G, esw, "r_a1", G.Wl("rw_a1", j)[d], 1024, 64)
                    a2b = load_w_bf16(G, esw, "r_a2", G.Wl("rw_a2", j)[d], 64, 1024)
                    a0c = colvec(G, esw, "r_a0c", G.Wl("rw_a0", j)[d], "r_a0c")
                    t1 = sb(nc, esw, "r_t1a", [128, 1, T], BF16)
                    def cons_t(oc, t0, tn, ps, pkey, m_):
                        P.op('act', lambda e: e.copy(out=t1[0:m_, 0, t0:t0 + tn], in_=ps[0:m_, 0:tn]), r=[pkey], w=["r_t1a", pkey])
                    fm_linear(G, xiT, "r_xiT", a1b, "r_a1_b", 64, cons_t, pbanks=(2, 3))
                    def cons_a(oc, t0, tn, ps, pkey, m_):
                        s, s2 = stg[oc % 2], stg2[oc % 2]
                        k_ = tcnt[0] % 2
                        tcnt[0] += 1
                        ta, tb_ = tmpa[k_], tmpb[k_]
                        if t0 == 0:
                            P.dma(ld1[oc % 2][:, :], FM["kT"][oc, :, :], r=[("fm_kT", oc)], w=[("r_ld1", 0)])
                            P.dma(ld2[oc % 2][:, :], FM["kk"][oc, :, :], r=[("fm_kk", oc)], w=[("r_ld2", 0)])
                        kt_, kkt = ld1[oc % 2], ld2[oc % 2]
                        P.op('act', lambda e: e.activation(out=ta[:, 0:tn], in_=ps[:, 0:tn], func=AF.Sigmoid, bias=a0c[:, oc:oc + 1], scale=1.0), r=[pkey, "r_a0c"], w=[("r_tmpa", k_), pkey])
                        P.op('pool', lambda e: e.tensor_tensor(out=s2[:, t0:t0 + tn], in0=kkt[:, t0:t0 + tn], in1=ta[:, 0:tn], op=ALU.mult), r=[("r_ld2", 0), ("r_tmpa", k_)], w=[("r_stg2", 0)])
                        P.op('dve', lambda e: e.tensor_scalar(out=tb_[:, 0:tn], in0=ta[:, 0:tn], scalar1=kac[:, oc:oc + 1], scalar2=omka[:, oc:oc + 1], op0=ALU.mult, op1=ALU.add),
                             r=[("r_tmpa", k_), "r_cv", "r_cv2"], w=[("r_tmpb", k_)])
                        P.op('dve', lambda e: e.tensor_tensor(out=s[:, t0:t0 + tn], in0=tb_[:, 0:tn], in1=kt_[:, t0:t0 + tn], op=ALU.mult), r=[("r_tmpb", k_), ("r_ld1", 0)], w=[("r_stg", oc % 2)])
                        if t0 == 2048:
                            P.dma(FM["kd%d" % d][oc, :, :], s[:, :], r=[("r_stg", oc % 2)], w=[("fm_kd%d" % d, oc)])
                            P.dma(FM["b%d" % d][oc, :, :], s2[:, :], r=[("r_stg2", 0)], w=[("fm_b%d" % d, oc)])
                    fm_linear(G, t1, "r_t1a", a2b, "r_a2_b", 1024, cons_a, kparts=[(0, 64)])
                    P.barrier()
            mk_xi(5)
            with ExitStack() as esw:
                g1b = load_w_bf16(G, esw, "r_g1", G.Wl("rw_g1", j), 1024, 160)
                g2b = load_w_bf16(G, esw, "r_g2", G.Wl("rw_g2", j), 160, 1024)
                t1 = sb(nc, esw, "r_t1g", [128, 2, T], BF16)
                def cons_t(oc, t0, tn, ps, pkey, m_):
                    P.op('act', lambda e: e.activation(out=t1[0:m_, oc, t0:t0 + tn], in_=ps[0:m_, 0:tn], func=AF.Sigmoid), r=[pkey], w=["r_t1g", pkey])
                fm_linear(G, xiT, "r_xiT", g1b, "r_g1_b", 160, cons_t, pbanks=(2, 3))
                def cons_g(oc, t0, tn, ps, pkey, m_):
                    s = stg[oc % 2]
                    P.op('act', lambda e: e.copy(out=s[:, t0:t0 + tn], in_=ps[:, 0:tn]), r=[pkey], w=[("r_stg", oc % 2), pkey])
                    if t0 == 2048:
                        P.dma(FM["gT"][oc, :, :], s[:, :], r=[("r_stg", oc % 2)], w=[("fm_gT", oc)])
                fm_linear(G, t1, "r_t1g", g2b, "r_g2_b", 1024, cons_g, kparts=[(0, 128), (1, 32)])
                P.barrier()
            P.barrier()
        if not SKIP_SCAN:
            scan_stage(G)
        with ExitStack() as esR:
            zT = sb(nc, esR, "o_zT", [128, 8, T], BF16)
            rkc = colvec(G, esR, "o_rkc", G.Wl("rw_rk", j).rearrange("h k -> (h k)"), "o_cv")
            lgc = colvec(G, esR, "o_lgc", G.Wl("rw_lnx_g", j), "o_cv")
            lbc = colvec(G, esR, "o_lbc", G.Wl("rw_lnx_b", j), "o_cv")
            epsg = sb(nc, esR, "o_epsg", [128, 1])
            P.op('dve', lambda e: e.memset(epsg[:], RW_GN_EPS), w=["o_epsg"])
            with ExitStack() as esL:
                L = {nm: [sb(nc, esL, "o_%s" % nm, [128, T])] * 2 for nm in ("y0", "y1", "r", "kd0", "kd1", "v", "g")}
                wa = [sb(nc, esL, "o_wa%d" % i, [128, 512]) for i in range(2)]
                wb_ = [sb(nc, esL, "o_wb%d" % i, [128, 512]) for i in range(2)]
                wc_ = [sb(nc, esL, "o_wc%d" % i, [128, 512]) for i in range(2)]
                it = 0
                for p in range(8):
                    b = 0
                    for d in range(2):
                        for h in range(2):
                            P.dma(L["y%d" % d][b][h * 64:(h + 1) * 64, :], G.y_d[d, :, 2 * p + h, :], r=[("y_d", None)] if False else ["y_d"], w=[("o_L_y%d" % d, b)])
                    for nm, fmn in (("r", "rT"), ("kd0", "kd0"), ("kd1", "kd1"), ("v", "vT"), ("g", "gT")):
                        P.dma(L[nm][b][:, :], FM[fmn][p, :, :], r=[("fm_" + fmn, p)], w=[("o_L_" + nm, b)])
                    for (t0, tn) in TBLK:
                        k_ = it % 2
                        it += 1
                        a_, b2, c_ = wa[k_], wb_[k_], wc_[k_]
                        ka, kb, kc = ("o_wa", k_), ("o_wb", k_), ("o_wc", k_)
                        ts_ = slice(t0, t0 + tn)
                        P.op('dve', lambda e: e.tensor_tensor(out=a_[:, 0:tn], in0=L["y0"][b][:, ts_], in1=L["y1"][b][:, ts_], op=ALU.add), r=[("o_L_y0", b), ("o_L_y1", b)], w=[ka])
                        P.op('pe', lambda e: e.matmul(G.ps[k_][:, 0:tn], lhsT=BO[:, :], rhs=a_[:, 0:tn], start=True, stop=True), r=["r_BO", ka], w=[("ps", k_)])
                        P.op('dve', lambda e: e.scalar_tensor_tensor(out=a_[:, 0:tn], in0=G.ps[k_][:, 0:tn], scalar=-1.0 / 64, in1=a_[:, 0:tn], op0=ALU.mult, op1=ALU.add), r=[("ps", k_), ka], w=[ka, ("ps", k_)])
                        P.op('pool', lambda e: e.tensor_tensor(out=b2[:, 0:tn], in0=a_[:, 0:tn], in1=a_[:, 0:tn], op=ALU.mult), r=[ka], w=[kb])
                        P.op('pe', lambda e: e.matmul(G.ps[2 + k_][:, 0:tn], lhsT=BO[:, :], rhs=b2[:, 0:tn], start=True, stop=True), r=["r_BO", kb], w=[("ps", 2 + k_)])
                        P.op('act', lambda e: e.activation(out=b2[:, 0:tn], in_=G.ps[2 + k_][:, 0:tn], func=AF.Sqrt, scale=1.0 / 64, bias=epsg[:, 0:1]), r=[("ps", 2 + k_), "o_epsg"], w=[kb, ("ps", 2 + k_)])
                        P.op('dve', lambda e: e.reciprocal(out=b2[:, 0:tn], in_=b2[:, 0:tn]), r=[kb], w=[kb])
                        P.op('dve', lambda e: e.tensor_tensor(out=a_[:, 0:tn], in0=a_[:, 0:tn], in1=b2[:, 0:tn], op=ALU.mult), r=[ka, kb], w=[ka])
                        P.op('dve', lambda e: e.tensor_scalar(out=a_[:, 0:tn], in0=a_[:, 0:tn], scalar1=lgc[:, p:p + 1], scalar2=lbc[:, p:p + 1], op0=ALU.mult, op1=ALU.add), r=[ka, "o_cv"], w=[ka])
                        P.op('pool', lambda e: e.tensor_tensor(out=c_[:, 0:tn], in0=L["kd0"][b][:, ts_], in1=L["kd1"][b][:, ts_], op=ALU.add), r=[("o_L_kd0", b), ("o_L_kd1", b)], w=[kc])
                        P.op('dve', lambda e: e.scalar_tensor_tensor(out=c_[:, 0:tn], in0=c_[:, 0:tn], scalar=rkc[:, p:p + 1], in1=L["r"][b][:, ts_], op0=ALU.mult, op1=ALU.mult), r=[kc, "o_cv", ("o_L_r", b)], w=[kc])
                        P.op('pe', lambda e: e.matmul(G.ps[4 + k_][:, 0:tn], lhsT=BO[:, :], rhs=c_[:, 0:tn], start=True, stop=True), r=["r_BO", kc], w=[("ps", 4 + k_)])
                        P.op('dve', lambda e: e.tensor_tensor(out=c_[:, 0:tn], in0=G.ps[4 + k_][:, 0:tn], in1=L["v"][b][:, ts_], op=ALU.mult), r=[("ps", 4 + k_), ("o_L_v", b)], w=[kc, ("ps", 4 + k_)])
                        P.op('dve', lambda e: e.tensor_tensor(out=a_[:, 0:tn], in0=a_[:, 0:tn], in1=c_[:, 0:tn], op=ALU.add), r=[ka, kc], w=[ka])
                        P.op('dve', lambda e: e.tensor_tensor(out=zT[:, p, ts_], in0=a_[:, 0:tn], in1=L["g"][b][:, ts_], op=ALU.mult), r=[ka, ("o_L_g", b)], w=[("o_zT", p)])
                P.barrier()
            wob = load_w_bf16(G, esR, "o_wo", G.Wl("rw_wo", j), 1024, 1024)
            gate = [sb(nc, esR, "o_gate%d" % r_, [128, 1024]) for r_ in range(2)]
            LG = sb(nc, esR, "o_lg", [128, 1024])
            LB = sb(nc, esR, "o_lb", [128, 1024])
            for r_ in range(2):
                P.dma(gate[r_][:], G.modv[l, r_:r_ + 1, 2 * 1024:3 * 1024].partition_broadcast(128), r=[("modv", l)], w=["o_bc"])
            P.dma(LG[:], row(G.Wl("ln1_g", l)).partition_broadcast(128), w=["o_bc"])
            P.dma(LB[:], row(G.Wl("ln1_b", l)).partition_broadcast(128), w=["o_bc"])
            xt = [sb(nc, esR, "o_x%d" % i, [128, 1024]) for i in range(2)]
            tmp = [sb(nc, esR, "o_tmp%d" % i, [128, 1024]) for i in range(2)]
            yt = [sb(nc, esR, "o_y%d" % i, [128, 1024]) for i in range(2)]
            st = [sb(nc, esR, "o_st%d" % i, [128, 16]) for i in range(2)]
            for t in range(NT):
                b = t % 2
                r_ = 0 if t < 16 else 1
                P.dma(xt[b][:], G.xres[t * 128:(t + 1) * 128, :], r=[("xres", t)], w=[("o_x", b)])
                banks = [G.ps[0 + 2 * b], G.ps[1 + 2 * b]]
                okeys = [("ps", 0 + 2 * b), ("ps", 1 + 2 * b)]
                for hh in range(2):
                    for c in range(8):
                        P.op('pe', lambda e_, hh=hh, c=c: e_.matmul(banks[hh][:, :], lhsT=zT[:, c, t * 128:(t + 1) * 128], rhs=wob[:, c, hh * 512:(hh + 1) * 512],
                                                                   start=(c == 0), stop=(c == 7)), r=["o_zT", "o_wo_b"], w=[okeys[hh]])
                ln_epilogue(G, banks, okeys, xt[b], ("o_x", b), gate[r_], LG, LB, ["o_bc"], tmp[b], ("o_tmp", b), st[b], ("o_st", b), yt[b], ("o_y", b))
                P.dma(G.xres[t * 128:(t + 1) * 128, :], yt[b][:], r=[("o_y", b)], w=[("xres", t)])
            P.barrier()
        P.barrier()


def build(layers=(0, 1, 2, 3), stages=None):
    nc = bass.Bass("TRN2", target_bir_lowering=False)
    G = Ctx()
    G.nc = nc

    def din(name, shape):
        return nc.dram_tensor(name, list(shape), F32, kind="ExternalInput").ap()
    G.x_d = din("x", [TL, D])
    G.ctx_d = din("ctx", [TC, D])
    G.c_d = din("c", [1, D])
    G.cc_d = din("c_ctx", [1, D])
    G.used = {}
    specs = dict(WEIGHT_SPECS)

    def Wl(name, l):
        key = "%s_%d" % (name, l)
        if key not in G.used:
            G.used[key] = (name, l, din(key, specs[name][1:]))
        return G.used[key][2]
    G.Wl = Wl
    G.kc = {}

    def KC(name, shape):
        if name not in G.kc:
            G.kc[name] = din(name, shape)
        return G.kc[name]
    G.KC = KC
    G.ident_d = KC("k_ident", [128, 128])
    G.out_d = nc.dram_tensor("out", [TL, D], F32, kind="ExternalOutput").ap()
    G.outc_d = nc.dram_tensor("outc", [TC, D], F32, kind="ExternalOutput").ap()
    G.xres = nc.dram_tensor("xres", [T, D], F32, kind="Internal").ap()
    G.modv = nc.dram_tensor("modv", [4, 2, 6144], F32, kind="Internal").ap()
    G.mo_d = nc.dram_tensor("mo_d", [T, D], BF16, kind="Internal").ap()
    G.fm = {nm: nc.dram_tensor("fm_" + nm, [8, 128, T], F32, kind="Internal").ap()
            for nm in ("rT", "kT", "vT", "vf", "kk", "gT", "lw0", "lw1", "kd0", "kd1", "b0", "b1")}
    G.y_d = nc.dram_tensor("y_d", [2, 64, 16, T], F32, kind="Internal").ap()
    P = Prog(nc)
    G.P = P
    with ExitStack() as es:
        G.ps = [es.enter_context(nc.psum_tensor("ps%d" % i, [128, 512], F32)) for i in range(8)]
        G.identF = sb(nc, es, "identF", [128, 128])
        G.identB = sb(nc, es, "identB", [128, 128], BF16)
        G.eps_ln = sb(nc, es, "eps_ln", [128, 1])
        P.dma(G.identF[:], G.ident_d[:, :], w=["ident"])
        P.op('dve', lambda e: e.tensor_copy(out=G.identB[:], in_=G.identF[:]), r=["ident"], w=["identB"])
        P.op('dve', lambda e: e.memset(G.eps_ln[:], LN_EPS), w=["consts"])
        G.one_c = sb(nc, es, "one_c", [128, 1])
        G.mhalf_c = sb(nc, es, "mhalf_c", [128, 1])
        P.op('dve', lambda e: e.memset(G.one_c[:], 1.0), w=["consts"])
        P.op('dve', lambda e: e.memset(G.mhalf_c[:], -0.5), w=["consts"])
        for t in range(16):
            P.dma(G.xres[t * 128:(t + 1) * 128, :], G.x_d[t * 128:(t + 1) * 128, :], w=[("xres", t)])
        for t in range(2):
            P.dma(G.xres[TL + t * 128:TL + (t + 1) * 128, :], G.ctx_d[t * 128:(t + 1) * 128, :], w=[("xres", 16 + t)])
        stage_modvec(G, layers)
        for l in layers:
            if l % 2 == 0 and (stages is None or "mix" in stages):
                stage_even(G, l)
            if l % 2 == 1 and (stages is None or "mix" in stages):
                stage_rwkv(G, l)
            if stages is None or "ffn" in stages:
                stage_ffn(G, l)
        for t in range(16):
            P.dma(G.out_d[t * 128:(t + 1) * 128, :], G.xres[t * 128:(t + 1) * 128, :], r=[("xres", t)], w=[("out", t)])
        for t in range(2):
            P.dma(G.outc_d[t * 128:(t + 1) * 128, :], G.xres[TL + t * 128:TL + (t + 1) * 128, :], r=[("xres", 16 + t)], w=[("outc", t)])
        P.barrier()
    P.es.close()
    return nc, P, G


def make_consts():
    k = {"k_ident": np.eye(128, dtype=np.float32)}
    t = np.arange(TL)
    rowi = (t // 64).astype(np.float32)
    coli = (t % 64).astype(np.float32)
    for nm, dim, ng in (("A", 64, 8), ("B", 128, 4)):
        nf = dim // 4
        inv = (10000.0 ** (-np.arange(nf, dtype=np.float32) / nf)).astype(np.float32)
        ang = np.concatenate([rowi[:, None] * inv, coli[:, None] * inv], -1).astype(np.float32)
        k["k_cos" + nm] = np.ascontiguousarray(np.tile(np.cos(ang).astype(np.float32), (1, ng)))
        k["k_sin" + nm] = np.ascontiguousarray(np.tile(np.sin(ang).astype(np.float32), (1, ng)))
    p = np.arange(128, dtype=np.float32)
    k["k_cols"] = np.stack([127 - p, p, p + 1, 128 - p], 1).astype(np.float32)
    jj = p[None, :]
    pp = p[:, None]
    k["k_mats"] = np.ascontiguousarray(np.stack([np.maximum(jj - pp, 0), np.maximum(pp - jj, 0), (jj >= pp).astype(np.float32),
                                                 (jj <= pp).astype(np.float32)], 1).astype(np.float32))
    bo = np.zeros((128, 128), np.float32)
    bo[:64, :64] = 1.0
    bo[64:, 64:] = 1.0
    k["k_bo"] = bo
    i64 = np.arange(64)
    strict = (i64[:, None] < i64[None, :]).astype(np.float32)
    incl = (i64[:, None] <= i64[None, :]).astype(np.float32)
    def bdm(m):
        z = np.zeros((128, 128), np.float32)
        z[:64, :64] = m
        z[64:, 64:] = m
        return z
    k["k_smask"] = np.ascontiguousarray(np.stack([bdm(strict), bdm(strict.T), bdm(incl)], 1))
    return k


def kernel(**inputs):
    nc, _, G = build()
    consts = make_consts()
    in_maps = []
    for b in range(8):
        m = {"x": np.ascontiguousarray(inputs["x"][b]), "ctx": np.ascontiguousarray(inputs["ctx"][b]),
             "c": np.ascontiguousarray(inputs["c"][b:b + 1]), "c_ctx": np.ascontiguousarray(inputs["c_ctx"][None, :])}
        for key, (name, l, _ap) in G.used.items():
            m[key] = np.ascontiguousarray(inputs[name][l])
        for key in G.kc:
            m[key] = consts[key]
        in_maps.append(m)
    res = run_bass_kernel_spmd(nc, in_maps, core_ids=list(range(8)))
    return np.stack([r["out"] for r in res.results], axis=0).astype(np.float32)
```

```python
import math
import numpy as np
from contextlib import ExitStack
import concourse.bass as bass
import concourse.mybir as mybir
from concourse.bass_utils import run_bass_kernel_spmd

F32 = mybir.dt.float32
BF16 = mybir.dt.bfloat16
AF = mybir.ActivationFunctionType
ALU = mybir.AluOpType
AX = mybir.AxisListType

NDS = 8
SAME_ENGINE_SYNC = True
EMBED_WAIT = True

D = 1024
TL = 2048
TC = 256
T = TL + TC
NT = T // 128
DFF = 2816
NFC = DFF // 128
DEPTH = 4
ALPHA = (2.0 * DEPTH) ** 0.25
LN_EPS = 1e-6

WEIGHT_SPECS = [
    ("mod_w", (4, 1024, 6144)), ("mod_b", (4, 6144)), ("ln1_g", (4, 1024)), ("ln1_b", (4, 1024)),
    ("ln2_g", (4, 1024)), ("ln2_b", (4, 1024)), ("ffn_w1", (4, 1024, 2816)), ("ffn_w3", (4, 1024, 2816)),
    ("ffn_w2", (4, 2816, 1024)), ("ev_w_in", (2, 1024, 3584)), ("ev_w_out", (2, 1024, 1024)),
    ("da_lam_q1", (2, 64)), ("da_lam_k1", (2, 64)), ("da_lam_q2", (2, 64)), ("da_lam_k2", (2, 64)),
    ("da_gn_g", (2, 512)), ("rt_decay_logit", (2, 2, 4)), ("rw_mu", (2, 6, 1024)),
    ("rw_wr", (2, 1024, 1024)), ("rw_wk", (2, 1024, 1024)), ("rw_wv", (2, 1024, 1024)), ("rw_wo", (2, 1024, 1024)),
    ("rw_w0", (2, 2, 1024)), ("rw_w1", (2, 2, 1024, 64)), ("rw_w2", (2, 2, 64, 1024)),
    ("rw_a0", (2, 2, 1024)), ("rw_a1", (2, 2, 1024, 64)), ("rw_a2", (2, 2, 64, 1024)),
    ("rw_v0", (1, 1024)), ("rw_v1", (1, 1024, 32)), ("rw_v2", (1, 32, 1024)),
    ("rw_g1", (2, 1024, 160)), ("rw_g2", (2, 160, 1024)), ("rw_kk", (2, 1024)), ("rw_ka", (2, 1024)),
    ("rw_rk", (2, 16, 64)), ("rw_lnx_g", (2, 1024)), ("rw_lnx_b", (2, 1024)),
]


class Prog:
    def __init__(self, nc):
        self.nc = nc
        self.engs = {'pe': nc.tensor, 'act': nc.scalar, 'dve': nc.vector, 'pool': nc.gpsimd, 'sp': nc.sync}
        self.es = ExitStack()
        self.sem = {}
        for e in ['pe', 'act', 'dve', 'pool']:
            self.sem[('e', e)] = self.es.enter_context(nc.semaphore('s_' + e))
        for i in range(NDS):
            self.sem[('d', i)] = self.es.enter_context(nc.semaphore('d%d' % i))
        self.cnt = {k: 0 for k in self.sem}
        self.dnext = 0
        self.known = {e: {} for e in self.engs}
        self.res = {}
        self.nops = 0
        self.nwaits = 0

    def _get(self, key):
        name, sub = key if isinstance(key, tuple) else (key, None)
        d = self.res.setdefault(name, {})
        if sub not in d:
            d[sub] = [None, {}]
        return d[sub]

    def _conf(self, key):
        name, sub = key if isinstance(key, tuple) else (key, None)
        d = self.res.setdefault(name, {})
        if sub is None:
            return list(d.values())
        out = []
        if sub in d:
            out.append(d[sub])
        if None in d:
            out.append(d[None])
        return out

    def op(self, eng, fn, r=(), w=(), dma=False, noembed=False):
        deps = {}

        def need(k, v):
            if deps.get(k, 0) < v:
                deps[k] = v
        for key in r:
            for st in self._conf(key):
                if st[0] is not None:
                    need(*st[0])
        for key in w:
            for st in self._conf(key):
                if st[0] is not None:
                    need(*st[0])
                for k, v in st[1].items():
                    need(k, v)
        E = self.engs[eng]
        kn = self.known[eng]
        if dma:
            d = self.dnext
            self.dnext = (d + 1) % NDS
            sk = ('d', d)
            if self.cnt[sk]:
                need(sk, self.cnt[sk])
        else:
            sk = ('e', eng)
        wl = []
        for k, v in deps.items():
            if (not dma) and k == ('e', eng) and (eng == 'pe' or not SAME_ENGINE_SYNC):
                continue
            if kn.get(k, 0) >= v:
                continue
            kn[k] = v
            wl.append((k, v))
        emb = None
        if wl and EMBED_WAIT and not dma and not noembed:
            emb = wl.pop()
        for k, v in wl:
            E.wait_ge(self.sem[k], v)
            self.nwaits += 1
        ins = fn(E)
        if emb is not None:
            ins.wait_op(self.sem[emb[0]], emb[1], "sem-ge")
        inc = 16 if dma else 1
        self.cnt[sk] += inc
        ins.then_inc(self.sem[sk], inc)
        ev = (sk, self.cnt[sk])
        self.nops += 1
        for key in r:
            st = self._get(key)
            if st[1].get(ev[0], 0) < ev[1]:
                st[1][ev[0]] = ev[1]
        for key in w:
            name, sub = key if isinstance(key, tuple) else (key, None)
            if sub is None:
                self.res[name] = {None: [ev, {}]}
            else:
                st = self._get(key)
                st[0] = ev
                st[1] = {}
        return ev

    def dma(self, out, in_, r=(), w=(), eng='sp', **kw):
        return self.op(eng, lambda e: e.dma_start(out=out, in_=in_, **kw), r=r, w=w, dma=True)

    def barrier(self, engines=('pe', 'act', 'dve', 'pool', 'sp')):
        for eng in engines:
            E = self.engs[eng]
            kn = self.known[eng]
            for k, v in self.cnt.items():
                if v and kn.get(k, 0) < v:
                    kn[k] = v
                    E.wait_ge(self.sem[k], v)
                    self.nwaits += 1


class Ctx:
    pass


_SBN = [0]


def sb(nc, es, name, shape, dt=F32):
    _SBN[0] += 1
    return es.enter_context(nc.sbuf_tensor("%s_u%d" % (name, _SBN[0]), list(shape), dt))


_CLN = [0]


def colload(G, es, dst2d, dkey, src_flat, n):
    nc, P = G.nc, G.P
    _CLN[0] += 1
    k = "cl_stg%d" % _CLN[0]
    stg = sb(nc, es, k, [n, 128])
    P.dma(stg[:], src_flat.rearrange("(j p) -> j p", p=128), w=[k])
    P.op('pe', lambda e: e.transpose(out=G.ps[7][:, 0:n], in_=stg[:, :], identity=G.identF[0:n, 0:n]), r=[k, "ident"], w=[("ps", 7)])
    P.op('dve', lambda e: e.tensor_copy(out=dst2d, in_=G.ps[7][:, 0:n]), r=[("ps", 7)], w=[dkey, ("ps", 7)])


def stage_modvec(G, layers):
    nc, P = G.nc, G.P
    with ExitStack() as es:
        craw = sb(nc, es, "mv_craw", [128, 2, 8])
        cT = sb(nc, es, "mv_cT", [128, 8, 2])
        wt = [sb(nc, es, "mv_w%d" % i, [128, 8, 512]) for i in range(2)]
        bt = [sb(nc, es, "mv_b%d" % i, [2, 512]) for i in range(2)]
        ot = [sb(nc, es, "mv_o%d" % i, [2, 512]) for i in range(2)]
        colload(G, es, craw[:, 0, :], "mv_craw", G.c_d[0, :], 8)
        colload(G, es, craw[:, 1, :], "mv_craw", G.cc_d[0, :], 8)
        for r in range(2):
            P.op('act', lambda e, r=r: e.activation(out=cT[:, :, r], in_=craw[:, r, :], func=AF.Silu),
                 r=["mv_craw"], w=[("mv_cT", r)])
        i = 0
        for l in layers:
            for nb in range(12):
                b = i % 2
                i += 1
                P.dma(wt[b][:], G.Wl("mod_w", l)[:, nb * 512:(nb + 1) * 512].rearrange("(c p) n -> p c n", p=128),
                      w=[("mv_w", b)])
                P.dma(bt[b][:], G.Wl("mod_b", l).rearrange("(o n) -> o n", o=1)[:, nb * 512:(nb + 1) * 512].partition_broadcast(2), w=[("mv_b", b)])
                ps = G.ps[b]
                for c in range(8):
                    P.op('pe', lambda e, c=c, b=b, ps=ps: e.matmul(ps[0:2, :], lhsT=cT[:, c, :], rhs=wt[b][:, c, :],
                                                                  start=(c == 0), stop=(c == 7)),
                         r=["mv_cT", ("mv_w", b)], w=[("ps", b)])
                P.op('dve', lambda e, b=b, ps=ps: e.tensor_tensor(out=ot[b][:], in0=ps[0:2, :], in1=bt[b][:], op=ALU.add),
                     r=[("ps", b), ("mv_b", b)], w=[("mv_o", b), ("ps", b)])
                P.dma(G.modv[l, :, nb * 512:(nb + 1) * 512], ot[b][:], r=[("mv_o", b)], w=[("modv", l)])
        P.barrier()


def load_modcols(G, es, l, name):
    nc, P = G.nc, G.P
    mc = sb(nc, es, name, [128, 2, 6, 8])
    for r in range(2):
        _CLN[0] += 1
        k = "cl_stg%d" % _CLN[0]
        stg = sb(nc, es, k, [48, 128])
        P.dma(stg[:], G.modv[l, r, :].rearrange("(j p) -> j p", p=128), r=[("modv", l)], w=[k])
        P.op('pe', lambda e, stg=stg: e.transpose(out=G.ps[7][:, 0:48], in_=stg[:, :], identity=G.identF[0:48, 0:48]), r=[k, "ident"], w=[("ps", 7)])
        P.op('dve', lambda e, r=r: e.tensor_copy(out=mc[:, r, :, :].rearrange("p i c -> p (i c)"), in_=G.ps[7][:, 0:48]), r=[("ps", 7)], w=[name, ("ps", 7)])
    for r in range(2):
        for i in (1, 4):
            P.op('dve', lambda e, r=r, i=i: e.tensor_scalar_add(out=mc[:, r, i, :], in0=mc[:, r, i, :], scalar1=1.0),
                 r=[name], w=[name])
    return mc


def transpose_modulate(G, xt, xkey, hT, hkey, col0, mc, r, ish, isc, pbase):
    P = G.P
    for half in range(2):
        pb = pbase + half
        ps = G.ps[pb]
        for j in range(4):
            c = half * 4 + j
            P.op('pe', lambda e, c=c, j=j, ps=ps: e.transpose(out=ps[:, j * 128:(j + 1) * 128], in_=xt[:, c * 128:(c + 1) * 128],
                                                           identity=G.identF[:]),
                 r=[xkey, "ident"], w=[("ps", pb)])
        for j in range(4):
            c = half * 4 + j
            eng = 'act' if j % 2 == 0 else 'dve'
            if eng == 'act':
                P.op('act', lambda e, c=c, j=j, ps=ps: e.activation(out=hT[:, c, col0:col0 + 128], in_=ps[:, j * 128:(j + 1) * 128],
                                                                   func=AF.Identity, scale=mc[:, r, isc, c:c + 1], bias=mc[:, r, ish, c:c + 1]),
                     r=[("ps", pb), "mc"], w=[hkey, ("ps", pb)])
            else:
                P.op('dve', lambda e, c=c, j=j, ps=ps: e.tensor_scalar(out=hT[:, c, col0:col0 + 128], in0=ps[:, j * 128:(j + 1) * 128],
                                                                      scalar1=mc[:, r, isc, c:c + 1], scalar2=mc[:, r, ish, c:c + 1],
                                                                      op0=ALU.mult, op1=ALU.add),
                     r=[("ps", pb), "mc"], w=[hkey, ("ps", pb)])


def ln_epilogue(G, o_banks, okeys, xt, xkey, Gt, LGt, LBt, bkeys, tmp, tkey, st, skey, yt, ykey):
    P = G.P
    for h in range(2):
        sl = slice(h * 512, (h + 1) * 512)
        P.op('dve', lambda e, h=h, sl=sl: e.tensor_tensor(out=tmp[:, sl], in0=o_banks[h][:, :], in1=Gt[:, sl], op=ALU.mult),
             r=[okeys[h]] + bkeys, w=[tkey, okeys[h]])
    P.op('dve', lambda e: e.scalar_tensor_tensor(out=tmp[:, :], in0=xt[:, :], scalar=ALPHA, in1=tmp[:, :], op0=ALU.mult, op1=ALU.add),
         r=[xkey, tkey], w=[tkey])
    for h in range(2):
        P.op('dve', lambda e, h=h: e.bn_stats(out=st[:, h * 6:(h + 1) * 6], in_=tmp[:, h * 512:(h + 1) * 512]), r=[tkey], w=[skey])
    P.op('dve', lambda e: e.bn_aggr(out=st[:, 12:14], in_=st[:, 0:12]), r=[skey], w=[skey])
    P.op('act', lambda e: e.activation(out=st[:, 14:15], in_=st[:, 13:14], func=AF.Sqrt, bias=G.eps_ln[:, 0:1], scale=1.0), r=[skey, "consts"], w=[skey])
    P.op('dve', lambda e: e.reciprocal(out=st[:, 15:16], in_=st[:, 14:15]), r=[skey], w=[skey])
    P.op('dve', lambda e: e.tensor_scalar(out=tmp[:, :], in0=tmp[:, :], scalar1=st[:, 12:13], scalar2=st[:, 15:16],
                                          op0=ALU.subtract, op1=ALU.mult), r=[skey, tkey], w=[tkey])
    P.op('pool', lambda e: e.tensor_tensor(out=tmp[:, :], in0=tmp[:, :], in1=LGt[:, :], op=ALU.mult), r=[tkey] + bkeys, w=[tkey])
    P.op('pool', lambda e: e.tensor_tensor(out=yt[:, :], in0=tmp[:, :], in1=LBt[:, :], op=ALU.add), r=[tkey] + bkeys, w=[ykey])


def load_bcast(G, tile, key, src_row):
    G.P.dma(tile[:], src_row.partition_broadcast(128), w=[key])


def stage_ffn(G, l):
    nc, P = G.nc, G.P
    W1, W3, W2 = G.Wl("ffn_w1", l), G.Wl("ffn_w3", l), G.Wl("ffn_w2", l)
    with ExitStack() as es:
        mc = load_modcols(G, es, l, "mc")
        gate = [sb(nc, es, "f_gate%d" % r, [128, 1024]) for r in range(2)]
        LG = sb(nc, es, "f_lg", [128, 1024])
        LB = sb(nc, es, "f_lb", [128, 1024])
        for r in range(2):
            P.dma(gate[r][:], G.modv[l, r:r + 1, 5 * 1024:6 * 1024].partition_broadcast(128), r=[("modv", l)], w=["f_bc"])
        P.dma(LG[:], G.Wl("ln2_g", l).rearrange("(o n) -> o n", o=1).partition_broadcast(128), w=["f_bc"])
        P.dma(LB[:], G.Wl("ln2_b", l).rearrange("(o n) -> o n", o=1).partition_broadcast(128), w=["f_bc"])
        w2b = sb(nc, es, "f_w2b", [128, NFC, 1024], BF16)
        w2s = [sb(nc, es, "f_w2s%d" % i, [128, 2, 1024]) for i in range(2)]
        for i in range(NFC // 2):
            b = i % 2
            P.dma(w2s[b][:], W2[i * 256:(i + 1) * 256, :].rearrange("(c p) n -> p c n", p=128), w=[("f_w2s", b)])
            P.op('pool', lambda e, i=i, b=b: e.tensor_copy(out=w2b[:, 2 * i:2 * i + 2, :], in_=w2s[b][:]),
                 r=[("f_w2s", b)], w=[("f_w2b", i)])
        NH = 2
        TPH = NT // NH
        TOKH = TPH * 128
        NTB = TOKH // 384
        hT = sb(nc, es, "f_hT", [128, 8, TOKH], BF16)
        gT = sb(nc, es, "f_gT", [128, NFC, TOKH], BF16)
        xt = [sb(nc, es, "f_x%d" % i, [128, 1024]) for i in range(2)]
        w1s = [sb(nc, es, "f_w1s%d" % i, [128, 8, 128]) for i in range(2)]
        w3s = [sb(nc, es, "f_w3s%d" % i, [128, 8, 128]) for i in range(2)]
        w1b = [sb(nc, es, "f_w1b%d" % i, [128, 8, 128], BF16) for i in range(2)]
        w3b = [sb(nc, es, "f_w3b%d" % i, [128, 8, 128], BF16) for i in range(2)]
        sa = [sb(nc, es, "f_sa%d" % i, [128, 384]) for i in range(2)]
        tmp = [sb(nc, es, "f_tmp%d" % i, [128, 1024]) for i in range(2)]
        yt = [sb(nc, es, "f_y%d" % i, [128, 1024]) for i in range(2)]
        st = [sb(nc, es, "f_st%d" % i, [128, 16]) for i in range(2)]
        xi = 0
        wi = 0
        for hf in range(NH):
            for tt in range(TPH):
                t = hf * TPH + tt
                b = xi % 2
                xi += 1
                r = 0 if t < 16 else 1
                P.dma(xt[b][:], G.xres[t * 128:(t + 1) * 128, :], r=[("xres", t)], w=[("f_x", b)])
                transpose_modulate(G, xt[b], ("f_x", b), hT, ("f_hT", tt), tt * 128, mc, r, 3, 4, 0)
            for cb in range(NFC):
                b = wi % 2
                wi += 1
                P.dma(w1s[b][:], W1[:, cb * 128:(cb + 1) * 128].rearrange("(c p) n -> p c n", p=128), w=[("f_w1s", b)])
                P.dma(w3s[b][:], W3[:, cb * 128:(cb + 1) * 128].rearrange("(c p) n -> p c n", p=128), w=[("f_w3s", b)])
                P.op('pool', lambda e, b=b: e.tensor_copy(out=w1b[b][:], in_=w1s[b][:]), r=[("f_w1s", b)], w=[("f_w1b", b)])
                P.op('pool', lambda e, b=b: e.tensor_copy(out=w3b[b][:], in_=w3s[b][:]), r=[("f_w3s", b)], w=[("f_w3b", b)])
                for sub in range(1):
                    fc = cb
                    for tb in range(NTB):
                        tsl = slice(tb * 384, (tb + 1) * 384)
                        k = (fc * NTB + tb) % 2
                        pa, pb_ = 2 + 2 * k, 3 + 2 * k
                        for c in range(8):
                            P.op('pe', lambda e, c=c, pa=pa, b=b, sub=sub, tsl=tsl: e.matmul(
                                G.ps[pa][:, 0:384], lhsT=w1b[b][:, c, sub * 128:(sub + 1) * 128], rhs=hT[:, c, tsl],
                                start=(c == 0), stop=(c == 7)), r=[("f_w1b", b), "f_hT"], w=[("ps", pa)])
                        for c in range(8):
                            P.op('pe', lambda e, c=c, pb_=pb_, b=b, sub=sub, tsl=tsl: e.matmul(
                                G.ps[pb_][:, 0:384], lhsT=w3b[b][:, c, sub * 128:(sub + 1) * 128], rhs=hT[:, c, tsl],
                                start=(c == 0), stop=(c == 7)), r=[("f_w3b", b), "f_hT"], w=[("ps", pb_)])
                        P.op('act', lambda e, k=k, pa=pa: e.activation(out=sa[k][:, :], in_=G.ps[pa][:, 0:384], func=AF.Silu),
                             r=[("ps", pa)], w=[("f_sa", k), ("ps", pa)])
                        P.op('dve', lambda e, k=k, pb_=pb_, fc=fc, tsl=tsl: e.tensor_tensor(
                            out=gT[:, fc, tsl], in0=G.ps[pb_][:, 0:384], in1=sa[k][:, :], op=ALU.mult),
                            r=[("ps", pb_), ("f_sa", k)], w=[("f_gT", fc), ("ps", pb_)])
            for tt in range(TPH):
                t = hf * TPH + tt
                b = xi % 2
                xi += 1
                r = 0 if t < 16 else 1
                P.dma(xt[b][:], G.xres[t * 128:(t + 1) * 128, :], r=[("xres", t)], w=[("f_x", b)])
                k = tt % 2
                banks = [G.ps[0 + 2 * k], G.ps[1 + 2 * k]]
                okeys = [("ps", 0 + 2 * k), ("ps", 1 + 2 * k)]
                for h in range(2):
                    for fc in range(NFC):
                        P.op('pe', lambda e, h=h, fc=fc, tt=tt, banks=banks: e.matmul(
                            banks[h][:, :], lhsT=gT[:, fc, tt * 128:(tt + 1) * 128], rhs=w2b[:, fc, h * 512:(h + 1) * 512],
                            start=(fc == 0), stop=(fc == NFC - 1)), r=["f_gT", "f_w2b"], w=[okeys[h]])
                ln_epilogue(G, banks, okeys, xt[b], ("f_x", b), gate[r], LG, LB, ["f_bc"], tmp[k], ("f_tmp", k),
                            st[k], ("f_st", k), yt[k], ("f_y", k))
                P.dma(G.xres[t * 128:(t + 1) * 128, :], yt[k][:], r=[("f_y", k)], w=[("xres", t)])
        P.barrier()


def row(ap):
    return ap.rearrange("(o n) -> o n", o=1)


def inproj_block(G, es_w, Wap, j0, ncols, hT, hkey, wtag):
    nc, P = G.nc, G.P
    ws = sb(nc, es_w, wtag + "_s", [128, 8, ncols])
    wb = sb(nc, es_w, wtag + "_b", [128, 8, ncols], BF16)
    P.dma(ws[:], Wap[:, j0:j0 + ncols].rearrange("(c p) n -> p c n", p=128), w=[wtag + "_s"])
    P.op('pool', lambda e: e.tensor_copy(out=wb[:], in_=ws[:]), r=[wtag + "_s"], w=[wtag + "_b"])
    return wb


def rope_evac(G, ps, pkey, t, qr, qkey, cosT, sinT, ckey, tmp, tkey, half, scale=None):
    P = G.P
    ng = 512 // (2 * half)
    if t >= 16:
        if scale is None:
            P.op('act', lambda e: e.copy(out=qr[:, :], in_=ps[:, :]), r=[pkey], w=[qkey, pkey])
        else:
            P.op('act', lambda e: e.mul(out=qr[:, :], in_=ps[:, :], mul=scale), r=[pkey], w=[qkey, pkey])
        return
    pv = ps[:, :].rearrange("p (g two d) -> p g two d", g=ng, two=2)
    qv = qr[:, :].rearrange("p (g two d) -> p g two d", g=ng, two=2)
    x1, x2 = pv[:, :, 0, :], pv[:, :, 1, :]
    cs = cosT[:, :].rearrange("p (g d) -> p g d", g=ng)
    sn = sinT[:, :].rearrange("p (g d) -> p g d", g=ng)
    t1 = tmp[:, 0:256].rearrange("p (g d) -> p g d", g=ng)
    t2 = tmp[:, 256:512].rearrange("p (g d) -> p g d", g=ng)
    P.op('dve', lambda e: e.tensor_tensor(out=t1, in0=x1, in1=cs, op=ALU.mult), r=[pkey, ckey], w=[tkey])
    P.op('dve', lambda e: e.tensor_tensor(out=t2, in0=x2, in1=sn, op=ALU.mult), r=[pkey, ckey], w=[tkey])
    P.op('dve', lambda e: e.tensor_tensor(out=qv[:, :, 0, :], in0=t1, in1=t2, op=ALU.subtract), r=[tkey], w=[qkey])
    P.op('dve', lambda e: e.tensor_tensor(out=t1, in0=x1, in1=sn, op=ALU.mult), r=[pkey, ckey], w=[tkey])
    P.op('dve', lambda e: e.tensor_tensor(out=t2, in0=x2, in1=cs, op=ALU.mult), r=[pkey, ckey], w=[tkey, pkey])
    P.op('dve', lambda e: e.tensor_tensor(out=qv[:, :, 1, :], in0=t1, in1=t2, op=ALU.add), r=[tkey], w=[qkey])
    if scale is not None:
        P.op('act', lambda e: e.mul(out=qr[:, :], in_=qr[:, :], mul=scale), r=[qkey], w=[qkey])


def transpose4(G, src, skey, dstT, dkey, t, pbank):
    P = G.P
    psb = G.ps[pbank][:, 0:256].bitcast(BF16)
    for g in range(4):
        P.op('pe', lambda e, g=g: e.transpose(out=psb[:, g * 128:(g + 1) * 128], in_=src[:, g * 128:(g + 1) * 128], identity=G.identB[:]),
             r=[skey, "identB"], w=[("ps", pbank)])
    P.op('act', lambda e: e.copy(out=dstT[:, :, t * 128:(t + 1) * 128], in_=psb.rearrange("p (g n) -> p g n", g=4)),
         r=[("ps", pbank)], w=[dkey, ("ps", pbank)])


def stage_even(G, l):
    nc, P = G.nc, G.P
    e = l // 2
    lam_init = 0.8 - 0.6 * math.exp(-0.3 * l)
    Win, Wout = G.Wl("ev_w_in", e), G.Wl("ev_w_out", e)
    mo_d = G.mo_d
    with ExitStack() as es:
        mc = load_modcols(G, es, l, "mc")
        hT = sb(nc, es, "e_hT", [128, 8, T], BF16)
        xt = [sb(nc, es, "e_x%d" % i, [128, 1024]) for i in range(2)]
        for t in range(NT):
            b = t % 2
            P.dma(xt[b][:], G.xres[t * 128:(t + 1) * 128, :], r=[("xres", t)], w=[("e_x", b)])
            transpose_modulate(G, xt[b], ("e_x", b), hT, ("e_hT", t), t * 128, mc, 0 if t < 16 else 1, 0, 1, 0)
        cosT = [sb(nc, es, "e_cos%d" % i, [128, 256]) for i in range(2)]
        sinT = [sb(nc, es, "e_sin%d" % i, [128, 256]) for i in range(2)]
        qr = [sb(nc, es, "e_qr%d" % i, [128, 512], BF16) for i in range(2)]
        rtmp = [sb(nc, es, "e_rtmp%d" % i, [128, 512]) for i in range(2)]

        def proj_tile(wb, wkey, t, pbank):
            ps = G.ps[pbank]
            for c in range(8):
                P.op('pe', lambda e_, c=c: e_.matmul(ps[:, :], lhsT=hT[:, c, t * 128:(t + 1) * 128], rhs=wb[:, c, :],
                                                    start=(c == 0), stop=(c == 7)), r=[("e_hT", t), wkey], w=[("ps", pbank)])
            return ps

        def load_tables(t, b, which):
            if t < 16:
                P.dma(cosT[b][:], G.KC("k_cos" + which, [TL, 256])[t * 128:(t + 1) * 128, :], w=[("e_cs", b)])
                P.dma(sinT[b][:], G.KC("k_sin" + which, [TL, 256])[t * 128:(t + 1) * 128, :], w=[("e_cs", b)])

        with ExitStack() as esA:
            aqT = sb(nc, esA, "a_qT", [128, 4, T], BF16)
            akT = sb(nc, esA, "a_kT", [128, 4, T], BF16)
            av = sb(nc, esA, "a_v", [128, NT, 512], BF16)
            for j, dst in ((0, aqT), (1, akT)):
                with ExitStack() as esw:
                    wb = inproj_block(G, esw, Win, j * 512, 512, hT, "e_hT", "a_w%d" % j)
                    for t in range(NT):
                        b = t % 2
                        load_tables(t, b, "A")
                        ps = proj_tile(wb, "a_w%d_b" % j, t, 4 + b)
                        rope_evac(G, ps, ("ps", 4 + b), t, qr[b], ("e_qr", b), cosT[b], sinT[b], ("e_cs", b), rtmp[b], ("e_rtmp", b), 32)
                        transpose4(G, qr[b], ("e_qr", b), dst, ("a_T%d" % j, t), t, 6 + b)
                    P.barrier()
            with ExitStack() as esw:
                wb = inproj_block(G, esw, Win, 2 * 512, 512, hT, "e_hT", "a_w2")
                for t in range(NT):
                    b = t % 2
                    ps = proj_tile(wb, "a_w2_b", t, 4 + b)
                    P.op('act', lambda e_, t=t, ps=ps: e_.copy(out=av[:, t, :], in_=ps[:, :]), r=[("ps", 4 + b)], w=[("a_v", t), ("ps", 4 + b)])
                P.barrier()
            lam4 = sb(nc, esA, "a_lam4", [128, 4, 64])
            lamc = sb(nc, esA, "a_lamc", [128, 8])
            for i, nm in enumerate(("da_lam_q1", "da_lam_k1", "da_lam_q2", "da_lam_k2")):
                P.dma(lam4[:, i, :], row(G.Wl(nm, e)).partition_broadcast(128), w=["a_lam4"])
            for i in range(2):
                P.op('dve', lambda e_, i=i: e_.tensor_tensor(out=lam4[:, 2 * i, :], in0=lam4[:, 2 * i, :], in1=lam4[:, 2 * i + 1, :], op=ALU.mult),
                     r=["a_lam4"], w=["a_lam4"])
                P.op('dve', lambda e_, i=i: e_.reduce_sum(out=lamc[:, i:i + 1], in_=lam4[:, 2 * i, :], axis=AX.X), r=["a_lam4"], w=["a_lamc"])
            P.op('act', lambda e_: e_.activation(out=lamc[:, 2:4], in_=lamc[:, 0:2], func=AF.Exp), r=["a_lamc"], w=["a_lamc"])
            P.op('dve', lambda e_: e_.tensor_tensor(out=lamc[:, 4:5], in0=lamc[:, 3:4], in1=lamc[:, 2:3], op=ALU.subtract), r=["a_lamc"], w=["a_lamc"])
            P.op('dve', lambda e_: e_.tensor_scalar_add(out=lamc[:, 5:6], in0=lamc[:, 4:5], scalar1=-lam_init), r=["a_lamc"], w=["a_lamc"])
            neglam = lamc[:, 5:6]
            Pm = [sb(nc, esA, "a_P%d" % i, [128, T], BF16) for i in range(2)]
            PT = [sb(nc, esA, "a_PT%d" % i, [128, NT, 128], BF16) for i in range(2)]
            sm = [sb(nc, esA, "a_sm%d" % i, [128, 16]) for i in range(2)]
            ao = sb(nc, esA, "a_ao", [128, 512])
            ao2 = sb(nc, esA, "a_ao2", [128, 512])
            aob = [sb(nc, esA, "a_aob%d" % i, [128, 512], BF16) for i in range(2)]
            rs = sb(nc, esA, "a_rs", [128, 16])
            SC = 64 ** -0.5
            def kinfo(qt):
                ktiles = list(range(NT)) if qt < 16 else [16, 17]
                k0 = ktiles[0] * 128
                nk = len(ktiles) * 128
                return ktiles, k0, nk, (nk + 511) // 512

            def stage1(ui, qt, h, m):
                ktiles, k0, nk, nbk = kinfo(qt)
                pb_ = ui % 2
                smt, Pmt = sm[pb_], Pm[pb_]
                psl = slice(m * 64, (m + 1) * 64)
                for jb in range(nbk):
                    w_ = min(512, nk - jb * 512)
                    P.op('pe', lambda e_, jb=jb, w_=w_: e_.matmul(G.ps[jb][:, 0:w_], lhsT=aqT[psl, h, qt * 128:(qt + 1) * 128],
                                                                 rhs=akT[psl, h, k0 + jb * 512:k0 + jb * 512 + w_], start=True, stop=True),
                         r=[("a_T0", qt), "a_T1"], w=[("ps", jb)])
                for jb in range(nbk):
                    w_ = min(512, nk - jb * 512)
                    P.op('dve', lambda e_, jb=jb, w_=w_: e_.reduce_max(out=smt[:, jb:jb + 1], in_=G.ps[jb][:, 0:w_], axis=AX.X),
                         r=[("ps", jb)], w=[("a_sm", pb_), ("ps", jb)])
                P.op('dve', lambda e_: e_.reduce_max(out=smt[:, 6:7], in_=smt[:, 0:nbk], axis=AX.X), r=[("a_sm", pb_)], w=[("a_sm", pb_)])
                P.op('dve', lambda e_: e_.tensor_scalar_mul(out=smt[:, 7:8], in0=smt[:, 6:7], scalar1=-SC), r=[("a_sm", pb_)], w=[("a_sm", pb_)])
                for jb in range(nbk):
                    w_ = min(512, nk - jb * 512)
                    P.op('act', lambda e_, jb=jb, w_=w_: e_.activation(out=Pmt[:, jb * 512:jb * 512 + w_], in_=G.ps[jb][:, 0:w_], func=AF.Exp,
                                                                      scale=SC, bias=smt[:, 7:8], accum_out=smt[:, 8 + jb:9 + jb]),
                         r=[("ps", jb), ("a_sm", pb_)], w=[("a_P", pb_), ("a_sm", pb_), ("ps", jb)], noembed=True)
                P.op('act', lambda e_: e_.copy(out=smt[:, 0:nbk], in_=smt[:, 8:8 + nbk]), r=[("a_sm", pb_)], w=[("a_sm", pb_)])
                P.op('dve', lambda e_: e_.reduce_sum(out=smt[:, 14:15], in_=smt[:, 0:nbk], axis=AX.X), r=[("a_sm", pb_)], w=[("a_sm", pb_)])
                P.op('dve', lambda e_: e_.reciprocal(out=smt[:, 15:16], in_=smt[:, 14:15]), r=[("a_sm", pb_)], w=[("a_sm", pb_)])

            def stage2(ui, qt, h, m):
                ktiles, k0, nk, nbk = kinfo(qt)
                pb_ = ui % 2
                smt, Pmt, PTt = sm[pb_], Pm[pb_], PT[pb_]
                nkt = len(ktiles)
                for g0 in range(0, nkt, 8):
                    gb = 5 + (g0 // 8) % 2
                    psb = G.ps[gb][:, :].bitcast(BF16)
                    n_ = min(8, nkt - g0)
                    for i in range(n_):
                        P.op('pe', lambda e_, i=i, g0=g0, psb=psb: e_.transpose(out=psb[:, i * 128:(i + 1) * 128],
                                                                             in_=Pmt[:, (g0 + i) * 128:(g0 + i + 1) * 128], identity=G.identB[:]),
                             r=[("a_P", pb_), "identB"], w=[("ps", gb)])
                    evac_eng = 'dve' if (g0 // 8) % 2 == 0 else 'act'
                    if evac_eng == 'dve':
                        P.op('dve', lambda e_, g0=g0, n_=n_, psb=psb: e_.tensor_copy(out=PTt[:, g0:g0 + n_, :],
                                                                                     in_=psb[:, 0:n_ * 128].rearrange("p (g n) -> p g n", g=n_)),
                             r=[("ps", gb)], w=[("a_PT", pb_), ("ps", gb)])
                    else:
                        P.op('act', lambda e_, g0=g0, n_=n_, psb=psb: e_.copy(out=PTt[:, g0:g0 + n_, :],
                                                                              in_=psb[:, 0:n_ * 128].rearrange("p (g n) -> p g n", g=n_)),
                             r=[("ps", gb)], w=[("a_PT", pb_), ("ps", gb)])
                osl = slice(m * 128, (m + 1) * 128)
                for i, kt in enumerate(ktiles):
                    P.op('pe', lambda e_, i=i, kt=kt: e_.matmul(G.ps[7][:, osl], lhsT=PTt[:, i, :], rhs=av[:, kt, h * 128:(h + 1) * 128],
                                                               start=(i == 0), stop=(i == nkt - 1)),
                         r=[("a_PT", pb_), "a_v"], w=[("ps", 7)])
                if m == 0:
                    P.op('dve', lambda e_: e_.tensor_scalar(out=ao2[:, 0:128], in0=G.ps[7][:, 0:128], scalar1=smt[:, 15:16], scalar2=None, op0=ALU.mult),
                         r=[("ps", 7), ("a_sm", pb_)], w=["a_ao2", ("ps", 7)])
                else:
                    P.op('dve', lambda e_: e_.tensor_tensor(out=smt[:, 13:14], in0=smt[:, 15:16], in1=neglam, op=ALU.mult),
                         r=[("a_sm", pb_), "a_lamc"], w=[("a_sm", pb_)])
                    P.op('dve', lambda e_: e_.scalar_tensor_tensor(out=ao[:, h * 128:(h + 1) * 128], in0=G.ps[7][:, 128:256], scalar=smt[:, 13:14],
                                                                  in1=ao2[:, 0:128], op0=ALU.mult, op1=ALU.add),
                         r=[("ps", 7), ("a_sm", pb_), "a_ao2"], w=["a_ao", ("ps", 7)])
                if h == 3 and m == 1:
                    P.op('pool', lambda e_: e_.tensor_tensor(out=ao2[:, :], in0=ao[:, :], in1=ao[:, :], op=ALU.mult), r=["a_ao"], w=["a_ao2"])
                    P.op('dve', lambda e_: e_.reduce_sum(out=rs[:, 0:4], in_=ao2[:, :].rearrange("p (h d) -> p h d", h=4), axis=AX.X), r=["a_ao2"], w=["a_rs"])
                    P.op('act', lambda e_: e_.activation(out=rs[:, 4:8], in_=rs[:, 0:4], func=AF.Sqrt, scale=1.0 / 128, bias=G.eps_ln[:, 0:1]), r=["a_rs", "consts"], w=["a_rs"])
                    P.op('dve', lambda e_: e_.reciprocal(out=rs[:, 8:12], in_=rs[:, 4:8]), r=["a_rs"], w=["a_rs"])
                    ab = aob[qt % 2]
                    for hh in range(4):
                        P.op('pool', lambda e_, hh=hh: e_.tensor_scalar(out=ab[:, hh * 128:(hh + 1) * 128], in0=ao[:, hh * 128:(hh + 1) * 128],
                                                                      scalar1=rs[:, 8 + hh:9 + hh], scalar2=None, op0=ALU.mult),
                             r=["a_ao", "a_rs"], w=[("a_aob", qt % 2)])
                    P.dma(mo_d[qt * 128:(qt + 1) * 128, 0:512], ab[:, :], r=[("a_aob", qt % 2)], w=[("mo", (qt, 0))])

            units = [(qt, h, m) for qt in range(NT) for h in range(4) for m in range(2)]
            for ui, u_ in enumerate(units):
                stage1(ui, *u_)
                if ui > 0:
                    stage2(ui - 1, *units[ui - 1])
            stage2(len(units) - 1, *units[-1])
            P.barrier()
        with ExitStack() as esB:
            bqT = sb(nc, esB, "b_qT", [128, 4, T], BF16)
            bkT = sb(nc, esB, "b_kT", [128, 4, T], BF16)
            bk = sb(nc, esB, "b_k", [128, NT, 512], BF16)
            bv = sb(nc, esB, "b_v", [128, NT, 512], BF16)
            for j in (3, 4):
                with ExitStack() as esw:
                    wb = inproj_block(G, esw, Win, j * 512, 512, hT, "e_hT", "b_w%d" % j)
                    for t in range(NT):
                        b = t % 2
                        load_tables(t, b, "B")
                        ps = proj_tile(wb, "b_w%d_b" % j, t, 4 + b)
                        if j == 3:
                            rope_evac(G, ps, ("ps", 4 + b), t, qr[b], ("e_qr", b), cosT[b], sinT[b], ("e_cs", b), rtmp[b], ("e_rtmp", b), 64)
                            transpose4(G, qr[b], ("e_qr", b), bqT, ("b_qT", t), t, 6 + b)
                        else:
                            rope_evac(G, ps, ("ps", 4 + b), t, bk[:, t, :], ("b_k", t), cosT[b], sinT[b], ("e_cs", b), rtmp[b], ("e_rtmp", b), 64,
                                      scale=128 ** -0.5)
                            transpose4(G, bk[:, t, :], ("b_k", t), bkT, ("b_kT", t), t, 6 + b)
                    P.barrier()
            with ExitStack() as esw:
                wb = inproj_block(G, esw, Win, 5 * 512, 512, hT, "e_hT", "b_w5")
                for t in range(NT):
                    b = t % 2
                    ps = proj_tile(wb, "b_w5_b", t, 4 + b)
                    P.op('act', lambda e_, t=t, ps=ps: e_.copy(out=bv[:, t, :], in_=ps[:, :]), r=[("ps", 4 + b)], w=[("b_v", t), ("ps", 4 + b)])
                P.barrier()
            dc = sb(nc, esB, "b_dc", [128, 64])
            kcol = sb(nc, esB, "b_kcol", [128, 4])
            kmat = sb(nc, esB, "b_kmat", [128, 4, 128])
            Mm = sb(nc, esB, "b_M", [128, 4, 128])
            Mt = sb(nc, esB, "b_Mt", [128, 128])
            P.dma(dc[:, 0:8], G.Wl("rt_decay_logit", e).rearrange("(o a) b -> o (a b)", o=1).partition_broadcast(128), w=["b_dc"])
            P.dma(kcol[:], G.KC("k_cols", [128, 4])[:, :], w=["b_kc"])
            P.dma(kmat[:], G.KC("k_mats", [128, 4, 128])[:, :, :], w=["b_kc"])
            P.op('act', lambda e_: e_.activation(out=dc[:, 8:16], in_=dc[:, 0:8], func=AF.Sigmoid), r=["b_dc"], w=["b_dc"])
            P.op('act', lambda e_: e_.activation(out=dc[:, 16:24], in_=dc[:, 8:16], func=AF.Ln), r=["b_dc"], w=["b_dc"])
            lg = lambda d, h: dc[:, 16 + d * 4 + h:17 + d * 4 + h]
            P.op('act', lambda e_: e_.activation(out=dc[:, 24:32], in_=dc[:, 16:24], func=AF.Exp, scale=128.0), r=["b_dc"], w=["b_dc"])
            for h in range(4):
                for (o_, col, d) in ((32, 0, 0), (36, 1, 1), (40, 2, 0), (44, 3, 1)):
                    P.op('act', lambda e_, o_=o_, col=col, d=d, h=h: e_.activation(out=dc[:, o_ + h:o_ + h + 1], in_=kcol[:, col:col + 1], func=AF.Exp, scale=lg(d, h)),
                         r=["b_dc", "b_kc"], w=["b_dc"])
                P.op('act', lambda e_, h=h: e_.activation(out=Mt[:, :], in_=kmat[:, 0, :], func=AF.Exp, scale=lg(0, h)), r=["b_dc", "b_kc"], w=["b_Mt"])
                P.op('dve', lambda e_, h=h: e_.tensor_tensor(out=Mm[:, h, :], in0=Mt[:, :], in1=kmat[:, 2, :], op=ALU.mult), r=["b_Mt", "b_kc"], w=["b_M"])
                P.op('act', lambda e_, h=h: e_.activation(out=Mt[:, :], in_=kmat[:, 1, :], func=AF.Exp, scale=lg(1, h)), r=["b_dc", "b_kc"], w=["b_Mt"])
                P.op('dve', lambda e_, h=h: e_.tensor_tensor(out=Mt[:, :], in0=Mt[:, :], in1=kmat[:, 3, :], op=ALU.mult), r=["b_Mt", "b_kc"], w=["b_Mt"])
                P.op('dve', lambda e_, h=h: e_.tensor_tensor(out=Mm[:, h, :], in0=Mm[:, h, :], in1=Mt[:, :], op=ALU.add), r=["b_Mt", "b_M"], w=["b_M"])
            SfA = sb(nc, esB, "b_SfA", [128, NT, 512], BF16)
            Sst = sb(nc, esB, "b_S", [128, 512])
            Sbb = sb(nc, esB, "b_Sbb", [128, 512], BF16)
            kz = [sb(nc, esB, "b_kz%d" % i, [128, 512], BF16) for i in range(2)]

            def state_update(n, d, i):
                zb = kz[i % 2]
                for h in range(4):
                    P.op('dve', lambda e_, h=h: e_.tensor_scalar(out=zb[:, h * 128:(h + 1) * 128], in0=bk[:, n, h * 128:(h + 1) * 128],
                                                                scalar1=dc[:, 32 + 4 * d + h:33 + 4 * d + h], scalar2=None, op0=ALU.mult),
                         r=[("b_k", n), "b_dc"], w=[("b_kz", i % 2)])
                for h in range(4):
                    P.op('pe', lambda e_, h=h: e_.matmul(G.ps[3][:, h * 128:(h + 1) * 128], lhsT=zb[:, h * 128:(h + 1) * 128],
                                                        rhs=bv[:, n, h * 128:(h + 1) * 128], start=True, stop=True),
                         r=[("b_kz", i % 2), ("b_v", n)], w=[("ps", 3)])
                for h in range(4):
                    P.op('dve', lambda e_, h=h: e_.scalar_tensor_tensor(out=Sst[:, h * 128:(h + 1) * 128], in0=Sst[:, h * 128:(h + 1) * 128],
                                                                       scalar=dc[:, 24 + 4 * d + h:25 + 4 * d + h], in1=G.ps[3][:, h * 128:(h + 1) * 128],
                                                                       op0=ALU.mult, op1=ALU.add),
                         r=["b_S", "b_dc", ("ps", 3)], w=["b_S", ("ps", 3)])
            P.op('dve', lambda e_: e_.memset(Sst[:, :], 0.0), w=["b_S"])
            fwd = [16, 17] + list(range(16))
            for i, n in enumerate(fwd):
                P.op('act', lambda e_, n=n: e_.copy(out=SfA[:, n, :], in_=Sst[:, :]), r=["b_S"], w=[("b_SfA", n)])
                if i < len(fwd) - 1:
                    state_update(n, 0, i)
            P.op('dve', lambda e_: e_.memset(Sst[:, :], 0.0), r=["b_SfA"], w=["b_S"])
            with ExitStack() as esw:
                wg = inproj_block(G, esw, Win, 6 * 512, 512, hT, "e_hT", "b_w6")
                Sm = [sb(nc, esw, "b_Sm%d" % i, [128, 4, 128], BF16) for i in range(2)]
                bo = [sb(nc, esw, "b_o%d" % i, [128, 512]) for i in range(2)]
                bo2 = [sb(nc, esw, "b_o2%d" % i, [128, 512]) for i in range(2)]
                gs = [sb(nc, esw, "b_gs%d" % i, [128, 512]) for i in range(2)]
                bob = [sb(nc, esw, "b_ob%d" % i, [128, 512], BF16) for i in range(2)]
                brs = [sb(nc, esw, "b_rs%d" % i, [128, 16]) for i in range(2)]
                bwd = [17, 16] + list(range(15, -1, -1))
                for i, n in enumerate(bwd):
                    b = i % 2
                    csl = slice(n * 128, (n + 1) * 128)
                    P.op('act', lambda e_: e_.copy(out=Sbb[:, :], in_=Sst[:, :]), r=["b_S"], w=["b_Sbb"])
                    for h in range(4):
                        P.op('pe', lambda e_, h=h: e_.matmul(G.ps[0][:, h * 128:(h + 1) * 128], lhsT=bkT[:, h, csl], rhs=bqT[:, h, csl], start=True, stop=True),
                             r=[("b_kT", n), ("b_qT", n)], w=[("ps", 0)])
                    P.op('dve', lambda e_, b=b: e_.tensor_tensor(out=Sm[b][:, :, :], in0=G.ps[0][:, :].rearrange("p (h n) -> p h n", h=4), in1=Mm[:, :, :], op=ALU.mult),
                         r=[("ps", 0), "b_M"], w=[("b_Sm", b), ("ps", 0)])
                    for h in range(4):
                        hs = slice(h * 128, (h + 1) * 128)
                        P.op('pe', lambda e_, h=h, hs=hs, b=b: e_.matmul(G.ps[1][:, hs], lhsT=Sm[b][:, h, :], rhs=bv[:, n, hs], start=True, stop=True),
                             r=[("b_Sm", b), ("b_v", n)], w=[("ps", 1)])
                    for h in range(4):
                        hs = slice(h * 128, (h + 1) * 128)
                        P.op('pe', lambda e_, h=h, hs=hs: e_.matmul(G.ps[2][:, hs], lhsT=bqT[:, h, csl], rhs=SfA[:, n, hs], start=True, stop=True),
                             r=[("b_qT", n), ("b_SfA", n)], w=[("ps", 2)])
                    for h in range(4):
                        hs = slice(h * 128, (h + 1) * 128)
                        P.op('pe', lambda e_, h=h, hs=hs: e_.matmul(G.ps[4][:, hs], lhsT=bqT[:, h, csl], rhs=Sbb[:, hs], start=True, stop=True),
                             r=[("b_qT", n), "b_Sbb"], w=[("ps", 4)])
                    for c in range(8):
                        P.op('pe', lambda e_, c=c: e_.matmul(G.ps[5][:, :], lhsT=hT[:, c, csl], rhs=wg[:, c, :], start=(c == 0), stop=(c == 7)),
                             r=[("e_hT", n), "b_w6_b"], w=[("ps", 5)])
                    P.op('act', lambda e_, b=b: e_.activation(out=gs[b][:, :], in_=G.ps[5][:, :], func=AF.Silu), r=[("ps", 5)], w=[("b_gs", b), ("ps", 5)])
                    for h in range(4):
                        hs = slice(h * 128, (h + 1) * 128)
                        P.op('dve', lambda e_, h=h, hs=hs, b=b: e_.tensor_scalar(out=bo2[b][:, hs], in0=G.ps[2][:, hs], scalar1=dc[:, 40 + h:41 + h], scalar2=None, op0=ALU.mult),
                             r=[("ps", 2), "b_dc"], w=[("b_o2", b), ("ps", 2)])
                        P.op('dve', lambda e_, h=h, hs=hs, b=b: e_.scalar_tensor_tensor(out=bo2[b][:, hs], in0=G.ps[4][:, hs], scalar=dc[:, 44 + h:45 + h], in1=bo2[b][:, hs],
                                                                                       op0=ALU.mult, op1=ALU.add),
                             r=[("ps", 4), "b_dc", ("b_o2", b)], w=[("b_o2", b), ("ps", 4)])
                    P.op('dve', lambda e_, b=b: e_.tensor_tensor(out=bo[b][:, :], in0=G.ps[1][:, :], in1=bo2[b][:, :], op=ALU.add),
                         r=[("ps", 1), ("b_o2", b)], w=[("b_o", b), ("ps", 1)])
                    P.op('dve', lambda e_, b=b: e_.tensor_tensor(out=bo2[b][:, :], in0=bo[b][:, :], in1=bo[b][:, :], op=ALU.mult), r=[("b_o", b)], w=[("b_o2", b)])
                    P.op('dve', lambda e_, b=b: e_.reduce_sum(out=brs[b][:, 0:4], in_=bo2[b][:, :].rearrange("p (h d) -> p h d", h=4), axis=AX.X), r=[("b_o2", b)], w=[("b_rs", b)])
                    P.op('act', lambda e_, b=b: e_.activation(out=brs[b][:, 4:8], in_=brs[b][:, 0:4], func=AF.Sqrt, scale=1.0 / 128, bias=G.eps_ln[:, 0:1]), r=[("b_rs", b), "consts"], w=[("b_rs", b)])
                    P.op('dve', lambda e_, b=b: e_.reciprocal(out=brs[b][:, 8:12], in_=brs[b][:, 4:8]), r=[("b_rs", b)], w=[("b_rs", b)])
                    for h in range(4):
                        hs = slice(h * 128, (h + 1) * 128)
                        P.op('dve', lambda e_, h=h, hs=hs, b=b: e_.scalar_tensor_tensor(out=bob[b][:, hs], in0=bo[b][:, hs], scalar=brs[b][:, 8 + h:9 + h], in1=gs[b][:, hs],
                                                                                       op0=ALU.mult, op1=ALU.mult),
                             r=[("b_o", b), ("b_rs", b), ("b_gs", b)], w=[("b_ob", b)])
                    P.dma(mo_d[n * 128:(n + 1) * 128, 512:1024], bob[b][:, :], r=[("b_ob", b)], w=[("mo", (n, 1))])
                    if i < len(bwd) - 1:
                        state_update(n, 1, i)
                P.barrier()
            P.barrier()
        with ExitStack() as esO:
            wos = sb(nc, esO, "o_ws", [128, 8, 1024])
            wob = sb(nc, esO, "o_wb", [128, 8, 1024], BF16)
            gn = sb(nc, esO, "o_gn", [128, 4])
            P.dma(wos[:], Wout[:, :].rearrange("(c p) n -> p c n", p=128), w=["o_ws"])
            colload(G, esO, gn[:, :], "o_gn", G.Wl("da_gn_g", e), 4)
            P.op('dve', lambda e_: e_.tensor_scalar_mul(out=gn[:], in0=gn[:], scalar1=1.0 - lam_init), r=["o_gn"], w=["o_gn"])
            for c in range(8):
                if c < 4:
                    P.op('dve', lambda e_, c=c: e_.tensor_scalar(out=wob[:, c, :], in0=wos[:, c, :], scalar1=gn[:, c:c + 1], scalar2=None, op0=ALU.mult),
                         r=["o_ws", "o_gn"], w=[("o_wb", c)])
                else:
                    P.op('pool', lambda e_, c=c: e_.tensor_copy(out=wob[:, c, :], in_=wos[:, c, :]), r=["o_ws"], w=[("o_wb", c)])
            gate = [sb(nc, esO, "o_gate%d" % r_, [128, 1024]) for r_ in range(2)]
            LG = sb(nc, esO, "o_lg", [128, 1024])
            LB = sb(nc, esO, "o_lb", [128, 1024])
            for r_ in range(2):
                P.dma(gate[r_][:], G.modv[l, r_:r_ + 1, 2 * 1024:3 * 1024].partition_broadcast(128), r=[("modv", l)], w=["o_bc"])
            P.dma(LG[:], row(G.Wl("ln1_g", l)).partition_broadcast(128), w=["o_bc"])
            P.dma(LB[:], row(G.Wl("ln1_b", l)).partition_broadcast(128), w=["o_bc"])
            mot = [sb(nc, esO, "o_mo%d" % i, [128, 1024], BF16) for i in range(2)]
            moT = [sb(nc, esO, "o_moT%d" % i, [128, 8, 128], BF16) for i in range(2)]
            tmp = [sb(nc, esO, "o_tmp%d" % i, [128, 1024]) for i in range(2)]
            yt = [sb(nc, esO, "o_y%d" % i, [128, 1024]) for i in range(2)]
            st = [sb(nc, esO, "o_st%d" % i, [128, 16]) for i in range(2)]
            def ld_o(t_):
                b_ = t_ % 2
                P.dma(mot[b_][:], mo_d[t_ * 128:(t_ + 1) * 128, :], r=[("mo", (t_, 0)), ("mo", (t_, 1))], w=[("o_mo", b_)])
                P.dma(xt[b_][:], G.xres[t_ * 128:(t_ + 1) * 128, :], r=[("xres", t_)], w=[("e_x", b_)])
            for t in range(NT):
                b = t % 2
                r_ = 0 if t < 16 else 1
                if t == 0:
                    ld_o(0)
                if t + 1 < NT:
                    ld_o(t + 1)
                for hh in range(2):
                    pbk = 4 + hh
                    psb = G.ps[pbk][:, 0:256].bitcast(BF16)
                    for g in range(4):
                        c = hh * 4 + g
                        P.op('pe', lambda e_, g=g, c=c, psb=psb: e_.transpose(out=psb[:, g * 128:(g + 1) * 128], in_=mot[b][:, c * 128:(c + 1) * 128], identity=G.identB[:]),
                             r=[("o_mo", b), "identB"], w=[("ps", pbk)])
                    P.op('act', lambda e_, hh=hh, psb=psb: e_.copy(out=moT[b][:, hh * 4:hh * 4 + 4, :], in_=psb.rearrange("p (g n) -> p g n", g=4)),
                         r=[("ps", pbk)], w=[("o_moT", b), ("ps", pbk)])
                banks = [G.ps[0 + 2 * b], G.ps[1 + 2 * b]]
                okeys = [("ps", 0 + 2 * b), ("ps", 1 + 2 * b)]
                for hh in range(2):
                    for c in range(8):
                        P.op('pe', lambda e_, hh=hh, c=c: e_.matmul(banks[hh][:, :], lhsT=moT[b][:, c, :], rhs=wob[:, c, hh * 512:(hh + 1) * 512],
                                                                   start=(c == 0), stop=(c == 7)), r=[("o_moT", b), "o_wb"], w=[okeys[hh]])
                ln_epilogue(G, banks, okeys, xt[b], ("e_x", b), gate[r_], LG, LB, ["o_bc"], tmp[b], ("o_tmp", b), st[b], ("o_st", b), yt[b], ("o_y", b))
                P.dma(G.xres[t * 128:(t + 1) * 128, :], yt[b][:], r=[("o_y", b)], w=[("xres", t)])
            P.barrier()
        P.barrier()


TBLK = [(0, 512), (512, 512), (1024, 512), (1536, 512), (2048, 256)]
RW_GN_EPS = 64e-5


def colvec(G, es, name, ap1024, key):
    t = sb(G.nc, es, name, [128, 8])
    colload(G, es, t[:, :], key, ap1024, 8)
    return t


def load_w_bf16(G, es, name, wap, rows, cols, eng='pool'):
    nc, P = G.nc, G.P
    nck = (rows + 127) // 128
    wb = sb(nc, es, name + "_b", [128, nck, cols], BF16)
    if rows % 128 == 0 and cols > 512:
        with ExitStack() as es2:
            ws = sb(nc, es2, name + "_s", [128, nck, 512])
            for h0 in range(0, cols, 512):
                P.dma(ws[:], wap[:, h0:h0 + 512].rearrange("(c p) n -> p c n", p=128), w=[name + "_s"])
                P.op(eng, lambda e, h0=h0: e.tensor_copy(out=wb[:, :, h0:h0 + 512], in_=ws[:]), r=[name + "_s"], w=[name + "_b"])
            P.barrier()
        return wb
    ws = sb(nc, es, name + "_s", [128, nck, cols])
    if rows % 128 == 0:
        P.dma(ws[:], wap.rearrange("(c p) n -> p c n", p=128), w=[name + "_s"])
        P.op(eng, lambda e: e.tensor_copy(out=wb[:], in_=ws[:]), r=[name + "_s"], w=[name + "_b"])
    else:
        for c in range(nck):
            n_ = min(128, rows - c * 128)
            P.dma(ws[0:n_, c, :], wap[c * 128:c * 128 + n_, :], w=[name + "_s"])
        for c in range(nck):
            n_ = min(128, rows - c * 128)
            P.op(eng, lambda e, c=c, n_=n_: e.tensor_copy(out=wb[0:n_, c, :], in_=ws[0:n_, c, :]), r=[name + "_s"], w=[name + "_b"])
    return wb


def fm_linear(G, xT, xkey, wb, wkey, nout, consumer, kparts=None, pbanks=(0, 1)):
    P = G.P
    if kparts is None:
        kparts = [(c, 128) for c in range(8)]
    it = 0
    for oc in range((nout + 127) // 128):
        m_ = min(128, nout - oc * 128)
        for (t0, tn) in TBLK:
            pb = pbanks[it % len(pbanks)]
            it += 1
            for i, (c, kn) in enumerate(kparts):
                P.op('pe', lambda e, c=c, kn=kn, i=i, pb=pb: e.matmul(G.ps[pb][0:m_, 0:tn], lhsT=wb[0:kn, c, oc * 128:oc * 128 + m_], rhs=xT[0:kn, c, t0:t0 + tn],
                                                                   start=(i == 0), stop=(i == len(kparts) - 1)), r=[xkey, wkey], w=[("ps", pb)])
            consumer(oc, t0, tn, G.ps[pb], ("ps", pb), m_)


F32R = mybir.dt.float32r
NCHAIN = 4
SKIP_SCAN = False
SCAN_F32R = False
STAGGER = 6


def rr(ap):
    return ap.bitcast(F32R) if SCAN_F32R else ap


def scan_stage(G):
    nc, P = G.nc, G.P
    FM = G.fm
    C = 64
    NCH = T // C
    with ExitStack() as esS:
        msk = sb(nc, esS, "s_msk", [128, 3, 128])
        P.dma(msk[:], G.KC("k_smask", [128, 3, 128])[:, :, :], w=["s_msk"])
        ones = sb(nc, esS, "s_ones", [128, 64])
        P.op('dve', lambda e: e.memset(ones[:], 1.0), w=["s_ones"])
        identR = sb(nc, esS, "s_identR", [128, 128])
        P.op('dve', lambda e: e.tensor_copy(out=rr(identR[:]), in_=G.identF[:]), r=["ident"], w=["s_identR"])
        NAMES = ("r", "v", "kk", "lw", "kd", "b")
        bufs = []
        for ci in range(NCHAIN):
            B = Ctx()
            B.src = [sb(nc, esS, "s_src%d_%d" % (ci, i), [128, 6, 256]) for i in range(2)]
            B.bd = sb(nc, esS, "s_bd%d" % ci, [128, 5, 128])
            P.op('pool', lambda e, B=B: e.memset(B.bd[:], 0.0), w=[("s_bd", ci)])
            B.cum = sb(nc, esS, "s_cum%d" % ci, [128, 4, 64])
            B.Am = sb(nc, esS, "s_Am%d" % ci, [128, 5, 128])
            B.Ap = [sb(nc, esS, "s_Ap%d_%d" % (ci, i), [128, 2, 128]) for i in range(2)]
            B.VT = sb(nc, esS, "s_VT%d" % ci, [128, 64])
            B.Z = sb(nc, esS, "s_Z%d" % ci, [128, 2, 64])
            B.BT = sb(nc, esS, "s_BT%d" % ci, [128, 2, 128])
            B.S = sb(nc, esS, "s_S%d" % ci, [128, 64])
            B.Sr = sb(nc, esS, "s_Sr%d" % ci, [128, 64])
            B.yo = [sb(nc, esS, "s_yo%d_%d" % (ci, i), [64, 2, 64]) for i in range(4)]
            bufs.append(B)

        def chain(ci, p, d):
            B = bufs[ci]
            P0, P1 = G.ps[2 * ci], G.ps[2 * ci + 1]
            k0, k1 = ("ps", 2 * ci), ("ps", 2 * ci + 1)
            K = lambda nm, sub=None: ("s%d_%s" % (ci, nm), sub)
            fmn = {"r": "rT", "v": "vT", "kk": "kk", "lw": "lw%d" % d, "kd": "kd%d" % d, "b": "b%d" % d}
            P.op('dve', lambda e: e.memset(B.S[:], 0.0), w=[K("S")])
            P.op('act', lambda e: e.copy(out=rr(B.Sr[:, :]), in_=B.S[:, :]), r=[K("S")], w=[K("Sr")])
            R_, K_, B_, A_, V_ = (B.bd[:, i, :] for i in range(5))
            for n in range(NCH):
                if n < TC // C:
                    t0 = TL + n * C if d == 0 else T - (n + 1) * C
                else:
                    m_ = n - TC // C
                    t0 = m_ * C if d == 0 else TL - (m_ + 1) * C
                blk0 = (t0 // 256) * 256
                sbi = (n // 4) % 2

                def blk_start(nb_):
                    n_ = 4 * nb_
                    if n_ < TC // C:
                        t_ = TL + n_ * C if d == 0 else T - (n_ + 1) * C
                    else:
                        mm_ = n_ - TC // C
                        t_ = mm_ * C if d == 0 else TL - (mm_ + 1) * C
                    return (t_ // 256) * 256

                def load_blk(nb_):
                    b0 = blk_start(nb_)
                    for i, nm in enumerate(NAMES):
                        P.dma(B.src[nb_ % 2][:, i, :], FM[fmn[nm]][p, :, b0:b0 + 256], r=[("fm_" + fmn[nm], p)], w=[K("src", nb_ % 2)])
                if n % 4 == 0:
                    nb = n // 4
                    if nb == 0:
                        load_blk(0)
                    if nb + 1 < NCH // 4:
                        load_blk(nb + 1)
                off = t0 - blk0
                sk = K("src", sbi)

                def tsl(i):
                    a_ = B.src[sbi][:, i, off:off + C]
                    return a_ if d == 0 else a_[:, ::-1]
                cm = B.cum
                ck = K("cum")
                P.op('dve', lambda e: e.tensor_tensor_scan(out=cm[:, 0, :], data0=ones[:, :], data1=tsl(3), initial=0.0, op0=ALU.mult, op1=ALU.add), r=[sk, "s_ones"], w=[ck])
                P.op('act', lambda e: e.activation(out=cm[:, 1, :], in_=cm[:, 0, :], func=AF.Exp), r=[ck], w=[ck])
                P.op('act', lambda e: e.activation(out=cm[:, 2, :], in_=cm[:, 0, :], func=AF.Exp, scale=-1.0), r=[ck], w=[ck])
                P.op('pool', lambda e: e.tensor_tensor(out=cm[:, 3, :], in0=cm[:, 0, :], in1=tsl(3), op=ALU.subtract), r=[ck, sk], w=[ck])
                P.op('act', lambda e: e.activation(out=cm[:, 3, :], in_=cm[:, 3, :], func=AF.Exp), r=[ck], w=[ck])
                yield
                bk = K("bd")
                for h in range(2):
                    hp = slice(h * 64, (h + 1) * 64)
                    P.op('dve', lambda e, hp=hp: e.tensor_tensor(out=rr(R_[hp, hp]), in0=tsl(0)[hp, :], in1=cm[hp, 1, :], op=ALU.mult), r=[sk, ck], w=[K("bd", 0)])
                    P.op('dve', lambda e, hp=hp: e.tensor_tensor(out=rr(K_[hp, hp]), in0=tsl(4)[hp, :], in1=cm[hp, 2, :], op=ALU.mult), r=[sk, ck], w=[K("bd", 1)])
                    P.op('dve', lambda e, hp=hp: e.tensor_tensor(out=rr(B_[hp, hp]), in0=tsl(5)[hp, :], in1=cm[hp, 2, :], op=ALU.mult), r=[sk, ck], w=[K("bd", 2)])
                    P.op('dve', lambda e, hp=hp: e.scalar_tensor_tensor(out=rr(A_[hp, hp]), in0=tsl(2)[hp, :], scalar=-1.0, in1=cm[hp, 3, :], op0=ALU.mult, op1=ALU.mult),
                         r=[sk, ck], w=[K("bd", 3)])
                    P.op('act', lambda e, hp=hp: e.copy(out=rr(V_[hp, hp]), in_=tsl(1)[hp, :]), r=[sk], w=[K("bd", 4)])
                yield
                A = B.Am
                specs = [(0, 2, 3, 0), (1, 3, 2, 1), (2, 1, 3, 0), (3, 2, 0, 2), (4, 1, 0, 2)]
                for (ai, li, ri, mi) in specs:
                    pt, pk = (P0, k0) if ai < 4 else (P1, k1)
                    sl = slice((ai % 4) * 128, (ai % 4 + 1) * 128)
                    P.op('pe', lambda e, li=li, ri=ri, pt=pt, sl=sl: e.matmul(pt[:, sl], lhsT=rr(B.bd[:, li, :]), rhs=rr(B.bd[:, ri, :]), start=True, stop=True),
                         r=[K("bd", li), K("bd", ri)], w=[pk])
                P.op('pe', lambda e: e.transpose(out=P1[:, 128:256], in_=V_, identity=G.identF[:]), r=[K("bd", 4), "ident"], w=[k1])
                yield
                for (ai, li, ri, mi) in specs:
                    pt, pk = (P0, k0) if ai < 4 else (P1, k1)
                    sl = slice((ai % 4) * 128, (ai % 4 + 1) * 128)
                    P.op('dve', lambda e, ai=ai, pt=pt, sl=sl, mi=mi: e.tensor_tensor(out=rr(A[:, ai, :]), in0=pt[:, sl], in1=msk[:, mi, :], op=ALU.mult),
                         r=[pk, "s_msk"], w=[K("Am", ai), pk])
                for h in range(2):
                    hp = slice(h * 64, (h + 1) * 64)
                    P.op('act', lambda e, hp=hp, h=h: e.copy(out=rr(B.VT[hp, :]), in_=P1[hp, 128 + h * 64:128 + (h + 1) * 64]), r=[k1], w=[K("VT"), k1])
                yield
                zs = slice(256, 320)
                P.op('pe', lambda e: e.matmul(P1[:, zs], lhsT=rr(A_), rhs=rr(B.Sr[:, :]), start=True, stop=False), r=[K("bd", 3), K("Sr")], w=[k1])
                P.op('pe', lambda e: e.matmul(P1[:, zs], lhsT=rr(A[:, 2, :]), rhs=rr(B.VT[:, :]), start=False, stop=True), r=[K("Am", 2), K("VT")], w=[k1])
                yield
                P.op('act', lambda e: e.copy(out=rr(B.Z[:, 0, :]), in_=P1[:, zs]), r=[k1], w=[K("Z", 0), k1])
                yield
                zc = 0
                curA, curAT = (A[:, 0, :], K("Am", 0)), (A[:, 1, :], K("Am", 1))
                for step in range(6):
                    P.op('pe', lambda e, zc=zc, curA=curA: e.matmul(P1[:, zs], lhsT=rr(curA[0]), rhs=rr(B.Z[:, zc, :]), start=True, stop=True), r=[curA[1], K("Z", zc)], w=[k1])
                    if step < 5:
                        dstt = B.Ap[step % 2]
                        dn = "Ap%d" % (step % 2)
                        P.op('pe', lambda e, curA=curA, curAT=curAT: e.matmul(P0[:, 0:128], lhsT=rr(curAT[0]), rhs=rr(curA[0]), start=True, stop=True), r=[curA[1], curAT[1]], w=[k0])
                        if step < 4:
                            P.op('pe', lambda e, curA=curA, curAT=curAT: e.matmul(P0[:, 128:256], lhsT=rr(curA[0]), rhs=rr(curAT[0]), start=True, stop=True), r=[curA[1], curAT[1]], w=[k0])
                    yield
                    P.op('dve', lambda e, zc=zc: e.tensor_tensor(out=rr(B.Z[:, 1 - zc, :]), in0=P1[:, zs], in1=B.Z[:, zc, :], op=ALU.add), r=[k1, K("Z", zc)], w=[K("Z", 1 - zc), k1])
                    zc = 1 - zc
                    if step < 5:
                        if step < 4:
                            P.op('act', lambda e, dstt=dstt: e.copy(out=rr(dstt[:, :, :]), in_=P0[:, 0:256].rearrange("p (a n) -> p a n", a=2)), r=[k0], w=[K(dn), k0])
                        else:
                            P.op('act', lambda e, dstt=dstt: e.copy(out=rr(dstt[:, 0, :]), in_=P0[:, 0:128]), r=[k0], w=[K(dn), k0])
                        curA, curAT = (dstt[:, 0, :], K(dn)), (dstt[:, 1, :], K(dn))
                    yield
                UT = B.Z[:, zc, :]
                uk = K("Z", zc)
                ys = slice(384, 512)
                P.op('pe', lambda e: e.matmul(P1[0:64, ys], lhsT=rr(B.Sr[:, :]), rhs=rr(R_), start=True, stop=False), r=[K("Sr"), K("bd", 0)], w=[k1])
                P.op('pe', lambda e: e.matmul(P1[0:64, ys], lhsT=rr(UT), rhs=rr(A[:, 3, :]), start=False, stop=False), r=[uk, K("Am", 3)], w=[k1])
                P.op('pe', lambda e: e.matmul(P1[0:64, ys], lhsT=rr(B.VT[:, :]), rhs=rr(A[:, 4, :]), start=False, stop=True), r=[K("VT"), K("Am", 4)], w=[k1])
                if n < NCH - 1:
                    P.op('pe', lambda e: e.transpose(out=P0[:, 256:384], in_=B_, identity=G.identF[:]), r=[K("bd", 2), "ident"], w=[k0])
                    P.op('pe', lambda e: e.transpose(out=P0[:, 384:512], in_=K_, identity=G.identF[:]), r=[K("bd", 1), "ident"], w=[k0])
                yield
                yq = B.yo[n % 4]
                yv = P1[0:64, ys].rearrange("p (h t) -> p h t", h=2)
                yov = yq[:, :, :] if d == 0 else yq[:, :, ::-1]
                P.op('act', lambda e: e.copy(out=yov, in_=yv), r=[k1], w=[K("yo", n % 4), k1])
                P.dma(G.y_d[d, :, 2 * p:2 * p + 2, t0:t0 + C], yq[:, :, :], r=[K("yo", n % 4)], w=[("y_d", (d, p, t0 // 128))])
                if n < NCH - 1:
                    P.op('dve', lambda e: e.tensor_copy(out=rr(B.BT[:, :, :]), in_=P0[:, 256:512].rearrange("p (a n) -> p a n", a=2)), r=[k0], w=[K("BT"), k0])
                    yield
                    P.op('pe', lambda e: e.matmul(P1[:, 0:64], lhsT=rr(B.BT[:, 0, :]), rhs=rr(UT), start=True, stop=False), r=[K("BT"), uk], w=[k1])
                    P.op('pe', lambda e: e.matmul(P1[:, 0:64], lhsT=rr(B.BT[:, 1, :]), rhs=rr(B.VT[:, :]), start=False, stop=True), r=[K("BT"), K("VT")], w=[k1])
                    yield
                    P.op('dve', lambda e: e.tensor_tensor(out=B.S[:, :], in0=B.S[:, :], in1=P1[:, 0:64], op=ALU.add), r=[K("S"), k1], w=[K("S"), k1])
                    P.op('dve', lambda e: e.tensor_scalar(out=B.S[:, :], in0=B.S[:, :], scalar1=cm[:, 1, 63:64], scalar2=None, op0=ALU.mult), r=[K("S"), ck], w=[K("S")])
                    P.op('act', lambda e: e.copy(out=rr(B.Sr[:, :]), in_=B.S[:, :]), r=[K("S")], w=[K("Sr")])
                yield

        todo = [(p, d) for p in range(8) for d in range(2)]
        active = [None] * NCHAIN
        for ci in range(NCHAIN):
            p, d = todo.pop(0)
            active[ci] = chain(ci, p, d)
            for _ in range(ci * STAGGER):
                next(active[ci])
        while todo or any(a is not None for a in active):
            for ci in range(NCHAIN):
                if active[ci] is None and todo:
                    p, d = todo.pop(0)
                    active[ci] = chain(ci, p, d)
                if active[ci] is not None:
                    try:
                        next(active[ci])
                    except StopIteration:
                        active[ci] = None
        P.barrier()


def stage_rwkv(G, l):
    nc, P = G.nc, G.P
    j = l // 2
    FM = G.fm
    with ExitStack() as es:
        mc = load_modcols(G, es, l, "mc")
        BO = sb(nc, es, "r_BO", [128, 128])
        P.dma(BO[:], G.KC("k_bo", [128, 128])[:, :], w=["r_BO"])
        with ExitStack() as esP:
            hT = sb(nc, esP, "r_hT", [128, 8, T], BF16)
            dT = sb(nc, esP, "r_dT", [128, 8, T], BF16)
            xiT = sb(nc, esP, "r_xiT", [128, 8, T], BF16)
            with ExitStack() as esx:
                xt = [sb(nc, esx, "r_x%d" % i, [128, 1024]) for i in range(2)]
                for t in range(NT):
                    b = t % 2
                    P.dma(xt[b][:], G.xres[t * 128:(t + 1) * 128, :], r=[("xres", t)], w=[("r_x", b)])
                    transpose_modulate(G, xt[b], ("r_x", b), hT, ("r_hT", t), t * 128, mc, 0 if t < 16 else 1, 0, 1, 0)
                P.barrier()
            def lat(tile_, c0, c1):
                return tile_[:, c0:c1, 0:TL].rearrange("p c (r w) -> p c r w", w=64)
            hl = lambda c0, c1: lat(hT, c0, c1)
            dl = lambda c0, c1: lat(dT, c0, c1)
            sub = ALU.subtract
            ops = [
                (dl(0, 2)[:, :, :, 1:64], hl(0, 2)[:, :, :, 0:63], hl(0, 2)[:, :, :, 1:64]),
                (dl(2, 4)[:, :, :, 0:63], hl(2, 4)[:, :, :, 1:64], hl(2, 4)[:, :, :, 0:63]),
                (dl(4, 6)[:, :, 1:32, :], hl(4, 6)[:, :, 0:31, :], hl(4, 6)[:, :, 1:32, :]),
                (dl(6, 8)[:, :, 0:31, :], hl(6, 8)[:, :, 1:32, :], hl(6, 8)[:, :, 0:31, :]),
                (dT[:, 0:4, TL + 1:T], hT[:, 0:4, TL:T - 1], hT[:, 0:4, TL + 1:T]),
                (dT[:, 4:8, TL:T - 1], hT[:, 4:8, TL + 1:T], hT[:, 4:8, TL:T - 1]),
            ]
            for (o_, a_, b_) in ops:
                for cc in range(o_.shape[1]):
                    P.op('dve', lambda e, o_=o_, a_=a_, b_=b_, cc=cc: e.tensor_tensor(out=o_[:, cc], in0=a_[:, cc], in1=b_[:, cc], op=sub), r=["r_hT"], w=["r_dT"])
            bnd = [
                (dl(0, 2)[:, :, :, 0:1], hl(0, 2)[:, :, :, 0:1]), (dl(2, 4)[:, :, :, 63:64], hl(2, 4)[:, :, :, 63:64]),
                (dl(4, 6)[:, :, 0:1, :], hl(4, 6)[:, :, 0:1, :]), (dl(6, 8)[:, :, 31:32, :], hl(6, 8)[:, :, 31:32, :]),
                (dT[:, 0:4, TL:TL + 1], hT[:, 0:4, TL:TL + 1]), (dT[:, 4:8, T - 1:T], hT[:, 4:8, T - 1:T]),
            ]
            for (o_, a_) in bnd:
                for cc in range(o_.shape[1]):
                    P.op('dve', lambda e, o_=o_, a_=a_, cc=cc: e.tensor_scalar_mul(out=o_[:, cc], in0=a_[:, cc], scalar1=-1.0), r=["r_hT"], w=["r_dT"])
            mu = sb(nc, esP, "r_mu", [128, 6, 8])
            colload(G, esP, mu[:, :, :].rearrange("p i c -> p (i c)"), "r_mu", G.Wl("rw_mu", j).rearrange("i n -> (i n)"), 48)
            kkc = colvec(G, esP, "r_kkc", G.Wl("rw_kk", j), "r_cv")
            kac = colvec(G, esP, "r_kac", G.Wl("rw_ka", j), "r_cv")
            omka = sb(nc, esP, "r_omka", [128, 8])
            P.op('dve', lambda e: e.tensor_scalar(out=omka[:], in0=kac[:], scalar1=-1.0, scalar2=1.0, op0=ALU.mult, op1=ALU.add), r=["r_cv"], w=["r_cv2"])
            stg = [sb(nc, esP, "r_stg%d" % i, [128, T]) for i in range(2)]
            stg2 = [sb(nc, esP, "r_stg2_0", [128, T])] * 2
            ld1 = [sb(nc, esP, "r_ld1_0", [128, T])] * 2
            ld2 = [sb(nc, esP, "r_ld2_0", [128, T])] * 2
            tmpa = [sb(nc, esP, "r_tmpa%d" % i, [128, 512]) for i in range(2)]
            tmpb = [sb(nc, esP, "r_tmpb%d" % i, [128, 512]) for i in range(2)]
            tcnt = [0]

            def mk_xi(i):
                for c in range(8):
                    P.op('dve', lambda e, c=c: e.scalar_tensor_tensor(out=xiT[:, c, :], in0=dT[:, c, :], scalar=mu[:, i, c:c + 1], in1=hT[:, c, :], op0=ALU.mult, op1=ALU.add),
                         r=["r_dT", "r_hT", "r_mu"], w=["r_xiT"])

            def store(oc, dst, src, skey):
                P.dma(dst[oc, :, :], src[:, :], r=[skey], w=[(dst.name if hasattr(dst, "name") else "fm", oc)])

            mk_xi(0)
            with ExitStack() as esw:
                wb = load_w_bf16(G, esw, "r_wr", G.Wl("rw_wr", j), 1024, 1024)
                def cons_r(oc, t0, tn, ps, pkey, m_):
                    s = stg[oc % 2]
                    P.op('act', lambda e: e.copy(out=s[:, t0:t0 + tn], in_=ps[:, 0:tn]), r=[pkey], w=[("r_stg", oc % 2), pkey])
                    if t0 == 2048:
                        P.dma(FM["rT"][oc, :, :], s[:, :], r=[("r_stg", oc % 2)], w=[("fm_rT", oc)])
                fm_linear(G, xiT, "r_xiT", wb, "r_wr_b", 1024, cons_r)
                P.barrier()
            mk_xi(1)
            for d in range(2):
                with ExitStack() as esw:
                    w1b = load_w_bf16(G, esw, "r_w1", G.Wl("rw_w1", j)[d], 1024, 64)
                    w2b = load_w_bf16(G, esw, "r_w2", G.Wl("rw_w2", j)[d], 64, 1024)
                    w0c = colvec(G, esw, "r_w0c", G.Wl("rw_w0", j)[d], "r_w0c")
                    P.op('dve', lambda e: e.tensor_scalar_mul(out=w0c[:], in0=w0c[:], scalar1=-1.0), r=["r_w0c"], w=["r_w0c"])
                    t1 = sb(nc, esw, "r_t1", [128, 1, T], BF16)
                    def cons_t(oc, t0, tn, ps, pkey, m_):
                        P.op('act', lambda e: e.activation(out=t1[0:m_, 0, t0:t0 + tn], in_=ps[0:m_, 0:tn], func=AF.Tanh), r=[pkey], w=["r_t1", pkey])
                    fm_linear(G, xiT, "r_xiT", w1b, "r_w1_b", 64, cons_t, pbanks=(2, 3))
                    def cons_w(oc, t0, tn, ps, pkey, m_):
                        s = stg[oc % 2]
                        k_ = tcnt[0] % 2
                        tcnt[0] += 1
                        ta, tb_ = tmpa[k_], tmpb[k_]
                        P.op('act', lambda e: e.activation(out=ta[:, 0:tn], in_=ps[:, 0:tn], func=AF.Exp, scale=-1.0, bias=w0c[:, oc:oc + 1]), r=[pkey, "r_w0c"], w=[("r_tmpa", k_), pkey])
                        P.op('act', lambda e: e.activation(out=tb_[:, 0:tn], in_=ta[:, 0:tn], func=AF.Ln, bias=G.one_c[:, 0:1], scale=1.0), r=[("r_tmpa", k_), "consts"], w=[("r_tmpb", k_)])
                        P.op('act', lambda e: e.activation(out=ta[:, 0:tn], in_=tb_[:, 0:tn], func=AF.Exp, scale=-1.0, bias=G.mhalf_c[:, 0:1]), r=[("r_tmpb", k_), "consts"], w=[("r_tmpa", k_)])
                        P.op('dve', lambda e: e.tensor_scalar_mul(out=s[:, t0:t0 + tn], in0=ta[:, 0:tn], scalar1=-1.0), r=[("r_tmpa", k_)], w=[("r_stg", oc % 2)])
                        if t0 == 2048:
                            P.dma(FM["lw%d" % d][oc, :, :], s[:, :], r=[("r_stg", oc % 2)], w=[("fm_lw%d" % d, oc)])
                    fm_linear(G, t1, "r_t1", w2b, "r_w2_b", 1024, cons_w, kparts=[(0, 64)])
                    P.barrier()
            mk_xi(2)
            with ExitStack() as esw:
                wb = load_w_bf16(G, esw, "r_wk", G.Wl("rw_wk", j), 1024, 1024)
                def cons_k(oc, t0, tn, ps, pkey, m_):
                    s, s2 = stg[oc % 2], stg2[oc % 2]
                    k_ = tcnt[0] % 2
                    tcnt[0] += 1
                    ta, tb_ = tmpa[k_], tmpb[k_]
                    P.op('act', lambda e: e.copy(out=s[:, t0:t0 + tn], in_=ps[:, 0:tn]), r=[pkey], w=[("r_stg", oc % 2), pkey])
                    P.op('dve', lambda e: e.tensor_scalar(out=ta[:, 0:tn], in0=s[:, t0:t0 + tn], scalar1=kkc[:, oc:oc + 1], scalar2=None, op0=ALU.mult), r=[("r_stg", oc % 2), "r_cv"], w=[("r_tmpa", k_)])
                    P.op('pool', lambda e: e.tensor_tensor(out=tb_[:, 0:tn], in0=ta[:, 0:tn], in1=ta[:, 0:tn], op=ALU.mult), r=[("r_tmpa", k_)], w=[("r_tmpb", k_)])
                    P.op('pe', lambda e: e.matmul(G.ps[4 + k_][:, 0:tn], lhsT=BO[:, :], rhs=tb_[:, 0:tn], start=True, stop=True), r=["r_BO", ("r_tmpb", k_)], w=[("ps", 4 + k_)])
                    P.op('act', lambda e: e.activation(out=tb_[:, 0:tn], in_=G.ps[4 + k_][:, 0:tn], func=AF.Sqrt), r=[("ps", 4 + k_)], w=[("r_tmpb", k_), ("ps", 4 + k_)])
                    P.op('dve', lambda e: e.tensor_scalar_max(out=tb_[:, 0:tn], in0=tb_[:, 0:tn], scalar1=1e-12), r=[("r_tmpb", k_)], w=[("r_tmpb", k_)])
                    P.op('dve', lambda e: e.reciprocal(out=tb_[:, 0:tn], in_=tb_[:, 0:tn]), r=[("r_tmpb", k_)], w=[("r_tmpb", k_)])
                    P.op('dve', lambda e: e.tensor_tensor(out=s2[:, t0:t0 + tn], in0=ta[:, 0:tn], in1=tb_[:, 0:tn], op=ALU.mult), r=[("r_tmpa", k_), ("r_tmpb", k_)], w=[("r_stg2", 0)])
                    if t0 == 2048:
                        P.dma(FM["kT"][oc, :, :], s[:, :], r=[("r_stg", oc % 2)], w=[("fm_kT", oc)])
                        P.dma(FM["kk"][oc, :, :], s2[:, :], r=[("r_stg2", 0)], w=[("fm_kk", oc)])
                fm_linear(G, xiT, "r_xiT", wb, "r_wk_b", 1024, cons_k)
                P.barrier()
            mk_xi(3)
            with ExitStack() as esw:
                wb = load_w_bf16(G, esw, "r_wv", G.Wl("rw_wv", j), 1024, 1024)
                if j > 0:
                    v1b = load_w_bf16(G, esw, "r_v1", G.Wl("rw_v1", j - 1), 1024, 32)
                    v2b = load_w_bf16(G, esw, "r_v2", G.Wl("rw_v2", j - 1), 32, 1024)
                    v0c = colvec(G, esw, "r_v0c", G.Wl("rw_v0", j - 1), "r_v0c")
                    t1 = sb(nc, esw, "r_t1v", [128, 1, T], BF16)
                    def cons_t(oc, t0, tn, ps, pkey, m_):
                        P.op('act', lambda e: e.copy(out=t1[0:m_, 0, t0:t0 + tn], in_=ps[0:m_, 0:tn]), r=[pkey], w=["r_t1v", pkey])
                    fm_linear(G, xiT, "r_xiT", v1b, "r_v1_b", 32, cons_t, pbanks=(2, 3))
                def cons_v(oc, t0, tn, ps, pkey, m_):
                    s = stg[oc % 2]
                    if j == 0:
                        P.op('act', lambda e: e.copy(out=s[:, t0:t0 + tn], in_=ps[:, 0:tn]), r=[pkey], w=[("r_stg", oc % 2), pkey])
                    else:
                        k_ = tcnt[0] % 2
                        tcnt[0] += 1
                        ta, tb_ = tmpa[k_], tmpb[k_]
                        if t0 == 0:
                            P.dma(ld1[oc % 2][:, :], FM["vf"][oc, :, :], r=[("fm_vf", oc)], w=[("r_ld1", 0)])
                        vf = ld1[oc % 2]
                        pb2 = 4 + k_
                        P.op('pe', lambda e: e.matmul(G.ps[pb2][:, 0:tn], lhsT=v2b[0:32, 0, oc * 128:(oc + 1) * 128], rhs=t1[0:32, 0, t0:t0 + tn], start=True, stop=True),
                             r=["r_t1v", "r_v2_b"], w=[("ps", pb2)])
                        P.op('act', lambda e: e.activation(out=ta[:, 0:tn], in_=G.ps[pb2][:, 0:tn], func=AF.Sigmoid, bias=v0c[:, oc:oc + 1], scale=1.0), r=[("ps", pb2), "r_v0c"], w=[("r_tmpa", k_), ("ps", pb2)])
                        P.op('dve', lambda e: e.tensor_tensor(out=tb_[:, 0:tn], in0=vf[:, t0:t0 + tn], in1=ps[:, 0:tn], op=ALU.subtract), r=[("r_ld1", 0), pkey], w=[("r_tmpb", k_)])
                        P.op('dve', lambda e: e.tensor_tensor(out=tb_[:, 0:tn], in0=tb_[:, 0:tn], in1=ta[:, 0:tn], op=ALU.mult), r=[("r_tmpa", k_), ("r_tmpb", k_)], w=[("r_tmpb", k_)])
                        P.op('dve', lambda e: e.tensor_tensor(out=s[:, t0:t0 + tn], in0=tb_[:, 0:tn], in1=ps[:, 0:tn], op=ALU.add), r=[("r_tmpb", k_), pkey], w=[("r_stg", oc % 2), pkey])
                    if t0 == 2048:
                        P.dma(FM["vT"][oc, :, :], s[:, :], r=[("r_stg", oc % 2)], w=[("fm_vT", oc)])
                        if j == 0:
                            P.dma(FM["vf"][oc, :, :], s[:, :], r=[("r_stg", oc % 2)], w=[("fm_vf", oc)])
                fm_linear(G, xiT, "r_xiT", wb, "r_wv_b", 1024, cons_v)
                P.barrier()
            mk_xi(4)
            for d in range(2):
                with ExitStack() as esw:
                    a1b = load_w_bf16(G, esw, "r_a1", G.Wl("rw_a1", j)[d], 1024, 64)
                    a2b = load_w_bf16(G, esw, "r_a2", G.Wl("rw_a2", j)[d], 64, 1024)
                    a0c = colvec(G, esw, "r_a0c", G.Wl("rw_a0", j)[d], "r_a0c")
                    t1 = sb(nc, esw, "r_t1a", [128, 1, T], BF16)
                    def cons_t(oc, t0, tn, ps, pkey, m_):
                        P.op('act', lambda e: e.copy(out=t1[0:m_, 0, t0:t0 + tn], in_=ps[0:m_, 0:tn]), r=[pkey], w=["r_t1a", pkey])
                    fm_linear(G, xiT, "r_xiT", a1b, "r_a1_b", 64, cons_t, pbanks=(2, 3))
                    def cons_a(oc, t0, tn, ps, pkey, m_):
                        s, s2 = stg[oc % 2], stg2[oc % 2]
                        k_ = tcnt[0] % 2
                        tcnt[0] += 1
                        ta, tb_ = tmpa[k_], tmpb[k_]
                        if t0 == 0:
                            P.dma(ld1[oc % 2][:, :], FM["kT"][oc, :, :], r=[("fm_kT", oc)], w=[("r_ld1", 0)])
                            P.dma(ld2[oc % 2][:, :], FM["kk"][oc, :, :], r=[("fm_kk", oc)], w=[("r_ld2", 0)])
                        kt_, kkt = ld1[oc % 2], ld2[oc % 2]
                        P.op('act', lambda e: e.activation(out=ta[:, 0:tn], in_=ps[:, 0:tn], func=AF.Sigmoid, bias=a0c[:, oc:oc + 1], scale=1.0), r=[pkey, "r_a0c"], w=[("r_tmpa", k_), pkey])
                        P.op('pool', lambda e: e.tensor_tensor(out=s2[:, t0:t0 + tn], in0=kkt[:, t0:t0 + tn], in1=ta[:, 0:tn], op=ALU.mult), r=[("r_ld2", 0), ("r_tmpa", k_)], w=[("r_stg2", 0)])
                        P.op('dve', lambda e: e.tensor_scalar(out=tb_[:, 0:tn], in0=ta[:, 0:tn], scalar1=kac[:, oc:oc + 1], scalar2=omka[:, oc:oc + 1], op0=ALU.mult, op1=ALU.add),
                             r=[("r_tmpa", k_), "r_cv", "r_cv2"], w=[("r_tmpb", k_)])
                        P.op('dve', lambda e: e.tensor_tensor(out=s[:, t0:t0 + tn], in0=tb_[:, 0:tn], in1=kt_[:, t0:t0 + tn], op=ALU.mult), r=[("r_tmpb", k_), ("r_ld1", 0)], w=[("r_stg", oc % 2)])
                        if t0 == 2048:
                            P.dma(FM["kd%d" % d][oc, :, :], s[:, :], r=[("r_stg", oc % 2)], w=[("fm_kd%d" % d, oc)])
                            P.dma(FM["b%d" % d][oc, :, :], s2[:, :], r=[("r_stg2", 0)], w=[("fm_b%d" % d, oc)])
                    fm_linear(G, t1, "r_t1a", a2b, "r_a2_b", 1024, cons_a, kparts=[(0, 64)])
                    P.barrier()
            mk_xi(5)
            with ExitStack() as esw:
                g1b = load_w_bf16(G, esw, "r_g1", G.Wl("rw_g1", j), 1024, 160)
                g2b = load_w_bf16(G, esw, "r_g2", G.Wl("rw_g2", j), 160, 1024)
                t1 = sb(nc, esw, "r_t1g", [128, 2, T], BF16)
                def cons_t(oc, t0, tn, ps, pkey, m_):
                    P.op('act', lambda e: e.activation(out=t1[0:m_, oc, t0:t0 + tn], in_=ps[0:m_, 0:tn], func=AF.Sigmoid), r=[pkey], w=["r_t1g", pkey])
                fm_linear(G, xiT, "r_xiT", g1b, "r_g1_b", 160, cons_t, pbanks=(2, 3))
                def cons_g(oc, t0, tn, ps, pkey, m_):
                    s = stg[oc % 2]
                    P.op('act', lambda e: e.copy(out=s[:, t0:t0 + tn], in_=ps[:, 0:tn]), r=[pkey], w=[("r_stg", oc % 2), pkey])
                    if t0 == 2048:
                        P.dma(FM["gT"][oc, :, :], s[:, :], r=[("r_stg", oc % 2)], w=[("fm_gT", oc)])
                fm_linear(G, t1, "r_t1g", g2b, "r_g2_b", 1024, cons_g, kparts=[(0, 128), (1, 32)])
                P.barrier()
            P.barrier()
        if not SKIP_SCAN:
            scan_stage(G)
        with ExitStack() as esR:
            zT = sb(nc, esR, "o_zT", [128, 8, T], BF16)
            rkc = colvec(G, esR, "o_rkc", G.Wl("rw_rk", j).rearrange("h k -> (h k)"), "o_cv")
            lgc = colvec(G, esR, "o_lgc", G.Wl("rw_lnx_g", j), "o_cv")
            lbc = colvec(G, esR, "o_lbc", G.Wl("rw_lnx_b", j), "o_cv")
            epsg = sb(nc, esR, "o_epsg", [128, 1])
            P.op('dve', lambda e: e.memset(epsg[:], RW_GN_EPS), w=["o_epsg"])
            with ExitStack() as esL:
                L = {nm: [sb(nc, esL, "o_%s%d" % (nm, i), [128, T]) for i in range(2)] for nm in ("y0", "y1", "r", "kd0", "kd1", "v", "g")}
                wa = [sb(nc, esL, "o_wa%d" % i, [128, 512]) for i in range(2)]
                wb_ = [sb(nc, esL, "o_wb%d" % i, [128, 512]) for i in range(2)]
                wc_ = [sb(nc, esL, "o_wc%d" % i, [128, 512]) for i in range(2)]
                it = 0
                def load_pair(p_):
                    b_ = p_ % 2
                    for d in range(2):
                        for h in range(2):
                            P.dma(L["y%d" % d][b_][h * 64:(h + 1) * 64, :], G.y_d[d, :, 2 * p_ + h, :], r=["y_d"], w=[("o_L_y%d" % d, b_)])
                    for nm, fmn in (("r", "rT"), ("kd0", "kd0"), ("kd1", "kd1"), ("v", "vT"), ("g", "gT")):
                        P.dma(L[nm][b_][:, :], FM[fmn][p_, :, :], r=[("fm_" + fmn, p_)], w=[("o_L_" + nm, b_)])
                for p in range(8):
                    b = p % 2
                    if p == 0:
                        load_pair(0)
                    if p + 1 < 8:
                        load_pair(p + 1)
                    for (t0, tn) in TBLK:
                        k_ = it % 2
                        it += 1
                        a_, b2, c_ = wa[k_], wb_[k_], wc_[k_]
                        ka, kb, kc = ("o_wa", k_), ("o_wb", k_), ("o_wc", k_)
                        ts_ = slice(t0, t0 + tn)
                        P.op('dve', lambda e: e.tensor_tensor(out=a_[:, 0:tn], in0=L["y0"][b][:, ts_], in1=L["y1"][b][:, ts_], op=ALU.add), r=[("o_L_y0", b), ("o_L_y1", b)], w=[ka])
                        P.op('pe', lambda e: e.matmul(G.ps[k_][:, 0:tn], lhsT=BO[:, :], rhs=a_[:, 0:tn], start=True, stop=True), r=["r_BO", ka], w=[("ps", k_)])
                        P.op('dve', lambda e: e.scalar_tensor_tensor(out=a_[:, 0:tn], in0=G.ps[k_][:, 0:tn], scalar=-1.0 / 64, in1=a_[:, 0:tn], op0=ALU.mult, op1=ALU.add), r=[("ps", k_), ka], w=[ka, ("ps", k_)])
                        P.op('pool', lambda e: e.tensor_tensor(out=b2[:, 0:tn], in0=a_[:, 0:tn], in1=a_[:, 0:tn], op=ALU.mult), r=[ka], w=[kb])
                        P.op('pe', lambda e: e.matmul(G.ps[2 + k_][:, 0:tn], lhsT=BO[:, :], rhs=b2[:, 0:tn], start=True, stop=True), r=["r_BO", kb], w=[("ps", 2 + k_)])
                        P.op('act', lambda e: e.activation(out=b2[:, 0:tn], in_=G.ps[2 + k_][:, 0:tn], func=AF.Sqrt, scale=1.0 / 64, bias=epsg[:, 0:1]), r=[("ps", 2 + k_), "o_epsg"], w=[kb, ("ps", 2 + k_)])
                        P.op('dve', lambda e: e.reciprocal(out=b2[:, 0:tn], in_=b2[:, 0:tn]), r=[kb], w=[kb])
                        P.op('dve', lambda e: e.tensor_tensor(out=a_[:, 0:tn], in0=a_[:, 0:tn], in1=b2[:, 0:tn], op=ALU.mult), r=[ka, kb], w=[ka])
                        P.op('dve', lambda e: e.tensor_scalar(out=a_[:, 0:tn], in0=a_[:, 0:tn], scalar1=lgc[:, p:p + 1], scalar2=lbc[:, p:p + 1], op0=ALU.mult, op1=ALU.add), r=[ka, "o_cv"], w=[ka])
                        P.op('pool', lambda e: e.tensor_tensor(out=c_[:, 0:tn], in0=L["kd0"][b][:, ts_], in1=L["kd1"][b][:, ts_], op=ALU.add), r=[("o_L_kd0", b), ("o_L_kd1", b)], w=[kc])
                        P.op('dve', lambda e: e.scalar_tensor_tensor(out=c_[:, 0:tn], in0=c_[:, 0:tn], scalar=rkc[:, p:p + 1], in1=L["r"][b][:, ts_], op0=ALU.mult, op1=ALU.mult), r=[kc, "o_cv", ("o_L_r", b)], w=[kc])
                        P.op('pe', lambda e: e.matmul(G.ps[4 + k_][:, 0:tn], lhsT=BO[:, :], rhs=c_[:, 0:tn], start=True, stop=True), r=["r_BO", kc], w=[("ps", 4 + k_)])
                        P.op('dve', lambda e: e.tensor_tensor(out=c_[:, 0:tn], in0=G.ps[4 + k_][:, 0:tn], in1=L["v"][b][:, ts_], op=ALU.mult), r=[("ps", 4 + k_), ("o_L_v", b)], w=[kc, ("ps", 4 + k_)])
                        P.op('dve', lambda e: e.tensor_tensor(out=a_[:, 0:tn], in0=a_[:, 0:tn], in1=c_[:, 0:tn], op=ALU.add), r=[ka, kc], w=[ka])
                        P.op('dve', lambda e: e.tensor_tensor(out=zT[:, p, ts_], in0=a_[:, 0:tn], in1=L["g"][b][:, ts_], op=ALU.mult), r=[ka, ("o_L_g", b)], w=[("o_zT", p)])
                P.barrier()
            wob = load_w_bf16(G, esR, "o_wo", G.Wl("rw_wo", j), 1024, 1024)
            gate = [sb(nc, esR, "o_gate%d" % r_, [128, 1024]) for r_ in range(2)]
            LG = sb(nc, esR, "o_lg", [128, 1024])
            LB = sb(nc, esR, "o_lb", [128, 1024])
            for r_ in range(2):
                P.dma(gate[r_][:], G.modv[l, r_:r_ + 1, 2 * 1024:3 * 1024].partition_broadcast(128), r=[("modv", l)], w=["o_bc"])
            P.dma(LG[:], row(G.Wl("ln1_g", l)).partition_broadcast(128), w=["o_bc"])
            P.dma(LB[:], row(G.Wl("ln1_b", l)).partition_broadcast(128), w=["o_bc"])
            xt = [sb(nc, esR, "o_x%d" % i, [128, 1024]) for i in range(2)]
            tmp = [sb(nc, esR, "o_tmp%d" % i, [128, 1024]) for i in range(2)]
            yt = [sb(nc, esR, "o_y%d" % i, [128, 1024]) for i in range(2)]
            st = [sb(nc, esR, "o_st%d" % i, [128, 16]) for i in range(2)]
            for t in range(NT):
                b = t % 2
                r_ = 0 if t < 16 else 1
                if t == 0:
                    P.dma(xt[0][:], G.xres[0:128, :], r=[("xres", 0)], w=[("o_x", 0)])
                if t + 1 < NT:
                    P.dma(xt[(t + 1) % 2][:], G.xres[(t + 1) * 128:(t + 2) * 128, :], r=[("xres", t + 1)], w=[("o_x", (t + 1) % 2)])
                banks = [G.ps[0 + 2 * b], G.ps[1 + 2 * b]]
                okeys = [("ps", 0 + 2 * b), ("ps", 1 + 2 * b)]
                for hh in range(2):
                    for c in range(8):
                        P.op('pe', lambda e_, hh=hh, c=c: e_.matmul(banks[hh][:, :], lhsT=zT[:, c, t * 128:(t + 1) * 128], rhs=wob[:, c, hh * 512:(hh + 1) * 512],
                                                                   start=(c == 0), stop=(c == 7)), r=["o_zT", "o_wo_b"], w=[okeys[hh]])
                ln_epilogue(G, banks, okeys, xt[b], ("o_x", b), gate[r_], LG, LB, ["o_bc"], tmp[b], ("o_tmp", b), st[b], ("o_st", b), yt[b], ("o_y", b))
                P.dma(G.xres[t * 128:(t + 1) * 128, :], yt[b][:], r=[("o_y", b)], w=[("xres", t)])
            P.barrier()
        P.barrier()


def build(layers=(0, 1, 2, 3), stages=None):
    nc = bass.Bass("TRN2", target_bir_lowering=False)
    G = Ctx()
    G.nc = nc

    def din(name, shape):
        return nc.dram_tensor(name, list(shape), F32, kind="ExternalInput").ap()
    G.x_d = din("x", [TL, D])
    G.ctx_d = din("ctx", [TC, D])
    G.c_d = din("c", [1, D])
    G.cc_d = din("c_ctx", [1, D])
    G.used = {}
    specs = dict(WEIGHT_SPECS)

    def Wl(name, l):
        key = "%s_%d" % (name, l)
        if key not in G.used:
            G.used[key] = (name, l, din(key, specs[name][1:]))
        return G.used[key][2]
    G.Wl = Wl
    G.kc = {}

    def KC(name, shape):
        if name not in G.kc:
            G.kc[name] = din(name, shape)
        return G.kc[name]
    G.KC = KC
    G.ident_d = KC("k_ident", [128, 128])
    G.out_d = nc.dram_tensor("out", [TL, D], F32, kind="ExternalOutput").ap()
    G.outc_d = nc.dram_tensor("outc", [TC, D], F32, kind="ExternalOutput").ap()
    G.xres = nc.dram_tensor("xres", [T, D], F32, kind="Internal").ap()
    G.modv = nc.dram_tensor("modv", [4, 2, 6144], F32, kind="Internal").ap()
    G.mo_d = nc.dram_tensor("mo_d", [T, D], BF16, kind="Internal").ap()
    G.fm = {nm: nc.dram_tensor("fm_" + nm, [8, 128, T], F32, kind="Internal").ap()
            for nm in ("rT", "kT", "vT", "vf", "kk", "gT", "lw0", "lw1", "kd0", "kd1", "b0", "b1")}
    G.y_d = nc.dram_tensor("y_d", [2, 64, 16, T], F32, kind="Internal").ap()
    P = Prog(nc)
    G.P = P
    with ExitStack() as es:
        G.ps = [es.enter_context(nc.psum_tensor("ps%d" % i, [128, 512], F32)) for i in range(8)]
        G.identF = sb(nc, es, "identF", [128, 128])
        G.identB = sb(nc, es, "identB", [128, 128], BF16)
        G.eps_ln = sb(nc, es, "eps_ln", [128, 1])
        P.dma(G.identF[:], G.ident_d[:, :], w=["ident"])
        P.op('dve', lambda e: e.tensor_copy(out=G.identB[:], in_=G.identF[:]), r=["ident"], w=["identB"])
        P.op('dve', lambda e: e.memset(G.eps_ln[:], LN_EPS), w=["consts"])
        G.one_c = sb(nc, es, "one_c", [128, 1])
        G.mhalf_c = sb(nc, es, "mhalf_c", [128, 1])
        P.op('dve', lambda e: e.memset(G.one_c[:], 1.0), w=["consts"])
        P.op('dve', lambda e: e.memset(G.mhalf_c[:], -0.5), w=["consts"])
        for t in range(16):
            P.dma(G.xres[t * 128:(t + 1) * 128, :], G.x_d[t * 128:(t + 1) * 128, :], w=[("xres", t)])
        for t in range(2):
            P.dma(G.xres[TL + t * 128:TL + (t + 1) * 128, :], G.ctx_d[t * 128:(t + 1) * 128, :], w=[("xres", 16 + t)])
        stage_modvec(G, layers)
        for l in layers:
            if l % 2 == 0 and (stages is None or "mix" in stages):
                stage_even(G, l)
            if l % 2 == 1 and (stages is None or "mix" in stages):
                stage_rwkv(G, l)
            if stages is None or "ffn" in stages:
                stage_ffn(G, l)
        for t in range(16):
            P.dma(G.out_d[t * 128:(t + 1) * 128, :], G.xres[t * 128:(t + 1) * 128, :], r=[("xres", t)], w=[("out", t)])
        for t in range(2):
            P.dma(G.outc_d[t * 128:(t + 1) * 128, :], G.xres[TL + t * 128:TL + (t + 1) * 128, :], r=[("xres", 16 + t)], w=[("outc", t)])
        P.barrier()
    P.es.close()
    return nc, P, G


def make_consts():
    k = {"k_ident": np.eye(128, dtype=np.float32)}
    t = np.arange(TL)
    rowi = (t // 64).astype(np.float32)
    coli = (t % 64).astype(np.float32)
    for nm, dim, ng in (("A", 64, 8), ("B", 128, 4)):
        nf = dim // 4
        inv = (10000.0 ** (-np.arange(nf, dtype=np.float32) / nf)).astype(np.float32)
        ang = np.concatenate([rowi[:, None] * inv, coli[:, None] * inv], -1).astype(np.float32)
        k["k_cos" + nm] = np.ascontiguousarray(np.tile(np.cos(ang).astype(np.float32), (1, ng)))
        k["k_sin" + nm] = np.ascontiguousarray(np.tile(np.sin(ang).astype(np.float32), (1, ng)))
    p = np.arange(128, dtype=np.float32)
    k["k_cols"] = np.stack([127 - p, p, p + 1, 128 - p], 1).astype(np.float32)
    jj = p[None, :]
    pp = p[:, None]
    k["k_mats"] = np.ascontiguousarray(np.stack([np.maximum(jj - pp, 0), np.maximum(pp - jj, 0), (jj >= pp).astype(np.float32),
                                                 (jj <= pp).astype(np.float32)], 1).astype(np.float32))
    bo = np.zeros((128, 128), np.float32)
    bo[:64, :64] = 1.0
    bo[64:, 64:] = 1.0
    k["k_bo"] = bo
    i64 = np.arange(64)
    strict = (i64[:, None] < i64[None, :]).astype(np.float32)
    incl = (i64[:, None] <= i64[None, :]).astype(np.float32)
    def bdm(m):
        z = np.zeros((128, 128), np.float32)
        z[:64, :64] = m
        z[64:, 64:] = m
        return z
    k["k_smask"] = np.ascontiguousarray(np.stack([bdm(strict), bdm(strict.T), bdm(incl)], 1))
    return k


def kernel(**inputs):
    nc, _, G = build()
    consts = make_consts()
    in_maps = []
    for b in range(8):
        m = {"x": np.ascontiguousarray(inputs["x"][b]), "ctx": np.ascontiguousarray(inputs["ctx"][b]),
             "c": np.ascontiguousarray(inputs["c"][b:b + 1]), "c_ctx": np.ascontiguousarray(inputs["c_ctx"][None, :])}
        for key, (name, l, _ap) in G.used.items():
            m[key] = np.ascontiguousarray(inputs[name][l])
        for key in G.kc:
            m[key] = consts[key]
        in_maps.append(m)
    res = run_bass_kernel_spmd(nc, in_maps, core_ids=list(range(8)))
    return np.stack([r["out"] for r in res.results], axis=0).astype(np.float32)
```

```python
import math
import numpy as np
from contextlib import ExitStack
import concourse.bass as bass
import concourse.mybir as mybir
from concourse.bass_utils import run_bass_kernel_spmd

F32 = mybir.dt.float32
BF16 = mybir.dt.bfloat16
AF = mybir.ActivationFunctionType
ALU = mybir.AluOpType
AX = mybir.AxisListType

NDS = 8
SAME_ENGINE_SYNC = True
EMBED_WAIT = True

D = 1024
TL = 2048
TC = 256
T = TL + TC
NT = T // 128
DFF = 2816
NFC = DFF // 128
DEPTH = 4
ALPHA = (2.0 * DEPTH) ** 0.25
LN_EPS = 1e-6

WEIGHT_SPECS = [
    ("mod_w", (4, 1024, 6144)), ("mod_b", (4, 6144)), ("ln1_g", (4, 1024)), ("ln1_b", (4, 1024)),
    ("ln2_g", (4, 1024)), ("ln2_b", (4, 1024)), ("ffn_w1", (4, 1024, 2816)), ("ffn_w3", (4, 1024, 2816)),
    ("ffn_w2", (4, 2816, 1024)), ("ev_w_in", (2, 1024, 3584)), ("ev_w_out", (2, 1024, 1024)),
    ("da_lam_q1", (2, 64)), ("da_lam_k1", (2, 64)), ("da_lam_q2", (2, 64)), ("da_lam_k2", (2, 64)),
    ("da_gn_g", (2, 512)), ("rt_decay_logit", (2, 2, 4)), ("rw_mu", (2, 6, 1024)),
    ("rw_wr", (2, 1024, 1024)), ("rw_wk", (2, 1024, 1024)), ("rw_wv", (2, 1024, 1024)), ("rw_wo", (2, 1024, 1024)),
    ("rw_w0", (2, 2, 1024)), ("rw_w1", (2, 2, 1024, 64)), ("rw_w2", (2, 2, 64, 1024)),
    ("rw_a0", (2, 2, 1024)), ("rw_a1", (2, 2, 1024, 64)), ("rw_a2", (2, 2, 64, 1024)),
    ("rw_v0", (1, 1024)), ("rw_v1", (1, 1024, 32)), ("rw_v2", (1, 32, 1024)),
    ("rw_g1", (2, 1024, 160)), ("rw_g2", (2, 160, 1024)), ("rw_kk", (2, 1024)), ("rw_ka", (2, 1024)),
    ("rw_rk", (2, 16, 64)), ("rw_lnx_g", (2, 1024)), ("rw_lnx_b", (2, 1024)),
]


class Prog:
    def __init__(self, nc):
        self.nc = nc
        self.engs = {'pe': nc.tensor, 'act': nc.scalar, 'dve': nc.vector, 'pool': nc.gpsimd, 'sp': nc.sync}
        self.es = ExitStack()
        self.sem = {}
        for e in ['pe', 'act', 'dve', 'pool']:
            self.sem[('e', e)] = self.es.enter_context(nc.semaphore('s_' + e))
        for i in range(NDS):
            self.sem[('d', i)] = self.es.enter_context(nc.semaphore('d%d' % i))
        self.cnt = {k: 0 for k in self.sem}
        self.dnext = 0
        self.known = {e: {} for e in self.engs}
        self.res = {}
        self.nops = 0
        self.nwaits = 0

    def _get(self, key):
        name, sub = key if isinstance(key, tuple) else (key, None)
        d = self.res.setdefault(name, {})
        if sub not in d:
            d[sub] = [None, {}]
        return d[sub]

    def _conf(self, key):
        name, sub = key if isinstance(key, tuple) else (key, None)
        d = self.res.setdefault(name, {})
        if sub is None:
            return list(d.values())
        out = []
        if sub in d:
            out.append(d[sub])
        if None in d:
            out.append(d[None])
        return out

    def op(self, eng, fn, r=(), w=(), dma=False, noembed=False):
        deps = {}

        def need(k, v):
            if deps.get(k, 0) < v:
                deps[k] = v
        for key in r:
            for st in self._conf(key):
                if st[0] is not None:
                    need(*st[0])
        for key in w:
            for st in self._conf(key):
                if st[0] is not None:
                    need(*st[0])
                for k, v in st[1].items():
                    need(k, v)
        E = self.engs[eng]
        kn = self.known[eng]
        if dma:
            d = self.dnext
            self.dnext = (d + 1) % NDS
            sk = ('d', d)
            if self.cnt[sk]:
                need(sk, self.cnt[sk])
        else:
            sk = ('e', eng)
        wl = []
        for k, v in deps.items():
            if (not dma) and k == ('e', eng) and (eng == 'pe' or not SAME_ENGINE_SYNC):
                continue
            if kn.get(k, 0) >= v:
                continue
            kn[k] = v
            wl.append((k, v))
        emb = None
        if wl and EMBED_WAIT and not dma and not noembed:
            emb = wl.pop()
        for k, v in wl:
            E.wait_ge(self.sem[k], v)
            self.nwaits += 1
        ins = fn(E)
        if emb is not None:
            ins.wait_op(self.sem[emb[0]], emb[1], "sem-ge")
        inc = 16 if dma else 1
        self.cnt[sk] += inc
        ins.then_inc(self.sem[sk], inc)
        ev = (sk, self.cnt[sk])
        self.nops += 1
        for key in r:
            st = self._get(key)
            if st[1].get(ev[0], 0) < ev[1]:
                st[1][ev[0]] = ev[1]
        for key in w:
            name, sub = key if isinstance(key, tuple) else (key, None)
            if sub is None:
                self.res[name] = {None: [ev, {}]}
            else:
                st = self._get(key)
                st[0] = ev
                st[1] = {}
        return ev

    def dma(self, out, in_, r=(), w=(), eng='sp', **kw):
        return self.op(eng, lambda e: e.dma_start(out=out, in_=in_, **kw), r=r, w=w, dma=True)

    def barrier(self, engines=('pe', 'act', 'dve', 'pool', 'sp')):
        for eng in engines:
            E = self.engs[eng]
            kn = self.known[eng]
            for k, v in self.cnt.items():
                if v and kn.get(k, 0) < v:
                    kn[k] = v
                    E.wait_ge(self.sem[k], v)
                    self.nwaits += 1


class Ctx:
    pass


_SBN = [0]


def sb(nc, es, name, shape, dt=F32):
    _SBN[0] += 1
    return es.enter_context(nc.sbuf_tensor("%s_u%d" % (name, _SBN[0]), list(shape), dt))


_CLN = [0]


def colload(G, es, dst2d, dkey, src_flat, n):
    nc, P = G.nc, G.P
    _CLN[0] += 1
    k = "cl_stg%d" % _CLN[0]
    stg = sb(nc, es, k, [n, 128])
    P.dma(stg[:], src_flat.rearrange("(j p) -> j p", p=128), w=[k])
    P.op('pe', lambda e: e.transpose(out=G.ps[7][:, 0:n], in_=stg[:, :], identity=G.identF[0:n, 0:n]), r=[k, "ident"], w=[("ps", 7)])
    P.op('dve', lambda e: e.tensor_copy(out=dst2d, in_=G.ps[7][:, 0:n]), r=[("ps", 7)], w=[dkey, ("ps", 7)])


def stage_modvec(G, layers):
    nc, P = G.nc, G.P
    with ExitStack() as es:
        craw = sb(nc, es, "mv_craw", [128, 2, 8])
        cT = sb(nc, es, "mv_cT", [128, 8, 2])
        wt = [sb(nc, es, "mv_w%d" % i, [128, 8, 512]) for i in range(2)]
        bt = [sb(nc, es, "mv_b%d" % i, [2, 512]) for i in range(2)]
        ot = [sb(nc, es, "mv_o%d" % i, [2, 512]) for i in range(2)]
        colload(G, es, craw[:, 0, :], "mv_craw", G.c_d[0, :], 8)
        colload(G, es, craw[:, 1, :], "mv_craw", G.cc_d[0, :], 8)
        for r in range(2):
            P.op('act', lambda e, r=r: e.activation(out=cT[:, :, r], in_=craw[:, r, :], func=AF.Silu),
                 r=["mv_craw"], w=[("mv_cT", r)])
        i = 0
        for l in layers:
            for nb in range(12):
                b = i % 2
                i += 1
                P.dma(wt[b][:], G.Wl("mod_w", l)[:, nb * 512:(nb + 1) * 512].rearrange("(c p) n -> p c n", p=128),
                      w=[("mv_w", b)])
                P.dma(bt[b][:], G.Wl("mod_b", l).rearrange("(o n) -> o n", o=1)[:, nb * 512:(nb + 1) * 512].partition_broadcast(2), w=[("mv_b", b)])
                ps = G.ps[b]
                for c in range(8):
                    P.op('pe', lambda e, c=c, b=b, ps=ps: e.matmul(ps[0:2, :], lhsT=cT[:, c, :], rhs=wt[b][:, c, :],
                                                                  start=(c == 0), stop=(c == 7)),
                         r=["mv_cT", ("mv_w", b)], w=[("ps", b)])
                P.op('dve', lambda e, b=b, ps=ps: e.tensor_tensor(out=ot[b][:], in0=ps[0:2, :], in1=bt[b][:], op=ALU.add),
                     r=[("ps", b), ("mv_b", b)], w=[("mv_o", b), ("ps", b)])
                P.dma(G.modv[l, :, nb * 512:(nb + 1) * 512], ot[b][:], r=[("mv_o", b)], w=[("modv", l)])
        P.barrier()


def load_modcols(G, es, l, name):
    nc, P = G.nc, G.P
    mc = sb(nc, es, name, [128, 2, 6, 8])
    for r in range(2):
        _CLN[0] += 1
        k = "cl_stg%d" % _CLN[0]
        stg = sb(nc, es, k, [48, 128])
        P.dma(stg[:], G.modv[l, r, :].rearrange("(j p) -> j p", p=128), r=[("modv", l)], w=[k])
        P.op('pe', lambda e, stg=stg: e.transpose(out=G.ps[7][:, 0:48], in_=stg[:, :], identity=G.identF[0:48, 0:48]), r=[k, "ident"], w=[("ps", 7)])
        P.op('dve', lambda e, r=r: e.tensor_copy(out=mc[:, r, :, :].rearrange("p i c -> p (i c)"), in_=G.ps[7][:, 0:48]), r=[("ps", 7)], w=[name, ("ps", 7)])
    for r in range(2):
        for i in (1, 4):
            P.op('dve', lambda e, r=r, i=i: e.tensor_scalar_add(out=mc[:, r, i, :], in0=mc[:, r, i, :], scalar1=1.0),
                 r=[name], w=[name])
    return mc


def transpose_modulate(G, xt, xkey, hT, hkey, col0, mc, r, ish, isc, pbase):
    P = G.P
    for half in range(2):
        pb = pbase + half
        ps = G.ps[pb]
        for j in range(4):
            c = half * 4 + j
            P.op('pe', lambda e, c=c, j=j, ps=ps: e.transpose(out=ps[:, j * 128:(j + 1) * 128], in_=xt[:, c * 128:(c + 1) * 128],
                                                           identity=G.identF[:]),
                 r=[xkey, "ident"], w=[("ps", pb)])
        for j in range(4):
            c = half * 4 + j
            eng = 'act' if j % 2 == 0 else 'dve'
            if eng == 'act':
                P.op('act', lambda e, c=c, j=j, ps=ps: e.activation(out=hT[:, c, col0:col0 + 128], in_=ps[:, j * 128:(j + 1) * 128],
                                                                   func=AF.Identity, scale=mc[:, r, isc, c:c + 1], bias=mc[:, r, ish, c:c + 1]),
                     r=[("ps", pb), "mc"], w=[hkey, ("ps", pb)])
            else:
                P.op('dve', lambda e, c=c, j=j, ps=ps: e.tensor_scalar(out=hT[:, c, col0:col0 + 128], in0=ps[:, j * 128:(j + 1) * 128],
                                                                      scalar1=mc[:, r, isc, c:c + 1], scalar2=mc[:, r, ish, c:c + 1],
                                                                      op0=ALU.mult, op1=ALU.add),
                     r=[("ps", pb), "mc"], w=[hkey, ("ps", pb)])


def ln_epilogue(G, o_banks, okeys, xt, xkey, Gt, LGt, LBt, bkeys, tmp, tkey, st, skey, yt, ykey):
    P = G.P
    for h in range(2):
        sl = slice(h * 512, (h + 1) * 512)
        P.op('dve', lambda e, h=h, sl=sl: e.tensor_tensor(out=tmp[:, sl], in0=o_banks[h][:, :], in1=Gt[:, sl], op=ALU.mult),
             r=[okeys[h]] + bkeys, w=[tkey, okeys[h]])
    P.op('dve', lambda e: e.scalar_tensor_tensor(out=tmp[:, :], in0=xt[:, :], scalar=ALPHA, in1=tmp[:, :], op0=ALU.mult, op1=ALU.add),
         r=[xkey, tkey], w=[tkey])
    for h in range(2):
        P.op('dve', lambda e, h=h: e.bn_stats(out=st[:, h * 6:(h + 1) * 6], in_=tmp[:, h * 512:(h + 1) * 512]), r=[tkey], w=[skey])
    P.op('dve', lambda e: e.bn_aggr(out=st[:, 12:14], in_=st[:, 0:12]), r=[skey], w=[skey])
    P.op('act', lambda e: e.activation(out=st[:, 14:15], in_=st[:, 13:14], func=AF.Sqrt, bias=G.eps_ln[:, 0:1], scale=1.0), r=[skey, "consts"], w=[skey])
    P.op('dve', lambda e: e.reciprocal(out=st[:, 15:16], in_=st[:, 14:15]), r=[skey], w=[skey])
    P.op('dve', lambda e: e.tensor_scalar(out=tmp[:, :], in0=tmp[:, :], scalar1=st[:, 12:13], scalar2=st[:, 15:16],
                                          op0=ALU.subtract, op1=ALU.mult), r=[skey, tkey], w=[tkey])
    P.op('pool', lambda e: e.tensor_tensor(out=tmp[:, :], in0=tmp[:, :], in1=LGt[:, :], op=ALU.mult), r=[tkey] + bkeys, w=[tkey])
    P.op('pool', lambda e: e.tensor_tensor(out=yt[:, :], in0=tmp[:, :], in1=LBt[:, :], op=ALU.add), r=[tkey] + bkeys, w=[ykey])


def load_bcast(G, tile, key, src_row):
    G.P.dma(tile[:], src_row.partition_broadcast(128), w=[key])


def stage_ffn(G, l):
    nc, P = G.nc, G.P
    W1, W3, W2 = G.Wl("ffn_w1", l), G.Wl("ffn_w3", l), G.Wl("ffn_w2", l)
    with ExitStack() as es:
        mc = load_modcols(G, es, l, "mc")
        gate = [sb(nc, es, "f_gate%d" % r, [128, 1024]) for r in range(2)]
        LG = sb(nc, es, "f_lg", [128, 1024])
        LB = sb(nc, es, "f_lb", [128, 1024])
        for r in range(2):
            P.dma(gate[r][:], G.modv[l, r:r + 1, 5 * 1024:6 * 1024].partition_broadcast(128), r=[("modv", l)], w=["f_bc"])
        P.dma(LG[:], G.Wl("ln2_g", l).rearrange("(o n) -> o n", o=1).partition_broadcast(128), w=["f_bc"])
        P.dma(LB[:], G.Wl("ln2_b", l).rearrange("(o n) -> o n", o=1).partition_broadcast(128), w=["f_bc"])
        w2b = sb(nc, es, "f_w2b", [128, NFC, 1024], BF16)
        w2s = [sb(nc, es, "f_w2s%d" % i, [128, 2, 1024]) for i in range(2)]
        for i in range(NFC // 2):
            b = i % 2
            P.dma(w2s[b][:], W2[i * 256:(i + 1) * 256, :].rearrange("(c p) n -> p c n", p=128), w=[("f_w2s", b)])
            P.op('pool', lambda e, i=i, b=b: e.tensor_copy(out=w2b[:, 2 * i:2 * i + 2, :], in_=w2s[b][:]),
                 r=[("f_w2s", b)], w=[("f_w2b", i)])
        NH = 2
        TPH = NT // NH
        TOKH = TPH * 128
        NTB = TOKH // 384
        hT = sb(nc, es, "f_hT", [128, 8, TOKH], BF16)
        gT = sb(nc, es, "f_gT", [128, NFC, TOKH], BF16)
        xt = [sb(nc, es, "f_x%d" % i, [128, 1024]) for i in range(2)]
        w1s = [sb(nc, es, "f_w1s%d" % i, [128, 8, 128]) for i in range(2)]
        w3s = [sb(nc, es, "f_w3s%d" % i, [128, 8, 128]) for i in range(2)]
        w1b = [sb(nc, es, "f_w1b%d" % i, [128, 8, 128], BF16) for i in range(2)]
        w3b = [sb(nc, es, "f_w3b%d" % i, [128, 8, 128], BF16) for i in range(2)]
        sa = [sb(nc, es, "f_sa%d" % i, [128, 384]) for i in range(2)]
        tmp = [sb(nc, es, "f_tmp%d" % i, [128, 1024]) for i in range(2)]
        yt = [sb(nc, es, "f_y%d" % i, [128, 1024]) for i in range(2)]
        st = [sb(nc, es, "f_st%d" % i, [128, 16]) for i in range(2)]
        xi = 0
        wi = 0
        for hf in range(NH):
            for tt in range(TPH):
                t = hf * TPH + tt
                b = xi % 2
                xi += 1
                r = 0 if t < 16 else 1
                P.dma(xt[b][:], G.xres[t * 128:(t + 1) * 128, :], r=[("xres", t)], w=[("f_x", b)])
                transpose_modulate(G, xt[b], ("f_x", b), hT, ("f_hT", tt), tt * 128, mc, r, 3, 4, 0)
            for cb in range(NFC):
                b = wi % 2
                wi += 1
                P.dma(w1s[b][:], W1[:, cb * 128:(cb + 1) * 128].rearrange("(c p) n -> p c n", p=128), w=[("f_w1s", b)])
                P.dma(w3s[b][:], W3[:, cb * 128:(cb + 1) * 128].rearrange("(c p) n -> p c n", p=128), w=[("f_w3s", b)])
                P.op('pool', lambda e, b=b: e.tensor_copy(out=w1b[b][:], in_=w1s[b][:]), r=[("f_w1s", b)], w=[("f_w1b", b)])
                P.op('pool', lambda e, b=b: e.tensor_copy(out=w3b[b][:], in_=w3s[b][:]), r=[("f_w3s", b)], w=[("f_w3b", b)])
                for sub in range(1):
                    fc = cb
                    for tb in range(NTB):
                        tsl = slice(tb * 384, (tb + 1) * 384)
                        k = (fc * NTB + tb) % 2
                        pa, pb_ = 2 + 2 * k, 3 + 2 * k
                        for c in range(8):
                            P.op('pe', lambda e, c=c, pa=pa, b=b, sub=sub, tsl=tsl: e.matmul(
                                G.ps[pa][:, 0:384], lhsT=w1b[b][:, c, sub * 128:(sub + 1) * 128], rhs=hT[:, c, tsl],
                                start=(c == 0), stop=(c == 7)), r=[("f_w1b", b), "f_hT"], w=[("ps", pa)])
                        for c in range(8):
                            P.op('pe', lambda e, c=c, pb_=pb_, b=b, sub=sub, tsl=tsl: e.matmul(
                                G.ps[pb_][:, 0:384], lhsT=w3b[b][:, c, sub * 128:(sub + 1) * 128], rhs=hT[:, c, tsl],
                                start=(c == 0), stop=(c == 7)), r=[("f_w3b", b), "f_hT"], w=[("ps", pb_)])
                        P.op('act', lambda e, k=k, pa=pa: e.activation(out=sa[k][:, :], in_=G.ps[pa][:, 0:384], func=AF.Silu),
                             r=[("ps", pa)], w=[("f_sa", k), ("ps", pa)])
                        P.op('dve', lambda e, k=k, pb_=pb_, fc=fc, tsl=tsl: e.tensor_tensor(
                            out=gT[:, fc, tsl], in0=G.ps[pb_][:, 0:384], in1=sa[k][:, :], op=ALU.mult),
                            r=[("ps", pb_), ("f_sa", k)], w=[("f_gT", fc), ("ps", pb_)])
            for tt in range(TPH):
                t = hf * TPH + tt
                b = xi % 2
                xi += 1
                r = 0 if t < 16 else 1
                P.dma(xt[b][:], G.xres[t * 128:(t + 1) * 128, :], r=[("xres", t)], w=[("f_x", b)])
                k = tt % 2
                banks = [G.ps[0 + 2 * k], G.ps[1 + 2 * k]]
                okeys = [("ps", 0 + 2 * k), ("ps", 1 + 2 * k)]
                for h in range(2):
                    for fc in range(NFC):
                        P.op('pe', lambda e, h=h, fc=fc, tt=tt, banks=banks: e.matmul(
                            banks[h][:, :], lhsT=gT[:, fc, tt * 128:(tt + 1) * 128], rhs=w2b[:, fc, h * 512:(h + 1) * 512],
                            start=(fc == 0), stop=(fc == NFC - 1)), r=["f_gT", "f_w2b"], w=[okeys[h]])
                ln_epilogue(G, banks, okeys, xt[b], ("f_x", b), gate[r], LG, LB, ["f_bc"], tmp[k], ("f_tmp", k),
                            st[k], ("f_st", k), yt[k], ("f_y", k))
                P.dma(G.xres[t * 128:(t + 1) * 128, :], yt[k][:], r=[("f_y", k)], w=[("xres", t)])
        P.barrier()


def row(ap):
    return ap.rearrange("(o n) -> o n", o=1)


def inproj_block(G, es_w, Wap, j0, ncols, hT, hkey, wtag):
    nc, P = G.nc, G.P
    ws = sb(nc, es_w, wtag + "_s", [128, 8, ncols])
    wb = sb(nc, es_w, wtag + "_b", [128, 8, ncols], BF16)
    P.dma(ws[:], Wap[:, j0:j0 + ncols].rearrange("(c p) n -> p c n", p=128), w=[wtag + "_s"])
    P.op('pool', lambda e: e.tensor_copy(out=wb[:], in_=ws[:]), r=[wtag + "_s"], w=[wtag + "_b"])
    return wb


def rope_evac(G, ps, pkey, t, qr, qkey, cosT, sinT, ckey, tmp, tkey, half, scale=None):
    P = G.P
    ng = 512 // (2 * half)
    if t >= 16:
        if scale is None:
            P.op('act', lambda e: e.copy(out=qr[:, :], in_=ps[:, :]), r=[pkey], w=[qkey, pkey])
        else:
            P.op('act', lambda e: e.mul(out=qr[:, :], in_=ps[:, :], mul=scale), r=[pkey], w=[qkey, pkey])
        return
    pv = ps[:, :].rearrange("p (g two d) -> p g two d", g=ng, two=2)
    qv = qr[:, :].rearrange("p (g two d) -> p g two d", g=ng, two=2)
    x1, x2 = pv[:, :, 0, :], pv[:, :, 1, :]
    cs = cosT[:, :].rearrange("p (g d) -> p g d", g=ng)
    sn = sinT[:, :].rearrange("p (g d) -> p g d", g=ng)
    t1 = tmp[:, 0:256].rearrange("p (g d) -> p g d", g=ng)
    t2 = tmp[:, 256:512].rearrange("p (g d) -> p g d", g=ng)
    P.op('dve', lambda e: e.tensor_tensor(out=t1, in0=x1, in1=cs, op=ALU.mult), r=[pkey, ckey], w=[tkey])
    P.op('dve', lambda e: e.tensor_tensor(out=t2, in0=x2, in1=sn, op=ALU.mult), r=[pkey, ckey], w=[tkey])
    P.op('dve', lambda e: e.tensor_tensor(out=qv[:, :, 0, :], in0=t1, in1=t2, op=ALU.subtract), r=[tkey], w=[qkey])
    P.op('dve', lambda e: e.tensor_tensor(out=t1, in0=x1, in1=sn, op=ALU.mult), r=[pkey, ckey], w=[tkey])
    P.op('dve', lambda e: e.tensor_tensor(out=t2, in0=x2, in1=cs, op=ALU.mult), r=[pkey, ckey], w=[tkey, pkey])
    P.op('dve', lambda e: e.tensor_tensor(out=qv[:, :, 1, :], in0=t1, in1=t2, op=ALU.add), r=[tkey], w=[qkey])
    if scale is not None:
        P.op('act', lambda e: e.mul(out=qr[:, :], in_=qr[:, :], mul=scale), r=[qkey], w=[qkey])


def transpose4(G, src, skey, dstT, dkey, t, pbank):
    P = G.P
    psb = G.ps[pbank][:, 0:256].bitcast(BF16)
    for g in range(4):
        P.op('pe', lambda e, g=g: e.transpose(out=psb[:, g * 128:(g + 1) * 128], in_=src[:, g * 128:(g + 1) * 128], identity=G.identB[:]),
             r=[skey, "identB"], w=[("ps", pbank)])
    P.op('act', lambda e: e.copy(out=dstT[:, :, t * 128:(t + 1) * 128], in_=psb.rearrange("p (g n) -> p g n", g=4)),
         r=[("ps", pbank)], w=[dkey, ("ps", pbank)])


def stage_even(G, l):
    nc, P = G.nc, G.P
    e = l // 2
    lam_init = 0.8 - 0.6 * math.exp(-0.3 * l)
    Win, Wout = G.Wl("ev_w_in", e), G.Wl("ev_w_out", e)
    mo_d = G.mo_d
    with ExitStack() as es:
        mc = load_modcols(G, es, l, "mc")
        hT = sb(nc, es, "e_hT", [128, 8, T], BF16)
        xt = [sb(nc, es, "e_x%d" % i, [128, 1024]) for i in range(2)]
        for t in range(NT):
            b = t % 2
            P.dma(xt[b][:], G.xres[t * 128:(t + 1) * 128, :], r=[("xres", t)], w=[("e_x", b)])
            transpose_modulate(G, xt[b], ("e_x", b), hT, ("e_hT", t), t * 128, mc, 0 if t < 16 else 1, 0, 1, 0)
        cosT = [sb(nc, es, "e_cos%d" % i, [128, 256]) for i in range(2)]
        sinT = [sb(nc, es, "e_sin%d" % i, [128, 256]) for i in range(2)]
        qr = [sb(nc, es, "e_qr%d" % i, [128, 512], BF16) for i in range(2)]
        rtmp = [sb(nc, es, "e_rtmp%d" % i, [128, 512]) for i in range(2)]

        def proj_tile(wb, wkey, t, pbank):
            ps = G.ps[pbank]
            for c in range(8):
                P.op('pe', lambda e_, c=c: e_.matmul(ps[:, :], lhsT=hT[:, c, t * 128:(t + 1) * 128], rhs=wb[:, c, :],
                                                    start=(c == 0), stop=(c == 7)), r=[("e_hT", t), wkey], w=[("ps", pbank)])
            return ps

        def load_tables(t, b, which):
            if t < 16:
                P.dma(cosT[b][:], G.KC("k_cos" + which, [TL, 256])[t * 128:(t + 1) * 128, :], w=[("e_cs", b)])
                P.dma(sinT[b][:], G.KC("k_sin" + which, [TL, 256])[t * 128:(t + 1) * 128, :], w=[("e_cs", b)])

        with ExitStack() as esA:
            aqT = sb(nc, esA, "a_qT", [128, 4, T], BF16)
            akT = sb(nc, esA, "a_kT", [128, 4, T], BF16)
            av = sb(nc, esA, "a_v", [128, NT, 512], BF16)
            for j, dst in ((0, aqT), (1, akT)):
                with ExitStack() as esw:
                    wb = inproj_block(G, esw, Win, j * 512, 512, hT, "e_hT", "a_w%d" % j)
                    for t in range(NT):
                        b = t % 2
                        load_tables(t, b, "A")
                        ps = proj_tile(wb, "a_w%d_b" % j, t, 4 + b)
                        rope_evac(G, ps, ("ps", 4 + b), t, qr[b], ("e_qr", b), cosT[b], sinT[b], ("e_cs", b), rtmp[b], ("e_rtmp", b), 32)
                        transpose4(G, qr[b], ("e_qr", b), dst, ("a_T%d" % j, t), t, 6 + b)
                    P.barrier()
            with ExitStack() as esw:
                wb = inproj_block(G, esw, Win, 2 * 512, 512, hT, "e_hT", "a_w2")
                for t in range(NT):
                    b = t % 2
                    ps = proj_tile(wb, "a_w2_b", t, 4 + b)
                    P.op('act', lambda e_, t=t, ps=ps: e_.copy(out=av[:, t, :], in_=ps[:, :]), r=[("ps", 4 + b)], w=[("a_v", t), ("ps", 4 + b)])
                P.barrier()
            lam4 = sb(nc, esA, "a_lam4", [128, 4, 64])
            lamc = sb(nc, esA, "a_lamc", [128, 8])
            for i, nm in enumerate(("da_lam_q1", "da_lam_k1", "da_lam_q2", "da_lam_k2")):
                P.dma(lam4[:, i, :], row(G.Wl(nm, e)).partition_broadcast(128), w=["a_lam4"])
            for i in range(2):
                P.op('dve', lambda e_, i=i: e_.tensor_tensor(out=lam4[:, 2 * i, :], in0=lam4[:, 2 * i, :], in1=lam4[:, 2 * i + 1, :], op=ALU.mult),
                     r=["a_lam4"], w=["a_lam4"])
                P.op('dve', lambda e_, i=i: e_.reduce_sum(out=lamc[:, i:i + 1], in_=lam4[:, 2 * i, :], axis=AX.X), r=["a_lam4"], w=["a_lamc"])
            P.op('act', lambda e_: e_.activation(out=lamc[:, 2:4], in_=lamc[:, 0:2], func=AF.Exp), r=["a_lamc"], w=["a_lamc"])
            P.op('dve', lambda e_: e_.tensor_tensor(out=lamc[:, 4:5], in0=lamc[:, 3:4], in1=lamc[:, 2:3], op=ALU.subtract), r=["a_lamc"], w=["a_lamc"])
            P.op('dve', lambda e_: e_.tensor_scalar_add(out=lamc[:, 5:6], in0=lamc[:, 4:5], scalar1=-lam_init), r=["a_lamc"], w=["a_lamc"])
            neglam = lamc[:, 5:6]
            Pm = [sb(nc, esA, "a_P%d" % i, [128, T], BF16) for i in range(2)]
            PT = [sb(nc, esA, "a_PT%d" % i, [128, NT, 128], BF16) for i in range(2)]
            sm = [sb(nc, esA, "a_sm%d" % i, [128, 16]) for i in range(2)]
            ao = sb(nc, esA, "a_ao", [128, 512])
            ao2 = sb(nc, esA, "a_ao2", [128, 512])
            aob = [sb(nc, esA, "a_aob%d" % i, [128, 512], BF16) for i in range(2)]
            rs = sb(nc, esA, "a_rs", [128, 16])
            SC = 64 ** -0.5
            def kinfo(qt):
                ktiles = list(range(NT)) if qt < 16 else [16, 17]
                k0 = ktiles[0] * 128
                nk = len(ktiles) * 128
                return ktiles, k0, nk, (nk + 511) // 512

            def stage1(ui, qt, h, m):
                ktiles, k0, nk, nbk = kinfo(qt)
                pb_ = ui % 2
                smt, Pmt = sm[pb_], Pm[pb_]
                psl = slice(m * 64, (m + 1) * 64)
                for jb in range(nbk):
                    w_ = min(512, nk - jb * 512)
                    P.op('pe', lambda e_, jb=jb, w_=w_: e_.matmul(G.ps[jb][:, 0:w_], lhsT=aqT[psl, h, qt * 128:(qt + 1) * 128],
                                                                 rhs=akT[psl, h, k0 + jb * 512:k0 + jb * 512 + w_], start=True, stop=True),
                         r=[("a_T0", qt), "a_T1"], w=[("ps", jb)])
                for jb in range(nbk):
                    w_ = min(512, nk - jb * 512)
                    P.op('dve', lambda e_, jb=jb, w_=w_: e_.reduce_max(out=smt[:, jb:jb + 1], in_=G.ps[jb][:, 0:w_], axis=AX.X),
                         r=[("ps", jb)], w=[("a_sm", pb_), ("ps", jb)])
                P.op('dve', lambda e_: e_.reduce_max(out=smt[:, 6:7], in_=smt[:, 0:nbk], axis=AX.X), r=[("a_sm", pb_)], w=[("a_sm", pb_)])
                P.op('dve', lambda e_: e_.tensor_scalar_mul(out=smt[:, 7:8], in0=smt[:, 6:7], scalar1=-SC), r=[("a_sm", pb_)], w=[("a_sm", pb_)])
                for jb in range(nbk):
                    w_ = min(512, nk - jb * 512)
                    P.op('act', lambda e_, jb=jb, w_=w_: e_.activation(out=Pmt[:, jb * 512:jb * 512 + w_], in_=G.ps[jb][:, 0:w_], func=AF.Exp,
                                                                      scale=SC, bias=smt[:, 7:8], accum_out=smt[:, 8 + jb:9 + jb]),
                         r=[("ps", jb), ("a_sm", pb_)], w=[("a_P", pb_), ("a_sm", pb_), ("ps", jb)], noembed=True)
                P.op('act', lambda e_: e_.copy(out=smt[:, 0:nbk], in_=smt[:, 8:8 + nbk]), r=[("a_sm", pb_)], w=[("a_sm", pb_)])
                P.op('dve', lambda e_: e_.reduce_sum(out=smt[:, 14:15], in_=smt[:, 0:nbk], axis=AX.X), r=[("a_sm", pb_)], w=[("a_sm", pb_)])
                P.op('dve', lambda e_: e_.reciprocal(out=smt[:, 15:16], in_=smt[:, 14:15]), r=[("a_sm", pb_)], w=[("a_sm", pb_)])

            def stage2(ui, qt, h, m):
                ktiles, k0, nk, nbk = kinfo(qt)
                pb_ = ui % 2
                smt, Pmt, PTt = sm[pb_], Pm[pb_], PT[pb_]
                nkt = len(ktiles)
                for g0 in range(0, nkt, 8):
                    gb = 5 + (g0 // 8) % 2
                    psb = G.ps[gb][:, :].bitcast(BF16)
                    n_ = min(8, nkt - g0)
                    for i in range(n_):
                        P.op('pe', lambda e_, i=i, g0=g0, psb=psb: e_.transpose(out=psb[:, i * 128:(i + 1) * 128],
                                                                             in_=Pmt[:, (g0 + i) * 128:(g0 + i + 1) * 128], identity=G.identB[:]),
                             r=[("a_P", pb_), "identB"], w=[("ps", gb)])
                    evac_eng = 'dve' if (g0 // 8) % 2 == 0 else 'act'
                    if evac_eng == 'dve':
                        P.op('dve', lambda e_, g0=g0, n_=n_, psb=psb: e_.tensor_copy(out=PTt[:, g0:g0 + n_, :],
                                                                                     in_=psb[:, 0:n_ * 128].rearrange("p (g n) -> p g n", g=n_)),
                             r=[("ps", gb)], w=[("a_PT", pb_), ("ps", gb)])
                    else:
                        P.op('act', lambda e_, g0=g0, n_=n_, psb=psb: e_.copy(out=PTt[:, g0:g0 + n_, :],
                                                                              in_=psb[:, 0:n_ * 128].rearrange("p (g n) -> p g n", g=n_)),
                             r=[("ps", gb)], w=[("a_PT", pb_), ("ps", gb)])
                osl = slice(m * 128, (m + 1) * 128)
                for i, kt in enumerate(ktiles):
                    P.op('pe', lambda e_, i=i, kt=kt: e_.matmul(G.ps[7][:, osl], lhsT=PTt[:, i, :], rhs=av[:, kt, h * 128:(h + 1) * 128],
                                                               start=(i == 0), stop=(i == nkt - 1)),
                         r=[("a_PT", pb_), "a_v"], w=[("ps", 7)])
                if m == 0:
                    P.op('dve', lambda e_: e_.tensor_scalar(out=ao2[:, 0:128], in0=G.ps[7][:, 0:128], scalar1=smt[:, 15:16], scalar2=None, op0=ALU.mult),
                         r=[("ps", 7), ("a_sm", pb_)], w=["a_ao2", ("ps", 7)])
                else:
                    P.op('dve', lambda e_: e_.tensor_tensor(out=smt[:, 13:14], in0=smt[:, 15:16], in1=neglam, op=ALU.mult),
                         r=[("a_sm", pb_), "a_lamc"], w=[("a_sm", pb_)])
                    P.op('dve', lambda e_: e_.scalar_tensor_tensor(out=ao[:, h * 128:(h + 1) * 128], in0=G.ps[7][:, 128:256], scalar=smt[:, 13:14],
                                                                  in1=ao2[:, 0:128], op0=ALU.mult, op1=ALU.add),
                         r=[("ps", 7), ("a_sm", pb_), "a_ao2"], w=["a_ao", ("ps", 7)])
                if h == 3 and m == 1:
                    P.op('pool', lambda e_: e_.tensor_tensor(out=ao2[:, :], in0=ao[:, :], in1=ao[:, :], op=ALU.mult), r=["a_ao"], w=["a_ao2"])
                    P.op('dve', lambda e_: e_.reduce_sum(out=rs[:, 0:4], in_=ao2[:, :].rearrange("p (h d) -> p h d", h=4), axis=AX.X), r=["a_ao2"], w=["a_rs"])
                    P.op('act', lambda e_: e_.activation(out=rs[:, 4:8], in_=rs[:, 0:4], func=AF.Sqrt, scale=1.0 / 128, bias=G.eps_ln[:, 0:1]), r=["a_rs", "consts"], w=["a_rs"])
                    P.op('dve', lambda e_: e_.reciprocal(out=rs[:, 8:12], in_=rs[:, 4:8]), r=["a_rs"], w=["a_rs"])
                    ab = aob[qt % 2]
                    for hh in range(4):
                        P.op('pool', lambda e_, hh=hh: e_.tensor_scalar(out=ab[:, hh * 128:(hh + 1) * 128], in0=ao[:, hh * 128:(hh + 1) * 128],
                                                                      scalar1=rs[:, 8 + hh:9 + hh], scalar2=None, op0=ALU.mult),
                             r=["a_ao", "a_rs"], w=[("a_aob", qt % 2)])
                    P.dma(mo_d[qt * 128:(qt + 1) * 128, 0:512], ab[:, :], r=[("a_aob", qt % 2)], w=[("mo", (qt, 0))])

            units = [(qt, h, m) for qt in range(NT) for h in range(4) for m in range(2)]
            for ui, u_ in enumerate(units):
                stage1(ui, *u_)
                if ui > 0:
                    stage2(ui - 1, *units[ui - 1])
            stage2(len(units) - 1, *units[-1])
            P.barrier()
        with ExitStack() as esB:
            bqT = sb(nc, esB, "b_qT", [128, 4, T], BF16)
            bkT = sb(nc, esB, "b_kT", [128, 4, T], BF16)
            bk = sb(nc, esB, "b_k", [128, NT, 512], BF16)
            bv = sb(nc, esB, "b_v", [128, NT, 512], BF16)
            for j in (3, 4):
                with ExitStack() as esw:
                    wb = inproj_block(G, esw, Win, j * 512, 512, hT, "e_hT", "b_w%d" % j)
                    for t in range(NT):
                        b = t % 2
                        load_tables(t, b, "B")
                        ps = proj_tile(wb, "b_w%d_b" % j, t, 4 + b)
                        if j == 3:
                            rope_evac(G, ps, ("ps", 4 + b), t, qr[b], ("e_qr", b), cosT[b], sinT[b], ("e_cs", b), rtmp[b], ("e_rtmp", b), 64)
                            transpose4(G, qr[b], ("e_qr", b), bqT, ("b_qT", t), t, 6 + b)
                        else:
                            rope_evac(G, ps, ("ps", 4 + b), t, bk[:, t, :], ("b_k", t), cosT[b], sinT[b], ("e_cs", b), rtmp[b], ("e_rtmp", b), 64,
                                      scale=128 ** -0.5)
                            transpose4(G, bk[:, t, :], ("b_k", t), bkT, ("b_kT", t), t, 6 + b)
                    P.barrier()
            with ExitStack() as esw:
                wb = inproj_block(G, esw, Win, 5 * 512, 512, hT, "e_hT", "b_w5")
                for t in range(NT):
                    b = t % 2
                    ps = proj_tile(wb, "b_w5_b", t, 4 + b)
                    P.op('act', lambda e_, t=t, ps=ps: e_.copy(out=bv[:, t, :], in_=ps[:, :]), r=[("ps", 4 + b)], w=[("b_v", t), ("ps", 4 + b)])
                P.barrier()
            dc = sb(nc, esB, "b_dc", [128, 64])
            kcol = sb(nc, esB, "b_kcol", [128, 4])
            kmat = sb(nc, esB, "b_kmat", [128, 4, 128])
            Mm = sb(nc, esB, "b_M", [128, 4, 128])
            Mt = sb(nc, esB, "b_Mt", [128, 128])
            P.dma(dc[:, 0:8], G.Wl("rt_decay_logit", e).rearrange("(o a) b -> o (a b)", o=1).partition_broadcast(128), w=["b_dc"])
            P.dma(kcol[:], G.KC("k_cols", [128, 4])[:, :], w=["b_kc"])
            P.dma(kmat[:], G.KC("k_mats", [128, 4, 128])[:, :, :], w=["b_kc"])
            P.op('act', lambda e_: e_.activation(out=dc[:, 8:16], in_=dc[:, 0:8], func=AF.Sigmoid), r=["b_dc"], w=["b_dc"])
            P.op('act', lambda e_: e_.activation(out=dc[:, 16:24], in_=dc[:, 8:16], func=AF.Ln), r=["b_dc"], w=["b_dc"])
            lg = lambda d, h: dc[:, 16 + d * 4 + h:17 + d * 4 + h]
            P.op('act', lambda e_: e_.activation(out=dc[:, 24:32], in_=dc[:, 16:24], func=AF.Exp, scale=128.0), r=["b_dc"], w=["b_dc"])
            for h in range(4):
                for (o_, col, d) in ((32, 0, 0), (36, 1, 1), (40, 2, 0), (44, 3, 1)):
                    P.op('act', lambda e_, o_=o_, col=col, d=d, h=h: e_.activation(out=dc[:, o_ + h:o_ + h + 1], in_=kcol[:, col:col + 1], func=AF.Exp, scale=lg(d, h)),
                         r=["b_dc", "b_kc"], w=["b_dc"])
                P.op('act', lambda e_, h=h: e_.activation(out=Mt[:, :], in_=kmat[:, 0, :], func=AF.Exp, scale=lg(0, h)), r=["b_dc", "b_kc"], w=["b_Mt"])
                P.op('dve', lambda e_, h=h: e_.tensor_tensor(out=Mm[:, h, :], in0=Mt[:, :], in1=kmat[:, 2, :], op=ALU.mult), r=["b_Mt", "b_kc"], w=["b_M"])
                P.op('act', lambda e_, h=h: e_.activation(out=Mt[:, :], in_=kmat[:, 1, :], func=AF.Exp, scale=lg(1, h)), r=["b_dc", "b_kc"], w=["b_Mt"])
                P.op('dve', lambda e_, h=h: e_.tensor_tensor(out=Mt[:, :], in0=Mt[:, :], in1=kmat[:, 3, :], op=ALU.mult), r=["b_Mt", "b_kc"], w=["b_Mt"])
                P.op('dve', lambda e_, h=h: e_.tensor_tensor(out=Mm[:, h, :], in0=Mm[:, h, :], in1=Mt[:, :], op=ALU.add), r=["b_Mt", "b_M"], w=["b_M"])
            SfA = sb(nc, esB, "b_SfA", [128, NT, 512], BF16)
            Sst = sb(nc, esB, "b_S", [128, 512])
            Sbb = sb(nc, esB, "b_Sbb", [128, 512], BF16)
            kz = [sb(nc, esB, "b_kz%d" % i, [128, 512], BF16) for i in range(2)]

            def state_update(n, d, i):
                zb = kz[i % 2]
                for h in range(4):
                    P.op('dve', lambda e_, h=h: e_.tensor_scalar(out=zb[:, h * 128:(h + 1) * 128], in0=bk[:, n, h * 128:(h + 1) * 128],
                                                                scalar1=dc[:, 32 + 4 * d + h:33 + 4 * d + h], scalar2=None, op0=ALU.mult),
                         r=[("b_k", n), "b_dc"], w=[("b_kz", i % 2)])
                for h in range(4):
                    P.op('pe', lambda e_, h=h: e_.matmul(G.ps[3][:, h * 128:(h + 1) * 128], lhsT=zb[:, h * 128:(h + 1) * 128],
                                                        rhs=bv[:, n, h * 128:(h + 1) * 128], start=True, stop=True),
                         r=[("b_kz", i % 2), ("b_v", n)], w=[("ps", 3)])
                for h in range(4):
                    P.op('dve', lambda e_, h=h: e_.scalar_tensor_tensor(out=Sst[:, h * 128:(h + 1) * 128], in0=Sst[:, h * 128:(h + 1) * 128],
                                                                       scalar=dc[:, 24 + 4 * d + h:25 + 4 * d + h], in1=G.ps[3][:, h * 128:(h + 1) * 128],
                                                                       op0=ALU.mult, op1=ALU.add),
                         r=["b_S", "b_dc", ("ps", 3)], w=["b_S", ("ps", 3)])
            P.op('dve', lambda e_: e_.memset(Sst[:, :], 0.0), w=["b_S"])
            fwd = [16, 17] + list(range(16))
            for i, n in enumerate(fwd):
                P.op('act', lambda e_, n=n: e_.copy(out=SfA[:, n, :], in_=Sst[:, :]), r=["b_S"], w=[("b_SfA", n)])
                if i < len(fwd) - 1:
                    state_update(n, 0, i)
            P.op('dve', lambda e_: e_.memset(Sst[:, :], 0.0), r=["b_SfA"], w=["b_S"])
            with ExitStack() as esw:
                wg = inproj_block(G, esw, Win, 6 * 512, 512, hT, "e_hT", "b_w6")
                Sm = [sb(nc, esw, "b_Sm%d" % i, [128, 4, 128], BF16) for i in range(2)]
                bo = [sb(nc, esw, "b_o%d" % i, [128, 512]) for i in range(2)]
                bo2 = [sb(nc, esw, "b_o2%d" % i, [128, 512]) for i in range(2)]
                gs = [sb(nc, esw, "b_gs%d" % i, [128, 512]) for i in range(2)]
                bob = [sb(nc, esw, "b_ob%d" % i, [128, 512], BF16) for i in range(2)]
                brs = [sb(nc, esw, "b_rs%d" % i, [128, 16]) for i in range(2)]
                bwd = [17, 16] + list(range(15, -1, -1))
                for i, n in enumerate(bwd):
                    b = i % 2
                    csl = slice(n * 128, (n + 1) * 128)
                    P.op('act', lambda e_: e_.copy(out=Sbb[:, :], in_=Sst[:, :]), r=["b_S"], w=["b_Sbb"])
                    for h in range(4):
                        P.op('pe', lambda e_, h=h: e_.matmul(G.ps[0][:, h * 128:(h + 1) * 128], lhsT=bkT[:, h, csl], rhs=bqT[:, h, csl], start=True, stop=True),
                             r=[("b_kT", n), ("b_qT", n)], w=[("ps", 0)])
                    P.op('dve', lambda e_, b=b: e_.tensor_tensor(out=Sm[b][:, :, :], in0=G.ps[0][:, :].rearrange("p (h n) -> p h n", h=4), in1=Mm[:, :, :], op=ALU.mult),
                         r=[("ps", 0), "b_M"], w=[("b_Sm", b), ("ps", 0)])
                    for h in range(4):
                        hs = slice(h * 128, (h + 1) * 128)
                        P.op('pe', lambda e_, h=h, hs=hs, b=b: e_.matmul(G.ps[1][:, hs], lhsT=Sm[b][:, h, :], rhs=bv[:, n, hs], start=True, stop=True),
                             r=[("b_Sm", b), ("b_v", n)], w=[("ps", 1)])
                    for h in range(4):
                        hs = slice(h * 128, (h + 1) * 128)
                        P.op('pe', lambda e_, h=h, hs=hs: e_.matmul(G.ps[2][:, hs], lhsT=bqT[:, h, csl], rhs=SfA[:, n, hs], start=True, stop=True),
                             r=[("b_qT", n), ("b_SfA", n)], w=[("ps", 2)])
                    for h in range(4):
                        hs = slice(h * 128, (h + 1) * 128)
                        P.op('pe', lambda e_, h=h, hs=hs: e_.matmul(G.ps[4][:, hs], lhsT=bqT[:, h, csl], rhs=Sbb[:, hs], start=True, stop=True),
                             r=[("b_qT", n), "b_Sbb"], w=[("ps", 4)])
                    for c in range(8):
                        P.op('pe', lambda e_, c=c: e_.matmul(G.ps[5][:, :], lhsT=hT[:, c, csl], rhs=wg[:, c, :], start=(c == 0), stop=(c == 7)),
                             r=[("e_hT", n), "b_w6_b"], w=[("ps", 5)])
                    P.op('act', lambda e_, b=b: e_.activation(out=gs[b][:, :], in_=G.ps[5][:, :], func=AF.Silu), r=[("ps", 5)], w=[("b_gs", b), ("ps", 5)])
                    for h in range(4):
                        hs = slice(h * 128, (h + 1) * 128)
                        P.op('dve', lambda e_, h=h, hs=hs, b=b: e_.tensor_scalar(out=bo2[b][:, hs], in0=G.ps[2][:, hs], scalar1=dc[:, 40 + h:41 + h], scalar2=None, op0=ALU.mult),
                             r=[("ps", 2), "b_dc"], w=[("b_o2", b), ("ps", 2)])
                        P.op('dve', lambda e_, h=h, hs=hs, b=b: e_.scalar_tensor_tensor(out=bo2[b][:, hs], in0=G.ps[4][:, hs], scalar=dc[:, 44 + h:45 + h], in1=bo2[b][:, hs],
                                                                                       op0=ALU.mult, op1=ALU.add),
                             r=[("ps", 4), "b_dc", ("b_o2", b)], w=[("b_o2", b), ("ps", 4)])
                    P.op('dve', lambda e_, b=b: e_.tensor_tensor(out=bo[b][:, :], in0=G.ps[1][:, :], in1=bo2[b][:, :], op=ALU.add),
                         r=[("ps", 1), ("b_o2", b)], w=[("b_o", b), ("ps", 1)])
                    P.op('dve', lambda e_, b=b: e_.tensor_tensor(out=bo2[b][:, :], in0=bo[b][:, :], in1=bo[b][:, :], op=ALU.mult), r=[("b_o", b)], w=[("b_o2", b)])
                    P.op('dve', lambda e_, b=b: e_.reduce_sum(out=brs[b][:, 0:4], in_=bo2[b][:, :].rearrange("p (h d) -> p h d", h=4), axis=AX.X), r=[("b_o2", b)], w=[("b_rs", b)])
                    P.op('act', lambda e_, b=b: e_.activation(out=brs[b][:, 4:8], in_=brs[b][:, 0:4], func=AF.Sqrt, scale=1.0 / 128, bias=G.eps_ln[:, 0:1]), r=[("b_rs", b), "consts"], w=[("b_rs", b)])
                    P.op('dve', lambda e_, b=b: e_.reciprocal(out=brs[b][:, 8:12], in_=brs[b][:, 4:8]), r=[("b_rs", b)], w=[("b_rs", b)])
                    for h in range(4):
                        hs = slice(h * 128, (h + 1) * 128)
                        P.op('dve', lambda e_, h=h, hs=hs, b=b: e_.scalar_tensor_tensor(out=bob[b][:, hs], in0=bo[b][:, hs], scalar=brs[b][:, 8 + h:9 + h], in1=gs[b][:, hs],
                                                                                       op0=ALU.mult, op1=ALU.mult),
                             r=[("b_o", b), ("b_rs", b), ("b_gs", b)], w=[("b_ob", b)])
                    P.dma(mo_d[n * 128:(n + 1) * 128, 512:1024], bob[b][:, :], r=[("b_ob", b)], w=[("mo", (n, 1))])
                    if i < len(bwd) - 1:
                        state_update(n, 1, i)
                P.barrier()
            P.barrier()
        with ExitStack() as esO:
            wos = sb(nc, esO, "o_ws", [128, 8, 1024])
            wob = sb(nc, esO, "o_wb", [128, 8, 1024], BF16)
            gn = sb(nc, esO, "o_gn", [128, 4])
            P.dma(wos[:], Wout[:, :].rearrange("(c p) n -> p c n", p=128), w=["o_ws"])
            colload(G, esO, gn[:, :], "o_gn", G.Wl("da_gn_g", e), 4)
            P.op('dve', lambda e_: e_.tensor_scalar_mul(out=gn[:], in0=gn[:], scalar1=1.0 - lam_init), r=["o_gn"], w=["o_gn"])
            for c in range(8):
                if c < 4:
                    P.op('dve', lambda e_, c=c: e_.tensor_scalar(out=wob[:, c, :], in0=wos[:, c, :], scalar1=gn[:, c:c + 1], scalar2=None, op0=ALU.mult),
                         r=["o_ws", "o_gn"], w=[("o_wb", c)])
                else:
                    P.op('pool', lambda e_, c=c: e_.tensor_copy(out=wob[:, c, :], in_=wos[:, c, :]), r=["o_ws"], w=[("o_wb", c)])
            gate = [sb(nc, esO, "o_gate%d" % r_, [128, 1024]) for r_ in range(2)]
            LG = sb(nc, esO, "o_lg", [128, 1024])
            LB = sb(nc, esO, "o_lb", [128, 1024])
            for r_ in range(2):
                P.dma(gate[r_][:], G.modv[l, r_:r_ + 1, 2 * 1024:3 * 1024].partition_broadcast(128), r=[("modv", l)], w=["o_bc"])
            P.dma(LG[:], row(G.Wl("ln1_g", l)).partition_broadcast(128), w=["o_bc"])
            P.dma(LB[:], row(G.Wl("ln1_b", l)).partition_broadcast(128), w=["o_bc"])
            mot = [sb(nc, esO, "o_mo%d" % i, [128, 1024], BF16) for i in range(2)]
            moT = [sb(nc, esO, "o_moT%d" % i, [128, 8, 128], BF16) for i in range(2)]
            tmp = [sb(nc, esO, "o_tmp%d" % i, [128, 1024]) for i in range(2)]
            yt = [sb(nc, esO, "o_y%d" % i, [128, 1024]) for i in range(2)]
            st = [sb(nc, esO, "o_st%d" % i, [128, 16]) for i in range(2)]
            def ld_o(t_):
                b_ = t_ % 2
                P.dma(mot[b_][:], mo_d[t_ * 128:(t_ + 1) * 128, :], r=[("mo", (t_, 0)), ("mo", (t_, 1))], w=[("o_mo", b_)])
                P.dma(xt[b_][:], G.xres[t_ * 128:(t_ + 1) * 128, :], r=[("xres", t_)], w=[("e_x", b_)])
            for t in range(NT):
                b = t % 2
                r_ = 0 if t < 16 else 1
                if t == 0:
                    ld_o(0)
                if t + 1 < NT:
                    ld_o(t + 1)
                for hh in range(2):
                    pbk = 4 + hh
                    psb = G.ps[pbk][:, 0:256].bitcast(BF16)
                    for g in range(4):
                        c = hh * 4 + g
                        P.op('pe', lambda e_, g=g, c=c, psb=psb: e_.transpose(out=psb[:, g * 128:(g + 1) * 128], in_=mot[b][:, c * 128:(c + 1) * 128], identity=G.identB[:]),
                             r=[("o_mo", b), "identB"], w=[("ps", pbk)])
                    P.op('act', lambda e_, hh=hh, psb=psb: e_.copy(out=moT[b][:, hh * 4:hh * 4 + 4, :], in_=psb.rearrange("p (g n) -> p g n", g=4)),
                         r=[("ps", pbk)], w=[("o_moT", b), ("ps", pbk)])
                banks = [G.ps[0 + 2 * b], G.ps[1 + 2 * b]]
                okeys = [("ps", 0 + 2 * b), ("ps", 1 + 2 * b)]
                for hh in range(2):
                    for c in range(8):
                        P.op('pe', lambda e_, hh=hh, c=c: e_.matmul(banks[hh][:, :], lhsT=moT[b][:, c, :], rhs=wob[:, c, hh * 512:(hh + 1) * 512],
                                                                   start=(c == 0), stop=(c == 7)), r=[("o_moT", b), "o_wb"], w=[okeys[hh]])
                ln_epilogue(G, banks, okeys, xt[b], ("e_x", b), gate[r_], LG, LB, ["o_bc"], tmp[b], ("o_tmp", b), st[b], ("o_st", b), yt[b], ("o_y", b))
                P.dma(G.xres[t * 128:(t + 1) * 128, :], yt[b][:], r=[("o_y", b)], w=[("xres", t)])
            P.barrier()
        P.barrier()


TBLK = [(0, 512), (512, 512), (1024, 512), (1536, 512), (2048, 256)]
RW_GN_EPS = 64e-5


def colvec(G, es, name, ap1024, key):
    t = sb(G.nc, es, name, [128, 8])
    colload(G, es, t[:, :], key, ap1024, 8)
    return t


def load_w_bf16(G, es, name, wap, rows, cols, eng='pool'):
    nc, P = G.nc, G.P
    nck = (rows + 127) // 128
    wb = sb(nc, es, name + "_b", [128, nck, cols], BF16)
    if rows % 128 == 0 and cols > 512:
        with ExitStack() as es2:
            ws = sb(nc, es2, name + "_s", [128, nck, 512])
            for h0 in range(0, cols, 512):
                P.dma(ws[:], wap[:, h0:h0 + 512].rearrange("(c p) n -> p c n", p=128), w=[name + "_s"])
                P.op(eng, lambda e, h0=h0: e.tensor_copy(out=wb[:, :, h0:h0 + 512], in_=ws[:]), r=[name + "_s"], w=[name + "_b"])
            P.barrier()
        return wb
    ws = sb(nc, es, name + "_s", [128, nck, cols])
    if rows % 128 == 0:
        P.dma(ws[:], wap.rearrange("(c p) n -> p c n", p=128), w=[name + "_s"])
        P.op(eng, lambda e: e.tensor_copy(out=wb[:], in_=ws[:]), r=[name + "_s"], w=[name + "_b"])
    else:
        for c in range(nck):
            n_ = min(128, rows - c * 128)
            P.dma(ws[0:n_, c, :], wap[c * 128:c * 128 + n_, :], w=[name + "_s"])
        for c in range(nck):
            n_ = min(128, rows - c * 128)
            P.op(eng, lambda e, c=c, n_=n_: e.tensor_copy(out=wb[0:n_, c, :], in_=ws[0:n_, c, :]), r=[name + "_s"], w=[name + "_b"])
    return wb


def fm_linear(G, xT, xkey, wb, wkey, nout, consumer, kparts=None, pbanks=(0, 1)):
    P = G.P
    if kparts is None:
        kparts = [(c, 128) for c in range(8)]
    it = 0
    for oc in range((nout + 127) // 128):
        m_ = min(128, nout - oc * 128)
        for (t0, tn) in TBLK:
            pb = pbanks[it % len(pbanks)]
            it += 1
            for i, (c, kn) in enumerate(kparts):
                P.op('pe', lambda e, c=c, kn=kn, i=i, pb=pb: e.matmul(G.ps[pb][0:m_, 0:tn], lhsT=wb[0:kn, c, oc * 128:oc * 128 + m_], rhs=xT[0:kn, c, t0:t0 + tn],
                                                                   start=(i == 0), stop=(i == len(kparts) - 1)), r=[xkey, wkey], w=[("ps", pb)])
            consumer(oc, t0, tn, G.ps[pb], ("ps", pb), m_)


F32R = mybir.dt.float32r
NCHAIN = 4
SKIP_SCAN = False
SCAN_F32R = False
STAGGER = 6


def rr(ap):
    return ap.bitcast(F32R) if SCAN_F32R else ap


def scan_stage(G):
    nc, P = G.nc, G.P
    FM = G.fm
    C = 64
    NCH = T // C
    with ExitStack() as esS:
        msk = sb(nc, esS, "s_msk", [128, 3, 128])
        P.dma(msk[:], G.KC("k_smask", [128, 3, 128])[:, :, :], w=["s_msk"])
        ones = sb(nc, esS, "s_ones", [128, 64])
        P.op('dve', lambda e: e.memset(ones[:], 1.0), w=["s_ones"])
        identR = sb(nc, esS, "s_identR", [128, 128])
        P.op('dve', lambda e: e.tensor_copy(out=rr(identR[:]), in_=G.identF[:]), r=["ident"], w=["s_identR"])
        NAMES = ("r", "v", "kk", "lw", "kd", "b")
        bufs = []
        for ci in range(NCHAIN):
            B = Ctx()
            B.src = [sb(nc, esS, "s_src%d_%d" % (ci, i), [128, 6, 256]) for i in range(2)]
            B.bd = sb(nc, esS, "s_bd%d" % ci, [128, 5, 128])
            P.op('pool', lambda e, B=B: e.memset(B.bd[:], 0.0), w=[("s_bd", ci)])
            B.cum = sb(nc, esS, "s_cum%d" % ci, [128, 5, 64])
            B.Am = sb(nc, esS, "s_Am%d" % ci, [128, 5, 128])
            B.Ap = [sb(nc, esS, "s_Ap%d_%d" % (ci, i), [128, 2, 128]) for i in range(2)]
            B.VT = sb(nc, esS, "s_VT%d" % ci, [128, 64])
            B.Z = sb(nc, esS, "s_Z%d" % ci, [128, 2, 64])
            B.BT = sb(nc, esS, "s_BT%d" % ci, [128, 2, 128])
            B.S = sb(nc, esS, "s_S%d" % ci, [128, 64])
            B.Sr = sb(nc, esS, "s_Sr%d" % ci, [128, 64])
            B.yo = [sb(nc, esS, "s_yo%d_%d" % (ci, i), [64, 2, 64]) for i in range(4)]
            bufs.append(B)

        def chain(ci, p, d):
            B = bufs[ci]
            P0, P1 = G.ps[2 * ci], G.ps[2 * ci + 1]
            k0, k1 = ("ps", 2 * ci), ("ps", 2 * ci + 1)
            K = lambda nm, sub=None: ("s%d_%s" % (ci, nm), sub)
            fmn = {"r": "rT", "v": "vT", "kk": "kk", "lw": "lw%d" % d, "kd": "kd%d" % d, "b": "b%d" % d}
            P.op('dve', lambda e: e.memset(B.S[:], 0.0), w=[K("S")])
            P.op('act', lambda e: e.copy(out=rr(B.Sr[:, :]), in_=B.S[:, :]), r=[K("S")], w=[K("Sr")])
            R_, K_, B_, A_, V_ = (B.bd[:, i, :] for i in range(5))
            for n in range(NCH):
                if n < TC // C:
                    t0 = TL + n * C if d == 0 else T - (n + 1) * C
                else:
                    m_ = n - TC // C
                    t0 = m_ * C if d == 0 else TL - (m_ + 1) * C
                blk0 = (t0 // 256) * 256
                sbi = (n // 4) % 2

                def blk_start(nb_):
                    n_ = 4 * nb_
                    if n_ < TC // C:
                        t_ = TL + n_ * C if d == 0 else T - (n_ + 1) * C
                    else:
                        mm_ = n_ - TC // C
                        t_ = mm_ * C if d == 0 else TL - (mm_ + 1) * C
                    return (t_ // 256) * 256

                def load_blk(nb_):
                    b0 = blk_start(nb_)
                    for i, nm in enumerate(NAMES):
                        P.dma(B.src[nb_ % 2][:, i, :], FM[fmn[nm]][p, :, b0:b0 + 256], r=[("fm_" + fmn[nm], p)], w=[K("src", nb_ % 2)])
                if n % 4 == 0:
                    nb = n // 4
                    if nb == 0:
                        load_blk(0)
                    if nb + 1 < NCH // 4:
                        load_blk(nb + 1)
                off = t0 - blk0
                sk = K("src", sbi)

                def tsl(i):
                    a_ = B.src[sbi][:, i, off:off + C]
                    return a_ if d == 0 else a_[:, ::-1]
                cm = B.cum
                ck = K("cum")
                P.op('dve', lambda e: e.tensor_tensor_scan(out=cm[:, 0, :], data0=ones[:, :], data1=tsl(3), initial=0.0, op0=ALU.mult, op1=ALU.add), r=[sk, "s_ones"], w=[ck])
                P.op('dve', lambda e: e.tensor_tensor(out=cm[:, 1, :], in0=cm[:, 0, :], in1=tsl(3), op=ALU.subtract), r=[ck, sk], w=[ck])
                P.op('act', lambda e: e.activation(out=cm[:, 2:4, :], in_=cm[:, 0:2, :], func=AF.Exp), r=[ck], w=[ck])
                P.op('act', lambda e: e.activation(out=cm[:, 4, :], in_=cm[:, 0, :], func=AF.Exp, scale=-1.0), r=[ck], w=[ck])
                yield
                bk = K("bd")
                for h in range(2):
                    hp = slice(h * 64, (h + 1) * 64)
                    P.op('dve', lambda e, hp=hp: e.tensor_tensor(out=rr(R_[hp, hp]), in0=tsl(0)[hp, :], in1=cm[hp, 2, :], op=ALU.mult), r=[sk, ck], w=[K("bd", 0)])
                    P.op('dve', lambda e, hp=hp: e.tensor_tensor(out=rr(K_[hp, hp]), in0=tsl(4)[hp, :], in1=cm[hp, 4, :], op=ALU.mult), r=[sk, ck], w=[K("bd", 1)])
                    P.op('dve', lambda e, hp=hp: e.tensor_tensor(out=rr(B_[hp, hp]), in0=tsl(5)[hp, :], in1=cm[hp, 4, :], op=ALU.mult), r=[sk, ck], w=[K("bd", 2)])
                    P.op('dve', lambda e, hp=hp: e.scalar_tensor_tensor(out=rr(A_[hp, hp]), in0=tsl(2)[hp, :], scalar=-1.0, in1=cm[hp, 3, :], op0=ALU.mult, op1=ALU.mult),
                         r=[sk, ck], w=[K("bd", 3)])
                    P.op('act', lambda e, hp=hp: e.copy(out=rr(V_[hp, hp]), in_=tsl(1)[hp, :]), r=[sk], w=[K("bd", 4)])
                yield
                A = B.Am
                specs = [(0, 2, 3, 0), (1, 3, 2, 1), (2, 1, 3, 0), (3, 2, 0, 2), (4, 1, 0, 2)]
                for (ai, li, ri, mi) in specs:
                    pt, pk = (P0, k0) if ai < 4 else (P1, k1)
                    sl = slice((ai % 4) * 128, (ai % 4 + 1) * 128)
                    P.op('pe', lambda e, li=li, ri=ri, pt=pt, sl=sl: e.matmul(pt[:, sl], lhsT=rr(B.bd[:, li, :]), rhs=rr(B.bd[:, ri, :]), start=True, stop=True),
                         r=[K("bd", li), K("bd", ri)], w=[pk])
                P.op('pe', lambda e: e.transpose(out=P1[:, 128:256], in_=V_, identity=G.identF[:]), r=[K("bd", 4), "ident"], w=[k1])
                yield
                for (ai, li, ri, mi) in specs:
                    pt, pk = (P0, k0) if ai < 4 else (P1, k1)
                    sl = slice((ai % 4) * 128, (ai % 4 + 1) * 128)
                    P.op('dve', lambda e, ai=ai, pt=pt, sl=sl, mi=mi: e.tensor_tensor(out=rr(A[:, ai, :]), in0=pt[:, sl], in1=msk[:, mi, :], op=ALU.mult),
                         r=[pk, "s_msk"], w=[K("Am", ai), pk])
                for h in range(2):
                    hp = slice(h * 64, (h + 1) * 64)
                    P.op('act', lambda e, hp=hp, h=h: e.copy(out=rr(B.VT[hp, :]), in_=P1[hp, 128 + h * 64:128 + (h + 1) * 64]), r=[k1], w=[K("VT"), k1])
                yield
                zs = slice(256, 320)
                P.op('pe', lambda e: e.matmul(P1[:, zs], lhsT=rr(A_), rhs=rr(B.Sr[:, :]), start=True, stop=False), r=[K("bd", 3), K("Sr")], w=[k1])
                P.op('pe', lambda e: e.matmul(P1[:, zs], lhsT=rr(A[:, 2, :]), rhs=rr(B.VT[:, :]), start=False, stop=True), r=[K("Am", 2), K("VT")], w=[k1])
                yield
                P.op('act', lambda e: e.copy(out=rr(B.Z[:, 0, :]), in_=P1[:, zs]), r=[k1], w=[K("Z", 0), k1])
                yield
                zc = 0
                curA, curAT = (A[:, 0, :], K("Am", 0)), (A[:, 1, :], K("Am", 1))
                for step in range(6):
                    P.op('pe', lambda e, zc=zc, curA=curA: e.matmul(P1[:, zs], lhsT=rr(curA[0]), rhs=rr(B.Z[:, zc, :]), start=True, stop=True), r=[curA[1], K("Z", zc)], w=[k1])
                    if step < 5:
                        dstt = B.Ap[step % 2]
                        dn = "Ap%d" % (step % 2)
                        P.op('pe', lambda e, curA=curA, curAT=curAT: e.matmul(P0[:, 0:128], lhsT=rr(curAT[0]), rhs=rr(curA[0]), start=True, stop=True), r=[curA[1], curAT[1]], w=[k0])
                        if step < 4:
                            P.op('pe', lambda e, curA=curA, curAT=curAT: e.matmul(P0[:, 128:256], lhsT=rr(curA[0]), rhs=rr(curAT[0]), start=True, stop=True), r=[curA[1], curAT[1]], w=[k0])
                    yield
                    P.op('dve', lambda e, zc=zc: e.tensor_tensor(out=rr(B.Z[:, 1 - zc, :]), in0=P1[:, zs], in1=B.Z[:, zc, :], op=ALU.add), r=[k1, K("Z", zc)], w=[K("Z", 1 - zc), k1])
                    zc = 1 - zc
                    if step < 5:
                        if step < 4:
                            P.op('act', lambda e, dstt=dstt: e.copy(out=rr(dstt[:, :, :]), in_=P0[:, 0:256].rearrange("p (a n) -> p a n", a=2)), r=[k0], w=[K(dn), k0])
                        else:
                            P.op('act', lambda e, dstt=dstt: e.copy(out=rr(dstt[:, 0, :]), in_=P0[:, 0:128]), r=[k0], w=[K(dn), k0])
                        curA, curAT = (dstt[:, 0, :], K(dn)), (dstt[:, 1, :], K(dn))
                    yield
                UT = B.Z[:, zc, :]
                uk = K("Z", zc)
                ys = slice(384, 512)
                P.op('pe', lambda e: e.matmul(P1[0:64, ys], lhsT=rr(B.Sr[:, :]), rhs=rr(R_), start=True, stop=False), r=[K("Sr"), K("bd", 0)], w=[k1])
                P.op('pe', lambda e: e.matmul(P1[0:64, ys], lhsT=rr(UT), rhs=rr(A[:, 3, :]), start=False, stop=False), r=[uk, K("Am", 3)], w=[k1])
                P.op('pe', lambda e: e.matmul(P1[0:64, ys], lhsT=rr(B.VT[:, :]), rhs=rr(A[:, 4, :]), start=False, stop=True), r=[K("VT"), K("Am", 4)], w=[k1])
                if n < NCH - 1:
                    P.op('pe', lambda e: e.transpose(out=P0[:, 256:384], in_=B_, identity=G.identF[:]), r=[K("bd", 2), "ident"], w=[k0])
                    P.op('pe', lambda e: e.transpose(out=P0[:, 384:512], in_=K_, identity=G.identF[:]), r=[K("bd", 1), "ident"], w=[k0])
                yield
                yq = B.yo[n % 4]
                yv = P1[0:64, ys].rearrange("p (h t) -> p h t", h=2)
                yov = yq[:, :, :] if d == 0 else yq[:, :, ::-1]
                P.op('act', lambda e: e.copy(out=yov, in_=yv), r=[k1], w=[K("yo", n % 4), k1])
                P.dma(G.y_d[d, :, 2 * p:2 * p + 2, t0:t0 + C], yq[:, :, :], r=[K("yo", n % 4)], w=[("y_d", (d, p, t0 // 128))])
                if n < NCH - 1:
                    P.op('dve', lambda e: e.tensor_copy(out=rr(B.BT[:, :, :]), in_=P0[:, 256:512].rearrange("p (a n) -> p a n", a=2)), r=[k0], w=[K("BT"), k0])
                    yield
                    P.op('pe', lambda e: e.matmul(P1[:, 0:64], lhsT=rr(B.BT[:, 0, :]), rhs=rr(UT), start=True, stop=False), r=[K("BT"), uk], w=[k1])
                    P.op('pe', lambda e: e.matmul(P1[:, 0:64], lhsT=rr(B.BT[:, 1, :]), rhs=rr(B.VT[:, :]), start=False, stop=True), r=[K("BT"), K("VT")], w=[k1])
                    yield
                    P.op('dve', lambda e: e.tensor_tensor(out=B.S[:, :], in0=B.S[:, :], in1=P1[:, 0:64], op=ALU.add), r=[K("S"), k1], w=[K("S"), k1])
                    P.op('dve', lambda e: e.tensor_scalar(out=B.S[:, :], in0=B.S[:, :], scalar1=cm[:, 2, 63:64], scalar2=None, op0=ALU.mult), r=[K("S"), ck], w=[K("S")])
                    P.op('act', lambda e: e.copy(out=rr(B.Sr[:, :]), in_=B.S[:, :]), r=[K("S")], w=[K("Sr")])
                yield

        todo = [(p, d) for p in range(8) for d in range(2)]
        active = [None] * NCHAIN
        for ci in range(NCHAIN):
            p, d = todo.pop(0)
            active[ci] = chain(ci, p, d)
            for _ in range(ci * STAGGER):
                next(active[ci])
        while todo or any(a is not None for a in active):
            for ci in range(NCHAIN):
                if active[ci] is None and todo:
                    p, d = todo.pop(0)
                    active[ci] = chain(ci, p, d)
                if active[ci] is not None:
                    try:
                        next(active[ci])
                    except StopIteration:
                        active[ci] = None
        P.barrier()


def stage_rwkv(G, l):
    nc, P = G.nc, G.P
    j = l // 2
    FM = G.fm
    with ExitStack() as es:
        mc = load_modcols(G, es, l, "mc")
        BO = sb(nc, es, "r_BO", [128, 128])
        P.dma(BO[:], G.KC("k_bo", [128, 128])[:, :], w=["r_BO"])
        with ExitStack() as esP:
            hT = sb(nc, esP, "r_hT", [128, 8, T], BF16)
            dT = sb(nc, esP, "r_dT", [128, 8, T], BF16)
            xiT = sb(nc, esP, "r_xiT", [128, 8, T], BF16)
            with ExitStack() as esx:
                xt = [sb(nc, esx, "r_x%d" % i, [128, 1024]) for i in range(2)]
                for t in range(NT):
                    b = t % 2
                    P.dma(xt[b][:], G.xres[t * 128:(t + 1) * 128, :], r=[("xres", t)], w=[("r_x", b)])
                    transpose_modulate(G, xt[b], ("r_x", b), hT, ("r_hT", t), t * 128, mc, 0 if t < 16 else 1, 0, 1, 0)
                P.barrier()
            def lat(tile_, c0, c1):
                return tile_[:, c0:c1, 0:TL].rearrange("p c (r w) -> p c r w", w=64)
            hl = lambda c0, c1: lat(hT, c0, c1)
            dl = lambda c0, c1: lat(dT, c0, c1)
            sub = ALU.subtract
            ops = [
                (dl(0, 2)[:, :, :, 1:64], hl(0, 2)[:, :, :, 0:63], hl(0, 2)[:, :, :, 1:64]),
                (dl(2, 4)[:, :, :, 0:63], hl(2, 4)[:, :, :, 1:64], hl(2, 4)[:, :, :, 0:63]),
                (dl(4, 6)[:, :, 1:32, :], hl(4, 6)[:, :, 0:31, :], hl(4, 6)[:, :, 1:32, :]),
                (dl(6, 8)[:, :, 0:31, :], hl(6, 8)[:, :, 1:32, :], hl(6, 8)[:, :, 0:31, :]),
                (dT[:, 0:4, TL + 1:T], hT[:, 0:4, TL:T - 1], hT[:, 0:4, TL + 1:T]),
                (dT[:, 4:8, TL:T - 1], hT[:, 4:8, TL + 1:T], hT[:, 4:8, TL:T - 1]),
            ]
            for (o_, a_, b_) in ops:
                for cc in range(o_.shape[1]):
                    P.op('dve', lambda e, o_=o_, a_=a_, b_=b_, cc=cc: e.tensor_tensor(out=o_[:, cc], in0=a_[:, cc], in1=b_[:, cc], op=sub), r=["r_hT"], w=["r_dT"])
            bnd = [
                (dl(0, 2)[:, :, :, 0:1], hl(0, 2)[:, :, :, 0:1]), (dl(2, 4)[:, :, :, 63:64], hl(2, 4)[:, :, :, 63:64]),
                (dl(4, 6)[:, :, 0:1, :], hl(4, 6)[:, :, 0:1, :]), (dl(6, 8)[:, :, 31:32, :], hl(6, 8)[:, :, 31:32, :]),
                (dT[:, 0:4, TL:TL + 1], hT[:, 0:4, TL:TL + 1]), (dT[:, 4:8, T - 1:T], hT[:, 4:8, T - 1:T]),
            ]
            for (o_, a_) in bnd:
                for cc in range(o_.shape[1]):
                    P.op('dve', lambda e, o_=o_, a_=a_, cc=cc: e.tensor_scalar_mul(out=o_[:, cc], in0=a_[:, cc], scalar1=-1.0), r=["r_hT"], w=["r_dT"])
            mu = sb(nc, esP, "r_mu", [128, 6, 8])
            colload(G, esP, mu[:, :, :].rearrange("p i c -> p (i c)"), "r_mu", G.Wl("rw_mu", j).rearrange("i n -> (i n)"), 48)
            kkc = colvec(G, esP, "r_kkc", G.Wl("rw_kk", j), "r_cv")
            kac = colvec(G, esP, "r_kac", G.Wl("rw_ka", j), "r_cv")
            omka = sb(nc, esP, "r_omka", [128, 8])
            P.op('dve', lambda e: e.tensor_scalar(out=omka[:], in0=kac[:], scalar1=-1.0, scalar2=1.0, op0=ALU.mult, op1=ALU.add), r=["r_cv"], w=["r_cv2"])
            stg = [sb(nc, esP, "r_stg%d" % i, [128, T]) for i in range(2)]
            stg2 = [sb(nc, esP, "r_stg2_0", [128, T])] * 2
            ld1 = [sb(nc, esP, "r_ld1_0", [128, T])] * 2
            ld2 = [sb(nc, esP, "r_ld2_0", [128, T])] * 2
            tmpa = [sb(nc, esP, "r_tmpa%d" % i, [128, 512]) for i in range(2)]
            tmpb = [sb(nc, esP, "r_tmpb%d" % i, [128, 512]) for i in range(2)]
            tcnt = [0]

            def mk_xi(i):
                for c in range(8):
                    P.op('dve', lambda e, c=c: e.scalar_tensor_tensor(out=xiT[:, c, :], in0=dT[:, c, :], scalar=mu[:, i, c:c + 1], in1=hT[:, c, :], op0=ALU.mult, op1=ALU.add),
                         r=["r_dT", "r_hT", "r_mu"], w=["r_xiT"])

            def store(oc, dst, src, skey):
                P.dma(dst[oc, :, :], src[:, :], r=[skey], w=[(dst.name if hasattr(dst, "name") else "fm", oc)])

            mk_xi(0)
            with ExitStack() as esw:
                wb = load_w_bf16(G, esw, "r_wr", G.Wl("rw_wr", j), 1024, 1024)
                def cons_r(oc, t0, tn, ps, pkey, m_):
                    s = stg[oc % 2]
                    P.op('act', lambda e: e.copy(out=s[:, t0:t0 + tn], in_=ps[:, 0:tn]), r=[pkey], w=[("r_stg", oc % 2), pkey])
                    if t0 == 2048:
                        P.dma(FM["rT"][oc, :, :], s[:, :], r=[("r_stg", oc % 2)], w=[("fm_rT", oc)])
                fm_linear(G, xiT, "r_xiT", wb, "r_wr_b", 1024, cons_r)
                P.barrier()
            mk_xi(1)
            for d in range(2):
                with ExitStack() as esw:
                    w1b = load_w_bf16(G, esw, "r_w1", G.Wl("rw_w1", j)[d], 1024, 64)
                    w2b = load_w_bf16(G, esw, "r_w2", G.Wl("rw_w2", j)[d], 64, 1024)
                    w0c = colvec(G, esw, "r_w0c", G.Wl("rw_w0", j)[d], "r_w0c")
                    P.op('dve', lambda e: e.tensor_scalar_mul(out=w0c[:], in0=w0c[:], scalar1=-1.0), r=["r_w0c"], w=["r_w0c"])
                    t1 = sb(nc, esw, "r_t1", [128, 1, T], BF16)
                    def cons_t(oc, t0, tn, ps, pkey, m_):
                        P.op('act', lambda e: e.activation(out=t1[0:m_, 0, t0:t0 + tn], in_=ps[0:m_, 0:tn], func=AF.Tanh), r=[pkey], w=["r_t1", pkey])
                    fm_linear(G, xiT, "r_xiT", w1b, "r_w1_b", 64, cons_t, pbanks=(2, 3))
                    def cons_w(oc, t0, tn, ps, pkey, m_):
                        s = stg[oc % 2]
                        k_ = tcnt[0] % 2
                        tcnt[0] += 1
                        ta, tb_ = tmpa[k_], tmpb[k_]
                        P.op('act', lambda e: e.activation(out=ta[:, 0:tn], in_=ps[:, 0:tn], func=AF.Exp, scale=-1.0, bias=w0c[:, oc:oc + 1]), r=[pkey, "r_w0c"], w=[("r_tmpa", k_), pkey])
                        P.op('act', lambda e: e.activation(out=tb_[:, 0:tn], in_=ta[:, 0:tn], func=AF.Ln, bias=G.one_c[:, 0:1], scale=1.0), r=[("r_tmpa", k_), "consts"], w=[("r_tmpb", k_)])
                        P.op('act', lambda e: e.activation(out=ta[:, 0:tn], in_=tb_[:, 0:tn], func=AF.Exp, scale=-1.0, bias=G.mhalf_c[:, 0:1]), r=[("r_tmpb", k_), "consts"], w=[("r_tmpa", k_)])
                        P.op('dve', lambda e: e.tensor_scalar_mul(out=s[:, t0:t0 + tn], in0=ta[:, 0:tn], scalar1=-1.0), r=[("r_tmpa", k_)], w=[("r_stg", oc % 2)])
                        if t0 == 2048:
                            P.dma(FM["lw%d" % d][oc, :, :], s[:, :], r=[("r_stg", oc % 2)], w=[("fm_lw%d" % d, oc)])
                    fm_linear(G, t1, "r_t1", w2b, "r_w2_b", 1024, cons_w, kparts=[(0, 64)])
                    P.barrier()
            mk_xi(2)
            with ExitStack() as esw:
                wb = load_w_bf16(G, esw, "r_wk", G.Wl("rw_wk", j), 1024, 1024)
                def cons_k(oc, t0, tn, ps, pkey, m_):
                    s, s2 = stg[oc % 2], stg2[oc % 2]
                    k_ = tcnt[0] % 2
                    tcnt[0] += 1
                    ta, tb_ = tmpa[k_], tmpb[k_]
                    P.op('act', lambda e: e.copy(out=s[:, t0:t0 + tn], in_=ps[:, 0:tn]), r=[pkey], w=[("r_stg", oc % 2), pkey])
                    P.op('dve', lambda e: e.tensor_scalar(out=ta[:, 0:tn], in0=s[:, t0:t0 + tn], scalar1=kkc[:, oc:oc + 1], scalar2=None, op0=ALU.mult), r=[("r_stg", oc % 2), "r_cv"], w=[("r_tmpa", k_)])
                    P.op('pool', lambda e: e.tensor_tensor(out=tb_[:, 0:tn], in0=ta[:, 0:tn], in1=ta[:, 0:tn], op=ALU.mult), r=[("r_tmpa", k_)], w=[("r_tmpb", k_)])
                    P.op('pe', lambda e: e.matmul(G.ps[4 + k_][:, 0:tn], lhsT=BO[:, :], rhs=tb_[:, 0:tn], start=True, stop=True), r=["r_BO", ("r_tmpb", k_)], w=[("ps", 4 + k_)])
                    P.op('act', lambda e: e.activation(out=tb_[:, 0:tn], in_=G.ps[4 + k_][:, 0:tn], func=AF.Sqrt), r=[("ps", 4 + k_)], w=[("r_tmpb", k_), ("ps", 4 + k_)])
                    P.op('dve', lambda e: e.tensor_scalar_max(out=tb_[:, 0:tn], in0=tb_[:, 0:tn], scalar1=1e-12), r=[("r_tmpb", k_)], w=[("r_tmpb", k_)])
                    P.op('dve', lambda e: e.reciprocal(out=tb_[:, 0:tn], in_=tb_[:, 0:tn]), r=[("r_tmpb", k_)], w=[("r_tmpb", k_)])
                    P.op('dve', lambda e: e.tensor_tensor(out=s2[:, t0:t0 + tn], in0=ta[:, 0:tn], in1=tb_[:, 0:tn], op=ALU.mult), r=[("r_tmpa", k_), ("r_tmpb", k_)], w=[("r_stg2", 0)])
                    if t0 == 2048:
                        P.dma(FM["kT"][oc, :, :], s[:, :], r=[("r_stg", oc % 2)], w=[("fm_kT", oc)])
                        P.dma(FM["kk"][oc, :, :], s2[:, :], r=[("r_stg2", 0)], w=[("fm_kk", oc)])
                fm_linear(G, xiT, "r_xiT", wb, "r_wk_b", 1024, cons_k)
                P.barrier()
            mk_xi(3)
            with ExitStack() as esw:
                wb = load_w_bf16(G, esw, "r_wv", G.Wl("rw_wv", j), 1024, 1024)
                if j > 0:
                    v1b = load_w_bf16(G, esw, "r_v1", G.Wl("rw_v1", j - 1), 1024, 32)
                    v2b = load_w_bf16(G, esw, "r_v2", G.Wl("rw_v2", j - 1), 32, 1024)
                    v0c = colvec(G, esw, "r_v0c", G.Wl("rw_v0", j - 1), "r_v0c")
                    t1 = sb(nc, esw, "r_t1v", [128, 1, T], BF16)
                    def cons_t(oc, t0, tn, ps, pkey, m_):
                        P.op('act', lambda e: e.copy(out=t1[0:m_, 0, t0:t0 + tn], in_=ps[0:m_, 0:tn]), r=[pkey], w=["r_t1v", pkey])
                    fm_linear(G, xiT, "r_xiT", v1b, "r_v1_b", 32, cons_t, pbanks=(2, 3))
                def cons_v(oc, t0, tn, ps, pkey, m_):
                    s = stg[oc % 2]
                    if j == 0:
                        P.op('act', lambda e: e.copy(out=s[:, t0:t0 + tn], in_=ps[:, 0:tn]), r=[pkey], w=[("r_stg", oc % 2), pkey])
                    else:
                        k_ = tcnt[0] % 2
                        tcnt[0] += 1
                        ta, tb_ = tmpa[k_], tmpb[k_]
                        if t0 == 0:
                            P.dma(ld1[oc % 2][:, :], FM["vf"][oc, :, :], r=[("fm_vf", oc)], w=[("r_ld1", 0)])
                        vf = ld1[oc % 2]
                        pb2 = 4 + k_
                        P.op('pe', lambda e: e.matmul(G.ps[pb2][:, 0:tn], lhsT=v2b[0:32, 0, oc * 128:(oc + 1) * 128], rhs=t1[0:32, 0, t0:t0 + tn], start=True, stop=True),
                             r=["r_t1v", "r_v2_b"], w=[("ps", pb2)])
                        P.op('act', lambda e: e.activation(out=ta[:, 0:tn], in_=G.ps[pb2][:, 0:tn], func=AF.Sigmoid, bias=v0c[:, oc:oc + 1], scale=1.0), r=[("ps", pb2), "r_v0c"], w=[("r_tmpa", k_), ("ps", pb2)])
                        P.op('dve', lambda e: e.tensor_tensor(out=tb_[:, 0:tn], in0=vf[:, t0:t0 + tn], in1=ps[:, 0:tn], op=ALU.subtract), r=[("r_ld1", 0), pkey], w=[("r_tmpb", k_)])
                        P.op('dve', lambda e: e.tensor_tensor(out=tb_[:, 0:tn], in0=tb_[:, 0:tn], in1=ta[:, 0:tn], op=ALU.mult), r=[("r_tmpa", k_), ("r_tmpb", k_)], w=[("r_tmpb", k_)])
                        P.op('dve', lambda e: e.tensor_tensor(out=s[:, t0:t0 + tn], in0=tb_[:, 0:tn], in1=ps[:, 0:tn], op=ALU.add), r=[("r_tmpb", k_), pkey], w=[("r_stg", oc % 2), pkey])
                    if t0 == 2048:
                        P.dma(FM["vT"][oc, :, :], s[:, :], r=[("r_stg", oc % 2)], w=[("fm_vT", oc)])
                        if j == 0:
                            P.dma(FM["vf"][oc, :, :], s[:, :], r=[("r_stg", oc % 2)], w=[("fm_vf", oc)])
                fm_linear(G, xiT, "r_xiT", wb, "r_wv_b", 1024, cons_v)
                P.barrier()
            mk_xi(4)
            for d in range(2):
                with ExitStack() as esw:
                    a1b = load_w_bf16(G, esw, "r_a1", G.Wl("rw_a1", j)[d], 1024, 64)
                    a2b = load_w_bf16(G, esw, "r_a2", G.Wl("rw_a2", j)[d], 64, 1024)
                    a0c = colvec(G, esw, "r_a0c", G.Wl("rw_a0", j)[d], "r_a0c")
                    t1 = sb(nc, esw, "r_t1a", [128, 1, T], BF16)
                    def cons_t(oc, t0, tn, ps, pkey, m_):
                        P.op('act', lambda e: e.copy(out=t1[0:m_, 0, t0:t0 + tn], in_=ps[0:m_, 0:tn]), r=[pkey], w=["r_t1a", pkey])
                    fm_linear(G, xiT, "r_xiT", a1b, "r_a1_b", 64, cons_t, pbanks=(2, 3))
                    def cons_a(oc, t0, tn, ps, pkey, m_):
                        s, s2 = stg[oc % 2], stg2[oc % 2]
                        k_ = tcnt[0] % 2
                        tcnt[0] += 1
                        ta, tb_ = tmpa[k_], tmpb[k_]
                        if t0 == 0:
                            P.dma(ld1[oc % 2][:, :], FM["kT"][oc, :, :], r=[("fm_kT", oc)], w=[("r_ld1", 0)])
                            P.dma(ld2[oc % 2][:, :], FM["kk"][oc, :, :], r=[("fm_kk", oc)], w=[("r_ld2", 0)])
                        kt_, kkt = ld1[oc % 2], ld2[oc % 2]
                        P.op('act', lambda e: e.activation(out=ta[:, 0:tn], in_=ps[:, 0:tn], func=AF.Sigmoid, bias=a0c[:, oc:oc + 1], scale=1.0), r=[pkey, "r_a0c"], w=[("r_tmpa", k_), pkey])
                        P.op('pool', lambda e: e.tensor_tensor(out=s2[:, t0:t0 + tn], in0=kkt[:, t0:t0 + tn], in1=ta[:, 0:tn], op=ALU.mult), r=[("r_ld2", 0), ("r_tmpa", k_)], w=[("r_stg2", 0)])
                        P.op('dve', lambda e: e.tensor_scalar(out=tb_[:, 0:tn], in0=ta[:, 0:tn], scalar1=kac[:, oc:oc + 1], scalar2=omka[:, oc:oc + 1], op0=ALU.mult, op1=ALU.add),
                             r=[("r_tmpa", k_), "r_cv", "r_cv2"], w=[("r_tmpb", k_)])
                        P.op('dve', lambda e: e.tensor_tensor(out=s[:, t0:t0 + tn], in0=tb_[:, 0:tn], in1=kt_[:, t0:t0 + tn], op=ALU.mult), r=[("r_tmpb", k_), ("r_ld1", 0)], w=[("r_stg", oc % 2)])
                        if t0 == 2048:
                            P.dma(FM["kd%d" % d][oc, :, :], s[:, :], r=[("r_stg", oc % 2)], w=[("fm_kd%d" % d, oc)])
                            P.dma(FM["b%d" % d][oc, :, :], s2[:, :], r=[("r_stg2", 0)], w=[("fm_b%d" % d, oc)])
                    fm_linear(G, t1, "r_t1a", a2b, "r_a2_b", 1024, cons_a, kparts=[(0, 64)])
                    P.barrier()
            mk_xi(5)
            with ExitStack() as esw:
                g1b = load_w_bf16(G, esw, "r_g1", G.Wl("rw_g1", j), 1024, 160)
                g2b = load_w_bf16(G, esw, "r_g2", G.Wl("rw_g2", j), 160, 1024)
                t1 = sb(nc, esw, "r_t1g", [128, 2, T], BF16)
                def cons_t(oc, t0, tn, ps, pkey, m_):
                    P.op('act', lambda e: e.activation(out=t1[0:m_, oc, t0:t0 + tn], in_=ps[0:m_, 0:tn], func=AF.Sigmoid), r=[pkey], w=["r_t1g", pkey])
                fm_linear(G, xiT, "r_xiT", g1b, "r_g1_b", 160, cons_t, pbanks=(2, 3))
                def cons_g(oc, t0, tn, ps, pkey, m_):
                    s = stg[oc % 2]
                    P.op('act', lambda e: e.copy(out=s[:, t0:t0 + tn], in_=ps[:, 0:tn]), r=[pkey], w=[("r_stg", oc % 2), pkey])
                    if t0 == 2048:
                        P.dma(FM["gT"][oc, :, :], s[:, :], r=[("r_stg", oc % 2)], w=[("fm_gT", oc)])
                fm_linear(G, t1, "r_t1g", g2b, "r_g2_b", 1024, cons_g, kparts=[(0, 128), (1, 32)])
                P.barrier()
            P.barrier()
        if not SKIP_SCAN:
            scan_stage(G)
        with ExitStack() as esR:
            zT = sb(nc, esR, "o_zT", [128, 8, T], BF16)
            rkc = colvec(G, esR, "o_rkc", G.Wl("rw_rk", j).rearrange("h k -> (h k)"), "o_cv")
            lgc = colvec(G, esR, "o_lgc", G.Wl("rw_lnx_g", j), "o_cv")
            lbc = colvec(G, esR, "o_lbc", G.Wl("rw_lnx_b", j), "o_cv")
            epsg = sb(nc, esR, "o_epsg", [128, 1])
            P.op('dve', lambda e: e.memset(epsg[:], RW_GN_EPS), w=["o_epsg"])
            with ExitStack() as esL:
                L = {nm: [sb(nc, esL, "o_%s%d" % (nm, i), [128, T]) for i in range(2)] for nm in ("y0", "y1", "r", "kd0", "kd1", "v", "g")}
                wa = [sb(nc, esL, "o_wa%d" % i, [128, 512]) for i in range(2)]
                wb_ = [sb(nc, esL, "o_wb%d" % i, [128, 512]) for i in range(2)]
                wc_ = [sb(nc, esL, "o_wc%d" % i, [128, 512]) for i in range(2)]
                it = 0
                def load_pair(p_):
                    b_ = p_ % 2
                    for d in range(2):
                        for h in range(2):
                            P.dma(L["y%d" % d][b_][h * 64:(h + 1) * 64, :], G.y_d[d, :, 2 * p_ + h, :], r=["y_d"], w=[("o_L_y%d" % d, b_)])
                    for nm, fmn in (("r", "rT"), ("kd0", "kd0"), ("kd1", "kd1"), ("v", "vT"), ("g", "gT")):
                        P.dma(L[nm][b_][:, :], FM[fmn][p_, :, :], r=[("fm_" + fmn, p_)], w=[("o_L_" + nm, b_)])
                for p in range(8):
                    b = p % 2
                    if p == 0:
                        load_pair(0)
                    if p + 1 < 8:
                        load_pair(p + 1)
                    for (t0, tn) in TBLK:
                        k_ = it % 2
                        it += 1
                        a_, b2, c_ = wa[k_], wb_[k_], wc_[k_]
                        ka, kb, kc = ("o_wa", k_), ("o_wb", k_), ("o_wc", k_)
                        ts_ = slice(t0, t0 + tn)
                        P.op('dve', lambda e: e.tensor_tensor(out=a_[:, 0:tn], in0=L["y0"][b][:, ts_], in1=L["y1"][b][:, ts_], op=ALU.add), r=[("o_L_y0", b), ("o_L_y1", b)], w=[ka])
                        P.op('pe', lambda e: e.matmul(G.ps[k_][:, 0:tn], lhsT=BO[:, :], rhs=a_[:, 0:tn], start=True, stop=True), r=["r_BO", ka], w=[("ps", k_)])
                        P.op('dve', lambda e: e.scalar_tensor_tensor(out=a_[:, 0:tn], in0=G.ps[k_][:, 0:tn], scalar=-1.0 / 64, in1=a_[:, 0:tn], op0=ALU.mult, op1=ALU.add), r=[("ps", k_), ka], w=[ka, ("ps", k_)])
                        P.op('pool', lambda e: e.tensor_tensor(out=b2[:, 0:tn], in0=a_[:, 0:tn], in1=a_[:, 0:tn], op=ALU.mult), r=[ka], w=[kb])
                        P.op('pe', lambda e: e.matmul(G.ps[2 + k_][:, 0:tn], lhsT=BO[:, :], rhs=b2[:, 0:tn], start=True, stop=True), r=["r_BO", kb], w=[("ps", 2 + k_)])
                        P.op('act', lambda e: e.activation(out=b2[:, 0:tn], in_=G.ps[2 + k_][:, 0:tn], func=AF.Sqrt, scale=1.0 / 64, bias=epsg[:, 0:1]), r=[("ps", 2 + k_), "o_epsg"], w=[kb, ("ps", 2 + k_)])
                        P.op('dve', lambda e: e.reciprocal(out=b2[:, 0:tn], in_=b2[:, 0:tn]), r=[kb], w=[kb])
                        P.op('dve', lambda e: e.tensor_tensor(out=a_[:, 0:tn], in0=a_[:, 0:tn], in1=b2[:, 0:tn], op=ALU.mult), r=[ka, kb], w=[ka])
                        P.op('dve', lambda e: e.tensor_scalar(out=a_[:, 0:tn], in0=a_[:, 0:tn], scalar1=lgc[:, p:p + 1], scalar2=lbc[:, p:p + 1], op0=ALU.mult, op1=ALU.add), r=[ka, "o_cv"], w=[ka])
                        P.op('pool', lambda e: e.tensor_tensor(out=c_[:, 0:tn], in0=L["kd0"][b][:, ts_], in1=L["kd1"][b][:, ts_], op=ALU.add), r=[("o_L_kd0", b), ("o_L_kd1", b)], w=[kc])
                        P.op('dve', lambda e: e.scalar_tensor_tensor(out=c_[:, 0:tn], in0=c_[:, 0:tn], scalar=rkc[:, p:p + 1], in1=L["r"][b][:, ts_], op0=ALU.mult, op1=ALU.mult), r=[kc, "o_cv", ("o_L_r", b)], w=[kc])
                        P.op('pe', lambda e: e.matmul(G.ps[4 + k_][:, 0:tn], lhsT=BO[:, :], rhs=c_[:, 0:tn], start=True, stop=True), r=["r_BO", kc], w=[("ps", 4 + k_)])
                        P.op('dve', lambda e: e.tensor_tensor(out=c_[:, 0:tn], in0=G.ps[4 + k_][:, 0:tn], in1=L["v"][b][:, ts_], op=ALU.mult), r=[("ps", 4 + k_), ("o_L_v", b)], w=[kc, ("ps", 4 + k_)])
                        P.op('dve', lambda e: e.tensor_tensor(out=a_[:, 0:tn], in0=a_[:, 0:tn], in1=c_[:, 0:tn], op=ALU.add), r=[ka, kc], w=[ka])
                        P.op('dve', lambda e: e.tensor_tensor(out=zT[:, p, ts_], in0=a_[:, 0:tn], in1=L["g"][b][:, ts_], op=ALU.mult), r=[ka, ("o_L_g", b)], w=[("o_zT", p)])
                P.barrier()
            wob = load_w_bf16(G, esR, "o_wo", G.Wl("rw_wo", j), 1024, 1024)
            gate = [sb(nc, esR, "o_gate%d" % r_, [128, 1024]) for r_ in range(2)]
            LG = sb(nc, esR, "o_lg", [128, 1024])
            LB = sb(nc, esR, "o_lb", [128, 1024])
            for r_ in range(2):
                P.dma(gate[r_][:], G.modv[l, r_:r_ + 1, 2 * 1024:3 * 1024].partition_broadcast(128), r=[("modv", l)], w=["o_bc"])
            P.dma(LG[:], row(G.Wl("ln1_g", l)).partition_broadcast(128), w=["o_bc"])
            P.dma(LB[:], row(G.Wl("ln1_b", l)).partition_broadcast(128), w=["o_bc"])
            xt = [sb(nc, esR, "o_x%d" % i, [128, 1024]) for i in range(2)]
            tmp = [sb(nc, esR, "o_tmp%d" % i, [128, 1024]) for i in range(2)]
            yt = [sb(nc, esR, "o_y%d" % i, [128, 1024]) for i in range(2)]
            st = [sb(nc, esR, "o_st%d" % i, [128, 16]) for i in range(2)]
            for t in range(NT):
                b = t % 2
                r_ = 0 if t < 16 else 1
                if t == 0:
                    P.dma(xt[0][:], G.xres[0:128, :], r=[("xres", 0)], w=[("o_x", 0)])
                if t + 1 < NT:
                    P.dma(xt[(t + 1) % 2][:], G.xres[(t + 1) * 128:(t + 2) * 128, :], r=[("xres", t + 1)], w=[("o_x", (t + 1) % 2)])
                banks = [G.ps[0 + 2 * b], G.ps[1 + 2 * b]]
                okeys = [("ps", 0 + 2 * b), ("ps", 1 + 2 * b)]
                for hh in range(2):
                    for c in range(8):
                        P.op('pe', lambda e_, hh=hh, c=c: e_.matmul(banks[hh][:, :], lhsT=zT[:, c, t * 128:(t + 1) * 128], rhs=wob[:, c, hh * 512:(hh + 1) * 512],
                                                                   start=(c == 0), stop=(c == 7)), r=["o_zT", "o_wo_b"], w=[okeys[hh]])
                ln_epilogue(G, banks, okeys, xt[b], ("o_x", b), gate[r_], LG, LB, ["o_bc"], tmp[b], ("o_tmp", b), st[b], ("o_st", b), yt[b], ("o_y", b))
                P.dma(G.xres[t * 128:(t + 1) * 128, :], yt[b][:], r=[("o_y", b)], w=[("xres", t)])
            P.barrier()
        P.barrier()


def build(layers=(0, 1, 2, 3), stages=None):
    nc = bass.Bass("TRN2", target_bir_lowering=False)
    G = Ctx()
    G.nc = nc

    def din(name, shape):
        return nc.dram_tensor(name, list(shape), F32, kind="ExternalInput").ap()
    G.x_d = din("x", [TL, D])
    G.ctx_d = din("ctx", [TC, D])
    G.c_d = din("c", [1, D])
    G.cc_d = din("c_ctx", [1, D])
    G.used = {}
    specs = dict(WEIGHT_SPECS)

    def Wl(name, l):
        key = "%s_%d" % (name, l)
        if key not in G.used:
            G.used[key] = (name, l, din(key, specs[name][1:]))
        return G.used[key][2]
    G.Wl = Wl
    G.kc = {}

    def KC(name, shape):
        if name not in G.kc:
            G.kc[name] = din(name, shape)
        return G.kc[name]
    G.KC = KC
    G.ident_d = KC("k_ident", [128, 128])
    G.out_d = nc.dram_tensor("out", [TL, D], F32, kind="ExternalOutput").ap()
    G.outc_d = nc.dram_tensor("outc", [TC, D], F32, kind="ExternalOutput").ap()
    G.xres = nc.dram_tensor("xres", [T, D], F32, kind="Internal").ap()
    G.modv = nc.dram_tensor("modv", [4, 2, 6144], F32, kind="Internal").ap()
    G.mo_d = nc.dram_tensor("mo_d", [T, D], BF16, kind="Internal").ap()
    G.fm = {nm: nc.dram_tensor("fm_" + nm, [8, 128, T], F32, kind="Internal").ap()
            for nm in ("rT", "kT", "vT", "vf", "kk", "gT", "lw0", "lw1", "kd0", "kd1", "b0", "b1")}
    G.y_d = nc.dram_tensor("y_d", [2, 64, 16, T], F32, kind="Internal").ap()
    P = Prog(nc)
    G.P = P
    with ExitStack() as es:
        G.ps = [es.enter_context(nc.psum_tensor("ps%d" % i, [128, 512], F32)) for i in range(8)]
        G.identF = sb(nc, es, "identF", [128, 128])
        G.identB = sb(nc, es, "identB", [128, 128], BF16)
        G.eps_ln = sb(nc, es, "eps_ln", [128, 1])
        P.dma(G.identF[:], G.ident_d[:, :], w=["ident"])
        P.op('dve', lambda e: e.tensor_copy(out=G.identB[:], in_=G.identF[:]), r=["ident"], w=["identB"])
        P.op('dve', lambda e: e.memset(G.eps_ln[:], LN_EPS), w=["consts"])
        G.one_c = sb(nc, es, "one_c", [128, 1])
        G.mhalf_c = sb(nc, es, "mhalf_c", [128, 1])
        P.op('dve', lambda e: e.memset(G.one_c[:], 1.0), w=["consts"])
        P.op('dve', lambda e: e.memset(G.mhalf_c[:], -0.5), w=["consts"])
        for t in range(16):
            P.dma(G.xres[t * 128:(t + 1) * 128, :], G.x_d[t * 128:(t + 1) * 128, :], w=[("xres", t)])
        for t in range(2):
            P.dma(G.xres[TL + t * 128:TL + (t + 1) * 128, :], G.ctx_d[t * 128:(t + 1) * 128, :], w=[("xres", 16 + t)])
        stage_modvec(G, layers)
        for l in layers:
            if l % 2 == 0 and (stages is None or "mix" in stages):
                stage_even(G, l)
            if l % 2 == 1 and (stages is None or "mix" in stages):
                stage_rwkv(G, l)
            if stages is None or "ffn" in stages:
                stage_ffn(G, l)
        for t in range(16):
            P.dma(G.out_d[t * 128:(t + 1) * 128, :], G.xres[t * 128:(t + 1) * 128, :], r=[("xres", t)], w=[("out", t)])
        for t in range(2):
            P.dma(G.outc_d[t * 128:(t + 1) * 128, :], G.xres[TL + t * 128:TL + (t + 1) * 128, :], r=[("xres", 16 + t)], w=[("outc", t)])
        P.barrier()
    P.es.close()
    return nc, P, G


def make_consts():
    k = {"k_ident": np.eye(128, dtype=np.float32)}
    t = np.arange(TL)
    rowi = (t // 64).astype(np.float32)
    coli = (t % 64).astype(np.float32)
    for nm, dim, ng in (("A", 64, 8), ("B", 128, 4)):
        nf = dim // 4
        inv = (10000.0 ** (-np.arange(nf, dtype=np.float32) / nf)).astype(np.float32)
        ang = np.concatenate([rowi[:, None] * inv, coli[:, None] * inv], -1).astype(np.float32)
        k["k_cos" + nm] = np.ascontiguousarray(np.tile(np.cos(ang).astype(np.float32), (1, ng)))
        k["k_sin" + nm] = np.ascontiguousarray(np.tile(np.sin(ang).astype(np.float32), (1, ng)))
    p = np.arange(128, dtype=np.float32)
    k["k_cols"] = np.stack([127 - p, p, p + 1, 128 - p], 1).astype(np.float32)
    jj = p[None, :]
    pp = p[:, None]
    k["k_mats"] = np.ascontiguousarray(np.stack([np.maximum(jj - pp, 0), np.maximum(pp - jj, 0), (jj >= pp).astype(np.float32),
                                                 (jj <= pp).astype(np.float32)], 1).astype(np.float32))
    bo = np.zeros((128, 128), np.float32)
    bo[:64, :64] = 1.0
    bo[64:, 64:] = 1.0
    k["k_bo"] = bo
    i64 = np.arange(64)
    strict = (i64[:, None] < i64[None, :]).astype(np.float32)
    incl = (i64[:, None] <= i64[None, :]).astype(np.float32)
    def bdm(m):
        z = np.zeros((128, 128), np.float32)
        z[:64, :64] = m
        z[64:, 64:] = m
        return z
    k["k_smask"] = np.ascontiguousarray(np.stack([bdm(strict), bdm(strict.T), bdm(incl)], 1))
    return k


def kernel(**inputs):
    nc, _, G = build()
    consts = make_consts()
    in_maps = []
    for b in range(8):
        m = {"x": np.ascontiguousarray(inputs["x"][b]), "ctx": np.ascontiguousarray(inputs["ctx"][b]),
             "c": np.ascontiguousarray(inputs["c"][b:b + 1]), "c_ctx": np.ascontiguousarray(inputs["c_ctx"][None, :])}
        for key, (name, l, _ap) in G.used.items():
            m[key] = np.ascontiguousarray(inputs[name][l])
        for key in G.kc:
            m[key] = consts[key]
        in_maps.append(m)
    res = run_bass_kernel_spmd(nc, in_maps, core_ids=list(range(8)))
    return np.stack([r["out"] for r in res.results], axis=0).astype(np.float32)
```

```python
import math
import numpy as np
from contextlib import ExitStack
import concourse.bass as bass
import concourse.mybir as mybir
from concourse.bass_utils import run_bass_kernel_spmd

F32 = mybir.dt.float32
BF16 = mybir.dt.bfloat16
AF = mybir.ActivationFunctionType
ALU = mybir.AluOpType
AX = mybir.AxisListType

NDS = 8
SAME_ENGINE_SYNC = True
EMBED_WAIT = True

D = 1024
TL = 2048
TC = 256
T = TL + TC
NT = T // 128
DFF = 2816
NFC = DFF // 128
DEPTH = 4
ALPHA = (2.0 * DEPTH) ** 0.25
LN_EPS = 1e-6

WEIGHT_SPECS = [
    ("mod_w", (4, 1024, 6144)), ("mod_b", (4, 6144)), ("ln1_g", (4, 1024)), ("ln1_b", (4, 1024)),
    ("ln2_g", (4, 1024)), ("ln2_b", (4, 1024)), ("ffn_w1", (4, 1024, 2816)), ("ffn_w3", (4, 1024, 2816)),
    ("ffn_w2", (4, 2816, 1024)), ("ev_w_in", (2, 1024, 3584)), ("ev_w_out", (2, 1024, 1024)),
    ("da_lam_q1", (2, 64)), ("da_lam_k1", (2, 64)), ("da_lam_q2", (2, 64)), ("da_lam_k2", (2, 64)),
    ("da_gn_g", (2, 512)), ("rt_decay_logit", (2, 2, 4)), ("rw_mu", (2, 6, 1024)),
    ("rw_wr", (2, 1024, 1024)), ("rw_wk", (2, 1024, 1024)), ("rw_wv", (2, 1024, 1024)), ("rw_wo", (2, 1024, 1024)),
    ("rw_w0", (2, 2, 1024)), ("rw_w1", (2, 2, 1024, 64)), ("rw_w2", (2, 2, 64, 1024)),
    ("rw_a0", (2, 2, 1024)), ("rw_a1", (2, 2, 1024, 64)), ("rw_a2", (2, 2, 64, 1024)),
    ("rw_v0", (1, 1024)), ("rw_v1", (1, 1024, 32)), ("rw_v2", (1, 32, 1024)),
    ("rw_g1", (2, 1024, 160)), ("rw_g2", (2, 160, 1024)), ("rw_kk", (2, 1024)), ("rw_ka", (2, 1024)),
    ("rw_rk", (2, 16, 64)), ("rw_lnx_g", (2, 1024)), ("rw_lnx_b", (2, 1024)),
]


class Prog:
    def __init__(self, nc):
        self.nc = nc
        self.engs = {'pe': nc.tensor, 'act': nc.scalar, 'dve': nc.vector, 'pool': nc.gpsimd, 'sp': nc.sync}
        self.es = ExitStack()
        self.sem = {}
        for e in ['pe', 'act', 'dve', 'pool']:
            self.sem[('e', e)] = self.es.enter_context(nc.semaphore('s_' + e))
        for i in range(NDS):
            self.sem[('d', i)] = self.es.enter_context(nc.semaphore('d%d' % i))
        self.cnt = {k: 0 for k in self.sem}
        self.dnext = 0
        self.known = {e: {} for e in self.engs}
        self.res = {}
        self.nops = 0
        self.nwaits = 0

    def _get(self, key):
        name, sub = key if isinstance(key, tuple) else (key, None)
        d = self.res.setdefault(name, {})
        if sub not in d:
            d[sub] = [None, {}]
        return d[sub]

    def _conf(self, key):
        name, sub = key if isinstance(key, tuple) else (key, None)
        d = self.res.setdefault(name, {})
        if sub is None:
            return list(d.values())
        out = []
        if sub in d:
            out.append(d[sub])
        if None in d:
            out.append(d[None])
        return out

    def op(self, eng, fn, r=(), w=(), dma=False, noembed=False):
        deps = {}

        def need(k, v):
            if deps.get(k, 0) < v:
                deps[k] = v
        for key in r:
            for st in self._conf(key):
                if st[0] is not None:
                    need(*st[0])
        for key in w:
            for st in self._conf(key):
                if st[0] is not None:
                    need(*st[0])
                for k, v in st[1].items():
                    need(k, v)
        E = self.engs[eng]
        kn = self.known[eng]
        if dma:
            d = self.dnext
            self.dnext = (d + 1) % NDS
            sk = ('d', d)
            if self.cnt[sk]:
                need(sk, self.cnt[sk])
        else:
            sk = ('e', eng)
        wl = []
        for k, v in deps.items():
            if (not dma) and k == ('e', eng) and (eng == 'pe' or not SAME_ENGINE_SYNC):
                continue
            if kn.get(k, 0) >= v:
                continue
            kn[k] = v
            wl.append((k, v))
        emb = None
        if wl and EMBED_WAIT and not dma and not noembed:
            emb = wl.pop()
        for k, v in wl:
            E.wait_ge(self.sem[k], v)
            self.nwaits += 1
        ins = fn(E)
        if emb is not None:
            ins.wait_op(self.sem[emb[0]], emb[1], "sem-ge")
        inc = 16 if dma else 1
        self.cnt[sk] += inc
        ins.then_inc(self.sem[sk], inc)
        ev = (sk, self.cnt[sk])
        self.nops += 1
        for key in r:
            st = self._get(key)
            if st[1].get(ev[0], 0) < ev[1]:
                st[1][ev[0]] = ev[1]
        for key in w:
            name, sub = key if isinstance(key, tuple) else (key, None)
            if sub is None:
                self.res[name] = {None: [ev, {}]}
            else:
                st = self._get(key)
                st[0] = ev
                st[1] = {}
        return ev

    def dma(self, out, in_, r=(), w=(), eng='sp', **kw):
        return self.op(eng, lambda e: e.dma_start(out=out, in_=in_, **kw), r=r, w=w, dma=True)

    def barrier(self, engines=('pe', 'act', 'dve', 'pool', 'sp')):
        for eng in engines:
            E = self.engs[eng]
            kn = self.known[eng]
            for k, v in self.cnt.items():
                if v and kn.get(k, 0) < v:
                    kn[k] = v
                    E.wait_ge(self.sem[k], v)
                    self.nwaits += 1


class Ctx:
    pass


_SBN = [0]


def sb(nc, es, name, shape, dt=F32):
    _SBN[0] += 1
    return es.enter_context(nc.sbuf_tensor("%s_u%d" % (name, _SBN[0]), list(shape), dt))


_CLN = [0]


def colload(G, es, dst2d, dkey, src_flat, n):
    nc, P = G.nc, G.P
    _CLN[0] += 1
    k = "cl_stg%d" % _CLN[0]
    stg = sb(nc, es, k, [n, 128])
    P.dma(stg[:], src_flat.rearrange("(j p) -> j p", p=128), w=[k])
    P.op('pe', lambda e: e.transpose(out=G.ps[7][:, 0:n], in_=stg[:, :], identity=G.identF[0:n, 0:n]), r=[k, "ident"], w=[("ps", 7)])
    P.op('dve', lambda e: e.tensor_copy(out=dst2d, in_=G.ps[7][:, 0:n]), r=[("ps", 7)], w=[dkey, ("ps", 7)])


def stage_modvec(G, layers):
    nc, P = G.nc, G.P
    with ExitStack() as es:
        craw = sb(nc, es, "mv_craw", [128, 2, 8])
        cT = sb(nc, es, "mv_cT", [128, 8, 2])
        wt = [sb(nc, es, "mv_w%d" % i, [128, 8, 512]) for i in range(2)]
        bt = [sb(nc, es, "mv_b%d" % i, [2, 512]) for i in range(2)]
        ot = [sb(nc, es, "mv_o%d" % i, [2, 512]) for i in range(2)]
        colload(G, es, craw[:, 0, :], "mv_craw", G.c_d[0, :], 8)
        colload(G, es, craw[:, 1, :], "mv_craw", G.cc_d[0, :], 8)
        for r in range(2):
            P.op('act', lambda e, r=r: e.activation(out=cT[:, :, r], in_=craw[:, r, :], func=AF.Silu),
                 r=["mv_craw"], w=[("mv_cT", r)])
        i = 0
        for l in layers:
            for nb in range(12):
                b = i % 2
                i += 1
                P.dma(wt[b][:], G.Wl("mod_w", l)[:, nb * 512:(nb + 1) * 512].rearrange("(c p) n -> p c n", p=128),
                      w=[("mv_w", b)])
                P.dma(bt[b][:], G.Wl("mod_b", l).rearrange("(o n) -> o n", o=1)[:, nb * 512:(nb + 1) * 512].partition_broadcast(2), w=[("mv_b", b)])
                ps = G.ps[b]
                for c in range(8):
                    P.op('pe', lambda e, c=c, b=b, ps=ps: e.matmul(ps[0:2, :], lhsT=cT[:, c, :], rhs=wt[b][:, c, :],
                                                                  start=(c == 0), stop=(c == 7)),
                         r=["mv_cT", ("mv_w", b)], w=[("ps", b)])
                P.op('dve', lambda e, b=b, ps=ps: e.tensor_tensor(out=ot[b][:], in0=ps[0:2, :], in1=bt[b][:], op=ALU.add),
                     r=[("ps", b), ("mv_b", b)], w=[("mv_o", b), ("ps", b)])
                P.dma(G.modv[l, :, nb * 512:(nb + 1) * 512], ot[b][:], r=[("mv_o", b)], w=[("modv", l)])
        P.barrier()


def load_modcols(G, es, l, name):
    nc, P = G.nc, G.P
    mc = sb(nc, es, name, [128, 2, 6, 8])
    for r in range(2):
        _CLN[0] += 1
        k = "cl_stg%d" % _CLN[0]
        stg = sb(nc, es, k, [48, 128])
        P.dma(stg[:], G.modv[l, r, :].rearrange("(j p) -> j p", p=128), r=[("modv", l)], w=[k])
        P.op('pe', lambda e, stg=stg: e.transpose(out=G.ps[7][:, 0:48], in_=stg[:, :], identity=G.identF[0:48, 0:48]), r=[k, "ident"], w=[("ps", 7)])
        P.op('dve', lambda e, r=r: e.tensor_copy(out=mc[:, r, :, :].rearrange("p i c -> p (i c)"), in_=G.ps[7][:, 0:48]), r=[("ps", 7)], w=[name, ("ps", 7)])
    for r in range(2):
        for i in (1, 4):
            P.op('dve', lambda e, r=r, i=i: e.tensor_scalar_add(out=mc[:, r, i, :], in0=mc[:, r, i, :], scalar1=1.0),
                 r=[name], w=[name])
    return mc


def transpose_modulate(G, xt, xkey, hT, hkey, col0, mc, r, ish, isc, pbase):
    P = G.P
    for half in range(2):
        pb = pbase + half
        ps = G.ps[pb]
        for j in range(4):
            c = half * 4 + j
            P.op('pe', lambda e, c=c, j=j, ps=ps: e.transpose(out=ps[:, j * 128:(j + 1) * 128], in_=xt[:, c * 128:(c + 1) * 128],
                                                           identity=G.identF[:]),
                 r=[xkey, "ident"], w=[("ps", pb)])
        for j in range(4):
            c = half * 4 + j
            eng = 'act' if j % 2 == 0 else 'dve'
            if eng == 'act':
                P.op('act', lambda e, c=c, j=j, ps=ps: e.activation(out=hT[:, c, col0:col0 + 128], in_=ps[:, j * 128:(j + 1) * 128],
                                                                   func=AF.Identity, scale=mc[:, r, isc, c:c + 1], bias=mc[:, r, ish, c:c + 1]),
                     r=[("ps", pb), "mc"], w=[hkey, ("ps", pb)])
            else:
                P.op('dve', lambda e, c=c, j=j, ps=ps: e.tensor_scalar(out=hT[:, c, col0:col0 + 128], in0=ps[:, j * 128:(j + 1) * 128],
                                                                      scalar1=mc[:, r, isc, c:c + 1], scalar2=mc[:, r, ish, c:c + 1],
                                                                      op0=ALU.mult, op1=ALU.add),
                     r=[("ps", pb), "mc"], w=[hkey, ("ps", pb)])


def ln_epilogue(G, o_banks, okeys, xt, xkey, Gt, LGt, LBt, bkeys, tmp, tkey, st, skey, yt, ykey):
    P = G.P
    for h in range(2):
        sl = slice(h * 512, (h + 1) * 512)
        P.op('dve', lambda e, h=h, sl=sl: e.tensor_tensor(out=tmp[:, sl], in0=o_banks[h][:, :], in1=Gt[:, sl], op=ALU.mult),
             r=[okeys[h]] + bkeys, w=[tkey, okeys[h]])
    P.op('dve', lambda e: e.scalar_tensor_tensor(out=tmp[:, :], in0=xt[:, :], scalar=ALPHA, in1=tmp[:, :], op0=ALU.mult, op1=ALU.add),
         r=[xkey, tkey], w=[tkey])
    for h in range(2):
        P.op('dve', lambda e, h=h: e.bn_stats(out=st[:, h * 6:(h + 1) * 6], in_=tmp[:, h * 512:(h + 1) * 512]), r=[tkey], w=[skey])
    P.op('dve', lambda e: e.bn_aggr(out=st[:, 12:14], in_=st[:, 0:12]), r=[skey], w=[skey])
    P.op('act', lambda e: e.activation(out=st[:, 14:15], in_=st[:, 13:14], func=AF.Sqrt, bias=G.eps_ln[:, 0:1], scale=1.0), r=[skey, "consts"], w=[skey])
    P.op('dve', lambda e: e.reciprocal(out=st[:, 15:16], in_=st[:, 14:15]), r=[skey], w=[skey])
    P.op('dve', lambda e: e.scalar_tensor_tensor(out=tmp[:, :], in0=tmp[:, :], scalar=st[:, 12:13], in1=LGt[:, :], op0=ALU.subtract, op1=ALU.mult),
         r=[skey, tkey] + bkeys, w=[tkey])
    P.op('dve', lambda e: e.scalar_tensor_tensor(out=yt[:, :], in0=tmp[:, :], scalar=st[:, 15:16], in1=LBt[:, :], op0=ALU.mult, op1=ALU.add),
         r=[skey, tkey] + bkeys, w=[ykey])


def load_bcast(G, tile, key, src_row):
    G.P.dma(tile[:], src_row.partition_broadcast(128), w=[key])


def stage_ffn(G, l):
    nc, P = G.nc, G.P
    W1, W3, W2 = G.Wl("ffn_w1", l), G.Wl("ffn_w3", l), G.Wl("ffn_w2", l)
    with ExitStack() as es:
        mc = load_modcols(G, es, l, "mc")
        gate = [sb(nc, es, "f_gate%d" % r, [128, 1024]) for r in range(2)]
        LG = sb(nc, es, "f_lg", [128, 1024])
        LB = sb(nc, es, "f_lb", [128, 1024])
        for r in range(2):
            P.dma(gate[r][:], G.modv[l, r:r + 1, 5 * 1024:6 * 1024].partition_broadcast(128), r=[("modv", l)], w=["f_bc"])
        P.dma(LG[:], G.Wl("ln2_g", l).rearrange("(o n) -> o n", o=1).partition_broadcast(128), w=["f_bc"])
        P.dma(LB[:], G.Wl("ln2_b", l).rearrange("(o n) -> o n", o=1).partition_broadcast(128), w=["f_bc"])
        w2b = sb(nc, es, "f_w2b", [128, NFC, 1024], BF16)
        w2s = [sb(nc, es, "f_w2s%d" % i, [128, 2, 1024]) for i in range(2)]
        for i in range(NFC // 2):
            b = i % 2
            P.dma(w2s[b][:], W2[i * 256:(i + 1) * 256, :].rearrange("(c p) n -> p c n", p=128), w=[("f_w2s", b)])
            P.op('pool', lambda e, i=i, b=b: e.tensor_copy(out=w2b[:, 2 * i:2 * i + 2, :], in_=w2s[b][:]),
                 r=[("f_w2s", b)], w=[("f_w2b", i)])
        NH = 2
        TPH = NT // NH
        TOKH = TPH * 128
        NTB = TOKH // 384
        hT = sb(nc, es, "f_hT", [128, 8, TOKH], BF16)
        gT = sb(nc, es, "f_gT", [128, NFC, TOKH], BF16)
        xt = [sb(nc, es, "f_x%d" % i, [128, 1024]) for i in range(2)]
        w1s = [sb(nc, es, "f_w1s%d" % i, [128, 8, 128]) for i in range(2)]
        w3s = [sb(nc, es, "f_w3s%d" % i, [128, 8, 128]) for i in range(2)]
        w1b = [sb(nc, es, "f_w1b%d" % i, [128, 8, 128], BF16) for i in range(2)]
        w3b = [sb(nc, es, "f_w3b%d" % i, [128, 8, 128], BF16) for i in range(2)]
        sa = [sb(nc, es, "f_sa%d" % i, [128, 384]) for i in range(2)]
        tmp = [sb(nc, es, "f_tmp%d" % i, [128, 1024]) for i in range(2)]
        yt = [sb(nc, es, "f_y%d" % i, [128, 1024]) for i in range(2)]
        st = [sb(nc, es, "f_st%d" % i, [128, 16]) for i in range(2)]
        xi = 0
        wi = 0
        for hf in range(NH):
            for tt in range(TPH):
                t = hf * TPH + tt
                b = xi % 2
                xi += 1
                r = 0 if t < 16 else 1
                P.dma(xt[b][:], G.xres[t * 128:(t + 1) * 128, :], r=[("xres", t)], w=[("f_x", b)])
                transpose_modulate(G, xt[b], ("f_x", b), hT, ("f_hT", tt), tt * 128, mc, r, 3, 4, 0)
            for cb in range(NFC):
                b = wi % 2
                wi += 1
                P.dma(w1s[b][:], W1[:, cb * 128:(cb + 1) * 128].rearrange("(c p) n -> p c n", p=128), w=[("f_w1s", b)])
                P.dma(w3s[b][:], W3[:, cb * 128:(cb + 1) * 128].rearrange("(c p) n -> p c n", p=128), w=[("f_w3s", b)])
                P.op('pool', lambda e, b=b: e.tensor_copy(out=w1b[b][:], in_=w1s[b][:]), r=[("f_w1s", b)], w=[("f_w1b", b)])
                P.op('pool', lambda e, b=b: e.tensor_copy(out=w3b[b][:], in_=w3s[b][:]), r=[("f_w3s", b)], w=[("f_w3b", b)])
                for sub in range(1):
                    fc = cb
                    for tb in range(NTB):
                        tsl = slice(tb * 384, (tb + 1) * 384)
                        k = (fc * NTB + tb) % 2
                        pa, pb_ = 2 + 2 * k, 3 + 2 * k
                        for c in range(8):
                            P.op('pe', lambda e, c=c, pa=pa, b=b, sub=sub, tsl=tsl: e.matmul(
                                G.ps[pa][:, 0:384], lhsT=w1b[b][:, c, sub * 128:(sub + 1) * 128], rhs=hT[:, c, tsl],
                                start=(c == 0), stop=(c == 7)), r=[("f_w1b", b), "f_hT"], w=[("ps", pa)])
                        for c in range(8):
                            P.op('pe', lambda e, c=c, pb_=pb_, b=b, sub=sub, tsl=tsl: e.matmul(
                                G.ps[pb_][:, 0:384], lhsT=w3b[b][:, c, sub * 128:(sub + 1) * 128], rhs=hT[:, c, tsl],
                                start=(c == 0), stop=(c == 7)), r=[("f_w3b", b), "f_hT"], w=[("ps", pb_)])
                        P.op('act', lambda e, k=k, pa=pa: e.activation(out=sa[k][:, :], in_=G.ps[pa][:, 0:384], func=AF.Silu),
                             r=[("ps", pa)], w=[("f_sa", k), ("ps", pa)])
                        P.op('dve', lambda e, k=k, pb_=pb_, fc=fc, tsl=tsl: e.tensor_tensor(
                            out=gT[:, fc, tsl], in0=G.ps[pb_][:, 0:384], in1=sa[k][:, :], op=ALU.mult),
                            r=[("ps", pb_), ("f_sa", k)], w=[("f_gT", fc), ("ps", pb_)])
            for tt in range(TPH):
                t = hf * TPH + tt
                b = xi % 2
                xi += 1
                r = 0 if t < 16 else 1
                P.dma(xt[b][:], G.xres[t * 128:(t + 1) * 128, :], r=[("xres", t)], w=[("f_x", b)])
                k = tt % 2
                banks = [G.ps[0 + 2 * k], G.ps[1 + 2 * k]]
                okeys = [("ps", 0 + 2 * k), ("ps", 1 + 2 * k)]
                for h in range(2):
                    for fc in range(NFC):
                        P.op('pe', lambda e, h=h, fc=fc, tt=tt, banks=banks: e.matmul(
                            banks[h][:, :], lhsT=gT[:, fc, tt * 128:(tt + 1) * 128], rhs=w2b[:, fc, h * 512:(h + 1) * 512],
                            start=(fc == 0), stop=(fc == NFC - 1)), r=["f_gT", "f_w2b"], w=[okeys[h]])
                ln_epilogue(G, banks, okeys, xt[b], ("f_x", b), gate[r], LG, LB, ["f_bc"], tmp[k], ("f_tmp", k),
                            st[k], ("f_st", k), yt[k], ("f_y", k))
                P.dma(G.xres[t * 128:(t + 1) * 128, :], yt[k][:], r=[("f_y", k)], w=[("xres", t)])
        P.barrier()


def row(ap):
    return ap.rearrange("(o n) -> o n", o=1)


def inproj_block(G, es_w, Wap, j0, ncols, hT, hkey, wtag):
    nc, P = G.nc, G.P
    ws = sb(nc, es_w, wtag + "_s", [128, 8, ncols])
    wb = sb(nc, es_w, wtag + "_b", [128, 8, ncols], BF16)
    P.dma(ws[:], Wap[:, j0:j0 + ncols].rearrange("(c p) n -> p c n", p=128), w=[wtag + "_s"])
    P.op('pool', lambda e: e.tensor_copy(out=wb[:], in_=ws[:]), r=[wtag + "_s"], w=[wtag + "_b"])
    return wb


def rope_evac(G, ps, pkey, t, qr, qkey, cosT, sinT, ckey, tmp, tkey, half, scale=None):
    P = G.P
    ng = 512 // (2 * half)
    if t >= 16:
        if scale is None:
            P.op('act', lambda e: e.copy(out=qr[:, :], in_=ps[:, :]), r=[pkey], w=[qkey, pkey])
        else:
            P.op('act', lambda e: e.mul(out=qr[:, :], in_=ps[:, :], mul=scale), r=[pkey], w=[qkey, pkey])
        return
    pv = ps[:, :].rearrange("p (g two d) -> p g two d", g=ng, two=2)
    qv = qr[:, :].rearrange("p (g two d) -> p g two d", g=ng, two=2)
    x1, x2 = pv[:, :, 0, :], pv[:, :, 1, :]
    cs = cosT[:, :].rearrange("p (g d) -> p g d", g=ng)
    sn = sinT[:, :].rearrange("p (g d) -> p g d", g=ng)
    t1 = tmp[:, 0:256].rearrange("p (g d) -> p g d", g=ng)
    t2 = tmp[:, 256:512].rearrange("p (g d) -> p g d", g=ng)
    P.op('dve', lambda e: e.tensor_tensor(out=t1, in0=x1, in1=cs, op=ALU.mult), r=[pkey, ckey], w=[tkey])
    P.op('dve', lambda e: e.tensor_tensor(out=t2, in0=x2, in1=sn, op=ALU.mult), r=[pkey, ckey], w=[tkey])
    P.op('dve', lambda e: e.tensor_tensor(out=qv[:, :, 0, :], in0=t1, in1=t2, op=ALU.subtract), r=[tkey], w=[qkey])
    P.op('dve', lambda e: e.tensor_tensor(out=t1, in0=x1, in1=sn, op=ALU.mult), r=[pkey, ckey], w=[tkey])
    P.op('dve', lambda e: e.tensor_tensor(out=t2, in0=x2, in1=cs, op=ALU.mult), r=[pkey, ckey], w=[tkey, pkey])
    P.op('dve', lambda e: e.tensor_tensor(out=qv[:, :, 1, :], in0=t1, in1=t2, op=ALU.add), r=[tkey], w=[qkey])
    if scale is not None:
        P.op('act', lambda e: e.mul(out=qr[:, :], in_=qr[:, :], mul=scale), r=[qkey], w=[qkey])


def transpose4(G, src, skey, dstT, dkey, t, pbank):
    P = G.P
    psb = G.ps[pbank][:, 0:256].bitcast(BF16)
    for g in range(4):
        P.op('pe', lambda e, g=g: e.transpose(out=psb[:, g * 128:(g + 1) * 128], in_=src[:, g * 128:(g + 1) * 128], identity=G.identB[:]),
             r=[skey, "identB"], w=[("ps", pbank)])
    P.op('act', lambda e: e.copy(out=dstT[:, :, t * 128:(t + 1) * 128], in_=psb.rearrange("p (g n) -> p g n", g=4)),
         r=[("ps", pbank)], w=[dkey, ("ps", pbank)])


def stage_even(G, l):
    nc, P = G.nc, G.P
    e = l // 2
    lam_init = 0.8 - 0.6 * math.exp(-0.3 * l)
    Win, Wout = G.Wl("ev_w_in", e), G.Wl("ev_w_out", e)
    mo_d = G.mo_d
    with ExitStack() as es:
        mc = load_modcols(G, es, l, "mc")
        hT = sb(nc, es, "e_hT", [128, 8, T], BF16)
        xt = [sb(nc, es, "e_x%d" % i, [128, 1024]) for i in range(2)]
        for t in range(NT):
            b = t % 2
            P.dma(xt[b][:], G.xres[t * 128:(t + 1) * 128, :], r=[("xres", t)], w=[("e_x", b)])
            transpose_modulate(G, xt[b], ("e_x", b), hT, ("e_hT", t), t * 128, mc, 0 if t < 16 else 1, 0, 1, 0)
        cosT = [sb(nc, es, "e_cos%d" % i, [128, 256]) for i in range(2)]
        sinT = [sb(nc, es, "e_sin%d" % i, [128, 256]) for i in range(2)]
        qr = [sb(nc, es, "e_qr%d" % i, [128, 512], BF16) for i in range(2)]
        rtmp = [sb(nc, es, "e_rtmp%d" % i, [128, 512]) for i in range(2)]

        def proj_tile(wb, wkey, t, pbank):
            ps = G.ps[pbank]
            for c in range(8):
                P.op('pe', lambda e_, c=c: e_.matmul(ps[:, :], lhsT=hT[:, c, t * 128:(t + 1) * 128], rhs=wb[:, c, :],
                                                    start=(c == 0), stop=(c == 7)), r=[("e_hT", t), wkey], w=[("ps", pbank)])
            return ps

        def load_tables(t, b, which):
            if t < 16:
                P.dma(cosT[b][:], G.KC("k_cos" + which, [TL, 256])[t * 128:(t + 1) * 128, :], w=[("e_cs", b)])
                P.dma(sinT[b][:], G.KC("k_sin" + which, [TL, 256])[t * 128:(t + 1) * 128, :], w=[("e_cs", b)])

        with ExitStack() as esA:
            aqT = sb(nc, esA, "a_qT", [128, 4, T], BF16)
            akT = sb(nc, esA, "a_kT", [128, 4, T], BF16)
            av = sb(nc, esA, "a_v", [128, NT, 512], BF16)
            for j, dst in ((0, aqT), (1, akT)):
                with ExitStack() as esw:
                    wb = inproj_block(G, esw, Win, j * 512, 512, hT, "e_hT", "a_w%d" % j)
                    for t in range(NT):
                        b = t % 2
                        load_tables(t, b, "A")
                        ps = proj_tile(wb, "a_w%d_b" % j, t, 4 + b)
                        rope_evac(G, ps, ("ps", 4 + b), t, qr[b], ("e_qr", b), cosT[b], sinT[b], ("e_cs", b), rtmp[b], ("e_rtmp", b), 32)
                        transpose4(G, qr[b], ("e_qr", b), dst, ("a_T%d" % j, t), t, 6 + b)
                    P.barrier()
            with ExitStack() as esw:
                wb = inproj_block(G, esw, Win, 2 * 512, 512, hT, "e_hT", "a_w2")
                for t in range(NT):
                    b = t % 2
                    ps = proj_tile(wb, "a_w2_b", t, 4 + b)
                    P.op('act', lambda e_, t=t, ps=ps: e_.copy(out=av[:, t, :], in_=ps[:, :]), r=[("ps", 4 + b)], w=[("a_v", t), ("ps", 4 + b)])
                P.barrier()
            lam4 = sb(nc, esA, "a_lam4", [128, 4, 64])
            lamc = sb(nc, esA, "a_lamc", [128, 8])
            for i, nm in enumerate(("da_lam_q1", "da_lam_k1", "da_lam_q2", "da_lam_k2")):
                P.dma(lam4[:, i, :], row(G.Wl(nm, e)).partition_broadcast(128), w=["a_lam4"])
            for i in range(2):
                P.op('dve', lambda e_, i=i: e_.tensor_tensor(out=lam4[:, 2 * i, :], in0=lam4[:, 2 * i, :], in1=lam4[:, 2 * i + 1, :], op=ALU.mult),
                     r=["a_lam4"], w=["a_lam4"])
                P.op('dve', lambda e_, i=i: e_.reduce_sum(out=lamc[:, i:i + 1], in_=lam4[:, 2 * i, :], axis=AX.X), r=["a_lam4"], w=["a_lamc"])
            P.op('act', lambda e_: e_.activation(out=lamc[:, 2:4], in_=lamc[:, 0:2], func=AF.Exp), r=["a_lamc"], w=["a_lamc"])
            P.op('dve', lambda e_: e_.tensor_tensor(out=lamc[:, 4:5], in0=lamc[:, 3:4], in1=lamc[:, 2:3], op=ALU.subtract), r=["a_lamc"], w=["a_lamc"])
            P.op('dve', lambda e_: e_.tensor_scalar_add(out=lamc[:, 5:6], in0=lamc[:, 4:5], scalar1=-lam_init), r=["a_lamc"], w=["a_lamc"])
            neglam = lamc[:, 5:6]
            Pm = [sb(nc, esA, "a_P%d" % i, [128, T], BF16) for i in range(2)]
            PT = [sb(nc, esA, "a_PT%d" % i, [128, NT, 128], BF16) for i in range(2)]
            sm = [sb(nc, esA, "a_sm%d" % i, [128, 16]) for i in range(2)]
            ao = sb(nc, esA, "a_ao", [128, 512])
            ao2 = sb(nc, esA, "a_ao2", [128, 512])
            aob = [sb(nc, esA, "a_aob%d" % i, [128, 512], BF16) for i in range(2)]
            rs = sb(nc, esA, "a_rs", [128, 16])
            SC = 64 ** -0.5
            def kinfo(qt):
                ktiles = list(range(NT)) if qt < 16 else [16, 17]
                k0 = ktiles[0] * 128
                nk = len(ktiles) * 128
                return ktiles, k0, nk, (nk + 511) // 512

            def stage1(ui, qt, h, m):
                ktiles, k0, nk, nbk = kinfo(qt)
                pb_ = ui % 2
                smt, Pmt = sm[pb_], Pm[pb_]
                psl = slice(m * 64, (m + 1) * 64)
                for jb in range(nbk):
                    w_ = min(512, nk - jb * 512)
                    P.op('pe', lambda e_, jb=jb, w_=w_: e_.matmul(G.ps[jb][:, 0:w_], lhsT=aqT[psl, h, qt * 128:(qt + 1) * 128],
                                                                 rhs=akT[psl, h, k0 + jb * 512:k0 + jb * 512 + w_], start=True, stop=True),
                         r=[("a_T0", qt), "a_T1"], w=[("ps", jb)])
                for jb in range(nbk):
                    w_ = min(512, nk - jb * 512)
                    P.op('dve', lambda e_, jb=jb, w_=w_: e_.reduce_max(out=smt[:, jb:jb + 1], in_=G.ps[jb][:, 0:w_], axis=AX.X),
                         r=[("ps", jb)], w=[("a_sm", pb_), ("ps", jb)])
                P.op('dve', lambda e_: e_.reduce_max(out=smt[:, 6:7], in_=smt[:, 0:nbk], axis=AX.X), r=[("a_sm", pb_)], w=[("a_sm", pb_)])
                P.op('dve', lambda e_: e_.tensor_scalar_mul(out=smt[:, 7:8], in0=smt[:, 6:7], scalar1=-SC), r=[("a_sm", pb_)], w=[("a_sm", pb_)])
                for jb in range(nbk):
                    w_ = min(512, nk - jb * 512)
                    P.op('act', lambda e_, jb=jb, w_=w_: e_.activation(out=Pmt[:, jb * 512:jb * 512 + w_], in_=G.ps[jb][:, 0:w_], func=AF.Exp,
                                                                      scale=SC, bias=smt[:, 7:8], accum_out=smt[:, 8 + jb:9 + jb]),
                         r=[("ps", jb), ("a_sm", pb_)], w=[("a_P", pb_), ("a_sm", pb_), ("ps", jb)], noembed=True)
                P.op('act', lambda e_: e_.copy(out=smt[:, 0:nbk], in_=smt[:, 8:8 + nbk]), r=[("a_sm", pb_)], w=[("a_sm", pb_)])
                P.op('dve', lambda e_: e_.reduce_sum(out=smt[:, 14:15], in_=smt[:, 0:nbk], axis=AX.X), r=[("a_sm", pb_)], w=[("a_sm", pb_)])
                P.op('dve', lambda e_: e_.reciprocal(out=smt[:, 15:16], in_=smt[:, 14:15]), r=[("a_sm", pb_)], w=[("a_sm", pb_)])

            def stage2(ui, qt, h, m):
                ktiles, k0, nk, nbk = kinfo(qt)
                pb_ = ui % 2
                smt, Pmt, PTt = sm[pb_], Pm[pb_], PT[pb_]
                nkt = len(ktiles)
                for g0 in range(0, nkt, 8):
                    gb = 5 + (g0 // 8) % 2
                    psb = G.ps[gb][:, :].bitcast(BF16)
                    n_ = min(8, nkt - g0)
                    for i in range(n_):
                        P.op('pe', lambda e_, i=i, g0=g0, psb=psb: e_.transpose(out=psb[:, i * 128:(i + 1) * 128],
                                                                             in_=Pmt[:, (g0 + i) * 128:(g0 + i + 1) * 128], identity=G.identB[:]),
                             r=[("a_P", pb_), "identB"], w=[("ps", gb)])
                    evac_eng = 'dve' if (g0 // 8) % 2 == 0 else 'act'
                    if evac_eng == 'dve':
                        P.op('dve', lambda e_, g0=g0, n_=n_, psb=psb: e_.tensor_copy(out=PTt[:, g0:g0 + n_, :],
                                                                                     in_=psb[:, 0:n_ * 128].rearrange("p (g n) -> p g n", g=n_)),
                             r=[("ps", gb)], w=[("a_PT", pb_), ("ps", gb)])
                    else:
                        P.op('act', lambda e_, g0=g0, n_=n_, psb=psb: e_.copy(out=PTt[:, g0:g0 + n_, :],
                                                                              in_=psb[:, 0:n_ * 128].rearrange("p (g n) -> p g n", g=n_)),
                             r=[("ps", gb)], w=[("a_PT", pb_), ("ps", gb)])
                osl = slice(m * 128, (m + 1) * 128)
                for i, kt in enumerate(ktiles):
                    P.op('pe', lambda e_, i=i, kt=kt: e_.matmul(G.ps[7][:, osl], lhsT=PTt[:, i, :], rhs=av[:, kt, h * 128:(h + 1) * 128],
                                                               start=(i == 0), stop=(i == nkt - 1)),
                         r=[("a_PT", pb_), "a_v"], w=[("ps", 7)])
                if m == 0:
                    P.op('dve', lambda e_: e_.tensor_scalar(out=ao2[:, 0:128], in0=G.ps[7][:, 0:128], scalar1=smt[:, 15:16], scalar2=None, op0=ALU.mult),
                         r=[("ps", 7), ("a_sm", pb_)], w=["a_ao2", ("ps", 7)])
                else:
                    P.op('dve', lambda e_: e_.tensor_tensor(out=smt[:, 13:14], in0=smt[:, 15:16], in1=neglam, op=ALU.mult),
                         r=[("a_sm", pb_), "a_lamc"], w=[("a_sm", pb_)])
                    P.op('dve', lambda e_: e_.scalar_tensor_tensor(out=ao[:, h * 128:(h + 1) * 128], in0=G.ps[7][:, 128:256], scalar=smt[:, 13:14],
                                                                  in1=ao2[:, 0:128], op0=ALU.mult, op1=ALU.add),
                         r=[("ps", 7), ("a_sm", pb_), "a_ao2"], w=["a_ao", ("ps", 7)])
                if h == 3 and m == 1:
                    P.op('pool', lambda e_: e_.tensor_tensor(out=ao2[:, :], in0=ao[:, :], in1=ao[:, :], op=ALU.mult), r=["a_ao"], w=["a_ao2"])
                    P.op('dve', lambda e_: e_.reduce_sum(out=rs[:, 0:4], in_=ao2[:, :].rearrange("p (h d) -> p h d", h=4), axis=AX.X), r=["a_ao2"], w=["a_rs"])
                    P.op('act', lambda e_: e_.activation(out=rs[:, 4:8], in_=rs[:, 0:4], func=AF.Sqrt, scale=1.0 / 128, bias=G.eps_ln[:, 0:1]), r=["a_rs", "consts"], w=["a_rs"])
                    P.op('dve', lambda e_: e_.reciprocal(out=rs[:, 8:12], in_=rs[:, 4:8]), r=["a_rs"], w=["a_rs"])
                    ab = aob[qt % 2]
                    for hh in range(4):
                        P.op('pool', lambda e_, hh=hh: e_.tensor_scalar(out=ab[:, hh * 128:(hh + 1) * 128], in0=ao[:, hh * 128:(hh + 1) * 128],
                                                                      scalar1=rs[:, 8 + hh:9 + hh], scalar2=None, op0=ALU.mult),
                             r=["a_ao", "a_rs"], w=[("a_aob", qt % 2)])
                    P.dma(mo_d[qt * 128:(qt + 1) * 128, 0:512], ab[:, :], r=[("a_aob", qt % 2)], w=[("mo", (qt, 0))])

            units = [(qt, h, m) for qt in range(NT) for h in range(4) for m in range(2)]
            for ui, u_ in enumerate(units):
                stage1(ui, *u_)
                if ui > 0:
                    stage2(ui - 1, *units[ui - 1])
            stage2(len(units) - 1, *units[-1])
            P.barrier()
        with ExitStack() as esB:
            bqT = sb(nc, esB, "b_qT", [128, 4, T], BF16)
            bkT = sb(nc, esB, "b_kT", [128, 4, T], BF16)
            bk = sb(nc, esB, "b_k", [128, NT, 512], BF16)
            bv = sb(nc, esB, "b_v", [128, NT, 512], BF16)
            for j in (3, 4):
                with ExitStack() as esw:
                    wb = inproj_block(G, esw, Win, j * 512, 512, hT, "e_hT", "b_w%d" % j)
                    for t in range(NT):
                        b = t % 2
                        load_tables(t, b, "B")
                        ps = proj_tile(wb, "b_w%d_b" % j, t, 4 + b)
                        if j == 3:
                            rope_evac(G, ps, ("ps", 4 + b), t, qr[b], ("e_qr", b), cosT[b], sinT[b], ("e_cs", b), rtmp[b], ("e_rtmp", b), 64)
                            transpose4(G, qr[b], ("e_qr", b), bqT, ("b_qT", t), t, 6 + b)
                        else:
                            rope_evac(G, ps, ("ps", 4 + b), t, bk[:, t, :], ("b_k", t), cosT[b], sinT[b], ("e_cs", b), rtmp[b], ("e_rtmp", b), 64,
                                      scale=128 ** -0.5)
                            transpose4(G, bk[:, t, :], ("b_k", t), bkT, ("b_kT", t), t, 6 + b)
                    P.barrier()
            with ExitStack() as esw:
                wb = inproj_block(G, esw, Win, 5 * 512, 512, hT, "e_hT", "b_w5")
                for t in range(NT):
                    b = t % 2
                    ps = proj_tile(wb, "b_w5_b", t, 4 + b)
                    P.op('act', lambda e_, t=t, ps=ps: e_.copy(out=bv[:, t, :], in_=ps[:, :]), r=[("ps", 4 + b)], w=[("b_v", t), ("ps", 4 + b)])
                P.barrier()
            dc = sb(nc, esB, "b_dc", [128, 64])
            kcol = sb(nc, esB, "b_kcol", [128, 4])
            kmat = sb(nc, esB, "b_kmat", [128, 4, 128])
            Mm = sb(nc, esB, "b_M", [128, 4, 128])
            Mt = sb(nc, esB, "b_Mt", [128, 128])
            P.dma(dc[:, 0:8], G.Wl("rt_decay_logit", e).rearrange("(o a) b -> o (a b)", o=1).partition_broadcast(128), w=["b_dc"])
            P.dma(kcol[:], G.KC("k_cols", [128, 4])[:, :], w=["b_kc"])
            P.dma(kmat[:], G.KC("k_mats", [128, 4, 128])[:, :, :], w=["b_kc"])
            P.op('act', lambda e_: e_.activation(out=dc[:, 8:16], in_=dc[:, 0:8], func=AF.Sigmoid), r=["b_dc"], w=["b_dc"])
            P.op('act', lambda e_: e_.activation(out=dc[:, 16:24], in_=dc[:, 8:16], func=AF.Ln), r=["b_dc"], w=["b_dc"])
            lg = lambda d, h: dc[:, 16 + d * 4 + h:17 + d * 4 + h]
            P.op('act', lambda e_: e_.activation(out=dc[:, 24:32], in_=dc[:, 16:24], func=AF.Exp, scale=128.0), r=["b_dc"], w=["b_dc"])
            for h in range(4):
                for (o_, col, d) in ((32, 0, 0), (36, 1, 1), (40, 2, 0), (44, 3, 1)):
                    P.op('act', lambda e_, o_=o_, col=col, d=d, h=h: e_.activation(out=dc[:, o_ + h:o_ + h + 1], in_=kcol[:, col:col + 1], func=AF.Exp, scale=lg(d, h)),
                         r=["b_dc", "b_kc"], w=["b_dc"])
                P.op('act', lambda e_, h=h: e_.activation(out=Mt[:, :], in_=kmat[:, 0, :], func=AF.Exp, scale=lg(0, h)), r=["b_dc", "b_kc"], w=["b_Mt"])
                P.op('dve', lambda e_, h=h: e_.tensor_tensor(out=Mm[:, h, :], in0=Mt[:, :], in1=kmat[:, 2, :], op=ALU.mult), r=["b_Mt", "b_kc"], w=["b_M"])
                P.op('act', lambda e_, h=h: e_.activation(out=Mt[:, :], in_=kmat[:, 1, :], func=AF.Exp, scale=lg(1, h)), r=["b_dc", "b_kc"], w=["b_Mt"])
                P.op('dve', lambda e_, h=h: e_.tensor_tensor(out=Mt[:, :], in0=Mt[:, :], in1=kmat[:, 3, :], op=ALU.mult), r=["b_Mt", "b_kc"], w=["b_Mt"])
                P.op('dve', lambda e_, h=h: e_.tensor_tensor(out=Mm[:, h, :], in0=Mm[:, h, :], in1=Mt[:, :], op=ALU.add), r=["b_Mt", "b_M"], w=["b_M"])
            SfA = sb(nc, esB, "b_SfA", [128, NT, 512], BF16)
            Sst = sb(nc, esB, "b_S", [128, 512])
            Sbb = sb(nc, esB, "b_Sbb", [128, 512], BF16)
            kz = [sb(nc, esB, "b_kz%d" % i, [128, 512], BF16) for i in range(2)]

            def state_update(n, d, i):
                zb = kz[i % 2]
                for h in range(4):
                    P.op('dve', lambda e_, h=h: e_.tensor_scalar(out=zb[:, h * 128:(h + 1) * 128], in0=bk[:, n, h * 128:(h + 1) * 128],
                                                                scalar1=dc[:, 32 + 4 * d + h:33 + 4 * d + h], scalar2=None, op0=ALU.mult),
                         r=[("b_k", n), "b_dc"], w=[("b_kz", i % 2)])
                for h in range(4):
                    P.op('pe', lambda e_, h=h: e_.matmul(G.ps[3][:, h * 128:(h + 1) * 128], lhsT=zb[:, h * 128:(h + 1) * 128],
                                                        rhs=bv[:, n, h * 128:(h + 1) * 128], start=True, stop=True),
                         r=[("b_kz", i % 2), ("b_v", n)], w=[("ps", 3)])
                for h in range(4):
                    P.op('dve', lambda e_, h=h: e_.scalar_tensor_tensor(out=Sst[:, h * 128:(h + 1) * 128], in0=Sst[:, h * 128:(h + 1) * 128],
                                                                       scalar=dc[:, 24 + 4 * d + h:25 + 4 * d + h], in1=G.ps[3][:, h * 128:(h + 1) * 128],
                                                                       op0=ALU.mult, op1=ALU.add),
                         r=["b_S", "b_dc", ("ps", 3)], w=["b_S", ("ps", 3)])
            P.op('dve', lambda e_: e_.memset(Sst[:, :], 0.0), w=["b_S"])
            fwd = [16, 17] + list(range(16))
            for i, n in enumerate(fwd):
                P.op('act', lambda e_, n=n: e_.copy(out=SfA[:, n, :], in_=Sst[:, :]), r=["b_S"], w=[("b_SfA", n)])
                if i < len(fwd) - 1:
                    state_update(n, 0, i)
            P.op('dve', lambda e_: e_.memset(Sst[:, :], 0.0), r=["b_SfA"], w=["b_S"])
            with ExitStack() as esw:
                wg = inproj_block(G, esw, Win, 6 * 512, 512, hT, "e_hT", "b_w6")
                Sm = [sb(nc, esw, "b_Sm%d" % i, [128, 4, 128], BF16) for i in range(2)]
                bo = [sb(nc, esw, "b_o%d" % i, [128, 512]) for i in range(2)]
                bo2 = [sb(nc, esw, "b_o2%d" % i, [128, 512]) for i in range(2)]
                gs = [sb(nc, esw, "b_gs%d" % i, [128, 512]) for i in range(2)]
                bob = [sb(nc, esw, "b_ob%d" % i, [128, 512], BF16) for i in range(2)]
                brs = [sb(nc, esw, "b_rs%d" % i, [128, 16]) for i in range(2)]
                bwd = [17, 16] + list(range(15, -1, -1))
                for i, n in enumerate(bwd):
                    b = i % 2
                    csl = slice(n * 128, (n + 1) * 128)
                    P.op('act', lambda e_: e_.copy(out=Sbb[:, :], in_=Sst[:, :]), r=["b_S"], w=["b_Sbb"])
                    for h in range(4):
                        P.op('pe', lambda e_, h=h: e_.matmul(G.ps[0][:, h * 128:(h + 1) * 128], lhsT=bkT[:, h, csl], rhs=bqT[:, h, csl], start=True, stop=True),
                             r=[("b_kT", n), ("b_qT", n)], w=[("ps", 0)])
                    P.op('dve', lambda e_, b=b: e_.tensor_tensor(out=Sm[b][:, :, :], in0=G.ps[0][:, :].rearrange("p (h n) -> p h n", h=4), in1=Mm[:, :, :], op=ALU.mult),
                         r=[("ps", 0), "b_M"], w=[("b_Sm", b), ("ps", 0)])
                    for h in range(4):
                        hs = slice(h * 128, (h + 1) * 128)
                        P.op('pe', lambda e_, h=h, hs=hs, b=b: e_.matmul(G.ps[1][:, hs], lhsT=Sm[b][:, h, :], rhs=bv[:, n, hs], start=True, stop=True),
                             r=[("b_Sm", b), ("b_v", n)], w=[("ps", 1)])
                    for h in range(4):
                        hs = slice(h * 128, (h + 1) * 128)
                        P.op('pe', lambda e_, h=h, hs=hs: e_.matmul(G.ps[2][:, hs], lhsT=bqT[:, h, csl], rhs=SfA[:, n, hs], start=True, stop=True),
                             r=[("b_qT", n), ("b_SfA", n)], w=[("ps", 2)])
                    for h in range(4):
                        hs = slice(h * 128, (h + 1) * 128)
                        P.op('pe', lambda e_, h=h, hs=hs: e_.matmul(G.ps[4][:, hs], lhsT=bqT[:, h, csl], rhs=Sbb[:, hs], start=True, stop=True),
                             r=[("b_qT", n), "b_Sbb"], w=[("ps", 4)])
                    for c in range(8):
                        P.op('pe', lambda e_, c=c: e_.matmul(G.ps[5][:, :], lhsT=hT[:, c, csl], rhs=wg[:, c, :], start=(c == 0), stop=(c == 7)),
                             r=[("e_hT", n), "b_w6_b"], w=[("ps", 5)])
                    P.op('act', lambda e_, b=b: e_.activation(out=gs[b][:, :], in_=G.ps[5][:, :], func=AF.Silu), r=[("ps", 5)], w=[("b_gs", b), ("ps", 5)])
                    for h in range(4):
                        hs = slice(h * 128, (h + 1) * 128)
                        P.op('dve', lambda e_, h=h, hs=hs, b=b: e_.tensor_scalar(out=bo2[b][:, hs], in0=G.ps[2][:, hs], scalar1=dc[:, 40 + h:41 + h], scalar2=None, op0=ALU.mult),
                             r=[("ps", 2), "b_dc"], w=[("b_o2", b), ("ps", 2)])
                        P.op('dve', lambda e_, h=h, hs=hs, b=b: e_.scalar_tensor_tensor(out=bo2[b][:, hs], in0=G.ps[4][:, hs], scalar=dc[:, 44 + h:45 + h], in1=bo2[b][:, hs],
                                                                                       op0=ALU.mult, op1=ALU.add),
                             r=[("ps", 4), "b_dc", ("b_o2", b)], w=[("b_o2", b), ("ps", 4)])
                    P.op('dve', lambda e_, b=b: e_.tensor_tensor(out=bo[b][:, :], in0=G.ps[1][:, :], in1=bo2[b][:, :], op=ALU.add),
                         r=[("ps", 1), ("b_o2", b)], w=[("b_o", b), ("ps", 1)])
                    P.op('dve', lambda e_, b=b: e_.tensor_tensor(out=bo2[b][:, :], in0=bo[b][:, :], in1=bo[b][:, :], op=ALU.mult), r=[("b_o", b)], w=[("b_o2", b)])
                    P.op('dve', lambda e_, b=b: e_.reduce_sum(out=brs[b][:, 0:4], in_=bo2[b][:, :].rearrange("p (h d) -> p h d", h=4), axis=AX.X), r=[("b_o2", b)], w=[("b_rs", b)])
                    P.op('act', lambda e_, b=b: e_.activation(out=brs[b][:, 4:8], in_=brs[b][:, 0:4], func=AF.Sqrt, scale=1.0 / 128, bias=G.eps_ln[:, 0:1]), r=[("b_rs", b), "consts"], w=[("b_rs", b)])
                    P.op('dve', lambda e_, b=b: e_.reciprocal(out=brs[b][:, 8:12], in_=brs[b][:, 4:8]), r=[("b_rs", b)], w=[("b_rs", b)])
                    for h in range(4):
                        hs = slice(h * 128, (h + 1) * 128)
                        P.op('dve', lambda e_, h=h, hs=hs, b=b: e_.scalar_tensor_tensor(out=bob[b][:, hs], in0=bo[b][:, hs], scalar=brs[b][:, 8 + h:9 + h], in1=gs[b][:, hs],
                                                                                       op0=ALU.mult, op1=ALU.mult),
                             r=[("b_o", b), ("b_rs", b), ("b_gs", b)], w=[("b_ob", b)])
                    P.dma(mo_d[n * 128:(n + 1) * 128, 512:1024], bob[b][:, :], r=[("b_ob", b)], w=[("mo", (n, 1))])
                    if i < len(bwd) - 1:
                        state_update(n, 1, i)
                P.barrier()
            P.barrier()
        with ExitStack() as esO:
            wos = sb(nc, esO, "o_ws", [128, 8, 1024])
            wob = sb(nc, esO, "o_wb", [128, 8, 1024], BF16)
            gn = sb(nc, esO, "o_gn", [128, 4])
            P.dma(wos[:], Wout[:, :].rearrange("(c p) n -> p c n", p=128), w=["o_ws"])
            colload(G, esO, gn[:, :], "o_gn", G.Wl("da_gn_g", e), 4)
            P.op('dve', lambda e_: e_.tensor_scalar_mul(out=gn[:], in0=gn[:], scalar1=1.0 - lam_init), r=["o_gn"], w=["o_gn"])
            for c in range(8):
                if c < 4:
                    P.op('dve', lambda e_, c=c: e_.tensor_scalar(out=wob[:, c, :], in0=wos[:, c, :], scalar1=gn[:, c:c + 1], scalar2=None, op0=ALU.mult),
                         r=["o_ws", "o_gn"], w=[("o_wb", c)])
                else:
                    P.op('pool', lambda e_, c=c: e_.tensor_copy(out=wob[:, c, :], in_=wos[:, c, :]), r=["o_ws"], w=[("o_wb", c)])
            gate = [sb(nc, esO, "o_gate%d" % r_, [128, 1024]) for r_ in range(2)]
            LG = sb(nc, esO, "o_lg", [128, 1024])
            LB = sb(nc, esO, "o_lb", [128, 1024])
            for r_ in range(2):
                P.dma(gate[r_][:], G.modv[l, r_:r_ + 1, 2 * 1024:3 * 1024].partition_broadcast(128), r=[("modv", l)], w=["o_bc"])
            P.dma(LG[:], row(G.Wl("ln1_g", l)).partition_broadcast(128), w=["o_bc"])
            P.dma(LB[:], row(G.Wl("ln1_b", l)).partition_broadcast(128), w=["o_bc"])
            mot = [sb(nc, esO, "o_mo%d" % i, [128, 1024], BF16) for i in range(2)]
            moT = [sb(nc, esO, "o_moT%d" % i, [128, 8, 128], BF16) for i in range(2)]
            tmp = [sb(nc, esO, "o_tmp%d" % i, [128, 1024]) for i in range(2)]
            yt = [sb(nc, esO, "o_y%d" % i, [128, 1024]) for i in range(2)]
            st = [sb(nc, esO, "o_st%d" % i, [128, 16]) for i in range(2)]
            def ld_o(t_):
                b_ = t_ % 2
                P.dma(mot[b_][:], mo_d[t_ * 128:(t_ + 1) * 128, :], r=[("mo", (t_, 0)), ("mo", (t_, 1))], w=[("o_mo", b_)])
                P.dma(xt[b_][:], G.xres[t_ * 128:(t_ + 1) * 128, :], r=[("xres", t_)], w=[("e_x", b_)])
            for t in range(NT):
                b = t % 2
                r_ = 0 if t < 16 else 1
                if t == 0:
                    ld_o(0)
                if t + 1 < NT:
                    ld_o(t + 1)
                for hh in range(2):
                    pbk = 4 + hh
                    psb = G.ps[pbk][:, 0:256].bitcast(BF16)
                    for g in range(4):
                        c = hh * 4 + g
                        P.op('pe', lambda e_, g=g, c=c, psb=psb: e_.transpose(out=psb[:, g * 128:(g + 1) * 128], in_=mot[b][:, c * 128:(c + 1) * 128], identity=G.identB[:]),
                             r=[("o_mo", b), "identB"], w=[("ps", pbk)])
                    P.op('act', lambda e_, hh=hh, psb=psb: e_.copy(out=moT[b][:, hh * 4:hh * 4 + 4, :], in_=psb.rearrange("p (g n) -> p g n", g=4)),
                         r=[("ps", pbk)], w=[("o_moT", b), ("ps", pbk)])
                banks = [G.ps[0 + 2 * b], G.ps[1 + 2 * b]]
                okeys = [("ps", 0 + 2 * b), ("ps", 1 + 2 * b)]
                for hh in range(2):
                    for c in range(8):
                        P.op('pe', lambda e_, hh=hh, c=c: e_.matmul(banks[hh][:, :], lhsT=moT[b][:, c, :], rhs=wob[:, c, hh * 512:(hh + 1) * 512],
                                                                   start=(c == 0), stop=(c == 7)), r=[("o_moT", b), "o_wb"], w=[okeys[hh]])
                ln_epilogue(G, banks, okeys, xt[b], ("e_x", b), gate[r_], LG, LB, ["o_bc"], tmp[b], ("o_tmp", b), st[b], ("o_st", b), yt[b], ("o_y", b))
                P.dma(G.xres[t * 128:(t + 1) * 128, :], yt[b][:], r=[("o_y", b)], w=[("xres", t)])
            P.barrier()
        P.barrier()


TBLK = [(0, 512), (512, 512), (1024, 512), (1536, 512), (2048, 256)]
RW_GN_EPS = 64e-5


def colvec(G, es, name, ap1024, key):
    t = sb(G.nc, es, name, [128, 8])
    colload(G, es, t[:, :], key, ap1024, 8)
    return t


def load_w_bf16(G, es, name, wap, rows, cols, eng='pool'):
    nc, P = G.nc, G.P
    nck = (rows + 127) // 128
    wb = sb(nc, es, name + "_b", [128, nck, cols], BF16)
    if rows % 128 == 0 and cols > 512:
        with ExitStack() as es2:
            ws = sb(nc, es2, name + "_s", [128, nck, 512])
            for h0 in range(0, cols, 512):
                P.dma(ws[:], wap[:, h0:h0 + 512].rearrange("(c p) n -> p c n", p=128), w=[name + "_s"])
                P.op(eng, lambda e, h0=h0: e.tensor_copy(out=wb[:, :, h0:h0 + 512], in_=ws[:]), r=[name + "_s"], w=[name + "_b"])
            P.barrier()
        return wb
    ws = sb(nc, es, name + "_s", [128, nck, cols])
    if rows % 128 == 0:
        P.dma(ws[:], wap.rearrange("(c p) n -> p c n", p=128), w=[name + "_s"])
        P.op(eng, lambda e: e.tensor_copy(out=wb[:], in_=ws[:]), r=[name + "_s"], w=[name + "_b"])
    else:
        for c in range(nck):
            n_ = min(128, rows - c * 128)
            P.dma(ws[0:n_, c, :], wap[c * 128:c * 128 + n_, :], w=[name + "_s"])
        for c in range(nck):
            n_ = min(128, rows - c * 128)
            P.op(eng, lambda e, c=c, n_=n_: e.tensor_copy(out=wb[0:n_, c, :], in_=ws[0:n_, c, :]), r=[name + "_s"], w=[name + "_b"])
    return wb


def fm_linear(G, xT, xkey, wb, wkey, nout, consumer, kparts=None, pbanks=(0, 1)):
    P = G.P
    if kparts is None:
        kparts = [(c, 128) for c in range(8)]
    it = 0
    for oc in range((nout + 127) // 128):
        m_ = min(128, nout - oc * 128)
        for (t0, tn) in TBLK:
            pb = pbanks[it % len(pbanks)]
            it += 1
            for i, (c, kn) in enumerate(kparts):
                P.op('pe', lambda e, c=c, kn=kn, i=i, pb=pb: e.matmul(G.ps[pb][0:m_, 0:tn], lhsT=wb[0:kn, c, oc * 128:oc * 128 + m_], rhs=xT[0:kn, c, t0:t0 + tn],
                                                                   start=(i == 0), stop=(i == len(kparts) - 1)), r=[xkey, wkey], w=[("ps", pb)])
            consumer(oc, t0, tn, G.ps[pb], ("ps", pb), m_)


F32R = mybir.dt.float32r
NCHAIN = 4
SKIP_SCAN = False
SCAN_F32R = False
STAGGER = 6


def rr(ap):
    return ap.bitcast(F32R) if SCAN_F32R else ap


def scan_stage(G):
    nc, P = G.nc, G.P
    FM = G.fm
    C = 64
    NCH = T // C
    with ExitStack() as esS:
        msk = sb(nc, esS, "s_msk", [128, 3, 128])
        P.dma(msk[:], G.KC("k_smask", [128, 3, 128])[:, :, :], w=["s_msk"])
        ones = sb(nc, esS, "s_ones", [128, 64])
        P.op('dve', lambda e: e.memset(ones[:], 1.0), w=["s_ones"])
        identR = sb(nc, esS, "s_identR", [128, 128])
        P.op('dve', lambda e: e.tensor_copy(out=rr(identR[:]), in_=G.identF[:]), r=["ident"], w=["s_identR"])
        NAMES = ("r", "v", "kk", "lw", "kd", "b")
        bufs = []
        for ci in range(NCHAIN):
            B = Ctx()
            B.src = [sb(nc, esS, "s_src%d_%d" % (ci, i), [128, 6, 256]) for i in range(2)]
            B.bd = sb(nc, esS, "s_bd%d" % ci, [128, 5, 128])
            P.op('pool', lambda e, B=B: e.memset(B.bd[:], 0.0), w=[("s_bd", ci)])
            B.cum = sb(nc, esS, "s_cum%d" % ci, [128, 5, 64])
            B.Am = sb(nc, esS, "s_Am%d" % ci, [128, 5, 128])
            B.Ap = [sb(nc, esS, "s_Ap%d_%d" % (ci, i), [128, 2, 128]) for i in range(2)]
            B.VT = sb(nc, esS, "s_VT%d" % ci, [128, 64])
            B.Z = sb(nc, esS, "s_Z%d" % ci, [128, 2, 64])
            B.BT = sb(nc, esS, "s_BT%d" % ci, [128, 2, 128])
            B.S = sb(nc, esS, "s_S%d" % ci, [128, 64])
            B.Sr = sb(nc, esS, "s_Sr%d" % ci, [128, 64])
            B.yo = [sb(nc, esS, "s_yo%d_%d" % (ci, i), [64, 2, 64]) for i in range(4)]
            bufs.append(B)

        def chain(ci, p, d):
            B = bufs[ci]
            P0, P1 = G.ps[2 * ci], G.ps[2 * ci + 1]
            k0, k1 = ("ps", 2 * ci), ("ps", 2 * ci + 1)
            K = lambda nm, sub=None: ("s%d_%s" % (ci, nm), sub)
            fmn = {"r": "rT", "v": "vT", "kk": "kk", "lw": "lw%d" % d, "kd": "kd%d" % d, "b": "b%d" % d}
            P.op('dve', lambda e: e.memset(B.S[:], 0.0), w=[K("S")])
            P.op('act', lambda e: e.copy(out=rr(B.Sr[:, :]), in_=B.S[:, :]), r=[K("S")], w=[K("Sr")])
            R_, K_, B_, A_, V_ = (B.bd[:, i, :] for i in range(5))
            for n in range(NCH):
                if n < TC // C:
                    t0 = TL + n * C if d == 0 else T - (n + 1) * C
                else:
                    m_ = n - TC // C
                    t0 = m_ * C if d == 0 else TL - (m_ + 1) * C
                blk0 = (t0 // 256) * 256
                sbi = (n // 4) % 2

                def blk_start(nb_):
                    n_ = 4 * nb_
                    if n_ < TC // C:
                        t_ = TL + n_ * C if d == 0 else T - (n_ + 1) * C
                    else:
                        mm_ = n_ - TC // C
                        t_ = mm_ * C if d == 0 else TL - (mm_ + 1) * C
                    return (t_ // 256) * 256

                def load_blk(nb_):
                    b0 = blk_start(nb_)
                    for i, nm in enumerate(NAMES):
                        P.dma(B.src[nb_ % 2][:, i, :], FM[fmn[nm]][p, :, b0:b0 + 256], r=[("fm_" + fmn[nm], p)], w=[K("src", nb_ % 2)])
                if n % 4 == 0:
                    nb = n // 4
                    if nb == 0:
                        load_blk(0)
                    if nb + 1 < NCH // 4:
                        load_blk(nb + 1)
                off = t0 - blk0
                sk = K("src", sbi)

                def tsl(i):
                    a_ = B.src[sbi][:, i, off:off + C]
                    return a_ if d == 0 else a_[:, ::-1]
                cm = B.cum
                ck = K("cum")
                P.op('dve', lambda e: e.tensor_tensor_scan(out=cm[:, 0, :], data0=ones[:, :], data1=tsl(3), initial=0.0, op0=ALU.mult, op1=ALU.add), r=[sk, "s_ones"], w=[ck])
                P.op('dve', lambda e: e.tensor_tensor(out=cm[:, 1, :], in0=cm[:, 0, :], in1=tsl(3), op=ALU.subtract), r=[ck, sk], w=[ck])
                P.op('act', lambda e: e.activation(out=cm[:, 2:4, :], in_=cm[:, 0:2, :], func=AF.Exp), r=[ck], w=[ck])
                P.op('act', lambda e: e.activation(out=cm[:, 4, :], in_=cm[:, 0, :], func=AF.Exp, scale=-1.0), r=[ck], w=[ck])
                yield
                bk = K("bd")
                for h in range(2):
                    hp = slice(h * 64, (h + 1) * 64)
                    P.op('dve', lambda e, hp=hp: e.tensor_tensor(out=rr(R_[hp, hp]), in0=tsl(0)[hp, :], in1=cm[hp, 2, :], op=ALU.mult), r=[sk, ck], w=[K("bd", 0)])
                    P.op('dve', lambda e, hp=hp: e.tensor_tensor(out=rr(K_[hp, hp]), in0=tsl(4)[hp, :], in1=cm[hp, 4, :], op=ALU.mult), r=[sk, ck], w=[K("bd", 1)])
                    P.op('dve', lambda e, hp=hp: e.tensor_tensor(out=rr(B_[hp, hp]), in0=tsl(5)[hp, :], in1=cm[hp, 4, :], op=ALU.mult), r=[sk, ck], w=[K("bd", 2)])
                    P.op('dve', lambda e, hp=hp: e.scalar_tensor_tensor(out=rr(A_[hp, hp]), in0=tsl(2)[hp, :], scalar=-1.0, in1=cm[hp, 3, :], op0=ALU.mult, op1=ALU.mult),
                         r=[sk, ck], w=[K("bd", 3)])
                    P.op('act', lambda e, hp=hp: e.copy(out=rr(V_[hp, hp]), in_=tsl(1)[hp, :]), r=[sk], w=[K("bd", 4)])
                yield
                A = B.Am
                specs = [(0, 2, 3, 0), (1, 3, 2, 1), (2, 1, 3, 0), (3, 2, 0, 2), (4, 1, 0, 2)]
                for (ai, li, ri, mi) in specs:
                    pt, pk = (P0, k0) if ai < 4 else (P1, k1)
                    sl = slice((ai % 4) * 128, (ai % 4 + 1) * 128)
                    P.op('pe', lambda e, li=li, ri=ri, pt=pt, sl=sl: e.matmul(pt[:, sl], lhsT=rr(B.bd[:, li, :]), rhs=rr(B.bd[:, ri, :]), start=True, stop=True),
                         r=[K("bd", li), K("bd", ri)], w=[pk])
                P.op('pe', lambda e: e.transpose(out=P1[:, 128:256], in_=V_, identity=G.identF[:]), r=[K("bd", 4), "ident"], w=[k1])
                yield
                for (ai, li, ri, mi) in specs:
                    pt, pk = (P0, k0) if ai < 4 else (P1, k1)
                    sl = slice((ai % 4) * 128, (ai % 4 + 1) * 128)
                    P.op('dve', lambda e, ai=ai, pt=pt, sl=sl, mi=mi: e.tensor_tensor(out=rr(A[:, ai, :]), in0=pt[:, sl], in1=msk[:, mi, :], op=ALU.mult),
                         r=[pk, "s_msk"], w=[K("Am", ai), pk])
                for h in range(2):
                    hp = slice(h * 64, (h + 1) * 64)
                    P.op('act', lambda e, hp=hp, h=h: e.copy(out=rr(B.VT[hp, :]), in_=P1[hp, 128 + h * 64:128 + (h + 1) * 64]), r=[k1], w=[K("VT"), k1])
                yield
                zs = slice(256, 320)
                P.op('pe', lambda e: e.matmul(P1[:, zs], lhsT=rr(A_), rhs=rr(B.Sr[:, :]), start=True, stop=False), r=[K("bd", 3), K("Sr")], w=[k1])
                P.op('pe', lambda e: e.matmul(P1[:, zs], lhsT=rr(A[:, 2, :]), rhs=rr(B.VT[:, :]), start=False, stop=True), r=[K("Am", 2), K("VT")], w=[k1])
                yield
                P.op('act', lambda e: e.copy(out=rr(B.Z[:, 0, :]), in_=P1[:, zs]), r=[k1], w=[K("Z", 0), k1])
                yield
                zc = 0
                curA, curAT = (A[:, 0, :], K("Am", 0)), (A[:, 1, :], K("Am", 1))
                for step in range(6):
                    P.op('pe', lambda e, zc=zc, curA=curA: e.matmul(P1[:, zs], lhsT=rr(curA[0]), rhs=rr(B.Z[:, zc, :]), start=True, stop=True), r=[curA[1], K("Z", zc)], w=[k1])
                    if step < 5:
                        dstt = B.Ap[step % 2]
                        dn = "Ap%d" % (step % 2)
                        P.op('pe', lambda e, curA=curA, curAT=curAT: e.matmul(P0[:, 0:128], lhsT=rr(curAT[0]), rhs=rr(curA[0]), start=True, stop=True), r=[curA[1], curAT[1]], w=[k0])
                        if step < 4:
                            P.op('pe', lambda e, curA=curA, curAT=curAT: e.matmul(P0[:, 128:256], lhsT=rr(curA[0]), rhs=rr(curAT[0]), start=True, stop=True), r=[curA[1], curAT[1]], w=[k0])
                    yield
                    P.op('dve', lambda e, zc=zc: e.tensor_tensor(out=rr(B.Z[:, 1 - zc, :]), in0=P1[:, zs], in1=B.Z[:, zc, :], op=ALU.add), r=[k1, K("Z", zc)], w=[K("Z", 1 - zc), k1])
                    zc = 1 - zc
                    if step < 5:
                        if step < 4:
                            P.op('act', lambda e, dstt=dstt: e.copy(out=rr(dstt[:, :, :]), in_=P0[:, 0:256].rearrange("p (a n) -> p a n", a=2)), r=[k0], w=[K(dn), k0])
                        else:
                            P.op('act', lambda e, dstt=dstt: e.copy(out=rr(dstt[:, 0, :]), in_=P0[:, 0:128]), r=[k0], w=[K(dn), k0])
                        curA, curAT = (dstt[:, 0, :], K(dn)), (dstt[:, 1, :], K(dn))
                    yield
                UT = B.Z[:, zc, :]
                uk = K("Z", zc)
                ys = slice(384, 512)
                P.op('pe', lambda e: e.matmul(P1[0:64, ys], lhsT=rr(B.Sr[:, :]), rhs=rr(R_), start=True, stop=False), r=[K("Sr"), K("bd", 0)], w=[k1])
                P.op('pe', lambda e: e.matmul(P1[0:64, ys], lhsT=rr(UT), rhs=rr(A[:, 3, :]), start=False, stop=False), r=[uk, K("Am", 3)], w=[k1])
                P.op('pe', lambda e: e.matmul(P1[0:64, ys], lhsT=rr(B.VT[:, :]), rhs=rr(A[:, 4, :]), start=False, stop=True), r=[K("VT"), K("Am", 4)], w=[k1])
                if n < NCH - 1:
                    P.op('pe', lambda e: e.transpose(out=P0[:, 256:384], in_=B_, identity=G.identF[:]), r=[K("bd", 2), "ident"], w=[k0])
                    P.op('pe', lambda e: e.transpose(out=P0[:, 384:512], in_=K_, identity=G.identF[:]), r=[K("bd", 1), "ident"], w=[k0])
                yield
                yq = B.yo[n % 4]
                yv = P1[0:64, ys].rearrange("p (h t) -> p h t", h=2)
                yov = yq[:, :, :] if d == 0 else yq[:, :, ::-1]
                P.op('act', lambda e: e.copy(out=yov, in_=yv), r=[k1], w=[K("yo", n % 4), k1])
                P.dma(G.y_d[d, :, 2 * p:2 * p + 2, t0:t0 + C], yq[:, :, :], r=[K("yo", n % 4)], w=[("y_d", (d, p, t0 // 128))])
                if n < NCH - 1:
                    P.op('dve', lambda e: e.tensor_copy(out=rr(B.BT[:, :, :]), in_=P0[:, 256:512].rearrange("p (a n) -> p a n", a=2)), r=[k0], w=[K("BT"), k0])
                    yield
                    P.op('pe', lambda e: e.matmul(P1[:, 0:64], lhsT=rr(B.BT[:, 0, :]), rhs=rr(UT), start=True, stop=False), r=[K("BT"), uk], w=[k1])
                    P.op('pe', lambda e: e.matmul(P1[:, 0:64], lhsT=rr(B.BT[:, 1, :]), rhs=rr(B.VT[:, :]), start=False, stop=True), r=[K("BT"), K("VT")], w=[k1])
                    yield
                    P.op('dve', lambda e: e.tensor_tensor(out=B.S[:, :], in0=B.S[:, :], in1=P1[:, 0:64], op=ALU.add), r=[K("S"), k1], w=[K("S"), k1])
                    P.op('dve', lambda e: e.tensor_scalar(out=B.S[:, :], in0=B.S[:, :], scalar1=cm[:, 2, 63:64], scalar2=None, op0=ALU.mult), r=[K("S"), ck], w=[K("S")])
                    P.op('act', lambda e: e.copy(out=rr(B.Sr[:, :]), in_=B.S[:, :]), r=[K("S")], w=[K("Sr")])
                yield

        todo = [(p, d) for p in range(8) for d in range(2)]
        active = [None] * NCHAIN
        for ci in range(NCHAIN):
            p, d = todo.pop(0)
            active[ci] = chain(ci, p, d)
            for _ in range(ci * STAGGER):
                next(active[ci])
        while todo or any(a is not None for a in active):
            for ci in range(NCHAIN):
                if active[ci] is None and todo:
                    p, d = todo.pop(0)
                    active[ci] = chain(ci, p, d)
                if active[ci] is not None:
                    try:
                        next(active[ci])
                    except StopIteration:
                        active[ci] = None
        P.barrier()


def stage_rwkv(G, l):
    nc, P = G.nc, G.P
    j = l // 2
    FM = G.fm
    with ExitStack() as es:
        mc = load_modcols(G, es, l, "mc")
        BO = sb(nc, es, "r_BO", [128, 128])
        P.dma(BO[:], G.KC("k_bo", [128, 128])[:, :], w=["r_BO"])
        with ExitStack() as esP:
            hT = sb(nc, esP, "r_hT", [128, 8, T], BF16)
            dT = sb(nc, esP, "r_dT", [128, 8, T], BF16)
            xiT = sb(nc, esP, "r_xiT", [128, 8, T], BF16)
            with ExitStack() as esx:
                xt = [sb(nc, esx, "r_x%d" % i, [128, 1024]) for i in range(2)]
                for t in range(NT):
                    b = t % 2
                    P.dma(xt[b][:], G.xres[t * 128:(t + 1) * 128, :], r=[("xres", t)], w=[("r_x", b)])
                    transpose_modulate(G, xt[b], ("r_x", b), hT, ("r_hT", t), t * 128, mc, 0 if t < 16 else 1, 0, 1, 0)
                P.barrier()
            def lat(tile_, c0, c1):
                return tile_[:, c0:c1, 0:TL].rearrange("p c (r w) -> p c r w", w=64)
            hl = lambda c0, c1: lat(hT, c0, c1)
            dl = lambda c0, c1: lat(dT, c0, c1)
            sub = ALU.subtract
            ops = [
                (dl(0, 2)[:, :, :, 1:64], hl(0, 2)[:, :, :, 0:63], hl(0, 2)[:, :, :, 1:64]),
                (dl(2, 4)[:, :, :, 0:63], hl(2, 4)[:, :, :, 1:64], hl(2, 4)[:, :, :, 0:63]),
                (dl(4, 6)[:, :, 1:32, :], hl(4, 6)[:, :, 0:31, :], hl(4, 6)[:, :, 1:32, :]),
                (dl(6, 8)[:, :, 0:31, :], hl(6, 8)[:, :, 1:32, :], hl(6, 8)[:, :, 0:31, :]),
                (dT[:, 0:4, TL + 1:T], hT[:, 0:4, TL:T - 1], hT[:, 0:4, TL + 1:T]),
                (dT[:, 4:8, TL:T - 1], hT[:, 4:8, TL + 1:T], hT[:, 4:8, TL:T - 1]),
            ]
            for (o_, a_, b_) in ops:
                for cc in range(o_.shape[1]):
                    P.op('dve', lambda e, o_=o_, a_=a_, b_=b_, cc=cc: e.tensor_tensor(out=o_[:, cc], in0=a_[:, cc], in1=b_[:, cc], op=sub), r=["r_hT"], w=["r_dT"])
            bnd = [
                (dl(0, 2)[:, :, :, 0:1], hl(0, 2)[:, :, :, 0:1]), (dl(2, 4)[:, :, :, 63:64], hl(2, 4)[:, :, :, 63:64]),
                (dl(4, 6)[:, :, 0:1, :], hl(4, 6)[:, :, 0:1, :]), (dl(6, 8)[:, :, 31:32, :], hl(6, 8)[:, :, 31:32, :]),
                (dT[:, 0:4, TL:TL + 1], hT[:, 0:4, TL:TL + 1]), (dT[:, 4:8, T - 1:T], hT[:, 4:8, T - 1:T]),
            ]
            for (o_, a_) in bnd:
                for cc in range(o_.shape[1]):
                    P.op('dve', lambda e, o_=o_, a_=a_, cc=cc: e.tensor_scalar_mul(out=o_[:, cc], in0=a_[:, cc], scalar1=-1.0), r=["r_hT"], w=["r_dT"])
            mu = sb(nc, esP, "r_mu", [128, 6, 8])
            colload(G, esP, mu[:, :, :].rearrange("p i c -> p (i c)"), "r_mu", G.Wl("rw_mu", j).rearrange("i n -> (i n)"), 48)
            kkc = colvec(G, esP, "r_kkc", G.Wl("rw_kk", j), "r_cv")
            kac = colvec(G, esP, "r_kac", G.Wl("rw_ka", j), "r_cv")
            omka = sb(nc, esP, "r_omka", [128, 8])
            P.op('dve', lambda e: e.tensor_scalar(out=omka[:], in0=kac[:], scalar1=-1.0, scalar2=1.0, op0=ALU.mult, op1=ALU.add), r=["r_cv"], w=["r_cv2"])
            stg = [sb(nc, esP, "r_stg%d" % i, [128, T]) for i in range(2)]
            stg2 = [sb(nc, esP, "r_stg2_0", [128, T])] * 2
            ld1 = [sb(nc, esP, "r_ld1_0", [128, T])] * 2
            ld2 = [sb(nc, esP, "r_ld2_0", [128, T])] * 2
            tmpa = [sb(nc, esP, "r_tmpa%d" % i, [128, 512]) for i in range(2)]
            tmpb = [sb(nc, esP, "r_tmpb%d" % i, [128, 512]) for i in range(2)]
            tcnt = [0]

            def mk_xi(i):
                for c in range(8):
                    P.op('dve', lambda e, c=c: e.scalar_tensor_tensor(out=xiT[:, c, :], in0=dT[:, c, :], scalar=mu[:, i, c:c + 1], in1=hT[:, c, :], op0=ALU.mult, op1=ALU.add),
                         r=["r_dT", "r_hT", "r_mu"], w=["r_xiT"])

            def store(oc, dst, src, skey):
                P.dma(dst[oc, :, :], src[:, :], r=[skey], w=[(dst.name if hasattr(dst, "name") else "fm", oc)])

            mk_xi(0)
            with ExitStack() as esw:
                wb = load_w_bf16(G, esw, "r_wr", G.Wl("rw_wr", j), 1024, 1024)
                def cons_r(oc, t0, tn, ps, pkey, m_):
                    s = stg[oc % 2]
                    P.op('act', lambda e: e.copy(out=s[:, t0:t0 + tn], in_=ps[:, 0:tn]), r=[pkey], w=[("r_stg", oc % 2), pkey])
                    if t0 == 2048:
                        P.dma(FM["rT"][oc, :, :], s[:, :], r=[("r_stg", oc % 2)], w=[("fm_rT", oc)])
                fm_linear(G, xiT, "r_xiT", wb, "r_wr_b", 1024, cons_r)
                P.barrier()
            mk_xi(1)
            for d in range(2):
                with ExitStack() as esw:
                    w1b = load_w_bf16(G, esw, "r_w1", G.Wl("rw_w1", j)[d], 1024, 64)
                    w2b = load_w_bf16(G, esw, "r_w2", G.Wl("rw_w2", j)[d], 64, 1024)
                    w0c = colvec(G, esw, "r_w0c", G.Wl("rw_w0", j)[d], "r_w0c")
                    P.op('dve', lambda e: e.tensor_scalar_mul(out=w0c[:], in0=w0c[:], scalar1=-1.0), r=["r_w0c"], w=["r_w0c"])
                    t1 = sb(nc, esw, "r_t1", [128, 1, T], BF16)
                    def cons_t(oc, t0, tn, ps, pkey, m_):
                        P.op('act', lambda e: e.activation(out=t1[0:m_, 0, t0:t0 + tn], in_=ps[0:m_, 0:tn], func=AF.Tanh), r=[pkey], w=["r_t1", pkey])
                    fm_linear(G, xiT, "r_xiT", w1b, "r_w1_b", 64, cons_t, pbanks=(2, 3))
                    def cons_w(oc, t0, tn, ps, pkey, m_):
                        s = stg[oc % 2]
                        k_ = tcnt[0] % 2
                        tcnt[0] += 1
                        ta, tb_ = tmpa[k_], tmpb[k_]
                        P.op('act', lambda e: e.activation(out=ta[:, 0:tn], in_=ps[:, 0:tn], func=AF.Exp, scale=-1.0, bias=w0c[:, oc:oc + 1]), r=[pkey, "r_w0c"], w=[("r_tmpa", k_), pkey])
                        P.op('act', lambda e: e.activation(out=tb_[:, 0:tn], in_=ta[:, 0:tn], func=AF.Ln, bias=G.one_c[:, 0:1], scale=1.0), r=[("r_tmpa", k_), "consts"], w=[("r_tmpb", k_)])
                        P.op('act', lambda e: e.activation(out=ta[:, 0:tn], in_=tb_[:, 0:tn], func=AF.Exp, scale=-1.0, bias=G.mhalf_c[:, 0:1]), r=[("r_tmpb", k_), "consts"], w=[("r_tmpa", k_)])
                        P.op('dve', lambda e: e.tensor_scalar_mul(out=s[:, t0:t0 + tn], in0=ta[:, 0:tn], scalar1=-1.0), r=[("r_tmpa", k_)], w=[("r_stg", oc % 2)])
                        if t0 == 2048:
                            P.dma(FM["lw%d" % d][oc, :, :], s[:, :], r=[("r_stg", oc % 2)], w=[("fm_lw%d" % d, oc)])
                    fm_linear(G, t1, "r_t1", w2b, "r_w2_b", 1024, cons_w, kparts=[(0, 64)])
                    P.barrier()
            mk_xi(2)
            with ExitStack() as esw:
                wb = load_w_bf16(G, esw, "r_wk", G.Wl("rw_wk", j), 1024, 1024)
                def cons_k(oc, t0, tn, ps, pkey, m_):
                    s, s2 = stg[oc % 2], stg2[oc % 2]
                    k_ = tcnt[0] % 2
                    tcnt[0] += 1
                    ta, tb_ = tmpa[k_], tmpb[k_]
                    P.op('act', lambda e: e.copy(out=s[:, t0:t0 + tn], in_=ps[:, 0:tn]), r=[pkey], w=[("r_stg", oc % 2), pkey])
                    P.op('dve', lambda e: e.tensor_scalar(out=ta[:, 0:tn], in0=s[:, t0:t0 + tn], scalar1=kkc[:, oc:oc + 1], scalar2=None, op0=ALU.mult), r=[("r_stg", oc % 2), "r_cv"], w=[("r_tmpa", k_)])
                    P.op('pool', lambda e: e.tensor_tensor(out=tb_[:, 0:tn], in0=ta[:, 0:tn], in1=ta[:, 0:tn], op=ALU.mult), r=[("r_tmpa", k_)], w=[("r_tmpb", k_)])
                    P.op('pe', lambda e: e.matmul(G.ps[4 + k_][:, 0:tn], lhsT=BO[:, :], rhs=tb_[:, 0:tn], start=True, stop=True), r=["r_BO", ("r_tmpb", k_)], w=[("ps", 4 + k_)])
                    P.op('act', lambda e: e.activation(out=tb_[:, 0:tn], in_=G.ps[4 + k_][:, 0:tn], func=AF.Sqrt), r=[("ps", 4 + k_)], w=[("r_tmpb", k_), ("ps", 4 + k_)])
                    P.op('dve', lambda e: e.tensor_scalar_max(out=tb_[:, 0:tn], in0=tb_[:, 0:tn], scalar1=1e-12), r=[("r_tmpb", k_)], w=[("r_tmpb", k_)])
                    P.op('dve', lambda e: e.reciprocal(out=tb_[:, 0:tn], in_=tb_[:, 0:tn]), r=[("r_tmpb", k_)], w=[("r_tmpb", k_)])
                    P.op('dve', lambda e: e.tensor_tensor(out=s2[:, t0:t0 + tn], in0=ta[:, 0:tn], in1=tb_[:, 0:tn], op=ALU.mult), r=[("r_tmpa", k_), ("r_tmpb", k_)], w=[("r_stg2", 0)])
                    if t0 == 2048:
                        P.dma(FM["kT"][oc, :, :], s[:, :], r=[("r_stg", oc % 2)], w=[("fm_kT", oc)])
                        P.dma(FM["kk"][oc, :, :], s2[:, :], r=[("r_stg2", 0)], w=[("fm_kk", oc)])
                fm_linear(G, xiT, "r_xiT", wb, "r_wk_b", 1024, cons_k)
                P.barrier()
            mk_xi(3)
            with ExitStack() as esw:
                wb = load_w_bf16(G, esw, "r_wv", G.Wl("rw_wv", j), 1024, 1024)
                if j > 0:
                    v1b = load_w_bf16(G, esw, "r_v1", G.Wl("rw_v1", j - 1), 1024, 32)
                    v2b = load_w_bf16(G, esw, "r_v2", G.Wl("rw_v2", j - 1), 32, 1024)
                    v0c = colvec(G, esw, "r_v0c", G.Wl("rw_v0", j - 1), "r_v0c")
                    t1 = sb(nc, esw, "r_t1v", [128, 1, T], BF16)
                    def cons_t(oc, t0, tn, ps, pkey, m_):
                        P.op('act', lambda e: e.copy(out=t1[0:m_, 0, t0:t0 + tn], in_=ps[0:m_, 0:tn]), r=[pkey], w=["r_t1v", pkey])
                    fm_linear(G, xiT, "r_xiT", v1b, "r_v1_b", 32, cons_t, pbanks=(2, 3))
                def cons_v(oc, t0, tn, ps, pkey, m_):
                    s = stg[oc % 2]
                    if j == 0:
                        P.op('act', lambda e: e.copy(out=s[:, t0:t0 + tn], in_=ps[:, 0:tn]), r=[pkey], w=[("r_stg", oc % 2), pkey])
                    else:
                        k_ = tcnt[0] % 2
                        tcnt[0] += 1
                        ta, tb_ = tmpa[k_], tmpb[k_]
                        if t0 == 0:
                            P.dma(ld1[oc % 2][:, :], FM["vf"][oc, :, :], r=[("fm_vf", oc)], w=[("r_ld1", 0)])
                        vf = ld1[oc % 2]
                        pb2 = 4 + k_
                        P.op('pe', lambda e: e.matmul(G.ps[pb2][:, 0:tn], lhsT=v2b[0:32, 0, oc * 128:(oc + 1) * 128], rhs=t1[0:32, 0, t0:t0 + tn], start=True, stop=True),
                             r=["r_t1v", "r_v2_b"], w=[("ps", pb2)])
                        P.op('act', lambda e: e.activation(out=ta[:, 0:tn], in_=G.ps[pb2][:, 0:tn], func=AF.Sigmoid, bias=v0c[:, oc:oc + 1], scale=1.0), r=[("ps", pb2), "r_v0c"], w=[("r_tmpa", k_), ("ps", pb2)])
                        P.op('dve', lambda e: e.tensor_tensor(out=tb_[:, 0:tn], in0=vf[:, t0:t0 + tn], in1=ps[:, 0:tn], op=ALU.subtract), r=[("r_ld1", 0), pkey], w=[("r_tmpb", k_)])
                        P.op('dve', lambda e: e.tensor_tensor(out=tb_[:, 0:tn], in0=tb_[:, 0:tn], in1=ta[:, 0:tn], op=ALU.mult), r=[("r_tmpa", k_), ("r_tmpb", k_)], w=[("r_tmpb", k_)])
                        P.op('dve', lambda e: e.tensor_tensor(out=s[:, t0:t0 + tn], in0=tb_[:, 0:tn], in1=ps[:, 0:tn], op=ALU.add), r=[("r_tmpb", k_), pkey], w=[("r_stg", oc % 2), pkey])
                    if t0 == 2048:
                        P.dma(FM["vT"][oc, :, :], s[:, :], r=[("r_stg", oc % 2)], w=[("fm_vT", oc)])
                        if j == 0:
                            P.dma(FM["vf"][oc, :, :], s[:, :], r=[("r_stg", oc % 2)], w=[("fm_vf", oc)])
                fm_linear(G, xiT, "r_xiT", wb, "r_wv_b", 1024, cons_v)
                P.barrier()
            mk_xi(4)
            for d in range(2):
                with ExitStack() as esw:
                    a1b = load_w_bf16(G, esw, "r_a1", G.Wl("rw_a1", j)[d], 1024, 64)
                    a2b = load_w_bf16(G, esw, "r_a2", G.Wl("rw_a2", j)[d], 64, 1024)
                    a0c = colvec(G, esw, "r_a0c", G.Wl("rw_a0", j)[d], "r_a0c")
                    t1 = sb(nc, esw, "r_t1a", [128, 1, T], BF16)
                    def cons_t(oc, t0, tn, ps, pkey, m_):
                        P.op('act', lambda e: e.copy(out=t1[0:m_, 0, t0:t0 + tn], in_=ps[0:m_, 0:tn]), r=[pkey], w=["r_t1a", pkey])
                    fm_linear(G, xiT, "r_xiT", a1b, "r_a1_b", 64, cons_t, pbanks=(2, 3))
                    def cons_a(oc, t0, tn, ps, pkey, m_):
                        s, s2 = stg[oc % 2], stg2[oc % 2]
                        k_ = tcnt[0] % 2
                        tcnt[0] += 1
                        ta, tb_ = tmpa[k_], tmpb[k_]
                        if t0 == 0:
                            P.dma(ld1[oc % 2][:, :], FM["kT"][oc, :, :], r=[("fm_kT", oc)], w=[("r_ld1", 0)])
                            P.dma(ld2[oc % 2][:, :], FM["kk"][oc, :, :], r=[("fm_kk", oc)], w=[("r_ld2", 0)])
                        kt_, kkt = ld1[oc % 2], ld2[oc % 2]
                        P.op('act', lambda e: e.activation(out=ta[:, 0:tn], in_=ps[:, 0:tn], func=AF.Sigmoid, bias=a0c[:, oc:oc + 1], scale=1.0), r=[pkey, "r_a0c"], w=[("r_tmpa", k_), pkey])
                        P.op('pool', lambda e: e.tensor_tensor(out=s2[:, t0:t0 + tn], in0=kkt[:, t0:t0 + tn], in1=ta[:, 0:tn], op=ALU.mult), r=[("r_ld2", 0), ("r_tmpa", k_)], w=[("r_stg2", 0)])
                        P.op('dve', lambda e: e.tensor_scalar(out=tb_[:, 0:tn], in0=ta[:, 0:tn], scalar1=kac[:, oc:oc + 1], scalar2=omka[:, oc:oc + 1], op0=ALU.mult, op1=ALU.add),
                             r=[("r_tmpa", k_), "r_cv", "r_cv2"], w=[("r_tmpb", k_)])
                        P.op('dve', lambda e: e.tensor_tensor(out=s[:, t0:t0 + tn], in0=tb_[:, 0:tn], in1=kt_[:, t0:t0 + tn], op=ALU.mult), r=[("r_tmpb", k_), ("r_ld1", 0)], w=[("r_stg", oc % 2)])
                        if t0 == 2048:
                            P.dma(FM["kd%d" % d][oc, :, :], s[:, :], r=[("r_stg", oc % 2)], w=[("fm_kd%d" % d, oc)])
                            P.dma(FM["b%d" % d][oc, :, :], s2[:, :], r=[("r_stg2", 0)], w=[("fm_b%d" % d, oc)])
                    fm_linear(G, t1, "r_t1a", a2b, "r_a2_b", 1024, cons_a, kparts=[(0, 64)])
                    P.barrier()
            mk_xi(5)
            with ExitStack() as esw:
                g1b = load_w_bf16(G, esw, "r_g1", G.Wl("rw_g1", j), 1024, 160)
                g2b = load_w_bf16(G, esw, "r_g2", G.Wl("rw_g2", j), 160, 1024)
                t1 = sb(nc, esw, "r_t1g", [128, 2, T], BF16)
                def cons_t(oc, t0, tn, ps, pkey, m_):
                    P.op('act', lambda e: e.activation(out=t1[0:m_, oc, t0:t0 + tn], in_=ps[0:m_, 0:tn], func=AF.Sigmoid), r=[pkey], w=["r_t1g", pkey])
                fm_linear(G, xiT, "r_xiT", g1b, "r_g1_b", 160, cons_t, pbanks=(2, 3))
                def cons_g(oc, t0, tn, ps, pkey, m_):
                    s = stg[oc % 2]
                    P.op('act', lambda e: e.copy(out=s[:, t0:t0 + tn], in_=ps[:, 0:tn]), r=[pkey], w=[("r_stg", oc % 2), pkey])
                    if t0 == 2048:
                        P.dma(FM["gT"][oc, :, :], s[:, :], r=[("r_stg", oc % 2)], w=[("fm_gT", oc)])
                fm_linear(G, t1, "r_t1g", g2b, "r_g2_b", 1024, cons_g, kparts=[(0, 128), (1, 32)])
                P.barrier()
            P.barrier()
        if not SKIP_SCAN:
            scan_stage(G)
        with ExitStack() as esR:
            zT = sb(nc, esR, "o_zT", [128, 8, T], BF16)
            rkc = colvec(G, esR, "o_rkc", G.Wl("rw_rk", j).rearrange("h k -> (h k)"), "o_cv")
            lgc = colvec(G, esR, "o_lgc", G.Wl("rw_lnx_g", j), "o_cv")
            lbc = colvec(G, esR, "o_lbc", G.Wl("rw_lnx_b", j), "o_cv")
            epsg = sb(nc, esR, "o_epsg", [128, 1])
            P.op('dve', lambda e: e.memset(epsg[:], RW_GN_EPS), w=["o_epsg"])
            with ExitStack() as esL:
                L = {nm: [sb(nc, esL, "o_%s%d" % (nm, i), [128, T]) for i in range(2)] for nm in ("y0", "y1", "r", "kd0", "kd1", "v", "g")}
                wa = [sb(nc, esL, "o_wa%d" % i, [128, 512]) for i in range(2)]
                wb_ = [sb(nc, esL, "o_wb%d" % i, [128, 512]) for i in range(2)]
                wc_ = [sb(nc, esL, "o_wc%d" % i, [128, 512]) for i in range(2)]
                it = 0
                def load_pair(p_):
                    b_ = p_ % 2
                    for d in range(2):
                        for h in range(2):
                            P.dma(L["y%d" % d][b_][h * 64:(h + 1) * 64, :], G.y_d[d, :, 2 * p_ + h, :], r=["y_d"], w=[("o_L_y%d" % d, b_)])
                    for nm, fmn in (("r", "rT"), ("kd0", "kd0"), ("kd1", "kd1"), ("v", "vT"), ("g", "gT")):
                        P.dma(L[nm][b_][:, :], FM[fmn][p_, :, :], r=[("fm_" + fmn, p_)], w=[("o_L_" + nm, b_)])
                for p in range(8):
                    b = p % 2
                    if p == 0:
                        load_pair(0)
                    if p + 1 < 8:
                        load_pair(p + 1)
                    for (t0, tn) in TBLK:
                        k_ = it % 2
                        it += 1
                        a_, b2, c_ = wa[k_], wb_[k_], wc_[k_]
                        ka, kb, kc = ("o_wa", k_), ("o_wb", k_), ("o_wc", k_)
                        ts_ = slice(t0, t0 + tn)
                        P.op('dve', lambda e: e.tensor_tensor(out=a_[:, 0:tn], in0=L["y0"][b][:, ts_], in1=L["y1"][b][:, ts_], op=ALU.add), r=[("o_L_y0", b), ("o_L_y1", b)], w=[ka])
                        P.op('pe', lambda e: e.matmul(G.ps[k_][:, 0:tn], lhsT=BO[:, :], rhs=a_[:, 0:tn], start=True, stop=True), r=["r_BO", ka], w=[("ps", k_)])
                        P.op('dve', lambda e: e.scalar_tensor_tensor(out=a_[:, 0:tn], in0=G.ps[k_][:, 0:tn], scalar=-1.0 / 64, in1=a_[:, 0:tn], op0=ALU.mult, op1=ALU.add), r=[("ps", k_), ka], w=[ka, ("ps", k_)])
                        P.op('pool', lambda e: e.tensor_tensor(out=b2[:, 0:tn], in0=a_[:, 0:tn], in1=a_[:, 0:tn], op=ALU.mult), r=[ka], w=[kb])
                        P.op('pe', lambda e: e.matmul(G.ps[2 + k_][:, 0:tn], lhsT=BO[:, :], rhs=b2[:, 0:tn], start=True, stop=True), r=["r_BO", kb], w=[("ps", 2 + k_)])
                        P.op('act', lambda e: e.activation(out=b2[:, 0:tn], in_=G.ps[2 + k_][:, 0:tn], func=AF.Sqrt, scale=1.0 / 64, bias=epsg[:, 0:1]), r=[("ps", 2 + k_), "o_epsg"], w=[kb, ("ps", 2 + k_)])
                        P.op('dve', lambda e: e.reciprocal(out=b2[:, 0:tn], in_=b2[:, 0:tn]), r=[kb], w=[kb])
                        P.op('dve', lambda e: e.tensor_tensor(out=a_[:, 0:tn], in0=a_[:, 0:tn], in1=b2[:, 0:tn], op=ALU.mult), r=[ka, kb], w=[ka])
                        P.op('dve', lambda e: e.tensor_scalar(out=a_[:, 0:tn], in0=a_[:, 0:tn], scalar1=lgc[:, p:p + 1], scalar2=lbc[:, p:p + 1], op0=ALU.mult, op1=ALU.add), r=[ka, "o_cv"], w=[ka])
                        P.op('pool', lambda e: e.tensor_tensor(out=c_[:, 0:tn], in0=L["kd0"][b][:, ts_], in1=L["kd1"][b][:, ts_], op=ALU.add), r=[("o_L_kd0", b), ("o_L_kd1", b)], w=[kc])
                        P.op('dve', lambda e: e.scalar_tensor_tensor(out=c_[:, 0:tn], in0=c_[:, 0:tn], scalar=rkc[:, p:p + 1], in1=L["r"][b][:, ts_], op0=ALU.mult, op1=ALU.mult), r=[kc, "o_cv", ("o_L_r", b)], w=[kc])
                        P.op('pe', lambda e: e.matmul(G.ps[4 + k_][:, 0:tn], lhsT=BO[:, :], rhs=c_[:, 0:tn], start=True, stop=True), r=["r_BO", kc], w=[("ps", 4 + k_)])
                        P.op('dve', lambda e: e.tensor_tensor(out=c_[:, 0:tn], in0=G.ps[4 + k_][:, 0:tn], in1=L["v"][b][:, ts_], op=ALU.mult), r=[("ps", 4 + k_), ("o_L_v", b)], w=[kc, ("ps", 4 + k_)])
                        P.op('dve', lambda e: e.tensor_tensor(out=a_[:, 0:tn], in0=a_[:, 0:tn], in1=c_[:, 0:tn], op=ALU.add), r=[ka, kc], w=[ka])
                        P.op('dve', lambda e: e.tensor_tensor(out=zT[:, p, ts_], in0=a_[:, 0:tn], in1=L["g"][b][:, ts_], op=ALU.mult), r=[ka, ("o_L_g", b)], w=[("o_zT", p)])
                P.barrier()
            wob = load_w_bf16(G, esR, "o_wo", G.Wl("rw_wo", j), 1024, 1024)
            gate = [sb(nc, esR, "o_gate%d" % r_, [128, 1024]) for r_ in range(2)]
            LG = sb(nc, esR, "o_lg", [128, 1024])
            LB = sb(nc, esR, "o_lb", [128, 1024])
            for r_ in range(2):
                P.dma(gate[r_][:], G.modv[l, r_:r_ + 1, 2 * 1024:3 * 1024].partition_broadcast(128), r=[("modv", l)], w=["o_bc"])
            P.dma(LG[:], row(G.Wl("ln1_g", l)).partition_broadcast(128), w=["o_bc"])
            P.dma(LB[:], row(G.Wl("ln1_b", l)).partition_broadcast(128), w=["o_bc"])
            xt = [sb(nc, esR, "o_x%d" % i, [128, 1024]) for i in range(2)]
            tmp = [sb(nc, esR, "o_tmp%d" % i, [128, 1024]) for i in range(2)]
            yt = [sb(nc, esR, "o_y%d" % i, [128, 1024]) for i in range(2)]
            st = [sb(nc, esR, "o_st%d" % i, [128, 16]) for i in range(2)]
            for t in range(NT):
                b = t % 2
                r_ = 0 if t < 16 else 1
                if t == 0:
                    P.dma(xt[0][:], G.xres[0:128, :], r=[("xres", 0)], w=[("o_x", 0)])
                if t + 1 < NT:
                    P.dma(xt[(t + 1) % 2][:], G.xres[(t + 1) * 128:(t + 2) * 128, :], r=[("xres", t + 1)], w=[("o_x", (t + 1) % 2)])
                banks = [G.ps[0 + 2 * b], G.ps[1 + 2 * b]]
                okeys = [("ps", 0 + 2 * b), ("ps", 1 + 2 * b)]
                for hh in range(2):
                    for c in range(8):
                        P.op('pe', lambda e_, hh=hh, c=c: e_.matmul(banks[hh][:, :], lhsT=zT[:, c, t * 128:(t + 1) * 128], rhs=wob[:, c, hh * 512:(hh + 1) * 512],
                                                                   start=(c == 0), stop=(c == 7)), r=["o_zT", "o_wo_b"], w=[okeys[hh]])
                ln_epilogue(G, banks, okeys, xt[b], ("o_x", b), gate[r_], LG, LB, ["o_bc"], tmp[b], ("o_tmp", b), st[b], ("o_st", b), yt[b], ("o_y", b))
                P.dma(G.xres[t * 128:(t + 1) * 128, :], yt[b][:], r=[("o_y", b)], w=[("xres", t)])
            P.barrier()
        P.barrier()


def build(layers=(0, 1, 2, 3), stages=None):
    nc = bass.Bass("TRN2", target_bir_lowering=False)
    G = Ctx()
    G.nc = nc

    def din(name, shape):
        return nc.dram_tensor(name, list(shape), F32, kind="ExternalInput").ap()
    G.x_d = din("x", [TL, D])
    G.ctx_d = din("ctx", [TC, D])
    G.c_d = din("c", [1, D])
    G.cc_d = din("c_ctx", [1, D])
    G.used = {}
    specs = dict(WEIGHT_SPECS)

    def Wl(name, l):
        key = "%s_%d" % (name, l)
        if key not in G.used:
            G.used[key] = (name, l, din(key, specs[name][1:]))
        return G.used[key][2]
    G.Wl = Wl
    G.kc = {}

    def KC(name, shape):
        if name not in G.kc:
            G.kc[name] = din(name, shape)
        return G.kc[name]
    G.KC = KC
    G.ident_d = KC("k_ident", [128, 128])
    G.out_d = nc.dram_tensor("out", [TL, D], F32, kind="ExternalOutput").ap()
    G.outc_d = nc.dram_tensor("outc", [TC, D], F32, kind="ExternalOutput").ap()
    G.xres = nc.dram_tensor("xres", [T, D], F32, kind="Internal").ap()
    G.modv = nc.dram_tensor("modv", [4, 2, 6144], F32, kind="Internal").ap()
    G.mo_d = nc.dram_tensor("mo_d", [T, D], BF16, kind="Internal").ap()
    G.fm = {nm: nc.dram_tensor("fm_" + nm, [8, 128, T], F32, kind="Internal").ap()
            for nm in ("rT", "kT", "vT", "vf", "kk", "gT", "lw0", "lw1", "kd0", "kd1", "b0", "b1")}
    G.y_d = nc.dram_tensor("y_d", [2, 64, 16, T], F32, kind="Internal").ap()
    P = Prog(nc)
    G.P = P
    with ExitStack() as es:
        G.ps = [es.enter_context(nc.psum_tensor("ps%d" % i, [128, 512], F32)) for i in range(8)]
        G.identF = sb(nc, es, "identF", [128, 128])
        G.identB = sb(nc, es, "identB", [128, 128], BF16)
        G.eps_ln = sb(nc, es, "eps_ln", [128, 1])
        P.dma(G.identF[:], G.ident_d[:, :], w=["ident"])
        P.op('dve', lambda e: e.tensor_copy(out=G.identB[:], in_=G.identF[:]), r=["ident"], w=["identB"])
        P.op('dve', lambda e: e.memset(G.eps_ln[:], LN_EPS), w=["consts"])
        G.one_c = sb(nc, es, "one_c", [128, 1])
        G.mhalf_c = sb(nc, es, "mhalf_c", [128, 1])
        P.op('dve', lambda e: e.memset(G.one_c[:], 1.0), w=["consts"])
        P.op('dve', lambda e: e.memset(G.mhalf_c[:], -0.5), w=["consts"])
        for t in range(16):
            P.dma(G.xres[t * 128:(t + 1) * 128, :], G.x_d[t * 128:(t + 1) * 128, :], w=[("xres", t)])
        for t in range(2):
            P.dma(G.xres[TL + t * 128:TL + (t + 1) * 128, :], G.ctx_d[t * 128:(t + 1) * 128, :], w=[("xres", 16 + t)])
        stage_modvec(G, layers)
        for l in layers:
            if l % 2 == 0 and (stages is None or "mix" in stages):
                stage_even(G, l)
            if l % 2 == 1 and (stages is None or "mix" in stages):
                stage_rwkv(G, l)
            if stages is None or "ffn" in stages:
                stage_ffn(G, l)
        for t in range(16):
            P.dma(G.out_d[t * 128:(t + 1) * 128, :], G.xres[t * 128:(t + 1) * 128, :], r=[("xres", t)], w=[("out", t)])
        for t in range(2):
            P.dma(G.outc_d[t * 128:(t + 1) * 128, :], G.xres[TL + t * 128:TL + (t + 1) * 128, :], r=[("xres", 16 + t)], w=[("outc", t)])
        P.barrier()
    P.es.close()
    return nc, P, G


def make_consts():
    k = {"k_ident": np.eye(128, dtype=np.float32)}
    t = np.arange(TL)
    rowi = (t // 64).astype(np.float32)
    coli = (t % 64).astype(np.float32)
    for nm, dim, ng in (("A", 64, 8), ("B", 128, 4)):
        nf = dim // 4
        inv = (10000.0 ** (-np.arange(nf, dtype=np.float32) / nf)).astype(np.float32)
        ang = np.concatenate([rowi[:, None] * inv, coli[:, None] * inv], -1).astype(np.float32)
        k["k_cos" + nm] = np.ascontiguousarray(np.tile(np.cos(ang).astype(np.float32), (1, ng)))
        k["k_sin" + nm] = np.ascontiguousarray(np.tile(np.sin(ang).astype(np.float32), (1, ng)))
    p = np.arange(128, dtype=np.float32)
    k["k_cols"] = np.stack([127 - p, p, p + 1, 128 - p], 1).astype(np.float32)
    jj = p[None, :]
    pp = p[:, None]
    k["k_mats"] = np.ascontiguousarray(np.stack([np.maximum(jj - pp, 0), np.maximum(pp - jj, 0), (jj >= pp).astype(np.float32),
                                                 (jj <= pp).astype(np.float32)], 1).astype(np.float32))
    bo = np.zeros((128, 128), np.float32)
    bo[:64, :64] = 1.0
    bo[64:, 64:] = 1.0
    k["k_bo"] = bo
    i64 = np.arange(64)
    strict = (i64[:, None] < i64[None, :]).astype(np.float32)
    incl = (i64[:, None] <= i64[None, :]).astype(np.float32)
    def bdm(m):
        z = np.zeros((128, 128), np.float32)
        z[:64, :64] = m
        z[64:, 64:] = m
        return z
    k["k_smask"] = np.ascontiguousarray(np.stack([bdm(strict), bdm(strict.T), bdm(incl)], 1))
    return k


def kernel(**inputs):
    nc, _, G = build()
    consts = make_consts()
    in_maps = []
    for b in range(8):
        m = {"x": np.ascontiguousarray(inputs["x"][b]), "ctx": np.ascontiguousarray(inputs["ctx"][b]),
             "c": np.ascontiguousarray(inputs["c"][b:b + 1]), "c_ctx": np.ascontiguousarray(inputs["c_ctx"][None, :])}
        for key, (name, l, _ap) in G.used.items():
            m[key] = np.ascontiguousarray(inputs[name][l])
        for key in G.kc:
            m[key] = consts[key]
        in_maps.append(m)
    res = run_bass_kernel_spmd(nc, in_maps, core_ids=list(range(8)))
    return np.stack([r["out"] for r in res.results], axis=0).astype(np.float32)
```
